# Optimizing a Trainium2 kernel written in Bass

```python
import math
import jax, jax.numpy as jnp
from jax import lax
import numpy as np

D_MODEL = 1024
BATCH = 2
SEQ = 16384
DEPTH = 2
DEC_BATCH = 8
DEC_SEQ = 8192
PAST_LEN = 128

CONV_CH = 512
CONV_W = 3
RET_HEADS = 4
RET_DIM = 128
RET_WIDTH = RET_HEADS * RET_DIM
RET_CHUNK = 128
DIFF_HEADS = 8
DIFF_DIM = 64
Q_BLOCK = 128
D_FF = ((8 * D_MODEL + 3 * 256 - 1) // (3 * 256)) * 256
ROPE_THETA = 10000.0
NORM_EPS = 1e-6
GN_EPS = 1e-5
SUBLN_EPS = 1e-5
IN_PROJ_COLS = 3 * CONV_CH + 4 * RET_WIDTH
IN_SPLITS = [CONV_CH, 2 * CONV_CH, 3 * CONV_CH, 3 * CONV_CH + RET_WIDTH,
             3 * CONV_CH + 2 * RET_WIDTH, 3 * CONV_CH + 3 * RET_WIDTH]
DIFF_QK = 2 * DIFF_HEADS * DIFF_DIM
DIFF_V = DIFF_HEADS * 2 * DIFF_DIM
N_EVEN = (DEPTH + 1) // 2
N_ODD = DEPTH // 2

kernel_name = "hybrid_conv_retention_diffattn_encoder"

F32 = jnp.float32


def _rmsnorm(x, w, eps=NORM_EPS):
    xf = x.astype(F32)
    y = xf * lax.rsqrt(jnp.mean(xf * xf, axis=-1, keepdims=True) + eps)
    return y.astype(x.dtype) * w


def _rope(x):
    s, d = x.shape[-2], x.shape[-1]
    inv = ROPE_THETA ** (-jnp.arange(0, d, 2, dtype=F32) / d)
    ang = jnp.arange(s, dtype=F32)[:, None] * inv[None, :]
    cos = jnp.cos(ang).astype(x.dtype)
    sin = jnp.sin(ang).astype(x.dtype)
    x1, x2 = x[..., : d // 2], x[..., d // 2:]
    return jnp.concatenate([x1 * cos - x2 * sin, x1 * sin + x2 * cos], axis=-1)


def _retention_one_dir(q, k, v, log_gamma, strict):
    b, h, s, d = q.shape
    c = RET_CHUNK
    n = s // c
    dt = q.dtype
    qc = q.reshape(b, h, n, c, d)
    kc = k.reshape(b, h, n, c, d)
    vc = v.reshape(b, h, n, c, d)
    idx = jnp.arange(c, dtype=F32)
    rel = idx[:, None] - idx[None, :]
    mask = rel > 0 if strict else rel >= 0
    lg = log_gamma.astype(F32)
    decay_intra = jnp.where(mask[None], jnp.exp(lg[:, None, None] * jnp.maximum(rel, 0.0)[None]), 0.0).astype(dt)
    scores = jnp.einsum('bhnid,bhnjd->bhnij', qc, kc) * decay_intra[None, :, None]
    o_intra = jnp.einsum('bhnij,bhnjd->bhnid', scores, vc)
    k_decay = jnp.exp(lg[:, None] * (c - 1 - idx)[None]).astype(dt)
    kv = jnp.einsum('bhnjd,hj,bhnje->bhnde', kc, k_decay, vc)
    chunk_decay = jnp.exp(lg * c).astype(dt)[None, :, None, None]

    def step(state, kv_t):
        return state * chunk_decay + kv_t, state

    _, s_prev = lax.scan(step, jnp.zeros((b, h, d, d), dt), jnp.moveaxis(kv, 2, 0))
    s_prev = jnp.moveaxis(s_prev, 0, 2)
    q_decay = jnp.exp(lg[:, None] * (idx + 1.0)[None]).astype(dt)
    o_cross = jnp.einsum('bhnid,hi,bhnde->bhnie', qc, q_decay, s_prev)
    return (o_intra + o_cross).reshape(b, h, s, d)


def _hybrid_conv_retention(xn, w_in, conv_w, decay_fwd, decay_bwd, gn_w, w_out):
    b, s, _ = xn.shape
    proj = xn @ w_in
    a_b, a_c, a_h, r_q, r_k, r_v, r_g = jnp.split(proj, IN_SPLITS, axis=-1)
    u = a_c * a_h
    up = jnp.pad(u, ((0, 0), (1, 1), (0, 0)))
    conv = conv_w[0] * up[:, :-2] + conv_w[1] * up[:, 1:-1] + conv_w[2] * up[:, 2:]
    y_a = a_b * conv
    def heads(t):
        return t.reshape(b, s, RET_HEADS, RET_DIM).transpose(0, 2, 1, 3)
    q = _rope(heads(r_q))
    k = _rope(heads(r_k)) * (RET_DIM ** -0.5)
    v = heads(r_v)
    lg_f = -jnp.exp(decay_fwd.astype(F32))
    lg_b = -jnp.exp(decay_bwd.astype(F32))
    o_f = _retention_one_dir(q, k, v, lg_f, strict=False)
    o_b = jnp.flip(_retention_one_dir(jnp.flip(q, 2), jnp.flip(k, 2), jnp.flip(v, 2), lg_b, strict=True), 2)
    o = (o_f + o_b).astype(F32)
    mu = jnp.mean(o, axis=-1, keepdims=True)
    var = jnp.mean(jnp.square(o - mu), axis=-1, keepdims=True)
    o = ((o - mu) * lax.rsqrt(var + GN_EPS)).astype(xn.dtype)
    o = o.transpose(0, 2, 1, 3).reshape(b, s, RET_WIDTH) * gn_w
    y_b = jax.nn.silu(r_g) * o
    return jnp.concatenate([y_a, y_b], axis=-1) @ w_out


def _diff_attention(xn, w_qkv, lq1, lk1, lq2, lk2, subln, w_out, lambda_init):
    b, s, _ = xn.shape
    qkv = xn @ w_qkv
    q, k, v = jnp.split(qkv, [DIFF_QK, 2 * DIFF_QK], axis=-1)
    q = q.reshape(b, s, 2 * DIFF_HEADS, DIFF_DIM).transpose(0, 2, 1, 3)
    k = k.reshape(b, s, 2 * DIFF_HEADS, DIFF_DIM).transpose(0, 2, 1, 3)
    v = v.reshape(b, s, DIFF_HEADS, 2 * DIFF_DIM).transpose(0, 2, 1, 3)
    q = _rope(q) * (DIFF_DIM ** -0.5)
    k = _rope(k)
    lam = (jnp.exp(jnp.sum(lq1.astype(F32) * lk1.astype(F32)))
           - jnp.exp(jnp.sum(lq2.astype(F32) * lk2.astype(F32))) + lambda_init)
    nb = s // Q_BLOCK
    qb = q.reshape(b, 2 * DIFF_HEADS, nb, Q_BLOCK, DIFF_DIM).transpose(2, 0, 1, 3, 4)

    def block(qblk):
        sc = jnp.einsum('bhqd,bhkd->bhqk', qblk, k).astype(F32)
        p = jax.nn.softmax(sc, axis=-1).reshape(b, DIFF_HEADS, 2, Q_BLOCK, s)
        a = p[:, :, 0] - lam * p[:, :, 1]
        return jnp.einsum('bhqk,bhke->bhqe', a.astype(v.dtype), v)

    o = lax.map(block, qb)
    o = o.transpose(1, 2, 0, 3, 4).reshape(b, DIFF_HEADS, s, 2 * DIFF_DIM)
    o = _rmsnorm(o, subln, SUBLN_EPS) * (1.0 - lambda_init)
    o = o.transpose(0, 2, 1, 3).reshape(b, s, DIFF_V)
    return o @ w_out


def _swiglu(xn, w_gate, w_up, w_down):
    return (jax.nn.silu(xn @ w_gate) * (xn @ w_up)) @ w_down


def _trunk(x, norm_mix, norm_ffn, norm_final, hyb_w_in, hyb_conv_w, hyb_decay_fwd, hyb_decay_bwd,
           hyb_gn, hyb_w_out, diff_w_qkv, diff_lq1, diff_lk1, diff_lq2, diff_lk2, diff_subln,
           diff_w_out, ffn_w_gate, ffn_w_up, ffn_w_down):
    for layer in range(DEPTH):
        xn = _rmsnorm(x, norm_mix[layer])
        if layer % 2 == 0:
            e = layer // 2
            x = x + _hybrid_conv_retention(xn, hyb_w_in[e], hyb_conv_w[e], hyb_decay_fwd[e],
                                           hyb_decay_bwd[e], hyb_gn[e], hyb_w_out[e])
        else:
            o = layer // 2
            lambda_init = 0.8 - 0.6 * math.exp(-0.3 * layer)
            x = x + _diff_attention(xn, diff_w_qkv[o], diff_lq1[o], diff_lk1[o], diff_lq2[o],
                                    diff_lk2[o], diff_subln[o], diff_w_out[o], lambda_init)
        x = x + _swiglu(_rmsnorm(x, norm_ffn[layer]), ffn_w_gate[layer], ffn_w_up[layer], ffn_w_down[layer])
    return _rmsnorm(x, norm_final)


def setup_inputs(seed: int = 0) -> dict:
    key = jax.random.key(seed)
    ks = jax.random.split(key, 24)

    def nrm(k, shape, fan_in):
        return jax.random.normal(k, shape, F32) * (fan_in ** -0.5)

    def gain(k, shape):
        return 1.0 + 0.02 * jax.random.normal(k, shape, F32)

    base_decay = jnp.log(-jnp.log1p(-(2.0 ** (-5.0 - jnp.arange(RET_HEADS, dtype=F32)))))
    return {
        "x_prompt": jax.random.normal(ks[0], (BATCH, SEQ, D_MODEL), F32),
        "x_sample": jax.random.normal(ks[1], (DEC_BATCH, DEC_SEQ, D_MODEL), F32),
        "norm_mix": gain(ks[2], (DEPTH, D_MODEL)),
        "norm_ffn": gain(ks[3], (DEPTH, D_MODEL)),
        "norm_final": gain(ks[4], (D_MODEL,)),
        "hyb_w_in": nrm(ks[5], (N_EVEN, D_MODEL, IN_PROJ_COLS), D_MODEL),
        "hyb_conv_w": nrm(ks[6], (N_EVEN, CONV_W, CONV_CH), CONV_W),
        "hyb_decay_fwd": base_decay[None] + 0.05 * jax.random.normal(ks[7], (N_EVEN, RET_HEADS), F32),
        "hyb_decay_bwd": base_decay[None] + 0.05 * jax.random.normal(ks[8], (N_EVEN, RET_HEADS), F32),
        "hyb_gn": gain(ks[9], (N_EVEN, RET_WIDTH)),
        "hyb_w_out": nrm(ks[10], (N_EVEN, CONV_CH + RET_WIDTH, D_MODEL), CONV_CH + RET_WIDTH),
        "diff_w_qkv": nrm(ks[11], (N_ODD, D_MODEL, 2 * DIFF_QK + DIFF_V), D_MODEL),
        "diff_lq1": 0.1 * jax.random.normal(ks[12], (N_ODD, DIFF_DIM), F32),
        "diff_lk1": 0.1 * jax.random.normal(ks[13], (N_ODD, DIFF_DIM), F32),
        "diff_lq2": 0.1 * jax.random.normal(ks[14], (N_ODD, DIFF_DIM), F32),
        "diff_lk2": 0.1 * jax.random.normal(ks[15], (N_ODD, DIFF_DIM), F32),
        "diff_subln": gain(ks[16], (N_ODD, 2 * DIFF_DIM)),
        "diff_w_out": nrm(ks[17], (N_ODD, DIFF_V, D_MODEL), DIFF_V),
        "ffn_w_gate": nrm(ks[18], (DEPTH, D_MODEL, D_FF), D_MODEL),
        "ffn_w_up": nrm(ks[19], (DEPTH, D_MODEL, D_FF), D_MODEL),
        "ffn_w_down": nrm(ks[20], (DEPTH, D_FF, D_MODEL), D_FF),
    }


def reference(x_prompt, x_sample, norm_mix, norm_ffn, norm_final, hyb_w_in, hyb_conv_w,
              hyb_decay_fwd, hyb_decay_bwd, hyb_gn, hyb_w_out, diff_w_qkv, diff_lq1, diff_lk1,
              diff_lq2, diff_lk2, diff_subln, diff_w_out, ffn_w_gate, ffn_w_up, ffn_w_down):
    y_prompt = _trunk(x_prompt, norm_mix, norm_ffn, norm_final, hyb_w_in, hyb_conv_w, hyb_decay_fwd,
                      hyb_decay_bwd, hyb_gn, hyb_w_out, diff_w_qkv, diff_lq1, diff_lk1, diff_lq2,
                      diff_lk2, diff_subln, diff_w_out, ffn_w_gate, ffn_w_up, ffn_w_down)
    y_sample = _trunk(x_sample, norm_mix, norm_ffn, norm_final, hyb_w_in, hyb_conv_w, hyb_decay_fwd,
                      hyb_decay_bwd, hyb_gn, hyb_w_out, diff_w_qkv, diff_lq1, diff_lk1, diff_lq2,
                      diff_lk2, diff_subln, diff_w_out, ffn_w_gate, ffn_w_up, ffn_w_down)
    return (y_prompt, y_sample)
```

```python
import math
from contextlib import ExitStack
import numpy as np
import concourse.bass as bass
import concourse.mybir as mybir
from concourse.bass_utils import run_bass_kernel_spmd

F32 = mybir.dt.float32
BF16 = mybir.dt.bfloat16
I32 = mybir.dt.int32
AF = mybir.ActivationFunctionType
ALU = mybir.AluOpType
AX = mybir.AxisListType

D = 1024
DFF = 2816
NFC = DFF // 128
LAMBDA_INIT = 0.8 - 0.6 * math.exp(-0.3 * 1)
ENG = ("pe", "act", "dve", "pool", "sp")


class Buf:
    __slots__ = ("w", "r", "excl")

    def __init__(self, excl=False):
        self.w = None
        self.r = {}
        self.excl = excl


class Op:
    __slots__ = ("eng", "fn", "deps", "chan", "signal", "val")

    def __init__(self, eng, fn, deps, chan):
        self.eng = eng
        self.fn = fn
        self.deps = deps
        self.chan = chan
        self.signal = chan is not None
        self.val = None


class Sched:
    def __init__(self, nc, es, nchan=40):
        self.nc = nc
        self.ops = []
        self.emitted = 0
        self.last = {}
        self.lastchan = {}
        self.bar = ()
        self.esem = {e: es.enter_context(nc.semaphore("s_" + e)) for e in ENG}
        self.freechan = [es.enter_context(nc.semaphore("c%d" % i)) for i in range(nchan)]
        self.csem = {}
        self.cnt = {e: 0 for e in ENG}
        self.ccnt = {}
        self.waited = {e: {} for e in ENG}
        self.dram = {}

    def dbuf(self, *key):
        b = self.dram.get(key)
        if b is None:
            b = self.dram[key] = Buf()
        return b

    stopped = False
    maxops = 10 ** 9
    names = {}

    def add(self, eng, fn, R=(), W=(), chan=None):
        if self.stopped:
            return -1
        if len(self.ops) >= self.maxops:
            self.stopped = True
            return -1
        deps = set(self.bar)
        key0 = chan if chan is not None else eng
        for b in R:
            if b.w is not None:
                deps.add(b.w)
            if b.excl:
                deps.update(v for k, v in b.r.items() if k != key0)
        for b in W:
            if b.w is not None:
                deps.add(b.w)
            deps.update(b.r.values())
        idx = len(self.ops)
        if chan is not None:
            if chan in self.lastchan:
                deps.add(self.lastchan[chan])
            self.lastchan[chan] = idx
        self.ops.append(Op(eng, fn, deps, chan))
        key = chan if chan is not None else eng
        for b in R:
            b.r[key] = idx
        for b in W:
            b.w = idx
            b.r = {}
        self.last[eng] = idx
        return idx

    def _event(self, op):
        if op.chan is not None:
            return self.csem[op.chan], op.val
        return self.esem[op.eng], op.val

    def flush(self):
        nc = self.nc
        lo = self.emitted
        if lo == len(self.ops):
            return
        ops = self.ops
        for i in range(lo, len(ops)):
            for d in ops[i].deps:
                if d >= lo:
                    od = ops[d]
                    if od.chan is None and od.eng == "pe" and ops[i].eng == "pe" and ops[i].chan is None:
                        continue
                    od.signal = True
        for e, i in self.last.items():
            ops[i].signal = True
        for i in range(lo, len(ops)):
            op = ops[i]
            if op.chan is not None:
                if op.chan not in self.csem:
                    self.csem[op.chan] = self.freechan.pop()
                    self.ccnt[op.chan] = 0
                self.ccnt[op.chan] += 16
                op.val = self.ccnt[op.chan]
            elif op.signal:
                self.cnt[op.eng] += 1
                op.val = self.cnt[op.eng]
        per = {e: [] for e in ENG}
        for i in range(lo, len(ops)):
            per[ops[i].eng].append(i)

        def emit(engname, e):
            waited = self.waited[engname]
            for i in per[engname]:
                op = ops[i]
                for d in sorted(op.deps):
                    od = ops[d]
                    if d < lo and d not in self.bar:
                        continue
                    if od.chan is None and od.eng == "pe" and engname == "pe" and op.chan is None:
                        continue
                    sem, val = self._event(od)
                    k = id(sem)
                    if waited.get(k, 0) < val:
                        e.wait_ge(sem, val)
                        waited[k] = val
                ins = op.fn(e)
                if op.signal:
                    sem, val = self._event(op)
                    ins.then_inc(sem, 16 if op.chan is not None else 1)

        with nc.Block() as block:
            @block.tensor
            def _(e):
                emit("pe", e)

            @block.scalar
            def _(e):
                emit("act", e)

            @block.vector
            def _(e):
                emit("dve", e)

            @block.gpsimd
            def _(e):
                emit("pool", e)

            @block.sync
            def _(e):
                emit("sp", e)
        self.emitted = len(ops)
        self.bar = tuple(set(list(self.last.values()) + list(self.lastchan.values())))
        for i in self.bar:
            assert ops[i].signal


class Pool:
    def __init__(self, es, alloc, name, shape, dt, n):
        self.t = [es.enter_context(alloc("%s%d" % (name, i), shape, dt)) for i in range(n)]
        self.b = [Buf(excl=(alloc.__name__ == 'pst')) for _ in range(n)]
        self.i = 0

    def get(self):
        k = self.i % len(self.t)
        self.i += 1
        return self.t[k], self.b[k]


class _Stop(Exception):
    pass


def build(SA, SB, NQB, debug=False, nph=99):
    nc = bass.Bass("TRN2", target_bir_lowering=False)
    dt_in = lambda n, s, d=F32: nc.dram_tensor(n, s, d, kind="ExternalInput").ap()
    dt_scr = lambda n, s, d: nc.dram_tensor(n, s, d, kind="Internal").ap()
    SQB = NQB * 128
    jobs = [dict(n="a", S=SA, SQ=SA, own=False), dict(n="b", S=SB, SQ=SQB, own=True)]
    xin_d = {"a": dt_in("xa", [SA, D]), "b": dt_in("xb", [SB, D])}
    idxq_d = dt_in("idxq", [128, NQB], I32)
    SM = max(SA, SB)
    rt_d = dt_in("rt", [SM, 256])
    dtk_d = dt_in("dtk", [SM, 64])
    dtq_d = {"a": dt_in("dtqa", [SA, 64]), "b": dt_in("dtqb", [SQB, 64])}
    w_in_d = dt_in("w_in", [D, 3584])
    w_out_d = dt_in("w_out", [D, D])
    w_qkv_d = dt_in("w_qkv", [D, 3072])
    w_do_d = dt_in("w_do", [D, D])
    w_g_d = [dt_in("w_g%d" % l, [D, DFF]) for l in range(2)]
    w_u_d = [dt_in("w_u%d" % l, [D, DFF]) for l in range(2)]
    w_d_d = [dt_in("w_d%d" % l, [DFF, D]) for l in range(2)]
    nmix_d = dt_in("nmix", [128, 16])
    nffn_d = dt_in("nffn", [128, 16])
    nfin_d = dt_in("nfin", [1, D])
    convw_d = dt_in("convw", [128, 12])
    decf_d = dt_in("decf", [1, 4])
    decb_d = dt_in("decb", [1, 4])
    gn_d = dt_in("gnw", [128, 4])
    lam_d = dt_in("lamv", [1, 256])
    subln_d = dt_in("subln", [128, 1])
    y_d = {j["n"]: nc.dram_tensor("y" + j["n"], [j["SQ"], D], F32, kind="ExternalOutput").ap() for j in jobs}
    scr = {}
    for j in jobs:
        n, S, SQ = j["n"], j["S"], j["SQ"]
        scr[n] = dict(
            uT=dt_scr("uT" + n, [512, S + 2], F32),
            kr=dt_scr("kr" + n, [S, 512], BF16),
            vr=dt_scr("vr" + n, [S, 512], BF16),
            sb=dt_scr("sb" + n, [S // 128, 128, 512], BF16),
            x1=dt_scr("x1" + n, [S, D], F32),
            x2=dt_scr("x2" + n, [S, D], F32),
            QT=dt_scr("QT" + n, [8, 128, SQ], BF16),
            KT=dt_scr("KT" + n, [8, 128, S], BF16),
            V=dt_scr("V" + n, [8, S, 128], BF16),
            O=dt_scr("O" + n, [SQ, D], BF16),
            x3=dt_scr("x3" + n, [SQ, D], F32),
        )
    dbg = {}
    if debug:
        for j in jobs:
            n, S, SQ = j["n"], j["S"], j["SQ"]
            dbg[n] = dict(
                x1=nc.dram_tensor("dbg_x1" + n, [S, D], F32, kind="ExternalOutput").ap(),
                x2=nc.dram_tensor("dbg_x2" + n, [S, D], F32, kind="ExternalOutput").ap(),
                x3=nc.dram_tensor("dbg_x3" + n, [SQ, D], F32, kind="ExternalOutput").ap(),
            )

    with ExitStack() as es:
        sch = Sched(nc, es)
        import os
        sch.maxops = int(os.environ.get('KSTOP', 10 ** 9))
        S_ = sch
        _uid = [0]

        def sb(name, shape, dt):
            _uid[0] += 1
            return nc.sbuf_tensor("s%d_%s" % (_uid[0], name), shape, dt)

        def pst(name, shape, dt):
            _uid[0] += 1
            return nc.psum_tensor("p%d_%s" % (_uid[0], name), shape, dt)
        ident = es.enter_context(sb("ident", [128, 128], BF16))
        epsN = es.enter_context(sb("epsN", [128, 4], F32))
        nmix = es.enter_context(sb("nmix", [128, 16], F32))
        nffn = es.enter_context(sb("nffn", [128, 16], F32))
        gnw = es.enter_context(sb("gnw", [128, 4], F32))
        subl = es.enter_context(sb("subl", [128, 1], F32))
        convw = es.enter_context(sb("convw", [128, 12], F32))
        idxq = es.enter_context(sb("idxq", [128, NQB], I32))
        lamt = es.enter_context(sb("lamt", [128, 8], F32))
        consts = Buf()

        def chk(k):
            if nph == k:
                sch.stopped = True

        try:
            with ExitStack() as ps:
                tmpa = ps.enter_context(sb("tmpa", [128, 128], F32))
                lamv = ps.enter_context(sb("lamv", [128, 256], F32))
                lamp = ps.enter_context(sb("lamp", [128, 128], F32))
                S_.add("pool", lambda e: e.iota(tmpa[:], pattern=[[1, 128]], base=0, channel_multiplier=-1,
                                                allow_small_or_imprecise_dtypes=True), W=[consts])
                S_.add("dve", lambda e: e.tensor_single_scalar(out=ident[:], in_=tmpa[:], scalar=0.0, op=ALU.is_equal),
                       R=[consts], W=[consts])
                S_.add("dve", lambda e: e.memset(epsN[:, 0:1], 1e-6), W=[consts])
                S_.add("dve", lambda e: e.memset(epsN[:, 1:3], 1e-5), W=[consts])
                S_.add("dve", lambda e: e.memset(epsN[:, 3:4], 0.0), W=[consts])
                for t, d in ((nmix, nmix_d), (nffn, nffn_d), (gnw, gn_d), (subl, subln_d), (convw, convw_d), (idxq, idxq_d)):
                    S_.add("sp", (lambda t, d: lambda e: e.dma_start(out=t[:], in_=d[:, :]))(t, d), W=[consts], chan="set")
                S_.add("sp", lambda e: e.dma_start(out=lamv[:], in_=lam_d.partition_broadcast(128)), W=[consts], chan="set")
                S_.add("dve", lambda e: e.tensor_single_scalar(out=subl[:], in_=subl[:], scalar=1.0 - LAMBDA_INIT, op=ALU.mult),
                       R=[consts], W=[consts])
                S_.add("dve", lambda e: e.tensor_tensor(out=lamp[:, 0:64], in0=lamv[:, 0:64], in1=lamv[:, 64:128], op=ALU.mult),
                       R=[consts], W=[consts])
                S_.add("dve", lambda e: e.tensor_tensor(out=lamp[:, 64:128], in0=lamv[:, 128:192], in1=lamv[:, 192:256], op=ALU.mult),
                       R=[consts], W=[consts])
                S_.add("dve", lambda e: e.reduce_sum(out=lamt[:, 1:2], in_=lamp[:, 0:64], axis=AX.X), R=[consts], W=[consts])
                S_.add("dve", lambda e: e.reduce_sum(out=lamt[:, 2:3], in_=lamp[:, 64:128], axis=AX.X), R=[consts], W=[consts])
                S_.add("act", lambda e: e.activation(out=lamt[:, 3:5], in_=lamt[:, 1:3], func=AF.Exp), R=[consts], W=[consts])
                S_.add("dve", lambda e: e.tensor_tensor(out=lamt[:, 5:6], in0=lamt[:, 4:5], in1=lamt[:, 3:4], op=ALU.subtract),
                       R=[consts], W=[consts])
                S_.add("dve", lambda e: e.tensor_single_scalar(out=lamt[:, 0:1], in_=lamt[:, 5:6], scalar=-LAMBDA_INIT, op=ALU.add),
                       R=[consts], W=[consts])
                S_.flush(); chk(0)

            def load_weight(dst, src, KC, C, rows, stg, sidx):
                srcv = src.rearrange("(kc p) c -> p kc c", p=128)
                wb = Buf()
                for kc in range(KC):
                    for c0 in range(0, C, 1024):
                        cw = min(1024, C - c0)
                        st, stb = stg.get()
                        S_.add("sp", (lambda st, kc, c0, cw: lambda e: e.dma_start(out=st[:, 0:cw], in_=srcv[:, kc, c0:c0 + cw]))(st, kc, c0, cw),
                               W=[stb], chan="wl%d" % (sidx[0] % 3))
                        eng = ("dve", "pool")[sidx[0] % 2]
                        sidx[0] += 1
                        r = rows(kc) if rows is not None else None
                        if r is None:
                            S_.add(eng, (lambda st, kc, c0, cw: lambda e: e.tensor_copy(out=dst[:, kc, c0:c0 + cw], in_=st[:, 0:cw]))(st, kc, c0, cw),
                                   R=[stb, consts], W=[wb])
                        else:
                            S_.add(eng, (lambda st, kc, c0, cw, r: lambda e: e.tensor_scalar(
                                out=dst[:, kc, c0:c0 + cw], in0=st[:, 0:cw], scalar1=r, scalar2=None, op0=ALU.mult))(st, kc, c0, cw, r),
                                   R=[stb, consts], W=[wb])
                return wb

            class NormT:
                def __init__(self, es_, psB):
                    self.junk = Pool(es_, sb, "njunk", [128, 1024], BF16, 2)
                    self.xs = Pool(es_, sb, "nxs", [128, 1024], BF16, 2)
                    self.st = Pool(es_, sb, "nst", [128, 4], F32, 4)
                    self.psB = psB

                def run(self, x, xb, dst, dstb, dst_is_write=True):
                    jk, jkb = self.junk.get()
                    xs, xsb = self.xs.get()
                    st, stb = self.st.get()
                    S_.add("act", lambda e: e.activation(out=jk[:], in_=x, func=AF.Square, accum_out=st[:, 0:1]), R=[xb], W=[jkb, stb])
                    S_.add("act", lambda e: e.activation(out=st[:, 1:2], in_=st[:, 0:1], func=AF.Sqrt, bias=epsN[:, 0:1], scale=1.0 / D),
                           R=[stb, consts], W=[stb])
                    S_.add("dve", lambda e: e.reciprocal(out=st[:, 2:3], in_=st[:, 1:2]), R=[stb], W=[stb])
                    S_.add("dve", lambda e: e.tensor_scalar(out=xs[:], in0=x, scalar1=st[:, 2:3], scalar2=None, op0=ALU.mult),
                           R=[xb, stb], W=[xsb])
                    pt, ptb = self.psB.get()

                    def tr(e):
                        for kc in range(8):
                            ins = e.transpose(out=pt[:, kc * 128:(kc + 1) * 128], in_=xs[:, kc * 128:(kc + 1) * 128], identity=ident[:])
                        return ins
                    S_.add("pe", tr, R=[xsb, consts], W=[ptb])
                    S_.add("act", lambda e: e.copy(out=dst, in_=pt[:].rearrange("p (k t) -> p k t", k=8)), R=[ptb], W=[dstb])
                    return st, stb

            def rope(eng1, eng2, src, srcb, H, half, cos, sin, tb, A, Ab, T, Tb, out, outb):
                v4 = lambda ap: ap.rearrange("p (h two d) -> p h two d", h=H, two=2)
                cb4 = cos.unsqueeze(1).unsqueeze(1).broadcast_to([128, H, 2, half])
                sb3 = sin.unsqueeze(1).broadcast_to([128, H, half])
                S_.add(eng1, lambda e: e.tensor_tensor(out=v4(A), in0=v4(src), in1=cb4, op=ALU.mult), R=[srcb, tb], W=[Ab])
                S_.add(eng2, lambda e: e.tensor_tensor(out=v4(T)[:, :, 0, :], in0=v4(src)[:, :, 1, :], in1=sb3, op=ALU.mult), R=[srcb, tb], W=[Tb])
                S_.add(eng2, lambda e: e.tensor_tensor(out=v4(T)[:, :, 1, :], in0=v4(src)[:, :, 0, :], in1=sb3, op=ALU.mult), R=[srcb, tb], W=[Tb])
                S_.add(eng1, lambda e: e.tensor_tensor(out=v4(out)[:, :, 0, :], in0=v4(A)[:, :, 0, :], in1=v4(T)[:, :, 0, :], op=ALU.subtract),
                       R=[Ab, Tb], W=[outb])
                S_.add(eng1, lambda e: e.tensor_tensor(out=v4(out)[:, :, 1, :], in0=v4(A)[:, :, 1, :], in1=v4(T)[:, :, 1, :], op=ALU.add),
                       R=[Ab, Tb], W=[outb])

            def mm_group(ps, psb, pairs, R):
                def f(e):
                    for (o, l, r, s0, s1) in pairs:
                        ins = e.matmul(o, lhsT=l, rhs=r, start=s0, stop=s1)
                    return ins
                S_.add("pe", f, R=R, W=[psb])

            with ExitStack() as ps:
                psF = Pool(ps, pst, "psF", [128, 512], F32, 6)
                psB = Pool(ps, pst, "psB", [128, 1024], BF16, 2)
                w_in = ps.enter_context(sb("w_in", [128, 8, 3584], BF16))
                w_out = ps.enter_context(sb("w_out", [128, 8, 1024], BF16))
                stg = Pool(ps, sb, "wstg", [128, 1024], F32, 3)
                sidx = [0]
                wb_in = load_weight(w_in, w_in_d, 8, 3584, lambda kc: nmix[:, kc:kc + 1], stg, sidx)
                wb_out = load_weight(w_out, w_out_d, 8, 1024, lambda kc: (gnw[:, kc - 4:kc - 3] if kc >= 4 else None), stg, sidx)
                lg = ps.enter_context(sb("lg", [128, 8], F32))
                tabs = {k: ps.enter_context(sb("tab" + k, [128, 512], F32)) for k in ("DT", "QF", "QB", "KF", "KB", "CF", "CB")}
                tb_ = Buf()
                with ExitStack() as ts:
                    io = {k: ts.enter_context(sb("io" + k, [128, 128], F32)) for k in ("rel", "relp", "reln", "mp", "mn", "i1", "ib", "kf", "kb", "c128", "e1", "e2")}
                    S_.add("sp", lambda e: e.dma_start(out=lg[:, 0:4], in_=decf_d.partition_broadcast(128)), W=[tb_], chan="set")
                    S_.add("sp", lambda e: e.dma_start(out=lg[:, 4:8], in_=decb_d.partition_broadcast(128)), W=[tb_], chan="set")
                    S_.add("act", lambda e: e.activation(out=lg[:], in_=lg[:], func=AF.Exp), R=[tb_], W=[tb_])
                    S_.add("dve", lambda e: e.tensor_single_scalar(out=lg[:], in_=lg[:], scalar=-1.0, op=ALU.mult), R=[tb_], W=[tb_])
                    io_ = lambda k, pat, base, cm: S_.add("pool", lambda e: e.iota(io[k][:], pattern=pat, base=base, channel_multiplier=cm,
                                                                                    allow_small_or_imprecise_dtypes=True), W=[tb_])
                    io_("rel", [[1, 128]], 0, -1)
                    io_("i1", [[1, 128]], 1, 0)
                    io_("ib", [[-1, 128]], 128, 0)
                    io_("kf", [[0, 128]], 127, -1)
                    io_("kb", [[0, 128]], 0, 1)
                    io_("c128", [[0, 128]], 128, 0)
                    S_.add("dve", lambda e: e.tensor_single_scalar(out=io["relp"][:], in_=io["rel"][:], scalar=0.0, op=ALU.max), R=[tb_], W=[tb_])
                    S_.add("dve", lambda e: e.tensor_scalar(out=io["reln"][:], in0=io["rel"][:], scalar1=-1.0, scalar2=0.0, op0=ALU.mult, op1=ALU.max),
                           R=[tb_], W=[tb_])
                    S_.add("dve", lambda e: e.tensor_single_scalar(out=io["mp"][:], in_=io["rel"][:], scalar=0.0, op=ALU.is_ge), R=[tb_], W=[tb_])
                    S_.add("dve", lambda e: e.tensor_single_scalar(out=io["mn"][:], in_=io["rel"][:], scalar=0.0, op=ALU.is_lt), R=[tb_], W=[tb_])
                    for h in range(4):
                        hs = slice(h * 128, (h + 1) * 128)
                        ex = lambda dst, src, col: S_.add("act", lambda e: e.activation(out=dst, in_=src, func=AF.Exp, scale=lg[:, col:col + 1]),
                                                          R=[tb_], W=[tb_])
                        ex(io["e1"][:], io["relp"][:], h)
                        ex(io["e2"][:], io["reln"][:], 4 + h)
                        S_.add("dve", lambda e: e.tensor_tensor(out=io["e1"][:], in0=io["e1"][:], in1=io["mp"][:], op=ALU.mult), R=[tb_], W=[tb_])
                        S_.add("dve", lambda e: e.tensor_tensor(out=io["e2"][:], in0=io["e2"][:], in1=io["mn"][:], op=ALU.mult), R=[tb_], W=[tb_])
                        S_.add("dve", (lambda hs: lambda e: e.tensor_tensor(out=tabs["DT"][:, hs], in0=io["e1"][:], in1=io["e2"][:], op=ALU.add))(hs),
                               R=[tb_], W=[tb_])
                        ex(tabs["QF"][:, hs], io["i1"][:], h)
                        ex(tabs["QB"][:, hs], io["ib"][:], 4 + h)
                        ex(tabs["KF"][:, hs], io["kf"][:], h)
                        ex(tabs["KB"][:, hs], io["kb"][:], 4 + h)
                        ex(tabs["CF"][:, hs], io["c128"][:], h)
                        ex(tabs["CB"][:, hs], io["c128"][:], 4 + h)
                    S_.flush(); chk(1)
                nt = NormT(ps, psB)
                xin = Pool(ps, sb, "xin", [128, 1024], F32, 3)
                xnT = Pool(ps, sb, "xnT", [128, 8, 128], BF16, 2)
                ropt = Pool(ps, sb, "ropt", [128, 256], F32, 3)
                rA = Pool(ps, sb, "rA", [128, 512], F32, 2)
                rT = Pool(ps, sb, "rT", [128, 512], F32, 2)
                kr = Pool(ps, sb, "kr", [128, 512], BF16, 3)
                vv = Pool(ps, sb, "vv", [128, 512], BF16, 3)
                sbl = Pool(ps, sb, "sbl", [128, 512], BF16, 2)
                kd = Pool(ps, sb, "kd", [128, 512], BF16, 2)
                acS = Pool(ps, sb, "acS", [128, 512], F32, 2)
                uT = Pool(ps, sb, "uT", [128, 4, 130], F32, 2)
                S32 = ps.enter_context(sb("S32", [128, 512], F32))
                S16 = Pool(ps, sb, "S16", [128, 512], BF16, 2)
                qr = Pool(ps, sb, "qr", [128, 512], BF16, 2)
                qT = Pool(ps, sb, "qT", [128, 512], BF16, 2)
                qfT = Pool(ps, sb, "qfT", [128, 512], BF16, 2)
                qbT = Pool(ps, sb, "qbT", [128, 512], BF16, 2)
                kT = Pool(ps, sb, "kT", [128, 512], BF16, 2)
                AT = Pool(ps, sb, "AT", [128, 512], BF16, 2)
                gst = Pool(ps, sb, "gst", [128, 4, 6], F32, 2)
                gmv = Pool(ps, sb, "gmv", [128, 4, 2], F32, 2)
                grs = Pool(ps, sb, "grs", [128, 8], F32, 2)
                on = Pool(ps, sb, "on", [128, 512], F32, 2)
                sg = Pool(ps, sb, "sg", [128, 512], F32, 2)
                yb = Pool(ps, sb, "yb", [128, 512], BF16, 2)
                yT = Pool(ps, sb, "yT", [128, 8, 128], BF16, 2)
                cv = Pool(ps, sb, "cv", [128, 4, 128], F32, 2)
                x1t = Pool(ps, sb, "x1t", [128, 1024], F32, 2)
                S32b = Buf()

                def run_job(job):
                    jn, S = job["n"], job["S"]
                    NCH = S // 128
                    xd = xin_d[jn]
                    sc = scr[jn]
                    uTv = sc["uT"].rearrange("(cg p) t -> p cg t", p=128)
                    zt, ztb = cv.get()
                    S_.add("pool", lambda e: e.memset(zt[:, :, 0:1], 0.0), W=[ztb])
                    S_.add("sp", lambda e: e.dma_start(out=uTv[:, :, 0:1], in_=zt[:, :, 0:1], allow_slow_non_contiguous=True), R=[ztb], W=[S_.dbuf("uT", jn, -1)], chan="st0")
                    S_.add("sp", lambda e: e.dma_start(out=uTv[:, :, S + 1:S + 2], in_=zt[:, :, 0:1], allow_slow_non_contiguous=True), R=[ztb], W=[S_.dbuf("uT", jn, -2)], chan="st0")
                    S_.add("dve", lambda e: e.memset(S32[:], 0.0), W=[S32b])
                    s16, s16b = S16.get()
                    S_.add("pool", (lambda s16: lambda e: e.memset(s16[:], 0.0))(s16), W=[s16b])
                    for c in range(NCH - 1, -1, -1):
                        x, xb = xin.get()
                        S_.add("sp", (lambda x, c: lambda e: e.dma_start(out=x[:], in_=xd[c * 128:(c + 1) * 128, :]))(x, c), W=[xb], chan="xl%d" % (c % 3))
                        rtb_t, rtb = ropt.get()
                        S_.add("sp", (lambda t, c: lambda e: e.dma_start(out=t[:], in_=rt_d[c * 128:(c + 1) * 128, :]))(rtb_t, c), W=[rtb], chan="rl%d" % (c % 3))
                        xt_, xtb = xnT.get()
                        nt.run(x[:], xb, xt_[:], xtb)
                        pk, pkb = psF.get()
                        mm_group(pk, pkb, [(pk[:], xt_[:, kc, :], w_in[:, kc, 2048:2560], kc == 0, kc == 7) for kc in range(8)], [xtb, wb_in])
                        pv, pvb = psF.get()
                        mm_group(pv, pvb, [(pv[:], xt_[:, kc, :], w_in[:, kc, 2560:3072], kc == 0, kc == 7) for kc in range(8)], [xtb, wb_in])
                        pc, pcb = psF.get()
                        mm_group(pc, pcb, [(pc[:, cg * 128:(cg + 1) * 128], w_in[:, kc, 512 + cg * 128:512 + (cg + 1) * 128], xt_[:, kc, :], kc == 0, kc == 7)
                                           for cg in range(4) for kc in range(8)], [xtb, wb_in])
                        ph, phb = psF.get()
                        mm_group(ph, phb, [(ph[:, cg * 128:(cg + 1) * 128], w_in[:, kc, 1024 + cg * 128:1024 + (cg + 1) * 128], xt_[:, kc, :], kc == 0, kc == 7)
                                           for cg in range(4) for kc in range(8)], [xtb, wb_in])
                        ac, acb = acS.get()
                        S_.add("act", (lambda ac, pc: lambda e: e.copy(out=ac[:], in_=pc[:]))(ac, pc), R=[pcb], W=[acb])
                        u, ub = uT.get()
                        S_.add("dve", (lambda u, ac, ph: lambda e: e.tensor_tensor(out=u[:, :, 0:128], in0=ac[:].rearrange("p (c t) -> p c t", c=4),
                                                                                  in1=ph[:].rearrange("p (c t) -> p c t", c=4), op=ALU.mult))(u, ac, ph),
                               R=[acb, phb], W=[ub])
                        S_.add("sp", (lambda u, c: lambda e: e.dma_start(out=uTv[:, :, 1 + c * 128:1 + (c + 1) * 128], in_=u[:, :, 0:128]))(u, c),
                               R=[ub], W=[S_.dbuf("uT", jn, c)], chan="st%d" % (c % 2))
                        A, Ab = rA.get()
                        T, Tb = rT.get()
                        k_, kb_ = kr.get()
                        rope("dve", "pool" if False else "dve", pk[:], pkb, 4, 64, rtb_t[:, 128:192], rtb_t[:, 192:256], rtb, A[:], Ab, T[:], Tb, k_[:], kb_)
                        v_, vb_ = vv.get()
                        S_.add("act", (lambda v_, pv: lambda e: e.copy(out=v_[:], in_=pv[:]))(v_, pv), R=[pvb], W=[vb_])
                        S_.add("sp", (lambda k_, c: lambda e: e.dma_start(out=sc["kr"][c * 128:(c + 1) * 128, :], in_=k_[:]))(k_, c),
                               R=[kb_], W=[S_.dbuf("kr", jn, c)], chan="st%d" % (c % 2))
                        S_.add("sp", (lambda v_, c: lambda e: e.dma_start(out=sc["vr"][c * 128:(c + 1) * 128, :], in_=v_[:]))(v_, c),
                               R=[vb_], W=[S_.dbuf("vr", jn, c)], chan="st%d" % (c % 2))
                        S_.add("sp", (lambda s16, c: lambda e: e.dma_start(out=sc["sb"][c, :, :], in_=s16[:]))(s16, c),
                               R=[s16b], W=[S_.dbuf("sb", jn, c)], chan="st%d" % (c % 2))
                        kd_, kdb = kd.get()
                        S_.add("pool", (lambda kd_, k_: lambda e: e.tensor_tensor(out=kd_[:], in0=k_[:], in1=tabs["KB"][:], op=ALU.mult))(kd_, k_),
                               R=[kb_, tb_], W=[kdb])
                        pS, pSb = psF.get()
                        mm_group(pS, pSb, [(pS[:, h * 128:(h + 1) * 128], kd_[:, h * 128:(h + 1) * 128], v_[:, h * 128:(h + 1) * 128], True, True) for h in range(4)],
                                 [kdb, vb_])
                        S_.add("pool", lambda e: e.tensor_tensor(out=S32[:], in0=S32[:], in1=tabs["CB"][:], op=ALU.mult), R=[S32b, tb_], W=[S32b])
                        S_.add("dve", (lambda pS: lambda e: e.tensor_tensor(out=S32[:], in0=S32[:], in1=pS[:], op=ALU.add))(pS), R=[S32b, pSb], W=[S32b])
                        s16, s16b = S16.get()
                        S_.add("act", (lambda s16: lambda e: e.copy(out=s16[:], in_=S32[:]))(s16), R=[S32b], W=[s16b])
                    S_.add("dve", lambda e: e.memset(S32[:], 0.0), W=[S32b])
                    s16, s16b = S16.get()
                    S_.add("pool", (lambda s16: lambda e: e.memset(s16[:], 0.0))(s16), W=[s16b])
                    for c in range(NCH):
                        x, xb = xin.get()
                        S_.add("sp", (lambda x, c: lambda e: e.dma_start(out=x[:], in_=xd[c * 128:(c + 1) * 128, :]))(x, c), W=[xb], chan="xl%d" % (c % 3))
                        rtb_t, rtb = ropt.get()
                        S_.add("sp", (lambda t, c: lambda e: e.dma_start(out=t[:], in_=rt_d[c * 128:(c + 1) * 128, :]))(rtb_t, c), W=[rtb], chan="rl%d" % (c % 3))
                        k_, kb_ = kr.get()
                        S_.add("sp", (lambda k_, c: lambda e: e.dma_start(out=k_[:], in_=sc["kr"][c * 128:(c + 1) * 128, :]))(k_, c),
                               R=[S_.dbuf("kr", jn, c)], W=[kb_], chan="kl%d" % (c % 3))
                        v_, vb_ = vv.get()
                        S_.add("sp", (lambda v_, c: lambda e: e.dma_start(out=v_[:], in_=sc["vr"][c * 128:(c + 1) * 128, :]))(v_, c),
                               R=[S_.dbuf("vr", jn, c)], W=[vb_], chan="vl%d" % (c % 3))
                        sl, slb = sbl.get()
                        S_.add("sp", (lambda sl, c: lambda e: e.dma_start(out=sl[:], in_=sc["sb"][c, :, :]))(sl, c),
                               R=[S_.dbuf("sb", jn, c)], W=[slb], chan="sl%d" % (c % 2))
                        u, ub = uT.get()
                        urd = [S_.dbuf("uT", jn, cc) for cc in (c - 1, c, c + 1) if 0 <= cc < NCH] + [S_.dbuf("uT", jn, -1), S_.dbuf("uT", jn, -2)]
                        S_.add("sp", (lambda u, c: lambda e: e.dma_start(out=u[:], in_=uTv[:, :, c * 128:c * 128 + 130]))(u, c),
                               R=urd, W=[ub], chan="ul%d" % (c % 2))
                        xt_, xtb = xnT.get()
                        nt.run(x[:], xb, xt_[:], xtb)
                        pq, pqb = psF.get()
                        mm_group(pq, pqb, [(pq[:], xt_[:, kc, :], w_in[:, kc, 1536:2048], kc == 0, kc == 7) for kc in range(8)], [xtb, wb_in])
                        pg, pgb = psF.get()
                        mm_group(pg, pgb, [(pg[:], xt_[:, kc, :], w_in[:, kc, 3072:3584], kc == 0, kc == 7) for kc in range(8)], [xtb, wb_in])
                        pab, pabb = psF.get()
                        mm_group(pab, pabb, [(pab[:, cg * 128:(cg + 1) * 128], w_in[:, kc, cg * 128:(cg + 1) * 128], xt_[:, kc, :], kc == 0, kc == 7)
                                             for cg in range(4) for kc in range(8)], [xtb, wb_in])
                        A, Ab = rA.get()
                        T, Tb = rT.get()
                        q_, qb_ = qr.get()
                        rope("dve", "dve", pq[:], pqb, 4, 64, rtb_t[:, 0:64], rtb_t[:, 64:128], rtb, A[:], Ab, T[:], Tb, q_[:], qb_)
                        pt, ptb = psB.get()

                        def trqk(e, pt=pt, q_=q_, k_=k_):
                            for h in range(4):
                                e.transpose(out=pt[:, h * 128:(h + 1) * 128], in_=q_[:, h * 128:(h + 1) * 128], identity=ident[:])
                            for h in range(4):
                                ins = e.transpose(out=pt[:, 512 + h * 128:512 + (h + 1) * 128], in_=k_[:, h * 128:(h + 1) * 128], identity=ident[:])
                            return ins
                        S_.add("pe", trqk, R=[qb_, kb_, consts], W=[ptb])
                        qT_, qTb = qT.get()
                        qf_, qfb = qfT.get()
                        qb2, qbb = qbT.get()
                        kT_, kTb = kT.get()
                        S_.add("act", (lambda o, pt: lambda e: e.copy(out=o[:], in_=pt[:, 0:512]))(qT_, pt), R=[ptb], W=[qTb])
                        S_.add("act", (lambda o, pt: lambda e: e.copy(out=o[:], in_=pt[:, 512:1024]))(kT_, pt), R=[ptb], W=[kTb])
                        S_.add("dve", (lambda o, pt: lambda e: e.tensor_tensor(out=o[:], in0=pt[:, 0:512], in1=tabs["QF"][:], op=ALU.mult))(qf_, pt),
                               R=[ptb, tb_], W=[qfb])
                        S_.add("dve", (lambda o, pt: lambda e: e.tensor_tensor(out=o[:], in0=pt[:, 0:512], in1=tabs["QB"][:], op=ALU.mult))(qb2, pt),
                               R=[ptb, tb_], W=[qbb])
                        psc, pscb = psF.get()
                        hs = lambda h: slice(h * 128, (h + 1) * 128)
                        mm_group(psc, pscb, [(psc[:, hs(h)], kT_[:, hs(h)], qT_[:, hs(h)], True, True) for h in range(4)], [kTb, qTb])
                        at, atb = AT.get()
                        S_.add("dve", (lambda at, psc: lambda e: e.tensor_tensor(out=at[:], in0=psc[:], in1=tabs["DT"][:], op=ALU.mult))(at, psc),
                               R=[pscb, tb_], W=[atb])
                        po, pob = psF.get()
                        prs = []
                        for h in range(4):
                            prs += [(po[:, hs(h)], at[:, hs(h)], v_[:, hs(h)], True, False),
                                    (po[:, hs(h)], qf_[:, hs(h)], s16[:, hs(h)], False, False),
                                    (po[:, hs(h)], qb2[:, hs(h)], sl[:, hs(h)], False, True)]
                        mm_group(po, pob, prs, [atb, vb_, qfb, qbb, s16b, slb])
                        kd_, kdb = kd.get()
                        S_.add("pool", (lambda kd_, k_: lambda e: e.tensor_tensor(out=kd_[:], in0=k_[:], in1=tabs["KF"][:], op=ALU.mult))(kd_, k_),
                               R=[kb_, tb_], W=[kdb])
                        pS, pSb = psF.get()
                        mm_group(pS, pSb, [(pS[:, hs(h)], kd_[:, hs(h)], v_[:, hs(h)], True, True) for h in range(4)], [kdb, vb_])
                        S_.add("pool", lambda e: e.tensor_tensor(out=S32[:], in0=S32[:], in1=tabs["CF"][:], op=ALU.mult), R=[S32b, tb_], W=[S32b])
                        S_.add("dve", (lambda pS: lambda e: e.tensor_tensor(out=S32[:], in0=S32[:], in1=pS[:], op=ALU.add))(pS), R=[S32b, pSb], W=[S32b])
                        s16, s16b = S16.get()
                        S_.add("act", (lambda s16: lambda e: e.copy(out=s16[:], in_=S32[:]))(s16), R=[S32b], W=[s16b])
                        st6, st6b = gst.get()
                        mv, mvb = gmv.get()
                        rs, rsb = grs.get()

                        def bns(e, st6=st6, po=po):
                            for h in range(4):
                                ins = e.bn_stats(out=st6[:, h, :], in_=po[:, h * 128:(h + 1) * 128])
                            return ins
                        S_.add("dve", bns, R=[pob], W=[st6b])

                        def bna(e, st6=st6, mv=mv):
                            for h in range(4):
                                ins = e.bn_aggr(out=mv[:, h, :], in_=st6[:, h, :])
                            return ins
                        S_.add("dve", bna, R=[st6b], W=[mvb])
                        S_.add("act", (lambda rs, mv: lambda e: e.activation(out=rs[:, 0:4], in_=mv[:, :, 1], func=AF.Sqrt, bias=epsN[:, 1:2], scale=1.0))(rs, mv),
                               R=[mvb, consts], W=[rsb])
                        S_.add("dve", (lambda rs: lambda e: e.reciprocal(out=rs[:, 4:8], in_=rs[:, 0:4]))(rs), R=[rsb], W=[rsb])
                        on_, onb = on.get()

                        def gnn(e, on_=on_, po=po, mv=mv, rs=rs):
                            for h in range(4):
                                ins = e.tensor_scalar(out=on_[:, h * 128:(h + 1) * 128], in0=po[:, h * 128:(h + 1) * 128], scalar1=mv[:, h, 0:1],
                                                      scalar2=rs[:, 4 + h:5 + h], op0=ALU.subtract, op1=ALU.mult)
                            return ins
                        S_.add("dve", gnn, R=[pob, mvb, rsb], W=[onb])
                        sg_, sgb = sg.get()
                        S_.add("act", (lambda sg_, pg: lambda e: e.activation(out=sg_[:], in_=pg[:], func=AF.Silu))(sg_, pg), R=[pgb], W=[sgb])
                        yb_, ybb = yb.get()
                        S_.add("pool", (lambda yb_, on_, sg_: lambda e: e.tensor_tensor(out=yb_[:], in0=on_[:], in1=sg_[:], op=ALU.mult))(yb_, on_, sg_),
                               R=[onb, sgb], W=[ybb])
                        pt2, pt2b = psB.get()

                        def try_(e, pt2=pt2, yb_=yb_):
                            for h in range(4):
                                ins = e.transpose(out=pt2[:, h * 128:(h + 1) * 128], in_=yb_[:, h * 128:(h + 1) * 128], identity=ident[:])
                            return ins
                        S_.add("pe", try_, R=[ybb, consts], W=[pt2b])
                        yT_, yTb = yT.get()
                        S_.add("act", (lambda yT_, pt2: lambda e: e.copy(out=yT_[:, 4:8, :], in_=pt2[:, 0:512].rearrange("p (k t) -> p k t", k=4)))(yT_, pt2),
                               R=[pt2b], W=[yTb])
                        cv_, cvb = cv.get()

                        def conv(e, cv_=cv_, u=u):
                            for cg in range(4):
                                e.tensor_scalar(out=cv_[:, cg, :], in0=u[:, cg, 1:129], scalar1=convw[:, cg * 3 + 1:cg * 3 + 2], scalar2=None, op0=ALU.mult)
                                e.scalar_tensor_tensor(out=cv_[:, cg, :], in0=u[:, cg, 0:128], scalar=convw[:, cg * 3:cg * 3 + 1], in1=cv_[:, cg, :],
                                                       op0=ALU.mult, op1=ALU.add)
                                ins = e.scalar_tensor_tensor(out=cv_[:, cg, :], in0=u[:, cg, 2:130], scalar=convw[:, cg * 3 + 2:cg * 3 + 3], in1=cv_[:, cg, :],
                                                             op0=ALU.mult, op1=ALU.add)
                            return ins
                        for cg in range(4):
                            S_.add("pool", (lambda cv_, u, cg: lambda e: e.tensor_scalar(out=cv_[:, cg, :], in0=u[:, cg, 1:129],
                                                                                       scalar1=convw[:, cg * 3 + 1:cg * 3 + 2], scalar2=None, op0=ALU.mult))(cv_, u, cg),
                                   R=[ub, consts], W=[cvb])
                        for tap, lo in ((0, 0), (2, 2)):
                            for cg in range(4):
                                S_.add("dve", (lambda cv_, u, cg, tap, lo: lambda e: e.scalar_tensor_tensor(
                                    out=cv_[:, cg, :], in0=u[:, cg, lo:lo + 128], scalar=convw[:, cg * 3 + tap:cg * 3 + tap + 1], in1=cv_[:, cg, :],
                                    op0=ALU.mult, op1=ALU.add))(cv_, u, cg, tap, lo), R=[ub, consts, cvb], W=[cvb])
                        S_.add("dve", (lambda yT_, cv_, pab: lambda e: e.tensor_tensor(out=yT_[:, 0:4, :], in0=cv_[:],
                                                                                       in1=pab[:].rearrange("p (c t) -> p c t", c=4), op=ALU.mult))(yT_, cv_, pab),
                               R=[cvb, pabb], W=[yTb])
                        x1_, x1b = x1t.get()
                        for n in range(2):
                            pp, ppb = psF.get()
                            mm_group(pp, ppb, [(pp[:], yT_[:, f, :], w_out[:, f, n * 512:(n + 1) * 512], f == 0, f == 7) for f in range(8)], [yTb, wb_out])
                            S_.add("dve", (lambda x1_, pp, x, n: lambda e: e.tensor_tensor(out=x1_[:, n * 512:(n + 1) * 512], in0=pp[:],
                                                                                          in1=x[:, n * 512:(n + 1) * 512], op=ALU.add))(x1_, pp, x, n),
                                   R=[ppb, xb], W=[x1b])
                        S_.add("sp", (lambda x1_, c: lambda e: e.dma_start(out=sc["x1"][c * 128:(c + 1) * 128, :], in_=x1_[:]))(x1_, c),
                               R=[x1b], W=[S_.dbuf("x1", jn, c)], chan="st%d" % (c % 2))
                        if debug:
                            S_.add("sp", (lambda x1_, c: lambda e: e.dma_start(out=dbg[jn]["x1"][c * 128:(c + 1) * 128, :], in_=x1_[:]))(x1_, c),
                                   R=[x1b], W=[S_.dbuf("dx1", jn, c)], chan="dbg")
                for job in jobs:
                    run_job(job)
                S_.flush(); chk(2)

            def ffn_phase(layer, final):
                with ExitStack() as ps:
                    psF = Pool(ps, pst, "psF", [128, 512], F32, 6)
                    psB = Pool(ps, pst, "psB", [128, 1024], BF16, 2)
                    wg = ps.enter_context(sb("wg", [128, 8, DFF], BF16))
                    wu = ps.enter_context(sb("wu", [128, 8, DFF], BF16))
                    wd = ps.enter_context(sb("wd", [128, NFC, 1024], BF16))
                    with ExitStack() as ws:
                        stg = Pool(ws, sb, "wstg", [128, 1024], F32, 3)
                        sidx = [0]
                        nr = lambda kc: nffn[:, layer * 8 + kc:layer * 8 + kc + 1]
                        wbg = load_weight(wg, w_g_d[layer], 8, DFF, nr, stg, sidx)
                        wbu = load_weight(wu, w_u_d[layer], 8, DFF, nr, stg, sidx)
                        wbd = load_weight(wd, w_d_d[layer], NFC, 1024, None, stg, sidx)
                        S_.flush(); chk(3)
                    nt = NormT(ps, psB)
                    xin = Pool(ps, sb, "xin", [128, 1024], F32, 2)
                    xres = Pool(ps, sb, "xres", [128, 1024], F32, 2)
                    xnT = Pool(ps, sb, "xnT", [128, 8, 512], BF16, 2)
                    hT = Pool(ps, sb, "hT", [128, NFC, 512], BF16, 1)
                    sgp = Pool(ps, sb, "sgp", [128, 512], F32, 2)
                    nf = None
                    if final:
                        nf = ps.enter_context(sb("nf", [128, 1024], F32))
                        S_.add("sp", lambda e: e.dma_start(out=nf[:], in_=nfin_d.partition_broadcast(128)), W=[consts], chan="set")
                        fj = Pool(ps, sb, "fj", [128, 1024], BF16, 1)
                        fst = Pool(ps, sb, "fst", [128, 4], F32, 2)
                    def run_job(job):
                        jn = job["n"]
                        S = job["SQ"] if final else job["S"]
                        sc = scr[jn]
                        src, srcn = (sc["x3"], "x3") if final else (sc["x1"], "x1")
                        dst, dstn = (y_d[jn], "y") if final else (sc["x2"], "x2")
                        for t in range(S // 512):
                            xt_, xtb = xnT.get()
                            for ci in range(4):
                                c = t * 4 + ci
                                x, xb = xin.get()
                                S_.add("sp", (lambda x, c: lambda e: e.dma_start(out=x[:], in_=src[c * 128:(c + 1) * 128, :]))(x, c),
                                       R=[S_.dbuf(srcn, jn, c)], W=[xb], chan="xl%d" % (c % 2))
                                nt.run(x[:], xb, xt_[:, :, ci * 128:(ci + 1) * 128], xtb)
                            h_, hb = hT.get()
                            for f in range(NFC):
                                pg, pgb = psF.get()
                                mm_group(pg, pgb, [(pg[:], wg[:, kc, f * 128:(f + 1) * 128], xt_[:, kc, :], kc == 0, kc == 7) for kc in range(8)], [xtb, wbg])
                                pu, pub = psF.get()
                                mm_group(pu, pub, [(pu[:], wu[:, kc, f * 128:(f + 1) * 128], xt_[:, kc, :], kc == 0, kc == 7) for kc in range(8)], [xtb, wbu])
                                s_, sb_ = sgp.get()
                                S_.add("act", (lambda s_, pg: lambda e: e.activation(out=s_[:], in_=pg[:], func=AF.Silu))(s_, pg), R=[pgb], W=[sb_])
                                S_.add("dve", (lambda h_, s_, pu, f: lambda e: e.tensor_tensor(out=h_[:, f, :], in0=s_[:], in1=pu[:], op=ALU.mult))(h_, s_, pu, f),
                                       R=[sb_, pub], W=[hb])
                            for ci in range(4):
                                c = t * 4 + ci
                                xr, xrb = xres.get()
                                S_.add("sp", (lambda xr, c: lambda e: e.dma_start(out=xr[:], in_=src[c * 128:(c + 1) * 128, :]))(xr, c),
                                       R=[S_.dbuf(srcn, jn, c)], W=[xrb], chan="rl%d" % (c % 2))
                                for n in range(2):
                                    pp, ppb = psF.get()
                                    mm_group(pp, ppb, [(pp[:], h_[:, f, ci * 128:(ci + 1) * 128], wd[:, f, n * 512:(n + 1) * 512], f == 0, f == NFC - 1)
                                                       for f in range(NFC)], [hb, wbd])
                                    S_.add("pool" if False else "dve", (lambda xr, pp, n: lambda e: e.tensor_tensor(out=xr[:, n * 512:(n + 1) * 512], in0=pp[:],
                                                                                                                   in1=xr[:, n * 512:(n + 1) * 512], op=ALU.add))(xr, pp, n),
                                           R=[ppb, xrb], W=[xrb])
                                if final:
                                    jk, jkb = fj.get()
                                    st, stb = fst.get()
                                    S_.add("act", (lambda jk, xr, st: lambda e: e.activation(out=jk[:], in_=xr[:], func=AF.Square, accum_out=st[:, 0:1]))(jk, xr, st),
                                           R=[xrb], W=[jkb, stb])
                                    S_.add("act", (lambda st: lambda e: e.activation(out=st[:, 1:2], in_=st[:, 0:1], func=AF.Sqrt, bias=epsN[:, 0:1], scale=1.0 / D))(st),
                                           R=[stb, consts], W=[stb])
                                    S_.add("dve", (lambda st: lambda e: e.reciprocal(out=st[:, 2:3], in_=st[:, 1:2]))(st), R=[stb], W=[stb])
                                    S_.add("dve", (lambda xr, st: lambda e: e.scalar_tensor_tensor(out=xr[:], in0=xr[:], scalar=st[:, 2:3], in1=nf[:],
                                                                                                    op0=ALU.mult, op1=ALU.mult))(xr, st),
                                           R=[xrb, stb, consts], W=[xrb])
                                S_.add("sp", (lambda xr, c: lambda e: e.dma_start(out=dst[c * 128:(c + 1) * 128, :], in_=xr[:]))(xr, c),
                                       R=[xrb], W=[S_.dbuf(dstn, jn, c)], chan="st%d" % (c % 2))
                                if debug and not final:
                                    S_.add("sp", (lambda xr, c: lambda e: e.dma_start(out=dbg[jn]["x2"][c * 128:(c + 1) * 128, :], in_=xr[:]))(xr, c),
                                           R=[xrb], W=[S_.dbuf("dx2", jn, c)], chan="dbg")
                    for job in jobs:
                        run_job(job)
                    S_.flush(); chk(4)

            ffn_phase(0, False)

            with ExitStack() as ps:
                psF = Pool(ps, pst, "psF", [128, 512], F32, 6)
                psB = Pool(ps, pst, "psB", [128, 1024], BF16, 2)
                wqkv = ps.enter_context(sb("wqkv", [128, 8, 3072], BF16))
                with ExitStack() as ws:
                    stg = Pool(ws, sb, "wstg", [128, 1024], F32, 3)
                    sidx = [0]
                    wbq = load_weight(wqkv, w_qkv_d, 8, 3072, lambda kc: nmix[:, 8 + kc:9 + kc], stg, sidx)
                    S_.flush(); chk(5)
                nt = NormT(ps, psB)
                xin = Pool(ps, sb, "xin", [128, 1024], F32, 3)
                xnT = Pool(ps, sb, "xnT", [128, 8, 128], BF16, 2)
                ropt = Pool(ps, sb, "ropt", [128, 64], F32, 3)
                rA = Pool(ps, sb, "rA", [128, 1024], F32, 2)
                rT = Pool(ps, sb, "rT", [128, 1024], F32, 2)
                rr = Pool(ps, sb, "rr", [128, 1024], BF16, 2)
                vS = Pool(ps, sb, "vS", [128, 1024], BF16, 2)
                stT = Pool(ps, sb, "stT", [128, 8, 512], BF16, 2)

                def qk_path(x_src_fn, R_x, rope_src, S, c0, dstT, dstname, jn, col0, with_v, vdst):
                    st_, stb = stT.get()
                    for ci in range(4):
                        c = c0 + ci
                        x, xb = xin.get()
                        x_src_fn(x, xb, c)
                        rt_, rtb = ropt.get()
                        S_.add("sp", (lambda t, c: lambda e: e.dma_start(out=t[:], in_=rope_src[c * 128:(c + 1) * 128, :]))(rt_, c), W=[rtb], chan="rl%d" % (c % 3))
                        xt_, xtb = xnT.get()
                        nt.run(x[:], xb, xt_[:], xtb)
                        pq = [psF.get(), psF.get()]
                        for n in range(2):
                            mm_group(pq[n][0], pq[n][1], [(pq[n][0][:], xt_[:, kc, :], wqkv[:, kc, col0 + n * 512:col0 + (n + 1) * 512], kc == 0, kc == 7)
                                                          for kc in range(8)], [xtb, wbq])
                        A, Ab = rA.get()
                        T, Tb = rT.get()
                        r_, rb_ = rr.get()
                        for n in range(2):
                            sl = slice(n * 512, (n + 1) * 512)
                            rope("dve", "pool" if False else "dve", pq[n][0][:], pq[n][1], 8, 32, rt_[:, 0:32], rt_[:, 32:64], rtb, A[:, sl], Ab, T[:, sl], Tb, r_[:, sl], rb_)
                        pt, ptb = psB.get()

                        def tr(e, pt=pt, r_=r_):
                            for hp in range(8):
                                ins = e.transpose(out=pt[:, hp * 128:(hp + 1) * 128], in_=r_[:, hp * 128:(hp + 1) * 128], identity=ident[:])
                            return ins
                        S_.add("pe", tr, R=[rb_, consts], W=[ptb])
                        S_.add("act", (lambda st_, pt, ci: lambda e: e.copy(out=st_[:, :, ci * 128:(ci + 1) * 128], in_=pt[:].rearrange("p (k t) -> p k t", k=8)))(st_, pt, ci),
                               R=[ptb], W=[stb])
                        if with_v:
                            pv = [psF.get(), psF.get()]
                            v_, vb_ = vS.get()
                            for n in range(2):
                                mm_group(pv[n][0], pv[n][1], [(pv[n][0][:], xt_[:, kc, :], wqkv[:, kc, 2048 + n * 512:2048 + (n + 1) * 512], kc == 0, kc == 7)
                                                              for kc in range(8)], [xtb, wbq])
                                S_.add("act", (lambda v_, p, n: lambda e: e.copy(out=v_[:, n * 512:(n + 1) * 512], in_=p[:]))(v_, pv[n][0], n), R=[pv[n][1]], W=[vb_])
                            S_.add("sp", (lambda v_, c: lambda e: e.dma_start(out=vdst[:, c * 128:(c + 1) * 128, :].rearrange("h t e -> t h e"),
                                                                              in_=v_[:].rearrange("p (h e) -> p h e", h=8)))(v_, c),
                                   R=[vb_], W=[S_.dbuf("V", jn, c)], chan="st%d" % (c % 2))
                    S_.add("sp", (lambda st_, c0: lambda e: e.dma_start(out=dstT[:, :, c0 * 128:c0 * 128 + 512].rearrange("h p t -> p h t"), in_=st_[:]))(st_, c0),
                           R=[stb], W=[S_.dbuf(dstname, jn, c0 // 4)], chan="st%d" % ((c0 // 4) % 2))

                def run_job(job):
                    jn, S, SQ = job["n"], job["S"], job["SQ"]
                    sc = scr[jn]

                    def src_plain(x, xb, c, sc=sc, jn=jn):
                        S_.add("sp", (lambda x, c: lambda e: e.dma_start(out=x[:], in_=sc["x2"][c * 128:(c + 1) * 128, :]))(x, c),
                               R=[S_.dbuf("x2", jn, c)], W=[xb], chan="xl%d" % (c % 3))

                    def src_gather(x, xb, c, sc=sc, jn=jn, S=S):
                        S_.add("pool", (lambda x, c: lambda e: e.indirect_dma_start(out=x[:, :], out_offset=None, in_=sc["x2"][:, :],
                                                                                    in_offset=bass.IndirectOffsetOnAxis(ap=idxq[:, c:c + 1], axis=0)))(x, c),
                               R=[S_.dbuf("x2", jn, cc) for cc in range(S // 128)] + [consts], W=[xb], chan="gl%d" % (c % 3))
                    for t in range(S // 512):
                        qk_path(src_plain, None, dtk_d, S, t * 4, sc["KT"], "KT", jn, 1024, True, sc["V"])
                    for t in range(SQ // 512):
                        qk_path(src_gather if job["own"] else src_plain, None, dtq_d[jn], SQ, t * 4, sc["QT"], "QT", jn, 0, False, None)
                for job in jobs:
                    run_job(job)
                S_.flush(); chk(6)

            with ExitStack() as ps:
                psS = Pool(ps, pst, "psS", [128, 1024], F32, 2)
                psO = Pool(ps, pst, "psO", [128, 2, 256], F32, 4)
                SMX = max(SA, SB)
                KTt = Pool(ps, sb, "KTt", [128, SMX], BF16, 2)
                Vt = Pool(ps, sb, "Vt", [128, SMX // 128, 130], BF16, 2)
                QTt = Pool(ps, sb, "QTt", [128, max(SA, SQB)], BF16, 2)
                PT = Pool(ps, sb, "PT", [128, 1024], BF16, 3)
                rc = Pool(ps, sb, "rc", [128, 8], F32, 4)
                o1 = Pool(ps, sb, "o1", [128, 128], F32, 3)
                oj = Pool(ps, sb, "oj", [128, 128], BF16, 2)
                ob = Pool(ps, sb, "ob", [128, 4, 128], BF16, 2)
                for i in range(2):
                    S_.add("pool", (lambda t: lambda e: e.memset(t[:, :, 128:130], 1.0))(Vt.t[i]), W=[Vt.b[i]])
                def run_job(job):
                    jn, S, SQ = job["n"], job["S"], job["SQ"]
                    sc = scr[jn]
                    NKT = S // 128
                    for h in range(8):
                        kt_, ktb = KTt.get()
                        S_.add("sp", (lambda kt_, h: lambda e: e.dma_start(out=kt_[:, 0:S], in_=sc["KT"][h, :, :]))(kt_, h),
                               R=[S_.dbuf("KT", jn, t) for t in range(S // 512)], W=[ktb], chan="kl%d" % (h % 2))
                        v_, vb_ = Vt.get()
                        VP = min(32, NKT)
                        for part in range(0, NKT, VP):
                            S_.add("sp", (lambda v_, h, part: lambda e: e.dma_start(out=v_[:, part:part + VP, 0:128],
                                                                                   in_=sc["V"][h, part * 128:(part + VP) * 128, :].rearrange("(k p) e -> p k e", p=128)))(v_, h, part),
                                   R=[S_.dbuf("V", jn, c) for c in range(part, min(part + VP, NKT))], W=[vb_], chan="vl%d" % (h % 2))
                        q_, qb_ = QTt.get()
                        S_.add("sp", (lambda q_, h: lambda e: e.dma_start(out=q_[:, 0:SQ], in_=sc["QT"][h, :, :]))(q_, h),
                               R=[S_.dbuf("QT", jn, t) for t in range(SQ // 512)], W=[qb_], chan="ql%d" % (h % 2))
                        for qt in range(SQ // 512):
                            acc = [psO.get(), psO.get(), psO.get(), psO.get()]
                            qs = slice(qt * 512, (qt + 1) * 512)
                            def qk(kt, qs=qs):
                                ks = slice(kt * 128, (kt + 1) * 128)
                                pS_, pSb = psS.get()
                                mm_group(pS_, pSb, [(pS_[:, 0:512], kt_[0:64, ks], q_[0:64, qs], True, True),
                                                    (pS_[:, 512:1024], kt_[64:128, ks], q_[64:128, qs], True, True)], [ktb, qb_])
                                return pS_, pSb
                            pend = qk(0)
                            for kt in range(NKT):
                                pS_, pSb = pend
                                if kt + 1 < NKT:
                                    pend = qk(kt + 1)
                                p_, pb_ = PT.get()
                                S_.add("act", (lambda p_, pS_: lambda e: e.activation(out=p_[:], in_=pS_[:], func=AF.Exp))(p_, pS_), R=[pSb], W=[pb_])
                                for sub in range(2):
                                    for pair in range(2):
                                        a_, ab_ = acc[sub * 2 + pair]
                                        mm_group(a_, ab_, [(a_[:, j, 0:129], p_[:, sub * 512 + (pair * 2 + j) * 128: sub * 512 + (pair * 2 + j + 1) * 128],
                                                            v_[:, kt, 0:129], kt == 0 and j == 0, kt == NKT - 1) for j in range(2)], [pb_, vb_])
                            ob_, obb = ob.get()
                            for qb4 in range(4):
                                pair, j = qb4 // 2, qb4 % 2
                                a0, a0b = acc[0 + pair]
                                a1, a1b = acc[2 + pair]
                                r_, rb2 = rc.get()
                                S_.add("dve", (lambda r_, a0, j: lambda e: e.reciprocal(out=r_[:, 0:1], in_=a0[:, j, 128:129]))(r_, a0, j), R=[a0b], W=[rb2])
                                S_.add("dve", (lambda r_, a1, j: lambda e: e.reciprocal(out=r_[:, 1:2], in_=a1[:, j, 128:129]))(r_, a1, j), R=[a1b, rb2], W=[rb2])
                                S_.add("dve", (lambda r_: lambda e: e.tensor_tensor(out=r_[:, 2:3], in0=r_[:, 1:2], in1=lamt[:, 0:1], op=ALU.mult))(r_),
                                       R=[rb2, consts], W=[rb2])
                                o_, o_b = o1.get()
                                S_.add("dve", (lambda o_, a0, j, r_: lambda e: e.tensor_scalar(out=o_[:], in0=a0[:, j, 0:128], scalar1=r_[:, 0:1], scalar2=None,
                                                                                              op0=ALU.mult))(o_, a0, j, r_), R=[a0b, rb2], W=[o_b])
                                S_.add("dve", (lambda o_, a1, j, r_: lambda e: e.scalar_tensor_tensor(out=o_[:], in0=a1[:, j, 0:128], scalar=r_[:, 2:3], in1=o_[:],
                                                                                                     op0=ALU.mult, op1=ALU.add))(o_, a1, j, r_),
                                       R=[a1b, rb2, o_b], W=[o_b])
                                jk, jkb = oj.get()
                                S_.add("act", (lambda jk, o_, r_: lambda e: e.activation(out=jk[:], in_=o_[:], func=AF.Square, accum_out=r_[:, 3:4]))(jk, o_, r_),
                                       R=[o_b, rb2], W=[jkb, rb2])
                                S_.add("act", (lambda r_: lambda e: e.activation(out=r_[:, 4:5], in_=r_[:, 3:4], func=AF.Sqrt, bias=epsN[:, 1:2], scale=1.0 / 128))(r_),
                                       R=[rb2, consts], W=[rb2])
                                S_.add("dve", (lambda r_: lambda e: e.reciprocal(out=r_[:, 5:6], in_=r_[:, 4:5]))(r_), R=[rb2], W=[rb2])
                                S_.add("pool", (lambda ob_, o_, r_, qb4: lambda e: e.tensor_scalar(out=ob_[:, qb4, :], in0=o_[:], scalar1=r_[:, 5:6], scalar2=None,
                                                                                                  op0=ALU.mult))(ob_, o_, r_, qb4), R=[o_b, rb2], W=[obb])
                            S_.add("sp", (lambda ob_, qt, h: lambda e: e.dma_start(
                                out=sc["O"][qt * 512:(qt + 1) * 512, h * 128:(h + 1) * 128].rearrange("(k p) e -> p k e", p=128), in_=ob_[:]))(ob_, qt, h),
                                   R=[obb], W=[S_.dbuf("O", jn, qt, h)], chan="st%d" % (qt % 2))
                for job in jobs:
                    run_job(job)
                S_.flush(); chk(7)

            with ExitStack() as ps:
                psF = Pool(ps, pst, "psF", [128, 512], F32, 6)
                psB = Pool(ps, pst, "psB", [128, 1024], BF16, 2)
                wdo = ps.enter_context(sb("wdo", [128, 8, 1024], BF16))
                with ExitStack() as ws:
                    stg = Pool(ws, sb, "wstg", [128, 1024], F32, 3)
                    sidx = [0]
                    wbo = load_weight(wdo, w_do_d, 8, 1024, lambda kc: subl[:, 0:1], stg, sidx)
                    S_.flush(); chk(8)
                oin = Pool(ps, sb, "oin", [128, 1024], BF16, 3)
                oT = Pool(ps, sb, "oT", [128, 8, 128], BF16, 2)
                xres = Pool(ps, sb, "xres", [128, 1024], F32, 3)
                def run_job(job):
                    jn, S, SQ = job["n"], job["S"], job["SQ"]
                    sc = scr[jn]
                    for c in range(SQ // 128):
                        o_, o_b = oin.get()
                        S_.add("sp", (lambda o_, c: lambda e: e.dma_start(out=o_[:], in_=sc["O"][c * 128:(c + 1) * 128, :]))(o_, c),
                               R=[S_.dbuf("O", jn, c // 4, h) for h in range(8)], W=[o_b], chan="xl%d" % (c % 3))
                        xr, xrb = xres.get()
                        if job["own"]:
                            S_.add("pool", (lambda xr, c: lambda e: e.indirect_dma_start(out=xr[:, :], out_offset=None, in_=sc["x2"][:, :],
                                                                                        in_offset=bass.IndirectOffsetOnAxis(ap=idxq[:, c:c + 1], axis=0)))(xr, c),
                                   R=[consts], W=[xrb], chan="gl%d" % (c % 3))
                        else:
                            S_.add("sp", (lambda xr, c: lambda e: e.dma_start(out=xr[:], in_=sc["x2"][c * 128:(c + 1) * 128, :]))(xr, c), W=[xrb], chan="rl%d" % (c % 3))
                        pt, ptb = psB.get()

                        def tr(e, pt=pt, o_=o_):
                            for f in range(8):
                                ins = e.transpose(out=pt[:, f * 128:(f + 1) * 128], in_=o_[:, f * 128:(f + 1) * 128], identity=ident[:])
                            return ins
                        S_.add("pe", tr, R=[o_b, consts], W=[ptb])
                        oT_, oTb = oT.get()
                        S_.add("act", (lambda oT_, pt: lambda e: e.copy(out=oT_[:], in_=pt[:].rearrange("p (k t) -> p k t", k=8)))(oT_, pt), R=[ptb], W=[oTb])
                        for n in range(2):
                            pp, ppb = psF.get()
                            mm_group(pp, ppb, [(pp[:], oT_[:, f, :], wdo[:, f, n * 512:(n + 1) * 512], f == 0, f == 7) for f in range(8)], [oTb, wbo])
                            S_.add("dve", (lambda xr, pp, n: lambda e: e.tensor_tensor(out=xr[:, n * 512:(n + 1) * 512], in0=pp[:], in1=xr[:, n * 512:(n + 1) * 512],
                                                                                      op=ALU.add))(xr, pp, n), R=[ppb, xrb], W=[xrb])
                        S_.add("sp", (lambda xr, c: lambda e: e.dma_start(out=sc["x3"][c * 128:(c + 1) * 128, :], in_=xr[:]))(xr, c),
                               R=[xrb], W=[S_.dbuf("x3", jn, c)], chan="st%d" % (c % 2))
                        if debug:
                            S_.add("sp", (lambda xr, c: lambda e: e.dma_start(out=dbg[jn]["x3"][c * 128:(c + 1) * 128, :], in_=xr[:]))(xr, c),
                                   R=[xrb], W=[S_.dbuf("dx3", jn, c)], chan="dbg")
                for job in jobs:
                    run_job(job)
                S_.flush(); chk(9)

            ffn_phase(1, True)

        except _Stop:
            pass
        sch.stopped = False
        sch.maxops = 10 ** 9
        S_.add("sp", lambda e: e.dma_start(out=scr["a"]["uT"][0:1, 0:1], in_=scr["a"]["uT"][0:1, 1:2]), chan="fin")
        S_.flush(); chk(10)
        fin = S_.ops[-1]

        with nc.Block() as block:
            @block.sync
            def _(e):
                e.wait_ge(S_.csem["fin"], fin.val)
    return nc


def _rope_tabs(S, dim, scale):
    inv = (10000.0 ** (-np.arange(0, dim, 2, dtype=np.float32) / np.float32(dim))).astype(np.float32)
    ang = np.arange(S, dtype=np.float32)[:, None] * inv[None, :]
    return (np.cos(ang) * scale).astype(np.float32), (np.sin(ang) * scale).astype(np.float32)


def make_inputs(core, SA, SB, NQB, inp, xa, xb, qoff):
    f = lambda a: np.ascontiguousarray(np.asarray(a, dtype=np.float32))
    SM = max(SA, SB)
    c1, s1 = _rope_tabs(SM, 128, 1.0)
    c2, s2 = _rope_tabs(SM, 128, 128 ** -0.5)
    rt = np.concatenate([c1, s1, c2, s2], axis=1)
    ck, sk = _rope_tabs(SM, 64, 1.0)
    cq, sq = _rope_tabs(SM, 64, 0.125)
    dtk = np.concatenate([ck, sk], 1)
    dtq = np.concatenate([cq, sq], 1)
    SQB = NQB * 128
    idx = (qoff + np.arange(NQB)[None, :] * 128 + np.arange(128)[:, None]).astype(np.int32)
    pk = lambda v: f(np.asarray(v).reshape(-1, 128).T)
    m = {
        "xa": f(xa), "xb": f(xb), "idxq": np.ascontiguousarray(idx),
        "rt": f(rt), "dtk": f(dtk), "dtqa": f(dtq[:SA]), "dtqb": f(dtq[qoff:qoff + SQB]),
        "w_in": f(inp["hyb_w_in"][0]), "w_out": f(inp["hyb_w_out"][0]), "w_qkv": f(inp["diff_w_qkv"][0]), "w_do": f(inp["diff_w_out"][0]),
        "nmix": pk(np.asarray(inp["norm_mix"]).reshape(-1)), "nffn": pk(np.asarray(inp["norm_ffn"]).reshape(-1)),
        "nfin": f(np.asarray(inp["norm_final"]).reshape(1, D)),
        "convw": f(np.asarray(inp["hyb_conv_w"][0]).reshape(3, 4, 128).transpose(2, 1, 0).reshape(128, 12)),
        "decf": f(np.asarray(inp["hyb_decay_fwd"]).reshape(1, 4)), "decb": f(np.asarray(inp["hyb_decay_bwd"]).reshape(1, 4)),
        "gnw": pk(np.asarray(inp["hyb_gn"]).reshape(-1)),
        "lamv": f(np.concatenate([np.asarray(inp[k]).reshape(-1) for k in ("diff_lq1", "diff_lk1", "diff_lq2", "diff_lk2")]).reshape(1, 256)),
        "subln": f(np.asarray(inp["diff_subln"]).reshape(128, 1)),
    }
    for l in range(2):
        m["w_g%d" % l] = f(inp["ffn_w_gate"][l])
        m["w_u%d" % l] = f(inp["ffn_w_up"][l])
        m["w_d%d" % l] = f(inp["ffn_w_down"][l])
    return m


def kernel(**inp):
    xp = np.asarray(inp["x_prompt"])
    xs = np.asarray(inp["x_sample"])
    SA, SB = xs.shape[1], xp.shape[1]
    NQB = SB // 4 // 128
    nc = build(SA, SB, NQB)
    in_maps = [make_inputs(c, SA, SB, NQB, inp, xs[c], xp[c // 4], (c % 4) * (SB // 4)) for c in range(8)]
    res = run_bass_kernel_spmd(nc, in_maps, core_ids=list(range(8)))
    ys = np.stack([res.results[c]["ya"] for c in range(8)], 0).astype(np.float32)
    yp = np.stack([np.concatenate([res.results[g * 4 + r]["yb"] for r in range(4)], 0) for g in range(2)], 0).astype(np.float32)
    return (yp, ys)
```

```python
import math
from contextlib import ExitStack
import numpy as np
import concourse.bass as bass
import concourse.mybir as mybir
from concourse.bass_utils import run_bass_kernel_spmd

F32 = mybir.dt.float32
BF16 = mybir.dt.bfloat16
I32 = mybir.dt.int32
AF = mybir.ActivationFunctionType
ALU = mybir.AluOpType
AX = mybir.AxisListType

D = 1024
DFF = 2816
NFC = DFF // 128
LAMBDA_INIT = 0.8 - 0.6 * math.exp(-0.3 * 1)
ENG = ("pe", "act", "dve", "pool", "sp")


class Buf:
    __slots__ = ("w", "r", "excl")

    def __init__(self, excl=False):
        self.w = None
        self.r = {}
        self.excl = excl


class Op:
    __slots__ = ("eng", "fn", "deps", "chan", "signal", "val")

    def __init__(self, eng, fn, deps, chan):
        self.eng = eng
        self.fn = fn
        self.deps = deps
        self.chan = chan
        self.signal = chan is not None
        self.val = None


class Sched:
    def __init__(self, nc, es, nchan=40):
        self.nc = nc
        self.ops = []
        self.emitted = 0
        self.last = {}
        self.lastchan = {}
        self.bar = ()
        self.esem = {e: es.enter_context(nc.semaphore("s_" + e)) for e in ENG}
        self.freechan = [es.enter_context(nc.semaphore("c%d" % i)) for i in range(nchan)]
        self.csem = {}
        self.cnt = {e: 0 for e in ENG}
        self.ccnt = {}
        self.waited = {e: {} for e in ENG}
        self.dram = {}

    def dbuf(self, *key):
        b = self.dram.get(key)
        if b is None:
            b = self.dram[key] = Buf()
        return b

    stopped = False
    maxops = 10 ** 9
    names = {}

    def add(self, eng, fn, R=(), W=(), chan=None):
        if self.stopped:
            return -1
        if len(self.ops) >= self.maxops:
            self.stopped = True
            return -1
        deps = set(self.bar)
        key0 = chan if chan is not None else eng
        for b in R:
            if b.w is not None:
                deps.add(b.w)
            if b.excl:
                deps.update(v for k, v in b.r.items() if k != key0)
        for b in W:
            if b.w is not None:
                deps.add(b.w)
            deps.update(b.r.values())
        idx = len(self.ops)
        if chan is not None:
            if chan in self.lastchan:
                deps.add(self.lastchan[chan])
            self.lastchan[chan] = idx
        self.ops.append(Op(eng, fn, deps, chan))
        key = chan if chan is not None else eng
        for b in R:
            b.r[key] = idx
        for b in W:
            b.w = idx
            b.r = {}
        self.last[eng] = idx
        return idx

    def _event(self, op):
        if op.chan is not None:
            return self.csem[op.chan], op.val
        return self.esem[op.eng], op.val

    def flush(self):
        nc = self.nc
        lo = self.emitted
        if lo == len(self.ops):
            return
        ops = self.ops
        for i in range(lo, len(ops)):
            for d in ops[i].deps:
                if d >= lo:
                    od = ops[d]
                    if od.chan is None and od.eng == "pe" and ops[i].eng == "pe" and ops[i].chan is None:
                        continue
                    od.signal = True
        for e, i in self.last.items():
            ops[i].signal = True
        for i in range(lo, len(ops)):
            op = ops[i]
            if op.chan is not None:
                if op.chan not in self.csem:
                    self.csem[op.chan] = self.freechan.pop()
                    self.ccnt[op.chan] = 0
                self.ccnt[op.chan] += 16
                op.val = self.ccnt[op.chan]
            elif op.signal:
                self.cnt[op.eng] += 1
                op.val = self.cnt[op.eng]
        per = {e: [] for e in ENG}
        for i in range(lo, len(ops)):
            per[ops[i].eng].append(i)

        def emit(engname, e):
            waited = self.waited[engname]
            for i in per[engname]:
                op = ops[i]
                for d in sorted(op.deps):
                    od = ops[d]
                    if d < lo and d not in self.bar:
                        continue
                    if od.chan is None and od.eng == "pe" and engname == "pe" and op.chan is None:
                        continue
                    sem, val = self._event(od)
                    k = id(sem)
                    if waited.get(k, 0) < val:
                        e.wait_ge(sem, val)
                        waited[k] = val
                ins = op.fn(e)
                if op.signal:
                    sem, val = self._event(op)
                    ins.then_inc(sem, 16 if op.chan is not None else 1)

        with nc.Block() as block:
            @block.tensor
            def _(e):
                emit("pe", e)

            @block.scalar
            def _(e):
                emit("act", e)

            @block.vector
            def _(e):
                emit("dve", e)

            @block.gpsimd
            def _(e):
                emit("pool", e)

            @block.sync
            def _(e):
                emit("sp", e)
        self.emitted = len(ops)
        self.bar = tuple(set(list(self.last.values()) + list(self.lastchan.values())))
        for i in self.bar:
            assert ops[i].signal


def staggered(fn, chunks, delay):
    def thread(idx):
        for k in idx:
            yield from fn(k, chunks[k])
    g = [thread(range(0, len(chunks), 2)), thread(range(1, len(chunks), 2))]
    alive = [True, True]
    for _ in range(delay):
        try:
            next(g[0])
        except StopIteration:
            alive[0] = False
            break
    while any(alive):
        for i in (0, 1):
            if alive[i]:
                try:
                    next(g[i])
                except StopIteration:
                    alive[i] = False


class Pool:
    def __init__(self, es, alloc, name, shape, dt, n):
        self.t = [es.enter_context(alloc("%s%d" % (name, i), shape, dt)) for i in range(n)]
        self.b = [Buf(excl=(alloc.__name__ == 'pst')) for _ in range(n)]
        self.i = 0

    def get(self):
        k = self.i % len(self.t)
        self.i += 1
        return self.t[k], self.b[k]


class _Stop(Exception):
    pass


def build(SA, SB, NQB, debug=False, nph=99):
    nc = bass.Bass("TRN2", target_bir_lowering=False)
    dt_in = lambda n, s, d=F32: nc.dram_tensor(n, s, d, kind="ExternalInput").ap()
    dt_scr = lambda n, s, d: nc.dram_tensor(n, s, d, kind="Internal").ap()
    SQB = NQB * 128
    jobs = [dict(n="a", S=SA, SQ=SA, own=False), dict(n="b", S=SB, SQ=SQB, own=True)]
    xin_d = {"a": dt_in("xa", [SA, D]), "b": dt_in("xb", [SB, D])}
    idxq_d = dt_in("idxq", [128, NQB], I32)
    SM = max(SA, SB)
    rt_d = dt_in("rt", [SM, 256])
    dtk_d = dt_in("dtk", [SM, 64])
    dtq_d = {"a": dt_in("dtqa", [SA, 64]), "b": dt_in("dtqb", [SQB, 64])}
    w_in_d = dt_in("w_in", [D, 3584])
    w_out_d = dt_in("w_out", [D, D])
    w_qkv_d = dt_in("w_qkv", [D, 3072])
    w_do_d = dt_in("w_do", [D, D])
    w_g_d = [dt_in("w_g%d" % l, [D, DFF]) for l in range(2)]
    w_u_d = [dt_in("w_u%d" % l, [D, DFF]) for l in range(2)]
    w_d_d = [dt_in("w_d%d" % l, [DFF, D]) for l in range(2)]
    nmix_d = dt_in("nmix", [128, 16])
    nffn_d = dt_in("nffn", [128, 16])
    nfin_d = dt_in("nfin", [1, D])
    convw_d = dt_in("convw", [128, 12])
    decf_d = dt_in("decf", [1, 4])
    decb_d = dt_in("decb", [1, 4])
    gn_d = dt_in("gnw", [128, 4])
    lam_d = dt_in("lamv", [1, 256])
    subln_d = dt_in("subln", [128, 1])
    y_d = {j["n"]: nc.dram_tensor("y" + j["n"], [j["SQ"], D], F32, kind="ExternalOutput").ap() for j in jobs}
    scr = {}
    for j in jobs:
        n, S, SQ = j["n"], j["S"], j["SQ"]
        scr[n] = dict(
            uT=dt_scr("uT" + n, [512, S + 2], F32),
            kr=dt_scr("kr" + n, [S, 512], BF16),
            vr=dt_scr("vr" + n, [S, 512], BF16),
            sb=dt_scr("sb" + n, [S // 128, 128, 512], BF16),
            x1=dt_scr("x1" + n, [S, D], F32),
            x2=dt_scr("x2" + n, [S, D], F32),
            QT=dt_scr("QT" + n, [8, 128, SQ], BF16),
            KT=dt_scr("KT" + n, [8, 128, S], BF16),
            V=dt_scr("V" + n, [8, S, 128], BF16),
            O=dt_scr("O" + n, [SQ, D], BF16),
            x3=dt_scr("x3" + n, [SQ, D], F32),
        )
    dbg = {}
    if debug:
        for j in jobs:
            n, S, SQ = j["n"], j["S"], j["SQ"]
            dbg[n] = dict(
                x1=nc.dram_tensor("dbg_x1" + n, [S, D], F32, kind="ExternalOutput").ap(),
                x2=nc.dram_tensor("dbg_x2" + n, [S, D], F32, kind="ExternalOutput").ap(),
                x3=nc.dram_tensor("dbg_x3" + n, [SQ, D], F32, kind="ExternalOutput").ap(),
            )

    with ExitStack() as es:
        sch = Sched(nc, es)
        import os
        sch.maxops = int(os.environ.get('KSTOP', 10 ** 9))
        S_ = sch
        _uid = [0]

        def sb(name, shape, dt):
            _uid[0] += 1
            return nc.sbuf_tensor("s%d_%s" % (_uid[0], name), shape, dt)

        def pst(name, shape, dt):
            _uid[0] += 1
            return nc.psum_tensor("p%d_%s" % (_uid[0], name), shape, dt)
        ident = es.enter_context(sb("ident", [128, 128], BF16))
        epsN = es.enter_context(sb("epsN", [128, 4], F32))
        nmix = es.enter_context(sb("nmix", [128, 16], F32))
        nffn = es.enter_context(sb("nffn", [128, 16], F32))
        gnw = es.enter_context(sb("gnw", [128, 4], F32))
        subl = es.enter_context(sb("subl", [128, 1], F32))
        convw = es.enter_context(sb("convw", [128, 12], F32))
        idxq = es.enter_context(sb("idxq", [128, NQB], I32))
        lamt = es.enter_context(sb("lamt", [128, 8], F32))
        consts = Buf()

        def chk(k):
            if nph == k:
                sch.stopped = True

        try:
            with ExitStack() as ps:
                tmpa = ps.enter_context(sb("tmpa", [128, 128], F32))
                lamv = ps.enter_context(sb("lamv", [128, 256], F32))
                lamp = ps.enter_context(sb("lamp", [128, 128], F32))
                S_.add("pool", lambda e: e.iota(tmpa[:], pattern=[[1, 128]], base=0, channel_multiplier=-1,
                                                allow_small_or_imprecise_dtypes=True), W=[consts])
                S_.add("dve", lambda e: e.tensor_single_scalar(out=ident[:], in_=tmpa[:], scalar=0.0, op=ALU.is_equal),
                       R=[consts], W=[consts])
                S_.add("dve", lambda e: e.memset(epsN[:, 0:1], 1e-6), W=[consts])
                S_.add("dve", lambda e: e.memset(epsN[:, 1:3], 1e-5), W=[consts])
                S_.add("dve", lambda e: e.memset(epsN[:, 3:4], 0.0), W=[consts])
                for t, d in ((nmix, nmix_d), (nffn, nffn_d), (gnw, gn_d), (subl, subln_d), (convw, convw_d), (idxq, idxq_d)):
                    S_.add("sp", (lambda t, d: lambda e: e.dma_start(out=t[:], in_=d[:, :]))(t, d), W=[consts], chan="set")
                S_.add("sp", lambda e: e.dma_start(out=lamv[:], in_=lam_d.partition_broadcast(128)), W=[consts], chan="set")
                S_.add("dve", lambda e: e.tensor_single_scalar(out=subl[:], in_=subl[:], scalar=1.0 - LAMBDA_INIT, op=ALU.mult),
                       R=[consts], W=[consts])
                S_.add("dve", lambda e: e.tensor_tensor(out=lamp[:, 0:64], in0=lamv[:, 0:64], in1=lamv[:, 64:128], op=ALU.mult),
                       R=[consts], W=[consts])
                S_.add("dve", lambda e: e.tensor_tensor(out=lamp[:, 64:128], in0=lamv[:, 128:192], in1=lamv[:, 192:256], op=ALU.mult),
                       R=[consts], W=[consts])
                S_.add("dve", lambda e: e.reduce_sum(out=lamt[:, 1:2], in_=lamp[:, 0:64], axis=AX.X), R=[consts], W=[consts])
                S_.add("dve", lambda e: e.reduce_sum(out=lamt[:, 2:3], in_=lamp[:, 64:128], axis=AX.X), R=[consts], W=[consts])
                S_.add("act", lambda e: e.activation(out=lamt[:, 3:5], in_=lamt[:, 1:3], func=AF.Exp), R=[consts], W=[consts])
                S_.add("dve", lambda e: e.tensor_tensor(out=lamt[:, 5:6], in0=lamt[:, 4:5], in1=lamt[:, 3:4], op=ALU.subtract),
                       R=[consts], W=[consts])
                S_.add("dve", lambda e: e.tensor_single_scalar(out=lamt[:, 0:1], in_=lamt[:, 5:6], scalar=-LAMBDA_INIT, op=ALU.add),
                       R=[consts], W=[consts])
                S_.flush(); chk(0)

            def load_weight(dst, src, KC, C, rows, stg, sidx):
                srcv = src.rearrange("(kc p) c -> p kc c", p=128)
                wb = Buf()
                for kc in range(KC):
                    for c0 in range(0, C, 1024):
                        cw = min(1024, C - c0)
                        st, stb = stg.get()
                        S_.add("sp", (lambda st, kc, c0, cw: lambda e: e.dma_start(out=st[:, 0:cw], in_=srcv[:, kc, c0:c0 + cw]))(st, kc, c0, cw),
                               W=[stb], chan="wl%d" % (sidx[0] % 3))
                        eng = ("dve", "pool")[sidx[0] % 2]
                        sidx[0] += 1
                        r = rows(kc) if rows is not None else None
                        if r is None:
                            S_.add(eng, (lambda st, kc, c0, cw: lambda e: e.tensor_copy(out=dst[:, kc, c0:c0 + cw], in_=st[:, 0:cw]))(st, kc, c0, cw),
                                   R=[stb, consts], W=[wb])
                        else:
                            S_.add(eng, (lambda st, kc, c0, cw, r: lambda e: e.tensor_scalar(
                                out=dst[:, kc, c0:c0 + cw], in0=st[:, 0:cw], scalar1=r, scalar2=None, op0=ALU.mult))(st, kc, c0, cw, r),
                                   R=[stb, consts], W=[wb])
                return wb

            class NormT:
                def __init__(self, es_, psB):
                    self.junk = Pool(es_, sb, "njunk", [128, 1024], BF16, 2)
                    self.xs = Pool(es_, sb, "nxs", [128, 1024], BF16, 2)
                    self.st = Pool(es_, sb, "nst", [128, 4], F32, 4)
                    self.psB = psB

                def run(self, x, xb, dst, dstb, dst_is_write=True):
                    jk, jkb = self.junk.get()
                    xs, xsb = self.xs.get()
                    st, stb = self.st.get()
                    S_.add("act", lambda e: e.activation(out=jk[:], in_=x, func=AF.Square, accum_out=st[:, 0:1]), R=[xb], W=[jkb, stb])
                    S_.add("act", lambda e: e.activation(out=st[:, 1:2], in_=st[:, 0:1], func=AF.Sqrt, bias=epsN[:, 0:1], scale=1.0 / D),
                           R=[stb, consts], W=[stb])
                    S_.add("dve", lambda e: e.reciprocal(out=st[:, 2:3], in_=st[:, 1:2]), R=[stb], W=[stb])
                    S_.add("dve", lambda e: e.tensor_scalar(out=xs[:], in0=x, scalar1=st[:, 2:3], scalar2=None, op0=ALU.mult),
                           R=[xb, stb], W=[xsb])
                    pt, ptb = self.psB.get()

                    def tr(e):
                        for kc in range(8):
                            ins = e.transpose(out=pt[:, kc * 128:(kc + 1) * 128], in_=xs[:, kc * 128:(kc + 1) * 128], identity=ident[:])
                        return ins
                    S_.add("pe", tr, R=[xsb, consts], W=[ptb])
                    S_.add("act", lambda e: e.copy(out=dst, in_=pt[:].rearrange("p (k t) -> p k t", k=8)), R=[ptb], W=[dstb])
                    return st, stb

            def rope(eng1, eng2, src, srcb, H, half, cos, sin, tb, A, Ab, T, Tb, out, outb):
                v4 = lambda ap: ap.rearrange("p (h two d) -> p h two d", h=H, two=2)
                cb4 = cos.unsqueeze(1).unsqueeze(1).broadcast_to([128, H, 2, half])
                sb3 = sin.unsqueeze(1).broadcast_to([128, H, half])
                S_.add(eng1, lambda e: e.tensor_tensor(out=v4(A), in0=v4(src), in1=cb4, op=ALU.mult), R=[srcb, tb], W=[Ab])
                S_.add(eng2, lambda e: e.tensor_tensor(out=v4(T)[:, :, 0, :], in0=v4(src)[:, :, 1, :], in1=sb3, op=ALU.mult), R=[srcb, tb], W=[Tb])
                S_.add(eng2, lambda e: e.tensor_tensor(out=v4(T)[:, :, 1, :], in0=v4(src)[:, :, 0, :], in1=sb3, op=ALU.mult), R=[srcb, tb], W=[Tb])
                S_.add(eng1, lambda e: e.tensor_tensor(out=v4(out)[:, :, 0, :], in0=v4(A)[:, :, 0, :], in1=v4(T)[:, :, 0, :], op=ALU.subtract),
                       R=[Ab, Tb], W=[outb])
                S_.add(eng1, lambda e: e.tensor_tensor(out=v4(out)[:, :, 1, :], in0=v4(A)[:, :, 1, :], in1=v4(T)[:, :, 1, :], op=ALU.add),
                       R=[Ab, Tb], W=[outb])

            def mm_group(ps, psb, pairs, R):
                def f(e):
                    for (o, l, r, s0, s1) in pairs:
                        ins = e.matmul(o, lhsT=l, rhs=r, start=s0, stop=s1)
                    return ins
                S_.add("pe", f, R=R, W=[psb])

            with ExitStack() as ps:
                psF = Pool(ps, pst, "psF", [128, 512], F32, 6)
                psB = Pool(ps, pst, "psB", [128, 1024], BF16, 2)
                w_in = ps.enter_context(sb("w_in", [128, 8, 3584], BF16))
                w_out = ps.enter_context(sb("w_out", [128, 8, 1024], BF16))
                stg = Pool(ps, sb, "wstg", [128, 1024], F32, 3)
                sidx = [0]
                wb_in = load_weight(w_in, w_in_d, 8, 3584, lambda kc: nmix[:, kc:kc + 1], stg, sidx)
                wb_out = load_weight(w_out, w_out_d, 8, 1024, lambda kc: (gnw[:, kc - 4:kc - 3] if kc >= 4 else None), stg, sidx)
                lg = ps.enter_context(sb("lg", [128, 8], F32))
                tabs = {k: ps.enter_context(sb("tab" + k, [128, 512], F32)) for k in ("DT", "QF", "QB", "KF", "KB", "CF", "CB")}
                tb_ = Buf()
                with ExitStack() as ts:
                    io = {k: ts.enter_context(sb("io" + k, [128, 128], F32)) for k in ("rel", "relp", "reln", "mp", "mn", "i1", "ib", "kf", "kb", "c128", "e1", "e2")}
                    S_.add("sp", lambda e: e.dma_start(out=lg[:, 0:4], in_=decf_d.partition_broadcast(128)), W=[tb_], chan="set")
                    S_.add("sp", lambda e: e.dma_start(out=lg[:, 4:8], in_=decb_d.partition_broadcast(128)), W=[tb_], chan="set")
                    S_.add("act", lambda e: e.activation(out=lg[:], in_=lg[:], func=AF.Exp), R=[tb_], W=[tb_])
                    S_.add("dve", lambda e: e.tensor_single_scalar(out=lg[:], in_=lg[:], scalar=-1.0, op=ALU.mult), R=[tb_], W=[tb_])
                    io_ = lambda k, pat, base, cm: S_.add("pool", lambda e: e.iota(io[k][:], pattern=pat, base=base, channel_multiplier=cm,
                                                                                    allow_small_or_imprecise_dtypes=True), W=[tb_])
                    io_("rel", [[1, 128]], 0, -1)
                    io_("i1", [[1, 128]], 1, 0)
                    io_("ib", [[-1, 128]], 128, 0)
                    io_("kf", [[0, 128]], 127, -1)
                    io_("kb", [[0, 128]], 0, 1)
                    io_("c128", [[0, 128]], 128, 0)
                    S_.add("dve", lambda e: e.tensor_single_scalar(out=io["relp"][:], in_=io["rel"][:], scalar=0.0, op=ALU.max), R=[tb_], W=[tb_])
                    S_.add("dve", lambda e: e.tensor_scalar(out=io["reln"][:], in0=io["rel"][:], scalar1=-1.0, scalar2=0.0, op0=ALU.mult, op1=ALU.max),
                           R=[tb_], W=[tb_])
                    S_.add("dve", lambda e: e.tensor_single_scalar(out=io["mp"][:], in_=io["rel"][:], scalar=0.0, op=ALU.is_ge), R=[tb_], W=[tb_])
                    S_.add("dve", lambda e: e.tensor_single_scalar(out=io["mn"][:], in_=io["rel"][:], scalar=0.0, op=ALU.is_lt), R=[tb_], W=[tb_])
                    for h in range(4):
                        hs = slice(h * 128, (h + 1) * 128)
                        ex = lambda dst, src, col: S_.add("act", lambda e: e.activation(out=dst, in_=src, func=AF.Exp, scale=lg[:, col:col + 1]),
                                                          R=[tb_], W=[tb_])
                        ex(io["e1"][:], io["relp"][:], h)
                        ex(io["e2"][:], io["reln"][:], 4 + h)
                        S_.add("dve", lambda e: e.tensor_tensor(out=io["e1"][:], in0=io["e1"][:], in1=io["mp"][:], op=ALU.mult), R=[tb_], W=[tb_])
                        S_.add("dve", lambda e: e.tensor_tensor(out=io["e2"][:], in0=io["e2"][:], in1=io["mn"][:], op=ALU.mult), R=[tb_], W=[tb_])
                        S_.add("dve", (lambda hs: lambda e: e.tensor_tensor(out=tabs["DT"][:, hs], in0=io["e1"][:], in1=io["e2"][:], op=ALU.add))(hs),
                               R=[tb_], W=[tb_])
                        ex(tabs["QF"][:, hs], io["i1"][:], h)
                        ex(tabs["QB"][:, hs], io["ib"][:], 4 + h)
                        ex(tabs["KF"][:, hs], io["kf"][:], h)
                        ex(tabs["KB"][:, hs], io["kb"][:], 4 + h)
                        ex(tabs["CF"][:, hs], io["c128"][:], h)
                        ex(tabs["CB"][:, hs], io["c128"][:], 4 + h)
                    S_.flush(); chk(1)
                nt = NormT(ps, psB)
                xin = Pool(ps, sb, "xin", [128, 1024], F32, 3)
                xnT = Pool(ps, sb, "xnT", [128, 8, 128], BF16, 2)
                ropt = Pool(ps, sb, "ropt", [128, 256], F32, 3)
                rA = Pool(ps, sb, "rA", [128, 512], F32, 2)
                rT = Pool(ps, sb, "rT", [128, 512], F32, 2)
                kr = Pool(ps, sb, "kr", [128, 512], BF16, 3)
                vv = Pool(ps, sb, "vv", [128, 512], BF16, 3)
                sbl = Pool(ps, sb, "sbl", [128, 512], BF16, 2)
                kd = Pool(ps, sb, "kd", [128, 512], BF16, 2)
                acS = Pool(ps, sb, "acS", [128, 512], F32, 2)
                uT = Pool(ps, sb, "uT", [128, 4, 130], F32, 2)
                S32 = ps.enter_context(sb("S32", [128, 512], F32))
                S16 = Pool(ps, sb, "S16", [128, 512], BF16, 2)
                qr = Pool(ps, sb, "qr", [128, 512], BF16, 2)
                qT = Pool(ps, sb, "qT", [128, 512], BF16, 2)
                qfT = Pool(ps, sb, "qfT", [128, 512], BF16, 2)
                qbT = Pool(ps, sb, "qbT", [128, 512], BF16, 2)
                kT = Pool(ps, sb, "kT", [128, 512], BF16, 2)
                AT = Pool(ps, sb, "AT", [128, 512], BF16, 2)
                gst = Pool(ps, sb, "gst", [128, 4, 6], F32, 2)
                gmv = Pool(ps, sb, "gmv", [128, 4, 2], F32, 2)
                grs = Pool(ps, sb, "grs", [128, 8], F32, 2)
                on = Pool(ps, sb, "on", [128, 512], F32, 2)
                sg = Pool(ps, sb, "sg", [128, 512], F32, 2)
                abS = Pool(ps, sb, "abS", [128, 512], F32, 2)
                yb = Pool(ps, sb, "yb", [128, 512], BF16, 2)
                yT = Pool(ps, sb, "yT", [128, 8, 128], BF16, 2)
                cv = Pool(ps, sb, "cv", [128, 4, 128], F32, 2)
                x1t = Pool(ps, sb, "x1t", [128, 1024], F32, 2)
                S32b = Buf()

                def run_job(job):
                    jn, S = job["n"], job["S"]
                    NCH = S // 128
                    xd = xin_d[jn]
                    sc = scr[jn]
                    uTv = sc["uT"].rearrange("(cg p) t -> p cg t", p=128)
                    zt, ztb = cv.get()
                    S_.add("pool", lambda e: e.memset(zt[:, :, 0:1], 0.0), W=[ztb])
                    S_.add("sp", lambda e: e.dma_start(out=uTv[:, :, 0:1], in_=zt[:, :, 0:1], allow_slow_non_contiguous=True), R=[ztb], W=[S_.dbuf("uT", jn, -1)], chan="st0")
                    S_.add("sp", lambda e: e.dma_start(out=uTv[:, :, S + 1:S + 2], in_=zt[:, :, 0:1], allow_slow_non_contiguous=True), R=[ztb], W=[S_.dbuf("uT", jn, -2)], chan="st0")
                    S_.add("dve", lambda e: e.memset(S32[:], 0.0), W=[S32b])
                    stv = list(S16.get())
                    tick = [0]
                    S_.add("pool", (lambda s16: lambda e: e.memset(s16[:], 0.0))(stv[0]), W=[stv[1]])

                    def pre_chunk(k, c):
                        x, xb = xin.get()
                        S_.add("sp", (lambda x, c: lambda e: e.dma_start(out=x[:], in_=xd[c * 128:(c + 1) * 128, :]))(x, c), W=[xb], chan="xl%d" % (c % 3))
                        rtb_t, rtb = ropt.get()
                        S_.add("sp", (lambda t, c: lambda e: e.dma_start(out=t[:], in_=rt_d[c * 128:(c + 1) * 128, :]))(rtb_t, c), W=[rtb], chan="rl%d" % (c % 3))
                        xt_, xtb = xnT.get()
                        nt.run(x[:], xb, xt_[:], xtb)
                        yield
                        pk, pkb = psF.get()
                        mm_group(pk, pkb, [(pk[:], xt_[:, kc, :], w_in[:, kc, 2048:2560], kc == 0, kc == 7) for kc in range(8)], [xtb, wb_in])
                        pv, pvb = psF.get()
                        mm_group(pv, pvb, [(pv[:], xt_[:, kc, :], w_in[:, kc, 2560:3072], kc == 0, kc == 7) for kc in range(8)], [xtb, wb_in])
                        pc, pcb = psF.get()
                        mm_group(pc, pcb, [(pc[:, cg * 128:(cg + 1) * 128], w_in[:, kc, 512 + cg * 128:512 + (cg + 1) * 128], xt_[:, kc, :], kc == 0, kc == 7)
                                           for cg in range(4) for kc in range(8)], [xtb, wb_in])
                        ph, phb = psF.get()
                        mm_group(ph, phb, [(ph[:, cg * 128:(cg + 1) * 128], w_in[:, kc, 1024 + cg * 128:1024 + (cg + 1) * 128], xt_[:, kc, :], kc == 0, kc == 7)
                                           for cg in range(4) for kc in range(8)], [xtb, wb_in])
                        ac, acb = acS.get()
                        S_.add("act", (lambda ac, pc: lambda e: e.copy(out=ac[:], in_=pc[:]))(ac, pc), R=[pcb], W=[acb])
                        u, ub = uT.get()
                        S_.add("dve", (lambda u, ac, ph: lambda e: e.tensor_tensor(out=u[:, :, 0:128], in0=ac[:].rearrange("p (c t) -> p c t", c=4),
                                                                                  in1=ph[:].rearrange("p (c t) -> p c t", c=4), op=ALU.mult))(u, ac, ph),
                               R=[acb, phb], W=[ub])
                        S_.add("sp", (lambda u, c: lambda e: e.dma_start(out=uTv[:, :, 1 + c * 128:1 + (c + 1) * 128], in_=u[:, :, 0:128]))(u, c),
                               R=[ub], W=[S_.dbuf("uT", jn, c)], chan="st%d" % (c % 2))
                        A, Ab = rA.get()
                        T, Tb = rT.get()
                        k_, kb_ = kr.get()
                        rope("dve", "pool" if False else "dve", pk[:], pkb, 4, 64, rtb_t[:, 128:192], rtb_t[:, 192:256], rtb, A[:], Ab, T[:], Tb, k_[:], kb_)
                        v_, vb_ = vv.get()
                        S_.add("act", (lambda v_, pv: lambda e: e.copy(out=v_[:], in_=pv[:]))(v_, pv), R=[pvb], W=[vb_])
                        S_.add("sp", (lambda k_, c: lambda e: e.dma_start(out=sc["kr"][c * 128:(c + 1) * 128, :], in_=k_[:]))(k_, c),
                               R=[kb_], W=[S_.dbuf("kr", jn, c)], chan="st%d" % (c % 2))
                        S_.add("sp", (lambda v_, c: lambda e: e.dma_start(out=sc["vr"][c * 128:(c + 1) * 128, :], in_=v_[:]))(v_, c),
                               R=[vb_], W=[S_.dbuf("vr", jn, c)], chan="st%d" % (c % 2))
                        yield
                        while tick[0] != k:
                            yield
                        S_.add("sp", (lambda s16, c: lambda e: e.dma_start(out=sc["sb"][c, :, :], in_=s16[:]))(stv[0], c),
                               R=[stv[1]], W=[S_.dbuf("sb", jn, c)], chan="st%d" % (c % 2))
                        kd_, kdb = kd.get()
                        S_.add("pool", (lambda kd_, k_: lambda e: e.tensor_tensor(out=kd_[:], in0=k_[:], in1=tabs["KB"][:], op=ALU.mult))(kd_, k_),
                               R=[kb_, tb_], W=[kdb])
                        pS, pSb = psF.get()
                        mm_group(pS, pSb, [(pS[:, h * 128:(h + 1) * 128], kd_[:, h * 128:(h + 1) * 128], v_[:, h * 128:(h + 1) * 128], True, True) for h in range(4)],
                                 [kdb, vb_])
                        S_.add("pool", lambda e: e.tensor_tensor(out=S32[:], in0=S32[:], in1=tabs["CB"][:], op=ALU.mult), R=[S32b, tb_], W=[S32b])
                        S_.add("dve", (lambda pS: lambda e: e.tensor_tensor(out=S32[:], in0=S32[:], in1=pS[:], op=ALU.add))(pS), R=[S32b, pSb], W=[S32b])
                        stv[0], stv[1] = S16.get()
                        S_.add("act", (lambda s16: lambda e: e.copy(out=s16[:], in_=S32[:]))(stv[0]), R=[S32b], W=[stv[1]])
                        tick[0] = k + 1
                        yield
                    staggered(pre_chunk, list(range(NCH - 1, -1, -1)), 4)
                    S_.add("dve", lambda e: e.memset(S32[:], 0.0), W=[S32b])
                    stv[0], stv[1] = S16.get()
                    tick[0] = 0
                    S_.add("pool", (lambda s16: lambda e: e.memset(s16[:], 0.0))(stv[0]), W=[stv[1]])

                    def main_chunk(k, c):
                        x, xb = xin.get()
                        S_.add("sp", (lambda x, c: lambda e: e.dma_start(out=x[:], in_=xd[c * 128:(c + 1) * 128, :]))(x, c), W=[xb], chan="xl%d" % (c % 3))
                        rtb_t, rtb = ropt.get()
                        S_.add("sp", (lambda t, c: lambda e: e.dma_start(out=t[:], in_=rt_d[c * 128:(c + 1) * 128, :]))(rtb_t, c), W=[rtb], chan="rl%d" % (c % 3))
                        k_, kb_ = kr.get()
                        S_.add("sp", (lambda k_, c: lambda e: e.dma_start(out=k_[:], in_=sc["kr"][c * 128:(c + 1) * 128, :]))(k_, c),
                               R=[S_.dbuf("kr", jn, c)], W=[kb_], chan="kl%d" % (c % 3))
                        v_, vb_ = vv.get()
                        S_.add("sp", (lambda v_, c: lambda e: e.dma_start(out=v_[:], in_=sc["vr"][c * 128:(c + 1) * 128, :]))(v_, c),
                               R=[S_.dbuf("vr", jn, c)], W=[vb_], chan="vl%d" % (c % 3))
                        sl, slb = sbl.get()
                        S_.add("sp", (lambda sl, c: lambda e: e.dma_start(out=sl[:], in_=sc["sb"][c, :, :]))(sl, c),
                               R=[S_.dbuf("sb", jn, c)], W=[slb], chan="sl%d" % (c % 2))
                        u, ub = uT.get()
                        urd = [S_.dbuf("uT", jn, cc) for cc in (c - 1, c, c + 1) if 0 <= cc < NCH] + [S_.dbuf("uT", jn, -1), S_.dbuf("uT", jn, -2)]
                        S_.add("sp", (lambda u, c: lambda e: e.dma_start(out=u[:], in_=uTv[:, :, c * 128:c * 128 + 130]))(u, c),
                               R=urd, W=[ub], chan="ul%d" % (c % 2))
                        yield
                        xt_, xtb = xnT.get()
                        nt.run(x[:], xb, xt_[:], xtb)
                        yield
                        pq, pqb = psF.get()
                        mm_group(pq, pqb, [(pq[:], xt_[:, kc, :], w_in[:, kc, 1536:2048], kc == 0, kc == 7) for kc in range(8)], [xtb, wb_in])
                        pg, pgb = psF.get()
                        mm_group(pg, pgb, [(pg[:], xt_[:, kc, :], w_in[:, kc, 3072:3584], kc == 0, kc == 7) for kc in range(8)], [xtb, wb_in])
                        pab, pabb = psF.get()
                        mm_group(pab, pabb, [(pab[:, cg * 128:(cg + 1) * 128], w_in[:, kc, cg * 128:(cg + 1) * 128], xt_[:, kc, :], kc == 0, kc == 7)
                                             for cg in range(4) for kc in range(8)], [xtb, wb_in])
                        sg_, sgb = sg.get()
                        S_.add("act", (lambda sg_, pg: lambda e: e.activation(out=sg_[:], in_=pg[:], func=AF.Silu))(sg_, pg), R=[pgb], W=[sgb])
                        ab_, abb = abS.get()
                        S_.add("act", (lambda ab_, pab: lambda e: e.copy(out=ab_[:], in_=pab[:]))(ab_, pab), R=[pabb], W=[abb])
                        yield
                        A, Ab = rA.get()
                        T, Tb = rT.get()
                        q_, qb_ = qr.get()
                        rope("dve", "dve", pq[:], pqb, 4, 64, rtb_t[:, 0:64], rtb_t[:, 64:128], rtb, A[:], Ab, T[:], Tb, q_[:], qb_)
                        yield
                        pt, ptb = psB.get()

                        def trqk(e, pt=pt, q_=q_, k_=k_):
                            for h in range(4):
                                e.transpose(out=pt[:, h * 128:(h + 1) * 128], in_=q_[:, h * 128:(h + 1) * 128], identity=ident[:])
                            for h in range(4):
                                ins = e.transpose(out=pt[:, 512 + h * 128:512 + (h + 1) * 128], in_=k_[:, h * 128:(h + 1) * 128], identity=ident[:])
                            return ins
                        S_.add("pe", trqk, R=[qb_, kb_, consts], W=[ptb])
                        qT_, qTb = qT.get()
                        qf_, qfb = qfT.get()
                        qb2, qbb = qbT.get()
                        kT_, kTb = kT.get()
                        S_.add("act", (lambda o, pt: lambda e: e.copy(out=o[:], in_=pt[:, 0:512]))(qT_, pt), R=[ptb], W=[qTb])
                        S_.add("act", (lambda o, pt: lambda e: e.copy(out=o[:], in_=pt[:, 512:1024]))(kT_, pt), R=[ptb], W=[kTb])
                        S_.add("dve", (lambda o, pt: lambda e: e.tensor_tensor(out=o[:], in0=pt[:, 0:512], in1=tabs["QF"][:], op=ALU.mult))(qf_, pt),
                               R=[ptb, tb_], W=[qfb])
                        S_.add("dve", (lambda o, pt: lambda e: e.tensor_tensor(out=o[:], in0=pt[:, 0:512], in1=tabs["QB"][:], op=ALU.mult))(qb2, pt),
                               R=[ptb, tb_], W=[qbb])
                        yield
                        psc, pscb = psF.get()
                        hs = lambda h: slice(h * 128, (h + 1) * 128)
                        mm_group(psc, pscb, [(psc[:, hs(h)], kT_[:, hs(h)], qT_[:, hs(h)], True, True) for h in range(4)], [kTb, qTb])
                        at, atb = AT.get()
                        S_.add("dve", (lambda at, psc: lambda e: e.tensor_tensor(out=at[:], in0=psc[:], in1=tabs["DT"][:], op=ALU.mult))(at, psc),
                               R=[pscb, tb_], W=[atb])
                        yield
                        while tick[0] != k:
                            yield
                        s16, s16b = stv
                        po, pob = psF.get()
                        prs = []
                        for h in range(4):
                            prs += [(po[:, hs(h)], at[:, hs(h)], v_[:, hs(h)], True, False),
                                    (po[:, hs(h)], qf_[:, hs(h)], s16[:, hs(h)], False, False),
                                    (po[:, hs(h)], qb2[:, hs(h)], sl[:, hs(h)], False, True)]
                        mm_group(po, pob, prs, [atb, vb_, qfb, qbb, s16b, slb])
                        kd_, kdb = kd.get()
                        S_.add("pool", (lambda kd_, k_: lambda e: e.tensor_tensor(out=kd_[:], in0=k_[:], in1=tabs["KF"][:], op=ALU.mult))(kd_, k_),
                               R=[kb_, tb_], W=[kdb])
                        pS, pSb = psF.get()
                        mm_group(pS, pSb, [(pS[:, hs(h)], kd_[:, hs(h)], v_[:, hs(h)], True, True) for h in range(4)], [kdb, vb_])
                        S_.add("pool", lambda e: e.tensor_tensor(out=S32[:], in0=S32[:], in1=tabs["CF"][:], op=ALU.mult), R=[S32b, tb_], W=[S32b])
                        S_.add("dve", (lambda pS: lambda e: e.tensor_tensor(out=S32[:], in0=S32[:], in1=pS[:], op=ALU.add))(pS), R=[S32b, pSb], W=[S32b])
                        stv[0], stv[1] = S16.get()
                        S_.add("act", (lambda s16: lambda e: e.copy(out=s16[:], in_=S32[:]))(stv[0]), R=[S32b], W=[stv[1]])
                        tick[0] = k + 1
                        yield
                        st6, st6b = gst.get()
                        mv, mvb = gmv.get()
                        rs, rsb = grs.get()

                        def bns(e, st6=st6, po=po):
                            for h in range(4):
                                ins = e.bn_stats(out=st6[:, h, :], in_=po[:, h * 128:(h + 1) * 128])
                            return ins
                        S_.add("dve", bns, R=[pob], W=[st6b])

                        def bna(e, st6=st6, mv=mv):
                            for h in range(4):
                                ins = e.bn_aggr(out=mv[:, h, :], in_=st6[:, h, :])
                            return ins
                        S_.add("dve", bna, R=[st6b], W=[mvb])
                        S_.add("act", (lambda rs, mv: lambda e: e.activation(out=rs[:, 0:4], in_=mv[:, :, 1], func=AF.Sqrt, bias=epsN[:, 1:2], scale=1.0))(rs, mv),
                               R=[mvb, consts], W=[rsb])
                        S_.add("dve", (lambda rs: lambda e: e.reciprocal(out=rs[:, 4:8], in_=rs[:, 0:4]))(rs), R=[rsb], W=[rsb])
                        on_, onb = on.get()

                        def gnn(e, on_=on_, po=po, mv=mv, rs=rs):
                            for h in range(4):
                                ins = e.tensor_scalar(out=on_[:, h * 128:(h + 1) * 128], in0=po[:, h * 128:(h + 1) * 128], scalar1=mv[:, h, 0:1],
                                                      scalar2=rs[:, 4 + h:5 + h], op0=ALU.subtract, op1=ALU.mult)
                            return ins
                        S_.add("dve", gnn, R=[pob, mvb, rsb], W=[onb])
                        yield
                        yb_, ybb = yb.get()
                        S_.add("pool", (lambda yb_, on_, sg_: lambda e: e.tensor_tensor(out=yb_[:], in0=on_[:], in1=sg_[:], op=ALU.mult))(yb_, on_, sg_),
                               R=[onb, sgb], W=[ybb])
                        pt2, pt2b = psB.get()

                        def try_(e, pt2=pt2, yb_=yb_):
                            for h in range(4):
                                ins = e.transpose(out=pt2[:, h * 128:(h + 1) * 128], in_=yb_[:, h * 128:(h + 1) * 128], identity=ident[:])
                            return ins
                        S_.add("pe", try_, R=[ybb, consts], W=[pt2b])
                        yT_, yTb = yT.get()
                        S_.add("act", (lambda yT_, pt2: lambda e: e.copy(out=yT_[:, 4:8, :], in_=pt2[:, 0:512].rearrange("p (k t) -> p k t", k=4)))(yT_, pt2),
                               R=[pt2b], W=[yTb])
                        yield
                        cv_, cvb = cv.get()

                        def conv(e, cv_=cv_, u=u):
                            for cg in range(4):
                                e.tensor_scalar(out=cv_[:, cg, :], in0=u[:, cg, 1:129], scalar1=convw[:, cg * 3 + 1:cg * 3 + 2], scalar2=None, op0=ALU.mult)
                                e.scalar_tensor_tensor(out=cv_[:, cg, :], in0=u[:, cg, 0:128], scalar=convw[:, cg * 3:cg * 3 + 1], in1=cv_[:, cg, :],
                                                       op0=ALU.mult, op1=ALU.add)
                                ins = e.scalar_tensor_tensor(out=cv_[:, cg, :], in0=u[:, cg, 2:130], scalar=convw[:, cg * 3 + 2:cg * 3 + 3], in1=cv_[:, cg, :],
                                                             op0=ALU.mult, op1=ALU.add)
                            return ins
                        for cg in range(4):
                            S_.add("pool", (lambda cv_, u, cg: lambda e: e.tensor_scalar(out=cv_[:, cg, :], in0=u[:, cg, 1:129],
                                                                                       scalar1=convw[:, cg * 3 + 1:cg * 3 + 2], scalar2=None, op0=ALU.mult))(cv_, u, cg),
                                   R=[ub, consts], W=[cvb])
                        for tap, lo in ((0, 0), (2, 2)):
                            for cg in range(4):
                                S_.add("dve", (lambda cv_, u, cg, tap, lo: lambda e: e.scalar_tensor_tensor(
                                    out=cv_[:, cg, :], in0=u[:, cg, lo:lo + 128], scalar=convw[:, cg * 3 + tap:cg * 3 + tap + 1], in1=cv_[:, cg, :],
                                    op0=ALU.mult, op1=ALU.add))(cv_, u, cg, tap, lo), R=[ub, consts, cvb], W=[cvb])
                        S_.add("pool", (lambda yT_, cv_, ab_: lambda e: e.tensor_tensor(out=yT_[:, 0:4, :], in0=cv_[:],
                                                                                       in1=ab_[:].rearrange("p (c t) -> p c t", c=4), op=ALU.mult))(yT_, cv_, ab_),
                               R=[cvb, abb], W=[yTb])
                        yield
                        x1_, x1b = x1t.get()
                        for n in range(2):
                            pp, ppb = psF.get()
                            mm_group(pp, ppb, [(pp[:], yT_[:, f, :], w_out[:, f, n * 512:(n + 1) * 512], f == 0, f == 7) for f in range(8)], [yTb, wb_out])
                            S_.add("dve", (lambda x1_, pp, x, n: lambda e: e.tensor_tensor(out=x1_[:, n * 512:(n + 1) * 512], in0=pp[:],
                                                                                          in1=x[:, n * 512:(n + 1) * 512], op=ALU.add))(x1_, pp, x, n),
                                   R=[ppb, xb], W=[x1b])
                        S_.add("sp", (lambda x1_, c: lambda e: e.dma_start(out=sc["x1"][c * 128:(c + 1) * 128, :], in_=x1_[:]))(x1_, c),
                               R=[x1b], W=[S_.dbuf("x1", jn, c)], chan="st%d" % (c % 2))
                        if debug:
                            S_.add("sp", (lambda x1_, c: lambda e: e.dma_start(out=dbg[jn]["x1"][c * 128:(c + 1) * 128, :], in_=x1_[:]))(x1_, c),
                                   R=[x1b], W=[S_.dbuf("dx1", jn, c)], chan="dbg")
                        yield
                    staggered(main_chunk, list(range(NCH)), 6)
                for job in jobs:
                    run_job(job)
                S_.flush(); chk(2)

            def ffn_phase(layer, final):
                with ExitStack() as ps:
                    psF = Pool(ps, pst, "psF", [128, 512], F32, 6)
                    psB = Pool(ps, pst, "psB", [128, 1024], BF16, 2)
                    wg = ps.enter_context(sb("wg", [128, 8, DFF], BF16))
                    wu = ps.enter_context(sb("wu", [128, 8, DFF], BF16))
                    wd = ps.enter_context(sb("wd", [128, NFC, 1024], BF16))
                    with ExitStack() as ws:
                        stg = Pool(ws, sb, "wstg", [128, 1024], F32, 3)
                        sidx = [0]
                        nr = lambda kc: nffn[:, layer * 8 + kc:layer * 8 + kc + 1]
                        wbg = load_weight(wg, w_g_d[layer], 8, DFF, nr, stg, sidx)
                        wbu = load_weight(wu, w_u_d[layer], 8, DFF, nr, stg, sidx)
                        wbd = load_weight(wd, w_d_d[layer], NFC, 1024, None, stg, sidx)
                        S_.flush(); chk(3)
                    nt = NormT(ps, psB)
                    xin = Pool(ps, sb, "xin", [128, 1024], F32, 2)
                    xres = Pool(ps, sb, "xres", [128, 1024], F32, 2)
                    xnT = Pool(ps, sb, "xnT", [128, 8, 512], BF16, 2)
                    hT = Pool(ps, sb, "hT", [128, NFC, 512], BF16, 1)
                    sgp = Pool(ps, sb, "sgp", [128, 512], F32, 2)
                    nf = None
                    if final:
                        nf = ps.enter_context(sb("nf", [128, 1024], F32))
                        S_.add("sp", lambda e: e.dma_start(out=nf[:], in_=nfin_d.partition_broadcast(128)), W=[consts], chan="set")
                        fj = Pool(ps, sb, "fj", [128, 1024], BF16, 1)
                        fst = Pool(ps, sb, "fst", [128, 4], F32, 2)
                    def run_job(job):
                        jn = job["n"]
                        S = job["SQ"] if final else job["S"]
                        sc = scr[jn]
                        src, srcn = (sc["x3"], "x3") if final else (sc["x1"], "x1")
                        dst, dstn = (y_d[jn], "y") if final else (sc["x2"], "x2")
                        for t in range(S // 512):
                            xt_, xtb = xnT.get()
                            for ci in range(4):
                                c = t * 4 + ci
                                x, xb = xin.get()
                                S_.add("sp", (lambda x, c: lambda e: e.dma_start(out=x[:], in_=src[c * 128:(c + 1) * 128, :]))(x, c),
                                       R=[S_.dbuf(srcn, jn, c)], W=[xb], chan="xl%d" % (c % 2))
                                nt.run(x[:], xb, xt_[:, :, ci * 128:(ci + 1) * 128], xtb)
                            h_, hb = hT.get()
                            for f in range(NFC):
                                pg, pgb = psF.get()
                                mm_group(pg, pgb, [(pg[:], wg[:, kc, f * 128:(f + 1) * 128], xt_[:, kc, :], kc == 0, kc == 7) for kc in range(8)], [xtb, wbg])
                                pu, pub = psF.get()
                                mm_group(pu, pub, [(pu[:], wu[:, kc, f * 128:(f + 1) * 128], xt_[:, kc, :], kc == 0, kc == 7) for kc in range(8)], [xtb, wbu])
                                s_, sb_ = sgp.get()
                                S_.add("act", (lambda s_, pg: lambda e: e.activation(out=s_[:], in_=pg[:], func=AF.Silu))(s_, pg), R=[pgb], W=[sb_])
                                S_.add("dve", (lambda h_, s_, pu, f: lambda e: e.tensor_tensor(out=h_[:, f, :], in0=s_[:], in1=pu[:], op=ALU.mult))(h_, s_, pu, f),
                                       R=[sb_, pub], W=[hb])
                            for ci in range(4):
                                c = t * 4 + ci
                                xr, xrb = xres.get()
                                S_.add("sp", (lambda xr, c: lambda e: e.dma_start(out=xr[:], in_=src[c * 128:(c + 1) * 128, :]))(xr, c),
                                       R=[S_.dbuf(srcn, jn, c)], W=[xrb], chan="rl%d" % (c % 2))
                                for n in range(2):
                                    pp, ppb = psF.get()
                                    mm_group(pp, ppb, [(pp[:], h_[:, f, ci * 128:(ci + 1) * 128], wd[:, f, n * 512:(n + 1) * 512], f == 0, f == NFC - 1)
                                                       for f in range(NFC)], [hb, wbd])
                                    S_.add("pool" if False else "dve", (lambda xr, pp, n: lambda e: e.tensor_tensor(out=xr[:, n * 512:(n + 1) * 512], in0=pp[:],
                                                                                                                   in1=xr[:, n * 512:(n + 1) * 512], op=ALU.add))(xr, pp, n),
                                           R=[ppb, xrb], W=[xrb])
                                if final:
                                    jk, jkb = fj.get()
                                    st, stb = fst.get()
                                    S_.add("act", (lambda jk, xr, st: lambda e: e.activation(out=jk[:], in_=xr[:], func=AF.Square, accum_out=st[:, 0:1]))(jk, xr, st),
                                           R=[xrb], W=[jkb, stb])
                                    S_.add("act", (lambda st: lambda e: e.activation(out=st[:, 1:2], in_=st[:, 0:1], func=AF.Sqrt, bias=epsN[:, 0:1], scale=1.0 / D))(st),
                                           R=[stb, consts], W=[stb])
                                    S_.add("dve", (lambda st: lambda e: e.reciprocal(out=st[:, 2:3], in_=st[:, 1:2]))(st), R=[stb], W=[stb])
                                    S_.add("dve", (lambda xr, st: lambda e: e.scalar_tensor_tensor(out=xr[:], in0=xr[:], scalar=st[:, 2:3], in1=nf[:],
                                                                                                    op0=ALU.mult, op1=ALU.mult))(xr, st),
                                           R=[xrb, stb, consts], W=[xrb])
                                S_.add("sp", (lambda xr, c: lambda e: e.dma_start(out=dst[c * 128:(c + 1) * 128, :], in_=xr[:]))(xr, c),
                                       R=[xrb], W=[S_.dbuf(dstn, jn, c)], chan="st%d" % (c % 2))
                                if debug and not final:
                                    S_.add("sp", (lambda xr, c: lambda e: e.dma_start(out=dbg[jn]["x2"][c * 128:(c + 1) * 128, :], in_=xr[:]))(xr, c),
                                           R=[xrb], W=[S_.dbuf("dx2", jn, c)], chan="dbg")
                    for job in jobs:
                        run_job(job)
                    S_.flush(); chk(4)

            ffn_phase(0, False)

            with ExitStack() as ps:
                psF = Pool(ps, pst, "psF", [128, 512], F32, 6)
                psB = Pool(ps, pst, "psB", [128, 1024], BF16, 2)
                wqkv = ps.enter_context(sb("wqkv", [128, 8, 3072], BF16))
                with ExitStack() as ws:
                    stg = Pool(ws, sb, "wstg", [128, 1024], F32, 3)
                    sidx = [0]
                    wbq = load_weight(wqkv, w_qkv_d, 8, 3072, lambda kc: nmix[:, 8 + kc:9 + kc], stg, sidx)
                    S_.flush(); chk(5)
                nt = NormT(ps, psB)
                xin = Pool(ps, sb, "xin", [128, 1024], F32, 3)
                xnT = Pool(ps, sb, "xnT", [128, 8, 128], BF16, 2)
                ropt = Pool(ps, sb, "ropt", [128, 64], F32, 3)
                rA = Pool(ps, sb, "rA", [128, 1024], F32, 2)
                rT = Pool(ps, sb, "rT", [128, 1024], F32, 2)
                rr = Pool(ps, sb, "rr", [128, 1024], BF16, 2)
                vS = Pool(ps, sb, "vS", [128, 1024], BF16, 2)
                stT = Pool(ps, sb, "stT", [128, 8, 512], BF16, 2)

                def qk_path(x_src_fn, R_x, rope_src, S, c0, dstT, dstname, jn, col0, with_v, vdst):
                    st_, stb = stT.get()
                    for ci in range(4):
                        c = c0 + ci
                        x, xb = xin.get()
                        x_src_fn(x, xb, c)
                        rt_, rtb = ropt.get()
                        S_.add("sp", (lambda t, c: lambda e: e.dma_start(out=t[:], in_=rope_src[c * 128:(c + 1) * 128, :]))(rt_, c), W=[rtb], chan="rl%d" % (c % 3))
                        yield
                        xt_, xtb = xnT.get()
                        nt.run(x[:], xb, xt_[:], xtb)
                        yield
                        pq = [psF.get(), psF.get()]
                        for n in range(2):
                            mm_group(pq[n][0], pq[n][1], [(pq[n][0][:], xt_[:, kc, :], wqkv[:, kc, col0 + n * 512:col0 + (n + 1) * 512], kc == 0, kc == 7)
                                                          for kc in range(8)], [xtb, wbq])
                        yield
                        A, Ab = rA.get()
                        T, Tb = rT.get()
                        r_, rb_ = rr.get()
                        for n in range(2):
                            sl = slice(n * 512, (n + 1) * 512)
                            rope("dve", "pool" if False else "dve", pq[n][0][:], pq[n][1], 8, 32, rt_[:, 0:32], rt_[:, 32:64], rtb, A[:, sl], Ab, T[:, sl], Tb, r_[:, sl], rb_)
                        pt, ptb = psB.get()

                        def tr(e, pt=pt, r_=r_):
                            for hp in range(8):
                                ins = e.transpose(out=pt[:, hp * 128:(hp + 1) * 128], in_=r_[:, hp * 128:(hp + 1) * 128], identity=ident[:])
                            return ins
                        S_.add("pe", tr, R=[rb_, consts], W=[ptb])
                        S_.add("act", (lambda st_, pt, ci: lambda e: e.copy(out=st_[:, :, ci * 128:(ci + 1) * 128], in_=pt[:].rearrange("p (k t) -> p k t", k=8)))(st_, pt, ci),
                               R=[ptb], W=[stb])
                        yield
                        if with_v:
                            pv = [psF.get(), psF.get()]
                            v_, vb_ = vS.get()
                            for n in range(2):
                                mm_group(pv[n][0], pv[n][1], [(pv[n][0][:], xt_[:, kc, :], wqkv[:, kc, 2048 + n * 512:2048 + (n + 1) * 512], kc == 0, kc == 7)
                                                              for kc in range(8)], [xtb, wbq])
                                S_.add("act", (lambda v_, p, n: lambda e: e.copy(out=v_[:, n * 512:(n + 1) * 512], in_=p[:]))(v_, pv[n][0], n), R=[pv[n][1]], W=[vb_])
                            S_.add("sp", (lambda v_, c: lambda e: e.dma_start(out=vdst[:, c * 128:(c + 1) * 128, :].rearrange("h t e -> t h e"),
                                                                              in_=v_[:].rearrange("p (h e) -> p h e", h=8)))(v_, c),
                                   R=[vb_], W=[S_.dbuf("V", jn, c)], chan="st%d" % (c % 2))
                    S_.add("sp", (lambda st_, c0: lambda e: e.dma_start(out=dstT[:, :, c0 * 128:c0 * 128 + 512].rearrange("h p t -> p h t"), in_=st_[:]))(st_, c0),
                           R=[stb], W=[S_.dbuf(dstname, jn, c0 // 4)], chan="st%d" % ((c0 // 4) % 2))

                def run_job(job):
                    jn, S, SQ = job["n"], job["S"], job["SQ"]
                    sc = scr[jn]

                    def src_plain(x, xb, c, sc=sc, jn=jn):
                        S_.add("sp", (lambda x, c: lambda e: e.dma_start(out=x[:], in_=sc["x2"][c * 128:(c + 1) * 128, :]))(x, c),
                               R=[S_.dbuf("x2", jn, c)], W=[xb], chan="xl%d" % (c % 3))

                    def src_gather(x, xb, c, sc=sc, jn=jn, S=S):
                        S_.add("pool", (lambda x, c: lambda e: e.indirect_dma_start(out=x[:, :], out_offset=None, in_=sc["x2"][:, :],
                                                                                    in_offset=bass.IndirectOffsetOnAxis(ap=idxq[:, c:c + 1], axis=0)))(x, c),
                               R=[S_.dbuf("x2", jn, cc) for cc in range(S // 128)] + [consts], W=[xb], chan="gl%d" % (c % 3))
                    staggered(lambda k, t: qk_path(src_plain, None, dtk_d, S, t * 4, sc["KT"], "KT", jn, 1024, True, sc["V"]),
                              list(range(S // 512)), 8)
                    staggered(lambda k, t: qk_path(src_gather if job["own"] else src_plain, None, dtq_d[jn], SQ, t * 4, sc["QT"], "QT", jn, 0, False, None),
                              list(range(SQ // 512)), 6)
                for job in jobs:
                    run_job(job)
                S_.flush(); chk(6)

            with ExitStack() as ps:
                psS = Pool(ps, pst, "psS", [128, 1024], F32, 2)
                psO = Pool(ps, pst, "psO", [128, 2, 256], F32, 4)
                SMX = max(SA, SB)
                KTt = Pool(ps, sb, "KTt", [128, SMX], BF16, 2)
                Vt = Pool(ps, sb, "Vt", [128, SMX // 128, 130], BF16, 2)
                QTt = Pool(ps, sb, "QTt", [128, max(SA, SQB)], BF16, 2)
                PT = Pool(ps, sb, "PT", [128, 1024], BF16, 3)
                rc = Pool(ps, sb, "rc", [128, 8], F32, 4)
                o1 = Pool(ps, sb, "o1", [128, 128], F32, 3)
                oj = Pool(ps, sb, "oj", [128, 128], BF16, 2)
                ob = Pool(ps, sb, "ob", [128, 4, 128], BF16, 2)
                for i in range(2):
                    S_.add("pool", (lambda t: lambda e: e.memset(t[:, :, 128:130], 1.0))(Vt.t[i]), W=[Vt.b[i]])
                def run_job(job):
                    jn, S, SQ = job["n"], job["S"], job["SQ"]
                    sc = scr[jn]
                    NKT = S // 128
                    for h in range(8):
                        kt_, ktb = KTt.get()
                        S_.add("sp", (lambda kt_, h: lambda e: e.dma_start(out=kt_[:, 0:S], in_=sc["KT"][h, :, :]))(kt_, h),
                               R=[S_.dbuf("KT", jn, t) for t in range(S // 512)], W=[ktb], chan="kl%d" % (h % 2))
                        v_, vb_ = Vt.get()
                        VP = min(32, NKT)
                        for part in range(0, NKT, VP):
                            S_.add("sp", (lambda v_, h, part: lambda e: e.dma_start(out=v_[:, part:part + VP, 0:128],
                                                                                   in_=sc["V"][h, part * 128:(part + VP) * 128, :].rearrange("(k p) e -> p k e", p=128)))(v_, h, part),
                                   R=[S_.dbuf("V", jn, c) for c in range(part, min(part + VP, NKT))], W=[vb_], chan="vl%d" % (h % 2))
                        q_, qb_ = QTt.get()
                        S_.add("sp", (lambda q_, h: lambda e: e.dma_start(out=q_[:, 0:SQ], in_=sc["QT"][h, :, :]))(q_, h),
                               R=[S_.dbuf("QT", jn, t) for t in range(SQ // 512)], W=[qb_], chan="ql%d" % (h % 2))
                        for qt in range(SQ // 512):
                            acc = [psO.get(), psO.get(), psO.get(), psO.get()]
                            qs = slice(qt * 512, (qt + 1) * 512)
                            def qk(kt, qs=qs):
                                ks = slice(kt * 128, (kt + 1) * 128)
                                pS_, pSb = psS.get()
                                mm_group(pS_, pSb, [(pS_[:, 0:512], kt_[0:64, ks], q_[0:64, qs], True, True),
                                                    (pS_[:, 512:1024], kt_[64:128, ks], q_[64:128, qs], True, True)], [ktb, qb_])
                                return pS_, pSb
                            pend = qk(0)
                            for kt in range(NKT):
                                pS_, pSb = pend
                                if kt + 1 < NKT:
                                    pend = qk(kt + 1)
                                p_, pb_ = PT.get()
                                S_.add("act", (lambda p_, pS_: lambda e: e.activation(out=p_[:], in_=pS_[:], func=AF.Exp))(p_, pS_), R=[pSb], W=[pb_])
                                for sub in range(2):
                                    for pair in range(2):
                                        a_, ab_ = acc[sub * 2 + pair]
                                        mm_group(a_, ab_, [(a_[:, j, 0:129], p_[:, sub * 512 + (pair * 2 + j) * 128: sub * 512 + (pair * 2 + j + 1) * 128],
                                                            v_[:, kt, 0:129], kt == 0 and j == 0, kt == NKT - 1) for j in range(2)], [pb_, vb_])
                            ob_, obb = ob.get()
                            for qb4 in range(4):
                                pair, j = qb4 // 2, qb4 % 2
                                a0, a0b = acc[0 + pair]
                                a1, a1b = acc[2 + pair]
                                r_, rb2 = rc.get()
                                S_.add("dve", (lambda r_, a0, j: lambda e: e.reciprocal(out=r_[:, 0:1], in_=a0[:, j, 128:129]))(r_, a0, j), R=[a0b], W=[rb2])
                                S_.add("dve", (lambda r_, a1, j: lambda e: e.reciprocal(out=r_[:, 1:2], in_=a1[:, j, 128:129]))(r_, a1, j), R=[a1b, rb2], W=[rb2])
                                S_.add("dve", (lambda r_: lambda e: e.tensor_tensor(out=r_[:, 2:3], in0=r_[:, 1:2], in1=lamt[:, 0:1], op=ALU.mult))(r_),
                                       R=[rb2, consts], W=[rb2])
                                o_, o_b = o1.get()
                                S_.add("dve", (lambda o_, a0, j, r_: lambda e: e.tensor_scalar(out=o_[:], in0=a0[:, j, 0:128], scalar1=r_[:, 0:1], scalar2=None,
                                                                                              op0=ALU.mult))(o_, a0, j, r_), R=[a0b, rb2], W=[o_b])
                                S_.add("dve", (lambda o_, a1, j, r_: lambda e: e.scalar_tensor_tensor(out=o_[:], in0=a1[:, j, 0:128], scalar=r_[:, 2:3], in1=o_[:],
                                                                                                     op0=ALU.mult, op1=ALU.add))(o_, a1, j, r_),
                                       R=[a1b, rb2, o_b], W=[o_b])
                                jk, jkb = oj.get()
                                S_.add("act", (lambda jk, o_, r_: lambda e: e.activation(out=jk[:], in_=o_[:], func=AF.Square, accum_out=r_[:, 3:4]))(jk, o_, r_),
                                       R=[o_b, rb2], W=[jkb, rb2])
                                S_.add("act", (lambda r_: lambda e: e.activation(out=r_[:, 4:5], in_=r_[:, 3:4], func=AF.Sqrt, bias=epsN[:, 1:2], scale=1.0 / 128))(r_),
                                       R=[rb2, consts], W=[rb2])
                                S_.add("dve", (lambda r_: lambda e: e.reciprocal(out=r_[:, 5:6], in_=r_[:, 4:5]))(r_), R=[rb2], W=[rb2])
                                S_.add("pool", (lambda ob_, o_, r_, qb4: lambda e: e.tensor_scalar(out=ob_[:, qb4, :], in0=o_[:], scalar1=r_[:, 5:6], scalar2=None,
                                                                                                  op0=ALU.mult))(ob_, o_, r_, qb4), R=[o_b, rb2], W=[obb])
                            S_.add("sp", (lambda ob_, qt, h: lambda e: e.dma_start(
                                out=sc["O"][qt * 512:(qt + 1) * 512, h * 128:(h + 1) * 128].rearrange("(k p) e -> p k e", p=128), in_=ob_[:]))(ob_, qt, h),
                                   R=[obb], W=[S_.dbuf("O", jn, qt, h)], chan="st%d" % (qt % 2))
                for job in jobs:
                    run_job(job)
                S_.flush(); chk(7)

            with ExitStack() as ps:
                psF = Pool(ps, pst, "psF", [128, 512], F32, 6)
                psB = Pool(ps, pst, "psB", [128, 1024], BF16, 2)
                wdo = ps.enter_context(sb("wdo", [128, 8, 1024], BF16))
                with ExitStack() as ws:
                    stg = Pool(ws, sb, "wstg", [128, 1024], F32, 3)
                    sidx = [0]
                    wbo = load_weight(wdo, w_do_d, 8, 1024, lambda kc: subl[:, 0:1], stg, sidx)
                    S_.flush(); chk(8)
                oin = Pool(ps, sb, "oin", [128, 1024], BF16, 3)
                oT = Pool(ps, sb, "oT", [128, 8, 128], BF16, 2)
                xres = Pool(ps, sb, "xres", [128, 1024], F32, 3)
                def run_job(job):
                    jn, S, SQ = job["n"], job["S"], job["SQ"]
                    sc = scr[jn]
                    for c in range(SQ // 128):
                        o_, o_b = oin.get()
                        S_.add("sp", (lambda o_, c: lambda e: e.dma_start(out=o_[:], in_=sc["O"][c * 128:(c + 1) * 128, :]))(o_, c),
                               R=[S_.dbuf("O", jn, c // 4, h) for h in range(8)], W=[o_b], chan="xl%d" % (c % 3))
                        xr, xrb = xres.get()
                        if job["own"]:
                            S_.add("pool", (lambda xr, c: lambda e: e.indirect_dma_start(out=xr[:, :], out_offset=None, in_=sc["x2"][:, :],
                                                                                        in_offset=bass.IndirectOffsetOnAxis(ap=idxq[:, c:c + 1], axis=0)))(xr, c),
                                   R=[consts], W=[xrb], chan="gl%d" % (c % 3))
                        else:
                            S_.add("sp", (lambda xr, c: lambda e: e.dma_start(out=xr[:], in_=sc["x2"][c * 128:(c + 1) * 128, :]))(xr, c), W=[xrb], chan="rl%d" % (c % 3))
                        pt, ptb = psB.get()

                        def tr(e, pt=pt, o_=o_):
                            for f in range(8):
                                ins = e.transpose(out=pt[:, f * 128:(f + 1) * 128], in_=o_[:, f * 128:(f + 1) * 128], identity=ident[:])
                            return ins
                        S_.add("pe", tr, R=[o_b, consts], W=[ptb])
                        oT_, oTb = oT.get()
                        S_.add("act", (lambda oT_, pt: lambda e: e.copy(out=oT_[:], in_=pt[:].rearrange("p (k t) -> p k t", k=8)))(oT_, pt), R=[ptb], W=[oTb])
                        for n in range(2):
                            pp, ppb = psF.get()
                            mm_group(pp, ppb, [(pp[:], oT_[:, f, :], wdo[:, f, n * 512:(n + 1) * 512], f == 0, f == 7) for f in range(8)], [oTb, wbo])
                            S_.add("dve", (lambda xr, pp, n: lambda e: e.tensor_tensor(out=xr[:, n * 512:(n + 1) * 512], in0=pp[:], in1=xr[:, n * 512:(n + 1) * 512],
                                                                                      op=ALU.add))(xr, pp, n), R=[ppb, xrb], W=[xrb])
                        S_.add("sp", (lambda xr, c: lambda e: e.dma_start(out=sc["x3"][c * 128:(c + 1) * 128, :], in_=xr[:]))(xr, c),
                               R=[xrb], W=[S_.dbuf("x3", jn, c)], chan="st%d" % (c % 2))
                        if debug:
                            S_.add("sp", (lambda xr, c: lambda e: e.dma_start(out=dbg[jn]["x3"][c * 128:(c + 1) * 128, :], in_=xr[:]))(xr, c),
                                   R=[xrb], W=[S_.dbuf("dx3", jn, c)], chan="dbg")
                for job in jobs:
                    run_job(job)
                S_.flush(); chk(9)

            ffn_phase(1, True)

        except _Stop:
            pass
        sch.stopped = False
        sch.maxops = 10 ** 9
        S_.add("sp", lambda e: e.dma_start(out=scr["a"]["uT"][0:1, 0:1], in_=scr["a"]["uT"][0:1, 1:2]), chan="fin")
        S_.flush(); chk(10)
        fin = S_.ops[-1]

        with nc.Block() as block:
            @block.sync
            def _(e):
                e.wait_ge(S_.csem["fin"], fin.val)
    return nc


def _rope_tabs(S, dim, scale):
    inv = (10000.0 ** (-np.arange(0, dim, 2, dtype=np.float32) / np.float32(dim))).astype(np.float32)
    ang = np.arange(S, dtype=np.float32)[:, None] * inv[None, :]
    return (np.cos(ang) * scale).astype(np.float32), (np.sin(ang) * scale).astype(np.float32)


def make_inputs(core, SA, SB, NQB, inp, xa, xb, qoff):
    f = lambda a: np.ascontiguousarray(np.asarray(a, dtype=np.float32))
    SM = max(SA, SB)
    c1, s1 = _rope_tabs(SM, 128, 1.0)
    c2, s2 = _rope_tabs(SM, 128, 128 ** -0.5)
    rt = np.concatenate([c1, s1, c2, s2], axis=1)
    ck, sk = _rope_tabs(SM, 64, 1.0)
    cq, sq = _rope_tabs(SM, 64, 0.125)
    dtk = np.concatenate([ck, sk], 1)
    dtq = np.concatenate([cq, sq], 1)
    SQB = NQB * 128
    idx = (qoff + np.arange(NQB)[None, :] * 128 + np.arange(128)[:, None]).astype(np.int32)
    pk = lambda v: f(np.asarray(v).reshape(-1, 128).T)
    m = {
        "xa": f(xa), "xb": f(xb), "idxq": np.ascontiguousarray(idx),
        "rt": f(rt), "dtk": f(dtk), "dtqa": f(dtq[:SA]), "dtqb": f(dtq[qoff:qoff + SQB]),
        "w_in": f(inp["hyb_w_in"][0]), "w_out": f(inp["hyb_w_out"][0]), "w_qkv": f(inp["diff_w_qkv"][0]), "w_do": f(inp["diff_w_out"][0]),
        "nmix": pk(np.asarray(inp["norm_mix"]).reshape(-1)), "nffn": pk(np.asarray(inp["norm_ffn"]).reshape(-1)),
        "nfin": f(np.asarray(inp["norm_final"]).reshape(1, D)),
        "convw": f(np.asarray(inp["hyb_conv_w"][0]).reshape(3, 4, 128).transpose(2, 1, 0).reshape(128, 12)),
        "decf": f(np.asarray(inp["hyb_decay_fwd"]).reshape(1, 4)), "decb": f(np.asarray(inp["hyb_decay_bwd"]).reshape(1, 4)),
        "gnw": pk(np.asarray(inp["hyb_gn"]).reshape(-1)),
        "lamv": f(np.concatenate([np.asarray(inp[k]).reshape(-1) for k in ("diff_lq1", "diff_lk1", "diff_lq2", "diff_lk2")]).reshape(1, 256)),
        "subln": f(np.asarray(inp["diff_subln"]).reshape(128, 1)),
    }
    for l in range(2):
        m["w_g%d" % l] = f(inp["ffn_w_gate"][l])
        m["w_u%d" % l] = f(inp["ffn_w_up"][l])
        m["w_d%d" % l] = f(inp["ffn_w_down"][l])
    return m


def kernel(**inp):
    xp = np.asarray(inp["x_prompt"])
    xs = np.asarray(inp["x_sample"])
    SA, SB = xs.shape[1], xp.shape[1]
    NQB = SB // 4 // 128
    nc = build(SA, SB, NQB)
    in_maps = [make_inputs(c, SA, SB, NQB, inp, xs[c], xp[c // 4], (c % 4) * (SB // 4)) for c in range(8)]
    res = run_bass_kernel_spmd(nc, in_maps, core_ids=list(range(8)))
    ys = np.stack([res.results[c]["ya"] for c in range(8)], 0).astype(np.float32)
    yp = np.stack([np.concatenate([res.results[g * 4 + r]["yb"] for r in range(4)], 0) for g in range(2)], 0).astype(np.float32)
    return (yp, ys)
```

```python
import math
from contextlib import ExitStack
import numpy as np
import concourse.bass as bass
import concourse.mybir as mybir
from concourse.bass_utils import run_bass_kernel_spmd

F32 = mybir.dt.float32
BF16 = mybir.dt.bfloat16
I32 = mybir.dt.int32
AF = mybir.ActivationFunctionType
ALU = mybir.AluOpType
AX = mybir.AxisListType

D = 1024
DFF = 2816
NFC = DFF // 128
LAMBDA_INIT = 0.8 - 0.6 * math.exp(-0.3 * 1)
ENG = ("pe", "act", "dve", "pool", "sp")


class Buf:
    __slots__ = ("w", "r", "excl")

    def __init__(self, excl=False):
        self.w = None
        self.r = {}
        self.excl = excl


class Op:
    __slots__ = ("eng", "fn", "deps", "chan", "signal", "val")

    def __init__(self, eng, fn, deps, chan):
        self.eng = eng
        self.fn = fn
        self.deps = deps
        self.chan = chan
        self.signal = chan is not None
        self.val = None


class Sched:
    def __init__(self, nc, es, nchan=40):
        self.nc = nc
        self.ops = []
        self.emitted = 0
        self.last = {}
        self.lastchan = {}
        self.bar = ()
        self.esem = {e: es.enter_context(nc.semaphore("s_" + e)) for e in ENG}
        self.freechan = [es.enter_context(nc.semaphore("c%d" % i)) for i in range(nchan)]
        self.csem = {}
        self.cnt = {e: 0 for e in ENG}
        self.ccnt = {}
        self.waited = {e: {} for e in ENG}
        self.dram = {}

    def dbuf(self, *key):
        b = self.dram.get(key)
        if b is None:
            b = self.dram[key] = Buf()
        return b

    stopped = False
    maxops = 10 ** 9
    names = {}

    def add(self, eng, fn, R=(), W=(), chan=None):
        if self.stopped:
            return -1
        if len(self.ops) >= self.maxops:
            self.stopped = True
            return -1
        deps = set(self.bar)
        key0 = chan if chan is not None else eng
        for b in R:
            if b.w is not None:
                deps.add(b.w)
            if b.excl:
                deps.update(v for k, v in b.r.items() if k != key0)
        for b in W:
            if b.w is not None:
                deps.add(b.w)
            deps.update(b.r.values())
        idx = len(self.ops)
        if chan is not None:
            if chan in self.lastchan:
                deps.add(self.lastchan[chan])
            self.lastchan[chan] = idx
        self.ops.append(Op(eng, fn, deps, chan))
        key = chan if chan is not None else eng
        for b in R:
            b.r[key] = idx
        for b in W:
            b.w = idx
            b.r = {}
        self.last[eng] = idx
        return idx

    def _event(self, op):
        if op.chan is not None:
            return self.csem[op.chan], op.val
        return self.esem[op.eng], op.val

    def flush(self):
        nc = self.nc
        lo = self.emitted
        if lo == len(self.ops):
            return
        ops = self.ops
        for i in range(lo, len(ops)):
            for d in ops[i].deps:
                if d >= lo:
                    od = ops[d]
                    if od.chan is None and od.eng == "pe" and ops[i].eng == "pe" and ops[i].chan is None:
                        continue
                    od.signal = True
        for e, i in self.last.items():
            ops[i].signal = True
        for i in range(lo, len(ops)):
            op = ops[i]
            if op.chan is not None:
                if op.chan not in self.csem:
                    self.csem[op.chan] = self.freechan.pop()
                    self.ccnt[op.chan] = 0
                self.ccnt[op.chan] += 16
                op.val = self.ccnt[op.chan]
            elif op.signal:
                self.cnt[op.eng] += 1
                op.val = self.cnt[op.eng]
        per = {e: [] for e in ENG}
        for i in range(lo, len(ops)):
            per[ops[i].eng].append(i)

        def emit(engname, e):
            waited = self.waited[engname]
            for i in per[engname]:
                op = ops[i]
                for d in sorted(op.deps):
                    od = ops[d]
                    if d < lo and d not in self.bar:
                        continue
                    if od.chan is None and od.eng == "pe" and engname == "pe" and op.chan is None:
                        continue
                    sem, val = self._event(od)
                    k = id(sem)
                    if waited.get(k, 0) < val:
                        e.wait_ge(sem, val)
                        waited[k] = val
                ins = op.fn(e)
                if op.signal:
                    sem, val = self._event(op)
                    ins.then_inc(sem, 16 if op.chan is not None else 1)

        with nc.Block() as block:
            @block.tensor
            def _(e):
                emit("pe", e)

            @block.scalar
            def _(e):
                emit("act", e)

            @block.vector
            def _(e):
                emit("dve", e)

            @block.gpsimd
            def _(e):
                emit("pool", e)

            @block.sync
            def _(e):
                emit("sp", e)
        self.emitted = len(ops)
        self.bar = tuple(set(list(self.last.values()) + list(self.lastchan.values())))
        for i in self.bar:
            assert ops[i].signal


def staggered(fn, chunks, delay):
    def thread(idx):
        for k in idx:
            yield from fn(k, chunks[k])
    g = [thread(range(0, len(chunks), 2)), thread(range(1, len(chunks), 2))]
    alive = [True, True]
    for _ in range(delay):
        try:
            next(g[0])
        except StopIteration:
            alive[0] = False
            break
    while any(alive):
        for i in (0, 1):
            if alive[i]:
                try:
                    next(g[i])
                except StopIteration:
                    alive[i] = False


class Pool:
    def __init__(self, es, alloc, name, shape, dt, n):
        self.t = [es.enter_context(alloc("%s%d" % (name, i), shape, dt)) for i in range(n)]
        self.b = [Buf(excl=(alloc.__name__ == 'pst')) for _ in range(n)]
        self.i = 0

    def get(self):
        k = self.i % len(self.t)
        self.i += 1
        return self.t[k], self.b[k]


class _Stop(Exception):
    pass


def build(SA, SB, NQB, debug=False, nph=99):
    nc = bass.Bass("TRN2", target_bir_lowering=False)
    dt_in = lambda n, s, d=F32: nc.dram_tensor(n, s, d, kind="ExternalInput").ap()
    dt_scr = lambda n, s, d: nc.dram_tensor(n, s, d, kind="Internal").ap()
    SQB = NQB * 128
    jobs = [dict(n="a", S=SA, SQ=SA, own=False), dict(n="b", S=SB, SQ=SQB, own=True)]
    xin_d = {"a": dt_in("xa", [SA, D]), "b": dt_in("xb", [SB, D])}
    idxq_d = dt_in("idxq", [128, NQB], I32)
    SM = max(SA, SB)
    rt_d = dt_in("rt", [SM, 256])
    dtk_d = dt_in("dtk", [SM, 64])
    dtq_d = {"a": dt_in("dtqa", [SA, 64]), "b": dt_in("dtqb", [SQB, 64])}
    w_in_d = dt_in("w_in", [D, 3584])
    w_out_d = dt_in("w_out", [D, D])
    w_qkv_d = dt_in("w_qkv", [D, 3072])
    w_do_d = dt_in("w_do", [D, D])
    w_g_d = [dt_in("w_g%d" % l, [D, DFF]) for l in range(2)]
    w_u_d = [dt_in("w_u%d" % l, [D, DFF]) for l in range(2)]
    w_d_d = [dt_in("w_d%d" % l, [DFF, D]) for l in range(2)]
    nmix_d = dt_in("nmix", [128, 16])
    nffn_d = dt_in("nffn", [128, 16])
    nfin_d = dt_in("nfin", [1, D])
    convw_d = dt_in("convw", [128, 12])
    convw3_d = dt_in("convw3", [3, 512])
    decf_d = dt_in("decf", [1, 4])
    decb_d = dt_in("decb", [1, 4])
    gn_d = dt_in("gnw", [128, 4])
    lam_d = dt_in("lamv", [1, 256])
    subln_d = dt_in("subln", [128, 1])
    y_d = {j["n"]: nc.dram_tensor("y" + j["n"], [j["SQ"], D], F32, kind="ExternalOutput").ap() for j in jobs}
    scr = {}
    for j in jobs:
        n, S, SQ = j["n"], j["S"], j["SQ"]
        scr[n] = dict(
            u=dt_scr("u" + n, [S + 2, 512], F32),
            kr=dt_scr("kr" + n, [S, 512], BF16),
            vr=dt_scr("vr" + n, [S, 512], BF16),
            sb=dt_scr("sb" + n, [S // 128, 128, 512], BF16),
            x1=dt_scr("x1" + n, [S, D], F32),
            x2=dt_scr("x2" + n, [S, D], F32),
            QT=dt_scr("QT" + n, [8, 128, SQ], BF16),
            KT=dt_scr("KT" + n, [8, 128, S], BF16),
            V=dt_scr("V" + n, [8, S, 128], BF16),
            O=dt_scr("O" + n, [SQ, D], BF16),
            x3=dt_scr("x3" + n, [SQ, D], F32),
        )
    dbg = {}
    if debug:
        for j in jobs:
            n, S, SQ = j["n"], j["S"], j["SQ"]
            dbg[n] = dict(
                x1=nc.dram_tensor("dbg_x1" + n, [S, D], F32, kind="ExternalOutput").ap(),
                x2=nc.dram_tensor("dbg_x2" + n, [S, D], F32, kind="ExternalOutput").ap(),
                x3=nc.dram_tensor("dbg_x3" + n, [SQ, D], F32, kind="ExternalOutput").ap(),
            )

    with ExitStack() as es:
        sch = Sched(nc, es)
        import os
        sch.maxops = int(os.environ.get('KSTOP', 10 ** 9))
        S_ = sch
        _uid = [0]

        def sb(name, shape, dt):
            _uid[0] += 1
            return nc.sbuf_tensor("s%d_%s" % (_uid[0], name), shape, dt)

        def pst(name, shape, dt):
            _uid[0] += 1
            return nc.psum_tensor("p%d_%s" % (_uid[0], name), shape, dt)
        ident = es.enter_context(sb("ident", [128, 128], BF16))
        epsN = es.enter_context(sb("epsN", [128, 4], F32))
        nmix = es.enter_context(sb("nmix", [128, 16], F32))
        nffn = es.enter_context(sb("nffn", [128, 16], F32))
        gnw = es.enter_context(sb("gnw", [128, 4], F32))
        subl = es.enter_context(sb("subl", [128, 1], F32))
        convw = es.enter_context(sb("convw", [128, 12], F32))
        idxq = es.enter_context(sb("idxq", [128, NQB], I32))
        lamt = es.enter_context(sb("lamt", [128, 8], F32))
        consts = Buf()

        def chk(k):
            if nph == k:
                sch.stopped = True

        try:
            with ExitStack() as ps:
                tmpa = ps.enter_context(sb("tmpa", [128, 128], F32))
                lamv = ps.enter_context(sb("lamv", [128, 256], F32))
                lamp = ps.enter_context(sb("lamp", [128, 128], F32))
                S_.add("pool", lambda e: e.iota(tmpa[:], pattern=[[1, 128]], base=0, channel_multiplier=-1,
                                                allow_small_or_imprecise_dtypes=True), W=[consts])
                S_.add("dve", lambda e: e.tensor_single_scalar(out=ident[:], in_=tmpa[:], scalar=0.0, op=ALU.is_equal),
                       R=[consts], W=[consts])
                S_.add("dve", lambda e: e.memset(epsN[:, 0:1], 1e-6), W=[consts])
                S_.add("dve", lambda e: e.memset(epsN[:, 1:3], 1e-5), W=[consts])
                S_.add("dve", lambda e: e.memset(epsN[:, 3:4], 0.0), W=[consts])
                for t, d in ((nmix, nmix_d), (nffn, nffn_d), (gnw, gn_d), (subl, subln_d), (convw, convw_d), (idxq, idxq_d)):
                    S_.add("sp", (lambda t, d: lambda e: e.dma_start(out=t[:], in_=d[:, :]))(t, d), W=[consts], chan="set")
                S_.add("sp", lambda e: e.dma_start(out=lamv[:], in_=lam_d.partition_broadcast(128)), W=[consts], chan="set")
                S_.add("dve", lambda e: e.tensor_single_scalar(out=subl[:], in_=subl[:], scalar=1.0 - LAMBDA_INIT, op=ALU.mult),
                       R=[consts], W=[consts])
                S_.add("dve", lambda e: e.tensor_tensor(out=lamp[:, 0:64], in0=lamv[:, 0:64], in1=lamv[:, 64:128], op=ALU.mult),
                       R=[consts], W=[consts])
                S_.add("dve", lambda e: e.tensor_tensor(out=lamp[:, 64:128], in0=lamv[:, 128:192], in1=lamv[:, 192:256], op=ALU.mult),
                       R=[consts], W=[consts])
                S_.add("dve", lambda e: e.reduce_sum(out=lamt[:, 1:2], in_=lamp[:, 0:64], axis=AX.X), R=[consts], W=[consts])
                S_.add("dve", lambda e: e.reduce_sum(out=lamt[:, 2:3], in_=lamp[:, 64:128], axis=AX.X), R=[consts], W=[consts])
                S_.add("act", lambda e: e.activation(out=lamt[:, 3:5], in_=lamt[:, 1:3], func=AF.Exp), R=[consts], W=[consts])
                S_.add("dve", lambda e: e.tensor_tensor(out=lamt[:, 5:6], in0=lamt[:, 4:5], in1=lamt[:, 3:4], op=ALU.subtract),
                       R=[consts], W=[consts])
                S_.add("dve", lambda e: e.tensor_single_scalar(out=lamt[:, 0:1], in_=lamt[:, 5:6], scalar=-LAMBDA_INIT, op=ALU.add),
                       R=[consts], W=[consts])
                S_.flush(); chk(0)

            def load_weight(dst, src, KC, C, rows, stg, sidx):
                srcv = src.rearrange("(kc p) c -> p kc c", p=128)
                wb = Buf()
                for kc in range(KC):
                    for c0 in range(0, C, 1024):
                        cw = min(1024, C - c0)
                        st, stb = stg.get()
                        S_.add("sp", (lambda st, kc, c0, cw: lambda e: e.dma_start(out=st[:, 0:cw], in_=srcv[:, kc, c0:c0 + cw]))(st, kc, c0, cw),
                               W=[stb], chan="wl%d" % (sidx[0] % 3))
                        eng = ("dve", "pool")[sidx[0] % 2]
                        sidx[0] += 1
                        r = rows(kc) if rows is not None else None
                        if r is None:
                            S_.add(eng, (lambda st, kc, c0, cw: lambda e: e.tensor_copy(out=dst[:, kc, c0:c0 + cw], in_=st[:, 0:cw]))(st, kc, c0, cw),
                                   R=[stb, consts], W=[wb])
                        else:
                            S_.add(eng, (lambda st, kc, c0, cw, r: lambda e: e.tensor_scalar(
                                out=dst[:, kc, c0:c0 + cw], in0=st[:, 0:cw], scalar1=r, scalar2=None, op0=ALU.mult))(st, kc, c0, cw, r),
                                   R=[stb, consts], W=[wb])
                return wb

            class NormT:
                def __init__(self, es_, psB):
                    self.junk = Pool(es_, sb, "njunk", [128, 1024], BF16, 2)
                    self.xs = Pool(es_, sb, "nxs", [128, 1024], BF16, 2)
                    self.st = Pool(es_, sb, "nst", [128, 4], F32, 4)
                    self.psB = psB

                def run(self, x, xb, dst, dstb, dst_is_write=True):
                    jk, jkb = self.junk.get()
                    xs, xsb = self.xs.get()
                    st, stb = self.st.get()
                    S_.add("act", lambda e: e.activation(out=jk[:], in_=x, func=AF.Square, accum_out=st[:, 0:1]), R=[xb], W=[jkb, stb])
                    S_.add("act", lambda e: e.activation(out=st[:, 1:2], in_=st[:, 0:1], func=AF.Sqrt, bias=epsN[:, 0:1], scale=1.0 / D),
                           R=[stb, consts], W=[stb])
                    S_.add("dve", lambda e: e.reciprocal(out=st[:, 2:3], in_=st[:, 1:2]), R=[stb], W=[stb])
                    S_.add("dve", lambda e: e.tensor_scalar(out=xs[:], in0=x, scalar1=st[:, 2:3], scalar2=None, op0=ALU.mult),
                           R=[xb, stb], W=[xsb])
                    pt, ptb = self.psB.get()

                    def tr(e):
                        for kc in range(8):
                            ins = e.transpose(out=pt[:, kc * 128:(kc + 1) * 128], in_=xs[:, kc * 128:(kc + 1) * 128], identity=ident[:])
                        return ins
                    S_.add("pe", tr, R=[xsb, consts], W=[ptb])
                    S_.add("act", lambda e: e.copy(out=dst, in_=pt[:].rearrange("p (k t) -> p k t", k=8)), R=[ptb], W=[dstb])
                    return st, stb

            def rope(eng1, eng2, src, srcb, H, half, cos, sin, tb, A, Ab, T, Tb, out, outb):
                v4 = lambda ap: ap.rearrange("p (h two d) -> p h two d", h=H, two=2)
                cb4 = cos.unsqueeze(1).unsqueeze(1).broadcast_to([128, H, 2, half])
                sb3 = sin.unsqueeze(1).broadcast_to([128, H, half])
                S_.add(eng1, lambda e: e.tensor_tensor(out=v4(A), in0=v4(src), in1=cb4, op=ALU.mult), R=[srcb, tb], W=[Ab])
                S_.add(eng2, lambda e: e.tensor_tensor(out=v4(T)[:, :, 0, :], in0=v4(src)[:, :, 1, :], in1=sb3, op=ALU.mult), R=[srcb, tb], W=[Tb])
                S_.add(eng2, lambda e: e.tensor_tensor(out=v4(T)[:, :, 1, :], in0=v4(src)[:, :, 0, :], in1=sb3, op=ALU.mult), R=[srcb, tb], W=[Tb])
                S_.add(eng1, lambda e: e.tensor_tensor(out=v4(out)[:, :, 0, :], in0=v4(A)[:, :, 0, :], in1=v4(T)[:, :, 0, :], op=ALU.subtract),
                       R=[Ab, Tb], W=[outb])
                S_.add(eng1, lambda e: e.tensor_tensor(out=v4(out)[:, :, 1, :], in0=v4(A)[:, :, 1, :], in1=v4(T)[:, :, 1, :], op=ALU.add),
                       R=[Ab, Tb], W=[outb])

            def mm_group(ps, psb, pairs, R):
                def f(e):
                    for (o, l, r, s0, s1) in pairs:
                        ins = e.matmul(o, lhsT=l, rhs=r, start=s0, stop=s1)
                    return ins
                S_.add("pe", f, R=R, W=[psb])

            with ExitStack() as ps:
                psF = Pool(ps, pst, "psF", [128, 512], F32, 6)
                psB = Pool(ps, pst, "psB", [128, 1024], BF16, 2)
                w_in = ps.enter_context(sb("w_in", [128, 8, 3584], BF16))
                w_out = ps.enter_context(sb("w_out", [128, 8, 1024], BF16))
                lg = ps.enter_context(sb("lg", [128, 8], F32))
                tabs = {k: ps.enter_context(sb("tab" + k, [128, 512], F32)) for k in ("DT", "QF", "QB", "KF", "KB", "CF", "CB")}
                tb_ = Buf()
                cwt = ps.enter_context(sb("cwt", [128, 3, 512], F32))
                for t in range(3):
                    S_.add("sp", (lambda t: lambda e: e.dma_start(out=cwt[:, t, :], in_=convw3_d[t:t + 1, :].partition_broadcast(128)))(t), W=[consts], chan="set")
                ws = ExitStack()
                stg = Pool(ws, sb, "wstg", [128, 1024], F32, 3)
                sidx = [0]
                wb_in = load_weight(w_in, w_in_d, 8, 3584, lambda kc: nmix[:, kc:kc + 1], stg, sidx)
                wb_out = load_weight(w_out, w_out_d, 8, 1024, lambda kc: (gnw[:, kc - 4:kc - 3] if kc >= 4 else None), stg, sidx)
                with ExitStack() as ts:
                    io = {k: ts.enter_context(sb("io" + k, [128, 128], F32)) for k in ("rel", "relp", "reln", "mp", "mn", "i1", "ib", "kf", "kb", "c128", "e1", "e2")}
                    S_.add("sp", lambda e: e.dma_start(out=lg[:, 0:4], in_=decf_d.partition_broadcast(128)), W=[tb_], chan="set")
                    S_.add("sp", lambda e: e.dma_start(out=lg[:, 4:8], in_=decb_d.partition_broadcast(128)), W=[tb_], chan="set")
                    S_.add("act", lambda e: e.activation(out=lg[:], in_=lg[:], func=AF.Exp), R=[tb_], W=[tb_])
                    S_.add("dve", lambda e: e.tensor_single_scalar(out=lg[:], in_=lg[:], scalar=-1.0, op=ALU.mult), R=[tb_], W=[tb_])
                    io_ = lambda k, pat, base, cm: S_.add("pool", lambda e: e.iota(io[k][:], pattern=pat, base=base, channel_multiplier=cm,
                                                                                    allow_small_or_imprecise_dtypes=True), W=[tb_])
                    io_("rel", [[1, 128]], 0, -1)
                    io_("i1", [[1, 128]], 1, 0)
                    io_("ib", [[-1, 128]], 128, 0)
                    io_("kf", [[0, 128]], 127, -1)
                    io_("kb", [[0, 128]], 0, 1)
                    io_("c128", [[0, 128]], 128, 0)
                    S_.add("dve", lambda e: e.tensor_single_scalar(out=io["relp"][:], in_=io["rel"][:], scalar=0.0, op=ALU.max), R=[tb_], W=[tb_])
                    S_.add("dve", lambda e: e.tensor_scalar(out=io["reln"][:], in0=io["rel"][:], scalar1=-1.0, scalar2=0.0, op0=ALU.mult, op1=ALU.max),
                           R=[tb_], W=[tb_])
                    S_.add("dve", lambda e: e.tensor_single_scalar(out=io["mp"][:], in_=io["rel"][:], scalar=0.0, op=ALU.is_ge), R=[tb_], W=[tb_])
                    S_.add("dve", lambda e: e.tensor_single_scalar(out=io["mn"][:], in_=io["rel"][:], scalar=0.0, op=ALU.is_lt), R=[tb_], W=[tb_])
                    for h in range(4):
                        hs = slice(h * 128, (h + 1) * 128)
                        ex = lambda dst, src, col: S_.add("act", lambda e: e.activation(out=dst, in_=src, func=AF.Exp, scale=lg[:, col:col + 1]),
                                                          R=[tb_], W=[tb_])
                        ex(io["e1"][:], io["relp"][:], h)
                        ex(io["e2"][:], io["reln"][:], 4 + h)
                        S_.add("dve", lambda e: e.tensor_tensor(out=io["e1"][:], in0=io["e1"][:], in1=io["mp"][:], op=ALU.mult), R=[tb_], W=[tb_])
                        S_.add("dve", lambda e: e.tensor_tensor(out=io["e2"][:], in0=io["e2"][:], in1=io["mn"][:], op=ALU.mult), R=[tb_], W=[tb_])
                        S_.add("dve", (lambda hs: lambda e: e.tensor_tensor(out=tabs["DT"][:, hs], in0=io["e1"][:], in1=io["e2"][:], op=ALU.add))(hs),
                               R=[tb_], W=[tb_])
                        ex(tabs["QF"][:, hs], io["i1"][:], h)
                        ex(tabs["QB"][:, hs], io["ib"][:], 4 + h)
                        ex(tabs["KF"][:, hs], io["kf"][:], h)
                        ex(tabs["KB"][:, hs], io["kb"][:], 4 + h)
                        ex(tabs["CF"][:, hs], io["c128"][:], h)
                        ex(tabs["CB"][:, hs], io["c128"][:], 4 + h)
                    S_.flush(); chk(1)
                ws.close()
                nt = NormT(ps, psB)
                xin = Pool(ps, sb, "xin", [128, 1024], F32, 3)
                xnT = Pool(ps, sb, "xnT", [128, 8, 128], BF16, 2)
                ropt = Pool(ps, sb, "ropt", [128, 256], F32, 3)
                rA = Pool(ps, sb, "rA", [128, 512], F32, 2)
                rT = Pool(ps, sb, "rT", [128, 512], F32, 2)
                kr = Pool(ps, sb, "kr", [128, 512], BF16, 3)
                vv = Pool(ps, sb, "vv", [128, 512], BF16, 3)
                sbl = Pool(ps, sb, "sbl", [128, 512], BF16, 2)
                kd = Pool(ps, sb, "kd", [128, 512], BF16, 2)
                acS = Pool(ps, sb, "acS", [128, 512], F32, 2)
                uP = Pool(ps, sb, "uP", [128, 512], F32, 2)
                u3p = Pool(ps, sb, "u3p", [128, 512], F32, 6)
                yall = Pool(ps, sb, "yall", [128, 1024], BF16, 2)
                S32 = ps.enter_context(sb("S32", [128, 512], F32))
                S16 = Pool(ps, sb, "S16", [128, 512], BF16, 2)
                qr = Pool(ps, sb, "qr", [128, 512], BF16, 2)
                qT = Pool(ps, sb, "qT", [128, 512], BF16, 2)
                qfT = Pool(ps, sb, "qfT", [128, 512], BF16, 2)
                qbT = Pool(ps, sb, "qbT", [128, 512], BF16, 2)
                kT = Pool(ps, sb, "kT", [128, 512], BF16, 2)
                AT = Pool(ps, sb, "AT", [128, 512], BF16, 2)
                gst = Pool(ps, sb, "gst", [128, 4, 6], F32, 2)
                gmv = Pool(ps, sb, "gmv", [128, 4, 2], F32, 2)
                grs = Pool(ps, sb, "grs", [128, 8], F32, 2)
                on = Pool(ps, sb, "on", [128, 512], F32, 2)
                sg = Pool(ps, sb, "sg", [128, 512], F32, 2)
                abS = Pool(ps, sb, "abS", [128, 512], F32, 2)
                yT = Pool(ps, sb, "yT", [128, 8, 128], BF16, 2)
                x1t = Pool(ps, sb, "x1t", [128, 1024], F32, 2)
                S32b = Buf()

                def run_job(job):
                    jn, S = job["n"], job["S"]
                    NCH = S // 128
                    xd = xin_d[jn]
                    sc = scr[jn]
                    zt, ztb = uP.get()
                    S_.add("pool", lambda e: e.memset(zt[0:1, :], 0.0), W=[ztb])
                    S_.add("sp", lambda e: e.dma_start(out=sc["u"][0:1, :], in_=zt[0:1, :]), R=[ztb], W=[S_.dbuf("uT", jn, -1)], chan="st0")
                    S_.add("sp", lambda e: e.dma_start(out=sc["u"][S + 1:S + 2, :], in_=zt[0:1, :]), R=[ztb], W=[S_.dbuf("uT", jn, -2)], chan="st0")
                    S_.add("dve", lambda e: e.memset(S32[:], 0.0), W=[S32b])
                    stv = list(S16.get())
                    tick = [0]
                    S_.add("pool", (lambda s16: lambda e: e.memset(s16[:], 0.0))(stv[0]), W=[stv[1]])

                    def pre_chunk(k, c):
                        x, xb = xin.get()
                        S_.add("sp", (lambda x, c: lambda e: e.dma_start(out=x[:], in_=xd[c * 128:(c + 1) * 128, :]))(x, c), W=[xb], chan="xl%d" % (c % 3))
                        rtb_t, rtb = ropt.get()
                        S_.add("sp", (lambda t, c: lambda e: e.dma_start(out=t[:], in_=rt_d[c * 128:(c + 1) * 128, :]))(rtb_t, c), W=[rtb], chan="rl%d" % (c % 3))
                        xt_, xtb = xnT.get()
                        nt.run(x[:], xb, xt_[:], xtb)
                        yield
                        pk, pkb = psF.get()
                        mm_group(pk, pkb, [(pk[:], xt_[:, kc, :], w_in[:, kc, 2048:2560], kc == 0, kc == 7) for kc in range(8)], [xtb, wb_in])
                        pv, pvb = psF.get()
                        mm_group(pv, pvb, [(pv[:], xt_[:, kc, :], w_in[:, kc, 2560:3072], kc == 0, kc == 7) for kc in range(8)], [xtb, wb_in])
                        pc, pcb = psF.get()
                        mm_group(pc, pcb, [(pc[:], xt_[:, kc, :], w_in[:, kc, 512:1024], kc == 0, kc == 7) for kc in range(8)], [xtb, wb_in])
                        ph, phb = psF.get()
                        mm_group(ph, phb, [(ph[:], xt_[:, kc, :], w_in[:, kc, 1024:1536], kc == 0, kc == 7) for kc in range(8)], [xtb, wb_in])
                        ac, acb = acS.get()
                        S_.add("act", (lambda ac, pc: lambda e: e.copy(out=ac[:], in_=pc[:]))(ac, pc), R=[pcb], W=[acb])
                        u, ub = uP.get()
                        S_.add("dve", (lambda u, ac, ph: lambda e: e.tensor_tensor(out=u[:], in0=ac[:], in1=ph[:], op=ALU.mult))(u, ac, ph),
                               R=[acb, phb], W=[ub])
                        S_.add("sp", (lambda u, c: lambda e: e.dma_start(out=sc["u"][1 + c * 128:1 + (c + 1) * 128, :], in_=u[:]))(u, c),
                               R=[ub], W=[S_.dbuf("uT", jn, c)], chan="st%d" % (c % 2))
                        A, Ab = rA.get()
                        T, Tb = rT.get()
                        k_, kb_ = kr.get()
                        rope("dve", "pool" if False else "dve", pk[:], pkb, 4, 64, rtb_t[:, 128:192], rtb_t[:, 192:256], rtb, A[:], Ab, T[:], Tb, k_[:], kb_)
                        v_, vb_ = vv.get()
                        S_.add("act", (lambda v_, pv: lambda e: e.copy(out=v_[:], in_=pv[:]))(v_, pv), R=[pvb], W=[vb_])
                        S_.add("sp", (lambda k_, c: lambda e: e.dma_start(out=sc["kr"][c * 128:(c + 1) * 128, :], in_=k_[:]))(k_, c),
                               R=[kb_], W=[S_.dbuf("kr", jn, c)], chan="st%d" % (c % 2))
                        S_.add("sp", (lambda v_, c: lambda e: e.dma_start(out=sc["vr"][c * 128:(c + 1) * 128, :], in_=v_[:]))(v_, c),
                               R=[vb_], W=[S_.dbuf("vr", jn, c)], chan="st%d" % (c % 2))
                        yield
                        while tick[0] != k:
                            yield
                        S_.add("sp", (lambda s16, c: lambda e: e.dma_start(out=sc["sb"][c, :, :], in_=s16[:]))(stv[0], c),
                               R=[stv[1]], W=[S_.dbuf("sb", jn, c)], chan="st%d" % (c % 2))
                        kd_, kdb = kd.get()
                        S_.add("pool", (lambda kd_, k_: lambda e: e.tensor_tensor(out=kd_[:], in0=k_[:], in1=tabs["KB"][:], op=ALU.mult))(kd_, k_),
                               R=[kb_, tb_], W=[kdb])
                        pS, pSb = psF.get()
                        mm_group(pS, pSb, [(pS[:, h * 128:(h + 1) * 128], kd_[:, h * 128:(h + 1) * 128], v_[:, h * 128:(h + 1) * 128], True, True) for h in range(4)],
                                 [kdb, vb_])
                        S_.add("pool", lambda e: e.tensor_tensor(out=S32[:], in0=S32[:], in1=tabs["CB"][:], op=ALU.mult), R=[S32b, tb_], W=[S32b])
                        S_.add("dve", (lambda pS: lambda e: e.tensor_tensor(out=S32[:], in0=S32[:], in1=pS[:], op=ALU.add))(pS), R=[S32b, pSb], W=[S32b])
                        stv[0], stv[1] = S16.get()
                        S_.add("act", (lambda s16: lambda e: e.copy(out=s16[:], in_=S32[:]))(stv[0]), R=[S32b], W=[stv[1]])
                        tick[0] = k + 1
                        yield
                    staggered(pre_chunk, list(range(NCH - 1, -1, -1)), 4)
                    S_.add("dve", lambda e: e.memset(S32[:], 0.0), W=[S32b])
                    stv[0], stv[1] = S16.get()
                    tick[0] = 0
                    S_.add("pool", (lambda s16: lambda e: e.memset(s16[:], 0.0))(stv[0]), W=[stv[1]])

                    def main_chunk(k, c):
                        x, xb = xin.get()
                        S_.add("sp", (lambda x, c: lambda e: e.dma_start(out=x[:], in_=xd[c * 128:(c + 1) * 128, :]))(x, c), W=[xb], chan="xl%d" % (c % 3))
                        rtb_t, rtb = ropt.get()
                        S_.add("sp", (lambda t, c: lambda e: e.dma_start(out=t[:], in_=rt_d[c * 128:(c + 1) * 128, :]))(rtb_t, c), W=[rtb], chan="rl%d" % (c % 3))
                        k_, kb_ = kr.get()
                        S_.add("sp", (lambda k_, c: lambda e: e.dma_start(out=k_[:], in_=sc["kr"][c * 128:(c + 1) * 128, :]))(k_, c),
                               R=[S_.dbuf("kr", jn, c)], W=[kb_], chan="kl%d" % (c % 3))
                        v_, vb_ = vv.get()
                        S_.add("sp", (lambda v_, c: lambda e: e.dma_start(out=v_[:], in_=sc["vr"][c * 128:(c + 1) * 128, :]))(v_, c),
                               R=[S_.dbuf("vr", jn, c)], W=[vb_], chan="vl%d" % (c % 3))
                        sl, slb = sbl.get()
                        S_.add("sp", (lambda sl, c: lambda e: e.dma_start(out=sl[:], in_=sc["sb"][c, :, :]))(sl, c),
                               R=[S_.dbuf("sb", jn, c)], W=[slb], chan="sl%d" % (c % 2))
                        u3 = [u3p.get() for _ in range(3)]
                        urd = [S_.dbuf("uT", jn, cc) for cc in (c - 1, c, c + 1) if 0 <= cc < NCH] + [S_.dbuf("uT", jn, -1), S_.dbuf("uT", jn, -2)]
                        for s_ in range(3):
                            S_.add("sp", (lambda t, c, s_: lambda e: e.dma_start(out=t[:], in_=sc["u"][c * 128 + s_:c * 128 + s_ + 128, :]))(u3[s_][0], c, s_),
                                   R=urd, W=[u3[s_][1]], chan="ul%d" % ((c * 3 + s_) % 3))
                        yield
                        xt_, xtb = xnT.get()
                        nt.run(x[:], xb, xt_[:], xtb)
                        yield
                        pq, pqb = psF.get()
                        mm_group(pq, pqb, [(pq[:], xt_[:, kc, :], w_in[:, kc, 1536:2048], kc == 0, kc == 7) for kc in range(8)], [xtb, wb_in])
                        pg, pgb = psF.get()
                        mm_group(pg, pgb, [(pg[:], xt_[:, kc, :], w_in[:, kc, 3072:3584], kc == 0, kc == 7) for kc in range(8)], [xtb, wb_in])
                        pab, pabb = psF.get()
                        mm_group(pab, pabb, [(pab[:], xt_[:, kc, :], w_in[:, kc, 0:512], kc == 0, kc == 7) for kc in range(8)], [xtb, wb_in])
                        sg_, sgb = sg.get()
                        S_.add("act", (lambda sg_, pg: lambda e: e.activation(out=sg_[:], in_=pg[:], func=AF.Silu))(sg_, pg), R=[pgb], W=[sgb])
                        ab_, abb = abS.get()
                        S_.add("act", (lambda ab_, pab: lambda e: e.copy(out=ab_[:], in_=pab[:]))(ab_, pab), R=[pabb], W=[abb])
                        ya_, yab = yall.get()
                        (u0, u0b), (u1, u1b), (u2, u2b) = u3
                        for (ut, utb, tap) in ((u1, u1b, 1), (u0, u0b, 0), (u2, u2b, 2)):
                            S_.add("pool", (lambda ut, tap: lambda e: e.tensor_tensor(out=ut[:], in0=ut[:], in1=cwt[:, tap, :], op=ALU.mult))(ut, tap),
                                   R=[utb, consts], W=[utb])
                        S_.add("dve", (lambda u1, u0: lambda e: e.tensor_tensor(out=u1[:], in0=u1[:], in1=u0[:], op=ALU.add))(u1, u0), R=[u1b, u0b], W=[u1b])
                        S_.add("dve", (lambda u1, u2: lambda e: e.tensor_tensor(out=u1[:], in0=u1[:], in1=u2[:], op=ALU.add))(u1, u2), R=[u1b, u2b], W=[u1b])
                        S_.add("pool", (lambda ya_, u1, ab_: lambda e: e.tensor_tensor(out=ya_[:, 0:512], in0=u1[:], in1=ab_[:], op=ALU.mult))(ya_, u1, ab_),
                               R=[u1b, abb], W=[yab])
                        yield
                        A, Ab = rA.get()
                        T, Tb = rT.get()
                        q_, qb_ = qr.get()
                        rope("dve", "dve", pq[:], pqb, 4, 64, rtb_t[:, 0:64], rtb_t[:, 64:128], rtb, A[:], Ab, T[:], Tb, q_[:], qb_)
                        yield
                        pt, ptb = psB.get()

                        def trqk(e, pt=pt, q_=q_, k_=k_):
                            for h in range(4):
                                e.transpose(out=pt[:, h * 128:(h + 1) * 128], in_=q_[:, h * 128:(h + 1) * 128], identity=ident[:])
                            for h in range(4):
                                ins = e.transpose(out=pt[:, 512 + h * 128:512 + (h + 1) * 128], in_=k_[:, h * 128:(h + 1) * 128], identity=ident[:])
                            return ins
                        S_.add("pe", trqk, R=[qb_, kb_, consts], W=[ptb])
                        qT_, qTb = qT.get()
                        qf_, qfb = qfT.get()
                        qb2, qbb = qbT.get()
                        kT_, kTb = kT.get()
                        S_.add("act", (lambda o, pt: lambda e: e.copy(out=o[:], in_=pt[:, 0:512]))(qT_, pt), R=[ptb], W=[qTb])
                        S_.add("act", (lambda o, pt: lambda e: e.copy(out=o[:], in_=pt[:, 512:1024]))(kT_, pt), R=[ptb], W=[kTb])
                        S_.add("dve", (lambda o, pt: lambda e: e.tensor_tensor(out=o[:], in0=pt[:, 0:512], in1=tabs["QF"][:], op=ALU.mult))(qf_, pt),
                               R=[ptb, tb_], W=[qfb])
                        S_.add("dve", (lambda o, pt: lambda e: e.tensor_tensor(out=o[:], in0=pt[:, 0:512], in1=tabs["QB"][:], op=ALU.mult))(qb2, pt),
                               R=[ptb, tb_], W=[qbb])
                        yield
                        psc, pscb = psF.get()
                        hs = lambda h: slice(h * 128, (h + 1) * 128)
                        mm_group(psc, pscb, [(psc[:, hs(h)], kT_[:, hs(h)], qT_[:, hs(h)], True, True) for h in range(4)], [kTb, qTb])
                        at, atb = AT.get()
                        S_.add("dve", (lambda at, psc: lambda e: e.tensor_tensor(out=at[:], in0=psc[:], in1=tabs["DT"][:], op=ALU.mult))(at, psc),
                               R=[pscb, tb_], W=[atb])
                        yield
                        while tick[0] != k:
                            yield
                        s16, s16b = stv
                        po, pob = psF.get()
                        prs = []
                        for h in range(4):
                            prs += [(po[:, hs(h)], at[:, hs(h)], v_[:, hs(h)], True, False),
                                    (po[:, hs(h)], qf_[:, hs(h)], s16[:, hs(h)], False, False),
                                    (po[:, hs(h)], qb2[:, hs(h)], sl[:, hs(h)], False, True)]
                        mm_group(po, pob, prs, [atb, vb_, qfb, qbb, s16b, slb])
                        kd_, kdb = kd.get()
                        S_.add("pool", (lambda kd_, k_: lambda e: e.tensor_tensor(out=kd_[:], in0=k_[:], in1=tabs["KF"][:], op=ALU.mult))(kd_, k_),
                               R=[kb_, tb_], W=[kdb])
                        pS, pSb = psF.get()
                        mm_group(pS, pSb, [(pS[:, hs(h)], kd_[:, hs(h)], v_[:, hs(h)], True, True) for h in range(4)], [kdb, vb_])
                        S_.add("pool", lambda e: e.tensor_tensor(out=S32[:], in0=S32[:], in1=tabs["CF"][:], op=ALU.mult), R=[S32b, tb_], W=[S32b])
                        S_.add("dve", (lambda pS: lambda e: e.tensor_tensor(out=S32[:], in0=S32[:], in1=pS[:], op=ALU.add))(pS), R=[S32b, pSb], W=[S32b])
                        stv[0], stv[1] = S16.get()
                        S_.add("act", (lambda s16: lambda e: e.copy(out=s16[:], in_=S32[:]))(stv[0]), R=[S32b], W=[stv[1]])
                        tick[0] = k + 1
                        yield
                        st6, st6b = gst.get()
                        mv, mvb = gmv.get()
                        rs, rsb = grs.get()

                        def bns(e, st6=st6, po=po):
                            for h in range(4):
                                ins = e.bn_stats(out=st6[:, h, :], in_=po[:, h * 128:(h + 1) * 128])
                            return ins
                        S_.add("dve", bns, R=[pob], W=[st6b])

                        def bna(e, st6=st6, mv=mv):
                            for h in range(4):
                                ins = e.bn_aggr(out=mv[:, h, :], in_=st6[:, h, :])
                            return ins
                        S_.add("dve", bna, R=[st6b], W=[mvb])
                        S_.add("act", (lambda rs, mv: lambda e: e.activation(out=rs[:, 0:4], in_=mv[:, :, 1], func=AF.Sqrt, bias=epsN[:, 1:2], scale=1.0))(rs, mv),
                               R=[mvb, consts], W=[rsb])
                        S_.add("dve", (lambda rs: lambda e: e.reciprocal(out=rs[:, 4:8], in_=rs[:, 0:4]))(rs), R=[rsb], W=[rsb])
                        on_, onb = on.get()

                        def gnn(e, on_=on_, po=po, mv=mv, rs=rs):
                            for h in range(4):
                                ins = e.tensor_scalar(out=on_[:, h * 128:(h + 1) * 128], in0=po[:, h * 128:(h + 1) * 128], scalar1=mv[:, h, 0:1],
                                                      scalar2=rs[:, 4 + h:5 + h], op0=ALU.subtract, op1=ALU.mult)
                            return ins
                        S_.add("dve", gnn, R=[pob, mvb, rsb], W=[onb])
                        yield
                        S_.add("pool", (lambda ya_, on_, sg_: lambda e: e.tensor_tensor(out=ya_[:, 512:1024], in0=on_[:], in1=sg_[:], op=ALU.mult))(ya_, on_, sg_),
                               R=[onb, sgb], W=[yab])
                        pt2, pt2b = psB.get()

                        def try_(e, pt2=pt2, ya_=ya_):
                            for f in range(8):
                                ins = e.transpose(out=pt2[:, f * 128:(f + 1) * 128], in_=ya_[:, f * 128:(f + 1) * 128], identity=ident[:])
                            return ins
                        S_.add("pe", try_, R=[yab, consts], W=[pt2b])
                        yT_, yTb = yT.get()
                        S_.add("act", (lambda yT_, pt2: lambda e: e.copy(out=yT_[:], in_=pt2[:].rearrange("p (k t) -> p k t", k=8)))(yT_, pt2),
                               R=[pt2b], W=[yTb])
                        x1_, x1b = x1t.get()
                        for n in range(2):
                            pp, ppb = psF.get()
                            mm_group(pp, ppb, [(pp[:], yT_[:, f, :], w_out[:, f, n * 512:(n + 1) * 512], f == 0, f == 7) for f in range(8)], [yTb, wb_out])
                            S_.add("dve", (lambda x1_, pp, x, n: lambda e: e.tensor_tensor(out=x1_[:, n * 512:(n + 1) * 512], in0=pp[:],
                                                                                          in1=x[:, n * 512:(n + 1) * 512], op=ALU.add))(x1_, pp, x, n),
                                   R=[ppb, xb], W=[x1b])
                        S_.add("sp", (lambda x1_, c: lambda e: e.dma_start(out=sc["x1"][c * 128:(c + 1) * 128, :], in_=x1_[:]))(x1_, c),
                               R=[x1b], W=[S_.dbuf("x1", jn, c)], chan="st%d" % (c % 2))
                        if debug:
                            S_.add("sp", (lambda x1_, c: lambda e: e.dma_start(out=dbg[jn]["x1"][c * 128:(c + 1) * 128, :], in_=x1_[:]))(x1_, c),
                                   R=[x1b], W=[S_.dbuf("dx1", jn, c)], chan="dbg")
                        yield
                    staggered(main_chunk, list(range(NCH)), 6)
                for job in jobs:
                    run_job(job)
                S_.flush(); chk(2)

            def ffn_phase(layer, final):
                with ExitStack() as ps:
                    psF = Pool(ps, pst, "psF", [128, 512], F32, 6)
                    psB = Pool(ps, pst, "psB", [128, 1024], BF16, 2)
                    wg = ps.enter_context(sb("wg", [128, 8, DFF], BF16))
                    wu = ps.enter_context(sb("wu", [128, 8, DFF], BF16))
                    wd = ps.enter_context(sb("wd", [128, NFC, 1024], BF16))
                    with ExitStack() as ws:
                        stg = Pool(ws, sb, "wstg", [128, 1024], F32, 3)
                        sidx = [0]
                        nr = lambda kc: nffn[:, layer * 8 + kc:layer * 8 + kc + 1]
                        wbg = load_weight(wg, w_g_d[layer], 8, DFF, nr, stg, sidx)
                        wbu = load_weight(wu, w_u_d[layer], 8, DFF, nr, stg, sidx)
                        wbd = load_weight(wd, w_d_d[layer], NFC, 1024, None, stg, sidx)
                        S_.flush(); chk(3)
                    nt = NormT(ps, psB)
                    xin = Pool(ps, sb, "xin", [128, 1024], F32, 2)
                    xres = Pool(ps, sb, "xres", [128, 1024], F32, 2)
                    xnT = Pool(ps, sb, "xnT", [128, 8, 512], BF16, 2)
                    hT = Pool(ps, sb, "hT", [128, NFC, 512], BF16, 1)
                    sgp = Pool(ps, sb, "sgp", [128, 512], F32, 2)
                    nf = None
                    if final:
                        nf = ps.enter_context(sb("nf", [128, 1024], F32))
                        S_.add("sp", lambda e: e.dma_start(out=nf[:], in_=nfin_d.partition_broadcast(128)), W=[consts], chan="set")
                        fj = Pool(ps, sb, "fj", [128, 1024], BF16, 1)
                        fst = Pool(ps, sb, "fst", [128, 4], F32, 2)
                    def run_job(job):
                        jn = job["n"]
                        S = job["SQ"] if final else job["S"]
                        sc = scr[jn]
                        src, srcn = (sc["x3"], "x3") if final else (sc["x1"], "x1")
                        dst, dstn = (y_d[jn], "y") if final else (sc["x2"], "x2")
                        for t in range(S // 512):
                            xt_, xtb = xnT.get()
                            for ci in range(4):
                                c = t * 4 + ci
                                x, xb = xin.get()
                                S_.add("sp", (lambda x, c: lambda e: e.dma_start(out=x[:], in_=src[c * 128:(c + 1) * 128, :]))(x, c),
                                       R=[S_.dbuf(srcn, jn, c)], W=[xb], chan="xl%d" % (c % 2))
                                nt.run(x[:], xb, xt_[:, :, ci * 128:(ci + 1) * 128], xtb)
                            h_, hb = hT.get()
                            for f in range(NFC):
                                pg, pgb = psF.get()
                                mm_group(pg, pgb, [(pg[:], wg[:, kc, f * 128:(f + 1) * 128], xt_[:, kc, :], kc == 0, kc == 7) for kc in range(8)], [xtb, wbg])
                                pu, pub = psF.get()
                                mm_group(pu, pub, [(pu[:], wu[:, kc, f * 128:(f + 1) * 128], xt_[:, kc, :], kc == 0, kc == 7) for kc in range(8)], [xtb, wbu])
                                s_, sb_ = sgp.get()
                                S_.add("act", (lambda s_, pg: lambda e: e.activation(out=s_[:], in_=pg[:], func=AF.Silu))(s_, pg), R=[pgb], W=[sb_])
                                S_.add("dve", (lambda h_, s_, pu, f: lambda e: e.tensor_tensor(out=h_[:, f, :], in0=s_[:], in1=pu[:], op=ALU.mult))(h_, s_, pu, f),
                                       R=[sb_, pub], W=[hb])
                            for ci in range(4):
                                c = t * 4 + ci
                                xr, xrb = xres.get()
                                S_.add("sp", (lambda xr, c: lambda e: e.dma_start(out=xr[:], in_=src[c * 128:(c + 1) * 128, :]))(xr, c),
                                       R=[S_.dbuf(srcn, jn, c)], W=[xrb], chan="rl%d" % (c % 2))
                                for n in range(2):
                                    pp, ppb = psF.get()
                                    mm_group(pp, ppb, [(pp[:], h_[:, f, ci * 128:(ci + 1) * 128], wd[:, f, n * 512:(n + 1) * 512], f == 0, f == NFC - 1)
                                                       for f in range(NFC)], [hb, wbd])
                                    S_.add("pool" if False else "dve", (lambda xr, pp, n: lambda e: e.tensor_tensor(out=xr[:, n * 512:(n + 1) * 512], in0=pp[:],
                                                                                                                   in1=xr[:, n * 512:(n + 1) * 512], op=ALU.add))(xr, pp, n),
                                           R=[ppb, xrb], W=[xrb])
                                if final:
                                    jk, jkb = fj.get()
                                    st, stb = fst.get()
                                    S_.add("act", (lambda jk, xr, st: lambda e: e.activation(out=jk[:], in_=xr[:], func=AF.Square, accum_out=st[:, 0:1]))(jk, xr, st),
                                           R=[xrb], W=[jkb, stb])
                                    S_.add("act", (lambda st: lambda e: e.activation(out=st[:, 1:2], in_=st[:, 0:1], func=AF.Sqrt, bias=epsN[:, 0:1], scale=1.0 / D))(st),
                                           R=[stb, consts], W=[stb])
                                    S_.add("dve", (lambda st: lambda e: e.reciprocal(out=st[:, 2:3], in_=st[:, 1:2]))(st), R=[stb], W=[stb])
                                    S_.add("dve", (lambda xr, st: lambda e: e.scalar_tensor_tensor(out=xr[:], in0=xr[:], scalar=st[:, 2:3], in1=nf[:],
                                                                                                    op0=ALU.mult, op1=ALU.mult))(xr, st),
                                           R=[xrb, stb, consts], W=[xrb])
                                S_.add("sp", (lambda xr, c: lambda e: e.dma_start(out=dst[c * 128:(c + 1) * 128, :], in_=xr[:]))(xr, c),
                                       R=[xrb], W=[S_.dbuf(dstn, jn, c)], chan="st%d" % (c % 2))
                                if debug and not final:
                                    S_.add("sp", (lambda xr, c: lambda e: e.dma_start(out=dbg[jn]["x2"][c * 128:(c + 1) * 128, :], in_=xr[:]))(xr, c),
                                           R=[xrb], W=[S_.dbuf("dx2", jn, c)], chan="dbg")
                    for job in jobs:
                        run_job(job)
                    S_.flush(); chk(4)

            ffn_phase(0, False)

            with ExitStack() as ps:
                psF = Pool(ps, pst, "psF", [128, 512], F32, 6)
                psB = Pool(ps, pst, "psB", [128, 1024], BF16, 2)
                wqkv = ps.enter_context(sb("wqkv", [128, 8, 3072], BF16))
                with ExitStack() as ws:
                    stg = Pool(ws, sb, "wstg", [128, 1024], F32, 3)
                    sidx = [0]
                    wbq = load_weight(wqkv, w_qkv_d, 8, 3072, lambda kc: nmix[:, 8 + kc:9 + kc], stg, sidx)
                    S_.flush(); chk(5)
                nt = NormT(ps, psB)
                xin = Pool(ps, sb, "xin", [128, 1024], F32, 3)
                xnT = Pool(ps, sb, "xnT", [128, 8, 128], BF16, 2)
                ropt = Pool(ps, sb, "ropt", [128, 64], F32, 3)
                rA = Pool(ps, sb, "rA", [128, 1024], F32, 2)
                rT = Pool(ps, sb, "rT", [128, 1024], F32, 2)
                rr = Pool(ps, sb, "rr", [128, 1024], BF16, 2)
                vS = Pool(ps, sb, "vS", [128, 1024], BF16, 2)
                stT = Pool(ps, sb, "stT", [128, 8, 512], BF16, 2)

                def qk_path(x_src_fn, R_x, rope_src, S, c0, dstT, dstname, jn, col0, with_v, vdst):
                    st_, stb = stT.get()
                    for ci in range(4):
                        c = c0 + ci
                        x, xb = xin.get()
                        x_src_fn(x, xb, c)
                        rt_, rtb = ropt.get()
                        S_.add("sp", (lambda t, c: lambda e: e.dma_start(out=t[:], in_=rope_src[c * 128:(c + 1) * 128, :]))(rt_, c), W=[rtb], chan="rl%d" % (c % 3))
                        yield
                        xt_, xtb = xnT.get()
                        nt.run(x[:], xb, xt_[:], xtb)
                        yield
                        pq = [psF.get(), psF.get()]
                        for n in range(2):
                            mm_group(pq[n][0], pq[n][1], [(pq[n][0][:], xt_[:, kc, :], wqkv[:, kc, col0 + n * 512:col0 + (n + 1) * 512], kc == 0, kc == 7)
                                                          for kc in range(8)], [xtb, wbq])
                        yield
                        A, Ab = rA.get()
                        T, Tb = rT.get()
                        r_, rb_ = rr.get()
                        for n in range(2):
                            sl = slice(n * 512, (n + 1) * 512)
                            rope("dve", "pool" if False else "dve", pq[n][0][:], pq[n][1], 8, 32, rt_[:, 0:32], rt_[:, 32:64], rtb, A[:, sl], Ab, T[:, sl], Tb, r_[:, sl], rb_)
                        pt, ptb = psB.get()

                        def tr(e, pt=pt, r_=r_):
                            for hp in range(8):
                                ins = e.transpose(out=pt[:, hp * 128:(hp + 1) * 128], in_=r_[:, hp * 128:(hp + 1) * 128], identity=ident[:])
                            return ins
                        S_.add("pe", tr, R=[rb_, consts], W=[ptb])
                        S_.add("act", (lambda st_, pt, ci: lambda e: e.copy(out=st_[:, :, ci * 128:(ci + 1) * 128], in_=pt[:].rearrange("p (k t) -> p k t", k=8)))(st_, pt, ci),
                               R=[ptb], W=[stb])
                        yield
                        if with_v:
                            pv = [psF.get(), psF.get()]
                            v_, vb_ = vS.get()
                            for n in range(2):
                                mm_group(pv[n][0], pv[n][1], [(pv[n][0][:], xt_[:, kc, :], wqkv[:, kc, 2048 + n * 512:2048 + (n + 1) * 512], kc == 0, kc == 7)
                                                              for kc in range(8)], [xtb, wbq])
                                S_.add("act", (lambda v_, p, n: lambda e: e.copy(out=v_[:, n * 512:(n + 1) * 512], in_=p[:]))(v_, pv[n][0], n), R=[pv[n][1]], W=[vb_])
                            S_.add("sp", (lambda v_, c: lambda e: e.dma_start(out=vdst[:, c * 128:(c + 1) * 128, :].rearrange("h t e -> t h e"),
                                                                              in_=v_[:].rearrange("p (h e) -> p h e", h=8)))(v_, c),
                                   R=[vb_], W=[S_.dbuf("V", jn, c)], chan="st%d" % (c % 2))
                    S_.add("sp", (lambda st_, c0: lambda e: e.dma_start(out=dstT[:, :, c0 * 128:c0 * 128 + 512].rearrange("h p t -> p h t"), in_=st_[:]))(st_, c0),
                           R=[stb], W=[S_.dbuf(dstname, jn, c0 // 4)], chan="st%d" % ((c0 // 4) % 2))

                def run_job(job):
                    jn, S, SQ = job["n"], job["S"], job["SQ"]
                    sc = scr[jn]

                    def src_plain(x, xb, c, sc=sc, jn=jn):
                        S_.add("sp", (lambda x, c: lambda e: e.dma_start(out=x[:], in_=sc["x2"][c * 128:(c + 1) * 128, :]))(x, c),
                               R=[S_.dbuf("x2", jn, c)], W=[xb], chan="xl%d" % (c % 3))

                    def src_gather(x, xb, c, sc=sc, jn=jn, S=S):
                        S_.add("pool", (lambda x, c: lambda e: e.indirect_dma_start(out=x[:, :], out_offset=None, in_=sc["x2"][:, :],
                                                                                    in_offset=bass.IndirectOffsetOnAxis(ap=idxq[:, c:c + 1], axis=0)))(x, c),
                               R=[S_.dbuf("x2", jn, cc) for cc in range(S // 128)] + [consts], W=[xb], chan="gl%d" % (c % 3))
                    staggered(lambda k, t: qk_path(src_plain, None, dtk_d, S, t * 4, sc["KT"], "KT", jn, 1024, True, sc["V"]),
                              list(range(S // 512)), 8)
                    staggered(lambda k, t: qk_path(src_gather if job["own"] else src_plain, None, dtq_d[jn], SQ, t * 4, sc["QT"], "QT", jn, 0, False, None),
                              list(range(SQ // 512)), 6)
                for job in jobs:
                    run_job(job)
                S_.flush(); chk(6)

            with ExitStack() as ps:
                psS = Pool(ps, pst, "psS", [128, 1024], F32, 2)
                psO = Pool(ps, pst, "psO", [128, 2, 256], F32, 4)
                SMX = max(SA, SB)
                KTt = Pool(ps, sb, "KTt", [128, SMX], BF16, 2)
                Vt = Pool(ps, sb, "Vt", [128, SMX // 128, 130], BF16, 2)
                QTt = Pool(ps, sb, "QTt", [128, max(SA, SQB)], BF16, 2)
                PT = Pool(ps, sb, "PT", [128, 1024], BF16, 3)
                rc = Pool(ps, sb, "rc", [128, 8], F32, 4)
                o1 = Pool(ps, sb, "o1", [128, 128], F32, 3)
                oj = Pool(ps, sb, "oj", [128, 128], BF16, 2)
                ob = Pool(ps, sb, "ob", [128, 4, 128], BF16, 2)
                for i in range(2):
                    S_.add("pool", (lambda t: lambda e: e.memset(t[:, :, 128:130], 1.0))(Vt.t[i]), W=[Vt.b[i]])
                def run_job(job):
                    jn, S, SQ = job["n"], job["S"], job["SQ"]
                    sc = scr[jn]
                    NKT = S // 128
                    for h in range(8):
                        kt_, ktb = KTt.get()
                        S_.add("sp", (lambda kt_, h: lambda e: e.dma_start(out=kt_[:, 0:S], in_=sc["KT"][h, :, :]))(kt_, h),
                               R=[S_.dbuf("KT", jn, t) for t in range(S // 512)], W=[ktb], chan="kl%d" % (h % 2))
                        v_, vb_ = Vt.get()
                        VP = min(32, NKT)
                        for part in range(0, NKT, VP):
                            S_.add("sp", (lambda v_, h, part: lambda e: e.dma_start(out=v_[:, part:part + VP, 0:128],
                                                                                   in_=sc["V"][h, part * 128:(part + VP) * 128, :].rearrange("(k p) e -> p k e", p=128)))(v_, h, part),
                                   R=[S_.dbuf("V", jn, c) for c in range(part, min(part + VP, NKT))], W=[vb_], chan="vl%d" % (h % 2))
                        q_, qb_ = QTt.get()
                        S_.add("sp", (lambda q_, h: lambda e: e.dma_start(out=q_[:, 0:SQ], in_=sc["QT"][h, :, :]))(q_, h),
                               R=[S_.dbuf("QT", jn, t) for t in range(SQ // 512)], W=[qb_], chan="ql%d" % (h % 2))
                        for qt in range(SQ // 512):
                            acc = [psO.get(), psO.get(), psO.get(), psO.get()]
                            qs = slice(qt * 512, (qt + 1) * 512)
                            def qk(kt, qs=qs):
                                ks = slice(kt * 128, (kt + 1) * 128)
                                pS_, pSb = psS.get()
                                mm_group(pS_, pSb, [(pS_[:, 0:512], kt_[0:64, ks], q_[0:64, qs], True, True),
                                                    (pS_[:, 512:1024], kt_[64:128, ks], q_[64:128, qs], True, True)], [ktb, qb_])
                                return pS_, pSb
                            pend = qk(0)
                            for kt in range(NKT):
                                pS_, pSb = pend
                                if kt + 1 < NKT:
                                    pend = qk(kt + 1)
                                p_, pb_ = PT.get()
                                S_.add("act", (lambda p_, pS_: lambda e: e.activation(out=p_[:], in_=pS_[:], func=AF.Exp))(p_, pS_), R=[pSb], W=[pb_])
                                for sub in range(2):
                                    for pair in range(2):
                                        a_, ab_ = acc[sub * 2 + pair]
                                        mm_group(a_, ab_, [(a_[:, j, 0:129], p_[:, sub * 512 + (pair * 2 + j) * 128: sub * 512 + (pair * 2 + j + 1) * 128],
                                                            v_[:, kt, 0:129], kt == 0 and j == 0, kt == NKT - 1) for j in range(2)], [pb_, vb_])
                            ob_, obb = ob.get()
                            for qb4 in range(4):
                                pair, j = qb4 // 2, qb4 % 2
                                a0, a0b = acc[0 + pair]
                                a1, a1b = acc[2 + pair]
                                r_, rb2 = rc.get()
                                S_.add("dve", (lambda r_, a0, j: lambda e: e.reciprocal(out=r_[:, 0:1], in_=a0[:, j, 128:129]))(r_, a0, j), R=[a0b], W=[rb2])
                                S_.add("dve", (lambda r_, a1, j: lambda e: e.reciprocal(out=r_[:, 1:2], in_=a1[:, j, 128:129]))(r_, a1, j), R=[a1b, rb2], W=[rb2])
                                S_.add("dve", (lambda r_: lambda e: e.tensor_tensor(out=r_[:, 2:3], in0=r_[:, 1:2], in1=lamt[:, 0:1], op=ALU.mult))(r_),
                                       R=[rb2, consts], W=[rb2])
                                o_, o_b = o1.get()
                                S_.add("dve", (lambda o_, a0, j, r_: lambda e: e.tensor_scalar(out=o_[:], in0=a0[:, j, 0:128], scalar1=r_[:, 0:1], scalar2=None,
                                                                                              op0=ALU.mult))(o_, a0, j, r_), R=[a0b, rb2], W=[o_b])
                                S_.add("dve", (lambda o_, a1, j, r_: lambda e: e.scalar_tensor_tensor(out=o_[:], in0=a1[:, j, 0:128], scalar=r_[:, 2:3], in1=o_[:],
                                                                                                     op0=ALU.mult, op1=ALU.add))(o_, a1, j, r_),
                                       R=[a1b, rb2, o_b], W=[o_b])
                                jk, jkb = oj.get()
                                S_.add("act", (lambda jk, o_, r_: lambda e: e.activation(out=jk[:], in_=o_[:], func=AF.Square, accum_out=r_[:, 3:4]))(jk, o_, r_),
                                       R=[o_b, rb2], W=[jkb, rb2])
                                S_.add("act", (lambda r_: lambda e: e.activation(out=r_[:, 4:5], in_=r_[:, 3:4], func=AF.Sqrt, bias=epsN[:, 1:2], scale=1.0 / 128))(r_),
                                       R=[rb2, consts], W=[rb2])
                                S_.add("dve", (lambda r_: lambda e: e.reciprocal(out=r_[:, 5:6], in_=r_[:, 4:5]))(r_), R=[rb2], W=[rb2])
                                S_.add("pool", (lambda ob_, o_, r_, qb4: lambda e: e.tensor_scalar(out=ob_[:, qb4, :], in0=o_[:], scalar1=r_[:, 5:6], scalar2=None,
                                                                                                  op0=ALU.mult))(ob_, o_, r_, qb4), R=[o_b, rb2], W=[obb])
                            S_.add("sp", (lambda ob_, qt, h: lambda e: e.dma_start(
                                out=sc["O"][qt * 512:(qt + 1) * 512, h * 128:(h + 1) * 128].rearrange("(k p) e -> p k e", p=128), in_=ob_[:]))(ob_, qt, h),
                                   R=[obb], W=[S_.dbuf("O", jn, qt, h)], chan="st%d" % (qt % 2))
                for job in jobs:
                    run_job(job)
                S_.flush(); chk(7)

            with ExitStack() as ps:
                psF = Pool(ps, pst, "psF", [128, 512], F32, 6)
                psB = Pool(ps, pst, "psB", [128, 1024], BF16, 2)
                wdo = ps.enter_context(sb("wdo", [128, 8, 1024], BF16))
                with ExitStack() as ws:
                    stg = Pool(ws, sb, "wstg", [128, 1024], F32, 3)
                    sidx = [0]
                    wbo = load_weight(wdo, w_do_d, 8, 1024, lambda kc: subl[:, 0:1], stg, sidx)
                    S_.flush(); chk(8)
                oin = Pool(ps, sb, "oin", [128, 1024], BF16, 3)
                oT = Pool(ps, sb, "oT", [128, 8, 128], BF16, 2)
                xres = Pool(ps, sb, "xres", [128, 1024], F32, 3)
                def run_job(job):
                    jn, S, SQ = job["n"], job["S"], job["SQ"]
                    sc = scr[jn]
                    for c in range(SQ // 128):
                        o_, o_b = oin.get()
                        S_.add("sp", (lambda o_, c: lambda e: e.dma_start(out=o_[:], in_=sc["O"][c * 128:(c + 1) * 128, :]))(o_, c),
                               R=[S_.dbuf("O", jn, c // 4, h) for h in range(8)], W=[o_b], chan="xl%d" % (c % 3))
                        xr, xrb = xres.get()
                        if job["own"]:
                            S_.add("pool", (lambda xr, c: lambda e: e.indirect_dma_start(out=xr[:, :], out_offset=None, in_=sc["x2"][:, :],
                                                                                        in_offset=bass.IndirectOffsetOnAxis(ap=idxq[:, c:c + 1], axis=0)))(xr, c),
                                   R=[consts], W=[xrb], chan="gl%d" % (c % 3))
                        else:
                            S_.add("sp", (lambda xr, c: lambda e: e.dma_start(out=xr[:], in_=sc["x2"][c * 128:(c + 1) * 128, :]))(xr, c), W=[xrb], chan="rl%d" % (c % 3))
                        pt, ptb = psB.get()

                        def tr(e, pt=pt, o_=o_):
                            for f in range(8):
                                ins = e.transpose(out=pt[:, f * 128:(f + 1) * 128], in_=o_[:, f * 128:(f + 1) * 128], identity=ident[:])
                            return ins
                        S_.add("pe", tr, R=[o_b, consts], W=[ptb])
                        oT_, oTb = oT.get()
                        S_.add("act", (lambda oT_, pt: lambda e: e.copy(out=oT_[:], in_=pt[:].rearrange("p (k t) -> p k t", k=8)))(oT_, pt), R=[ptb], W=[oTb])
                        for n in range(2):
                            pp, ppb = psF.get()
                            mm_group(pp, ppb, [(pp[:], oT_[:, f, :], wdo[:, f, n * 512:(n + 1) * 512], f == 0, f == 7) for f in range(8)], [oTb, wbo])
                            S_.add("dve", (lambda xr, pp, n: lambda e: e.tensor_tensor(out=xr[:, n * 512:(n + 1) * 512], in0=pp[:], in1=xr[:, n * 512:(n + 1) * 512],
                                                                                      op=ALU.add))(xr, pp, n), R=[ppb, xrb], W=[xrb])
                        S_.add("sp", (lambda xr, c: lambda e: e.dma_start(out=sc["x3"][c * 128:(c + 1) * 128, :], in_=xr[:]))(xr, c),
                               R=[xrb], W=[S_.dbuf("x3", jn, c)], chan="st%d" % (c % 2))
                        if debug:
                            S_.add("sp", (lambda xr, c: lambda e: e.dma_start(out=dbg[jn]["x3"][c * 128:(c + 1) * 128, :], in_=xr[:]))(xr, c),
                                   R=[xrb], W=[S_.dbuf("dx3", jn, c)], chan="dbg")
                for job in jobs:
                    run_job(job)
                S_.flush(); chk(9)

            ffn_phase(1, True)

        except _Stop:
            pass
        sch.stopped = False
        sch.maxops = 10 ** 9
        S_.add("sp", lambda e: e.dma_start(out=scr["a"]["u"][0:1, 0:1], in_=scr["a"]["u"][0:1, 1:2]), chan="fin")
        S_.flush(); chk(10)
        fin = S_.ops[-1]

        with nc.Block() as block:
            @block.sync
            def _(e):
                e.wait_ge(S_.csem["fin"], fin.val)
    return nc


def _rope_tabs(S, dim, scale):
    inv = (10000.0 ** (-np.arange(0, dim, 2, dtype=np.float32) / np.float32(dim))).astype(np.float32)
    ang = np.arange(S, dtype=np.float32)[:, None] * inv[None, :]
    return (np.cos(ang) * scale).astype(np.float32), (np.sin(ang) * scale).astype(np.float32)


def make_inputs(core, SA, SB, NQB, inp, xa, xb, qoff):
    f = lambda a: np.ascontiguousarray(np.asarray(a, dtype=np.float32))
    SM = max(SA, SB)
    c1, s1 = _rope_tabs(SM, 128, 1.0)
    c2, s2 = _rope_tabs(SM, 128, 128 ** -0.5)
    rt = np.concatenate([c1, s1, c2, s2], axis=1)
    ck, sk = _rope_tabs(SM, 64, 1.0)
    cq, sq = _rope_tabs(SM, 64, 0.125)
    dtk = np.concatenate([ck, sk], 1)
    dtq = np.concatenate([cq, sq], 1)
    SQB = NQB * 128
    idx = (qoff + np.arange(NQB)[None, :] * 128 + np.arange(128)[:, None]).astype(np.int32)
    pk = lambda v: f(np.asarray(v).reshape(-1, 128).T)
    m = {
        "xa": f(xa), "xb": f(xb), "idxq": np.ascontiguousarray(idx),
        "rt": f(rt), "dtk": f(dtk), "dtqa": f(dtq[:SA]), "dtqb": f(dtq[qoff:qoff + SQB]),
        "w_in": f(inp["hyb_w_in"][0]), "w_out": f(inp["hyb_w_out"][0]), "w_qkv": f(inp["diff_w_qkv"][0]), "w_do": f(inp["diff_w_out"][0]),
        "nmix": pk(np.asarray(inp["norm_mix"]).reshape(-1)), "nffn": pk(np.asarray(inp["norm_ffn"]).reshape(-1)),
        "nfin": f(np.asarray(inp["norm_final"]).reshape(1, D)),
        "convw3": f(np.asarray(inp["hyb_conv_w"][0]).reshape(3, 512)),
        "convw": f(np.asarray(inp["hyb_conv_w"][0]).reshape(3, 4, 128).transpose(2, 1, 0).reshape(128, 12)),
        "decf": f(np.asarray(inp["hyb_decay_fwd"]).reshape(1, 4)), "decb": f(np.asarray(inp["hyb_decay_bwd"]).reshape(1, 4)),
        "gnw": pk(np.asarray(inp["hyb_gn"]).reshape(-1)),
        "lamv": f(np.concatenate([np.asarray(inp[k]).reshape(-1) for k in ("diff_lq1", "diff_lk1", "diff_lq2", "diff_lk2")]).reshape(1, 256)),
        "subln": f(np.asarray(inp["diff_subln"]).reshape(128, 1)),
    }
    for l in range(2):
        m["w_g%d" % l] = f(inp["ffn_w_gate"][l])
        m["w_u%d" % l] = f(inp["ffn_w_up"][l])
        m["w_d%d" % l] = f(inp["ffn_w_down"][l])
    return m


def kernel(**inp):
    xp = np.asarray(inp["x_prompt"])
    xs = np.asarray(inp["x_sample"])
    SA, SB = xs.shape[1], xp.shape[1]
    NQB = SB // 4 // 128
    nc = build(SA, SB, NQB)
    in_maps = [make_inputs(c, SA, SB, NQB, inp, xs[c], xp[c // 4], (c % 4) * (SB // 4)) for c in range(8)]
    res = run_bass_kernel_spmd(nc, in_maps, core_ids=list(range(8)))
    ys = np.stack([res.results[c]["ya"] for c in range(8)], 0).astype(np.float32)
    yp = np.stack([np.concatenate([res.results[g * 4 + r]["yb"] for r in range(4)], 0) for g in range(2)], 0).astype(np.float32)
    return (yp, ys)
```

```python
import math
from contextlib import ExitStack
import numpy as np
import concourse.bass as bass
import concourse.mybir as mybir
from concourse.bass_utils import run_bass_kernel_spmd

F32 = mybir.dt.float32
BF16 = mybir.dt.bfloat16
I32 = mybir.dt.int32
AF = mybir.ActivationFunctionType
ALU = mybir.AluOpType
AX = mybir.AxisListType

D = 1024
DFF = 2816
NFC = DFF // 128
LAMBDA_INIT = 0.8 - 0.6 * math.exp(-0.3 * 1)
ENG = ("pe", "act", "dve", "pool", "sp")


class Buf:
    __slots__ = ("w", "r", "excl")

    def __init__(self, excl=False):
        self.w = None
        self.r = {}
        self.excl = excl


class Op:
    __slots__ = ("eng", "fn", "deps", "chan", "signal", "val")

    def __init__(self, eng, fn, deps, chan):
        self.eng = eng
        self.fn = fn
        self.deps = deps
        self.chan = chan
        self.signal = chan is not None
        self.val = None


class Sched:
    def __init__(self, nc, es, nchan=40):
        self.nc = nc
        self.ops = []
        self.emitted = 0
        self.last = {}
        self.lastchan = {}
        self.bar = ()
        self.esem = {e: es.enter_context(nc.semaphore("s_" + e)) for e in ENG}
        self.freechan = [es.enter_context(nc.semaphore("c%d" % i)) for i in range(nchan)]
        self.csem = {}
        self.cnt = {e: 0 for e in ENG}
        self.ccnt = {}
        self.waited = {e: {} for e in ENG}
        self.dram = {}

    def dbuf(self, *key):
        b = self.dram.get(key)
        if b is None:
            b = self.dram[key] = Buf()
        return b

    stopped = False
    maxops = 10 ** 9
    names = {}

    def add(self, eng, fn, R=(), W=(), chan=None):
        if self.stopped:
            return -1
        if len(self.ops) >= self.maxops:
            self.stopped = True
            return -1
        deps = set(self.bar)
        key0 = chan if chan is not None else eng
        for b in R:
            if b.w is not None:
                deps.add(b.w)
            if b.excl:
                deps.update(v for k, v in b.r.items() if k != key0)
        for b in W:
            if b.w is not None:
                deps.add(b.w)
            deps.update(b.r.values())
        idx = len(self.ops)
        if chan is not None:
            if chan in self.lastchan:
                deps.add(self.lastchan[chan])
            self.lastchan[chan] = idx
        self.ops.append(Op(eng, fn, deps, chan))
        key = chan if chan is not None else eng
        for b in R:
            b.r[key] = idx
        for b in W:
            b.w = idx
            b.r = {}
        self.last[eng] = idx
        return idx

    def _event(self, op):
        if op.chan is not None:
            return self.csem[op.chan], op.val
        return self.esem[op.eng], op.val

    def flush(self):
        nc = self.nc
        lo = self.emitted
        if lo == len(self.ops):
            return
        ops = self.ops
        for i in range(lo, len(ops)):
            for d in ops[i].deps:
                if d >= lo:
                    od = ops[d]
                    if od.chan is None and od.eng == "pe" and ops[i].eng == "pe" and ops[i].chan is None:
                        continue
                    od.signal = True
        for e, i in self.last.items():
            ops[i].signal = True
        for i in range(lo, len(ops)):
            op = ops[i]
            if op.chan is not None:
                if op.chan not in self.csem:
                    self.csem[op.chan] = self.freechan.pop()
                    self.ccnt[op.chan] = 0
                self.ccnt[op.chan] += 16
                op.val = self.ccnt[op.chan]
            elif op.signal:
                self.cnt[op.eng] += 1
                op.val = self.cnt[op.eng]
        per = {e: [] for e in ENG}
        for i in range(lo, len(ops)):
            per[ops[i].eng].append(i)

        def emit(engname, e):
            waited = self.waited[engname]
            for i in per[engname]:
                op = ops[i]
                for d in sorted(op.deps):
                    od = ops[d]
                    if d < lo and d not in self.bar:
                        continue
                    if od.chan is None and od.eng == "pe" and engname == "pe" and op.chan is None:
                        continue
                    sem, val = self._event(od)
                    k = id(sem)
                    if waited.get(k, 0) < val:
                        e.wait_ge(sem, val)
                        waited[k] = val
                ins = op.fn(e)
                if op.signal:
                    sem, val = self._event(op)
                    ins.then_inc(sem, 16 if op.chan is not None else 1)

        with nc.Block() as block:
            @block.tensor
            def _(e):
                emit("pe", e)

            @block.scalar
            def _(e):
                emit("act", e)

            @block.vector
            def _(e):
                emit("dve", e)

            @block.gpsimd
            def _(e):
                emit("pool", e)

            @block.sync
            def _(e):
                emit("sp", e)
        self.emitted = len(ops)
        self.bar = tuple(set(list(self.last.values()) + list(self.lastchan.values())))
        for i in self.bar:
            assert ops[i].signal


def staggered(fn, chunks, delay):
    def thread(idx):
        for k in idx:
            yield from fn(k, chunks[k])
    g = [thread(range(0, len(chunks), 2)), thread(range(1, len(chunks), 2))]
    alive = [True, True]
    for _ in range(delay):
        try:
            next(g[0])
        except StopIteration:
            alive[0] = False
            break
    while any(alive):
        for i in (0, 1):
            if alive[i]:
                try:
                    next(g[i])
                except StopIteration:
                    alive[i] = False


class Pool:
    def __init__(self, es, alloc, name, shape, dt, n):
        self.t = [es.enter_context(alloc("%s%d" % (name, i), shape, dt)) for i in range(n)]
        self.b = [Buf(excl=(alloc.__name__ == 'pst')) for _ in range(n)]
        self.i = 0

    def get(self):
        k = self.i % len(self.t)
        self.i += 1
        return self.t[k], self.b[k]


class _Stop(Exception):
    pass


def build(SA, SB, NQB, debug=False, nph=99):
    nc = bass.Bass("TRN2", target_bir_lowering=False)
    dt_in = lambda n, s, d=F32: nc.dram_tensor(n, s, d, kind="ExternalInput").ap()
    dt_scr = lambda n, s, d: nc.dram_tensor(n, s, d, kind="Internal").ap()
    SQB = NQB * 128
    jobs = [dict(n="a", S=SA, SQ=SA, own=False), dict(n="b", S=SB, SQ=SQB, own=True)]
    xin_d = {"a": dt_in("xa", [SA, D]), "b": dt_in("xb", [SB, D])}
    idxq_d = dt_in("idxq", [128, NQB], I32)
    SM = max(SA, SB)
    rt_d = dt_in("rt", [SM, 256])
    dtk_d = dt_in("dtk", [SM, 64])
    dtq_d = {"a": dt_in("dtqa", [SA, 64]), "b": dt_in("dtqb", [SQB, 64])}
    w_in_d = dt_in("w_in", [D, 3584])
    w_out_d = dt_in("w_out", [D, D])
    w_qkv_d = dt_in("w_qkv", [D, 3072])
    w_do_d = dt_in("w_do", [D, D])
    w_g_d = [dt_in("w_g%d" % l, [D, DFF]) for l in range(2)]
    w_u_d = [dt_in("w_u%d" % l, [D, DFF]) for l in range(2)]
    w_d_d = [dt_in("w_d%d" % l, [DFF, D]) for l in range(2)]
    nmix_d = dt_in("nmix", [128, 16])
    nffn_d = dt_in("nffn", [128, 16])
    nfin_d = dt_in("nfin", [1, D])
    convw_d = dt_in("convw", [128, 12])
    convw3_d = dt_in("convw3", [3, 512])
    decf_d = dt_in("decf", [1, 4])
    decb_d = dt_in("decb", [1, 4])
    gn_d = dt_in("gnw", [128, 4])
    lam_d = dt_in("lamv", [1, 256])
    subln_d = dt_in("subln", [128, 1])
    y_d = {j["n"]: nc.dram_tensor("y" + j["n"], [j["SQ"], D], F32, kind="ExternalOutput").ap() for j in jobs}
    scr = {}
    for j in jobs:
        n, S, SQ = j["n"], j["S"], j["SQ"]
        scr[n] = dict(
            u=dt_scr("u" + n, [S + 2, 512], F32),
            kr=dt_scr("kr" + n, [S, 512], BF16),
            vr=dt_scr("vr" + n, [S, 512], BF16),
            sb=dt_scr("sb" + n, [S // 128, 128, 512], BF16),
            x1=dt_scr("x1" + n, [S, D], F32),
            x2=dt_scr("x2" + n, [S, D], F32),
            QT=dt_scr("QT" + n, [8, 128, SQ], BF16),
            KT=dt_scr("KT" + n, [8, 128, S], BF16),
            V=dt_scr("V" + n, [8, S, 128], BF16),
            O=dt_scr("O" + n, [SQ, D], BF16),
            x3=dt_scr("x3" + n, [SQ, D], F32),
        )
    dbg = {}
    if debug:
        for j in jobs:
            n, S, SQ = j["n"], j["S"], j["SQ"]
            dbg[n] = dict(
                x1=nc.dram_tensor("dbg_x1" + n, [S, D], F32, kind="ExternalOutput").ap(),
                x2=nc.dram_tensor("dbg_x2" + n, [S, D], F32, kind="ExternalOutput").ap(),
                x3=nc.dram_tensor("dbg_x3" + n, [SQ, D], F32, kind="ExternalOutput").ap(),
            )

    with ExitStack() as es:
        sch = Sched(nc, es)
        import os
        sch.maxops = int(os.environ.get('KSTOP', 10 ** 9))
        S_ = sch
        _uid = [0]

        def sb(name, shape, dt):
            _uid[0] += 1
            return nc.sbuf_tensor("s%d_%s" % (_uid[0], name), shape, dt)

        def pst(name, shape, dt):
            _uid[0] += 1
            return nc.psum_tensor("p%d_%s" % (_uid[0], name), shape, dt)
        ident = es.enter_context(sb("ident", [128, 128], BF16))
        epsN = es.enter_context(sb("epsN", [128, 4], F32))
        nmix = es.enter_context(sb("nmix", [128, 16], F32))
        nffn = es.enter_context(sb("nffn", [128, 16], F32))
        gnw = es.enter_context(sb("gnw", [128, 4], F32))
        subl = es.enter_context(sb("subl", [128, 1], F32))
        convw = es.enter_context(sb("convw", [128, 12], F32))
        idxq = es.enter_context(sb("idxq", [128, NQB], I32))
        lamt = es.enter_context(sb("lamt", [128, 8], F32))
        consts = Buf()

        def chk(k):
            if nph == k:
                sch.stopped = True

        try:
            with ExitStack() as ps:
                tmpa = ps.enter_context(sb("tmpa", [128, 128], F32))
                lamv = ps.enter_context(sb("lamv", [128, 256], F32))
                lamp = ps.enter_context(sb("lamp", [128, 128], F32))
                S_.add("pool", lambda e: e.iota(tmpa[:], pattern=[[1, 128]], base=0, channel_multiplier=-1,
                                                allow_small_or_imprecise_dtypes=True), W=[consts])
                S_.add("dve", lambda e: e.tensor_single_scalar(out=ident[:], in_=tmpa[:], scalar=0.0, op=ALU.is_equal),
                       R=[consts], W=[consts])
                S_.add("dve", lambda e: e.memset(epsN[:, 0:1], 1e-6), W=[consts])
                S_.add("dve", lambda e: e.memset(epsN[:, 1:3], 1e-5), W=[consts])
                S_.add("dve", lambda e: e.memset(epsN[:, 3:4], 0.0), W=[consts])
                for t, d in ((nmix, nmix_d), (nffn, nffn_d), (gnw, gn_d), (subl, subln_d), (convw, convw_d), (idxq, idxq_d)):
                    S_.add("sp", (lambda t, d: lambda e: e.dma_start(out=t[:], in_=d[:, :]))(t, d), W=[consts], chan="set")
                S_.add("sp", lambda e: e.dma_start(out=lamv[:], in_=lam_d.partition_broadcast(128)), W=[consts], chan="set")
                S_.add("dve", lambda e: e.tensor_single_scalar(out=subl[:], in_=subl[:], scalar=1.0 - LAMBDA_INIT, op=ALU.mult),
                       R=[consts], W=[consts])
                S_.add("dve", lambda e: e.tensor_tensor(out=lamp[:, 0:64], in0=lamv[:, 0:64], in1=lamv[:, 64:128], op=ALU.mult),
                       R=[consts], W=[consts])
                S_.add("dve", lambda e: e.tensor_tensor(out=lamp[:, 64:128], in0=lamv[:, 128:192], in1=lamv[:, 192:256], op=ALU.mult),
                       R=[consts], W=[consts])
                S_.add("dve", lambda e: e.reduce_sum(out=lamt[:, 1:2], in_=lamp[:, 0:64], axis=AX.X), R=[consts], W=[consts])
                S_.add("dve", lambda e: e.reduce_sum(out=lamt[:, 2:3], in_=lamp[:, 64:128], axis=AX.X), R=[consts], W=[consts])
                S_.add("act", lambda e: e.activation(out=lamt[:, 3:5], in_=lamt[:, 1:3], func=AF.Exp), R=[consts], W=[consts])
                S_.add("dve", lambda e: e.tensor_tensor(out=lamt[:, 5:6], in0=lamt[:, 4:5], in1=lamt[:, 3:4], op=ALU.subtract),
                       R=[consts], W=[consts])
                S_.add("dve", lambda e: e.tensor_single_scalar(out=lamt[:, 0:1], in_=lamt[:, 5:6], scalar=-LAMBDA_INIT, op=ALU.add),
                       R=[consts], W=[consts])
                S_.flush(); chk(0)

            def load_weight(dst, src, KC, C, rows, stg, sidx):
                srcv = src.rearrange("(kc p) c -> p kc c", p=128)
                wb = Buf()
                for kc in range(KC):
                    for c0 in range(0, C, 1024):
                        cw = min(1024, C - c0)
                        st, stb = stg.get()
                        S_.add("sp", (lambda st, kc, c0, cw: lambda e: e.dma_start(out=st[:, 0:cw], in_=srcv[:, kc, c0:c0 + cw]))(st, kc, c0, cw),
                               W=[stb], chan="wl%d" % (sidx[0] % 3))
                        eng = ("dve", "pool")[sidx[0] % 2]
                        sidx[0] += 1
                        r = rows(kc) if rows is not None else None
                        if r is None:
                            S_.add(eng, (lambda st, kc, c0, cw: lambda e: e.tensor_copy(out=dst[:, kc, c0:c0 + cw], in_=st[:, 0:cw]))(st, kc, c0, cw),
                                   R=[stb, consts], W=[wb])
                        else:
                            S_.add(eng, (lambda st, kc, c0, cw, r: lambda e: e.tensor_scalar(
                                out=dst[:, kc, c0:c0 + cw], in0=st[:, 0:cw], scalar1=r, scalar2=None, op0=ALU.mult))(st, kc, c0, cw, r),
                                   R=[stb, consts], W=[wb])
                return wb

            class NormT:
                def __init__(self, es_, psB):
                    self.junk = Pool(es_, sb, "njunk", [128, 1024], BF16, 2)
                    self.xs = Pool(es_, sb, "nxs", [128, 1024], BF16, 2)
                    self.st = Pool(es_, sb, "nst", [128, 4], F32, 4)
                    self.psB = psB

                def run(self, x, xb, dst, dstb, dst_is_write=True):
                    jk, jkb = self.junk.get()
                    xs, xsb = self.xs.get()
                    st, stb = self.st.get()
                    S_.add("act", lambda e: e.activation(out=jk[:], in_=x, func=AF.Square, accum_out=st[:, 0:1]), R=[xb], W=[jkb, stb])
                    S_.add("act", lambda e: e.activation(out=st[:, 1:2], in_=st[:, 0:1], func=AF.Sqrt, bias=epsN[:, 0:1], scale=1.0 / D),
                           R=[stb, consts], W=[stb])
                    S_.add("dve", lambda e: e.reciprocal(out=st[:, 2:3], in_=st[:, 1:2]), R=[stb], W=[stb])
                    S_.add("dve", lambda e: e.tensor_scalar(out=xs[:], in0=x, scalar1=st[:, 2:3], scalar2=None, op0=ALU.mult),
                           R=[xb, stb], W=[xsb])
                    pt, ptb = self.psB.get()

                    def tr(e):
                        for kc in range(8):
                            ins = e.transpose(out=pt[:, kc * 128:(kc + 1) * 128], in_=xs[:, kc * 128:(kc + 1) * 128], identity=ident[:])
                        return ins
                    S_.add("pe", tr, R=[xsb, consts], W=[ptb])
                    S_.add("act", lambda e: e.copy(out=dst, in_=pt[:].rearrange("p (k t) -> p k t", k=8)), R=[ptb], W=[dstb])
                    return st, stb

            def rope(eng1, eng2, src, srcb, H, half, cos, sin, tb, A, Ab, T, Tb, out, outb):
                v4 = lambda ap: ap.rearrange("p (h two d) -> p h two d", h=H, two=2)
                cb4 = cos.unsqueeze(1).unsqueeze(1).broadcast_to([128, H, 2, half])
                sb3 = sin.unsqueeze(1).broadcast_to([128, H, half])
                S_.add(eng1, lambda e: e.tensor_tensor(out=v4(A), in0=v4(src), in1=cb4, op=ALU.mult), R=[srcb, tb], W=[Ab])
                S_.add(eng2, lambda e: e.tensor_tensor(out=v4(T)[:, :, 0, :], in0=v4(src)[:, :, 1, :], in1=sb3, op=ALU.mult), R=[srcb, tb], W=[Tb])
                S_.add(eng2, lambda e: e.tensor_tensor(out=v4(T)[:, :, 1, :], in0=v4(src)[:, :, 0, :], in1=sb3, op=ALU.mult), R=[srcb, tb], W=[Tb])
                S_.add(eng1, lambda e: e.tensor_tensor(out=v4(out)[:, :, 0, :], in0=v4(A)[:, :, 0, :], in1=v4(T)[:, :, 0, :], op=ALU.subtract),
                       R=[Ab, Tb], W=[outb])
                S_.add(eng1, lambda e: e.tensor_tensor(out=v4(out)[:, :, 1, :], in0=v4(A)[:, :, 1, :], in1=v4(T)[:, :, 1, :], op=ALU.add),
                       R=[Ab, Tb], W=[outb])

            def mm_group(ps, psb, pairs, R):
                def f(e):
                    for (o, l, r, s0, s1) in pairs:
                        ins = e.matmul(o, lhsT=l, rhs=r, start=s0, stop=s1)
                    return ins
                S_.add("pe", f, R=R, W=[psb])

            with ExitStack() as ps:
                psF = Pool(ps, pst, "psF", [128, 512], F32, 6)
                psB = Pool(ps, pst, "psB", [128, 1024], BF16, 2)
                w_in = ps.enter_context(sb("w_in", [128, 8, 3584], BF16))
                w_out = ps.enter_context(sb("w_out", [128, 8, 1024], BF16))
                lg = ps.enter_context(sb("lg", [128, 8], F32))
                tabs = {k: ps.enter_context(sb("tab" + k, [128, 512], F32)) for k in ("DT", "QF", "QB", "KF", "KB", "CF", "CB")}
                tb_ = Buf()
                cwt = ps.enter_context(sb("cwt", [128, 3, 512], F32))
                for t in range(3):
                    S_.add("sp", (lambda t: lambda e: e.dma_start(out=cwt[:, t, :], in_=convw3_d[t:t + 1, :].partition_broadcast(128)))(t), W=[consts], chan="set")
                ws = ExitStack()
                stg = Pool(ws, sb, "wstg", [128, 1024], F32, 3)
                sidx = [0]
                wb_in = load_weight(w_in, w_in_d, 8, 3584, lambda kc: nmix[:, kc:kc + 1], stg, sidx)
                wb_out = load_weight(w_out, w_out_d, 8, 1024, lambda kc: (gnw[:, kc - 4:kc - 3] if kc >= 4 else None), stg, sidx)
                with ExitStack() as ts:
                    io = {k: ts.enter_context(sb("io" + k, [128, 128], F32)) for k in ("rel", "relp", "reln", "mp", "mn", "i1", "ib", "kf", "kb", "c128", "e1", "e2")}
                    S_.add("sp", lambda e: e.dma_start(out=lg[:, 0:4], in_=decf_d.partition_broadcast(128)), W=[tb_], chan="set")
                    S_.add("sp", lambda e: e.dma_start(out=lg[:, 4:8], in_=decb_d.partition_broadcast(128)), W=[tb_], chan="set")
                    S_.add("act", lambda e: e.activation(out=lg[:], in_=lg[:], func=AF.Exp), R=[tb_], W=[tb_])
                    S_.add("dve", lambda e: e.tensor_single_scalar(out=lg[:], in_=lg[:], scalar=-1.0, op=ALU.mult), R=[tb_], W=[tb_])
                    io_ = lambda k, pat, base, cm: S_.add("pool", lambda e: e.iota(io[k][:], pattern=pat, base=base, channel_multiplier=cm,
                                                                                    allow_small_or_imprecise_dtypes=True), W=[tb_])
                    io_("rel", [[1, 128]], 0, -1)
                    io_("i1", [[1, 128]], 1, 0)
                    io_("ib", [[-1, 128]], 128, 0)
                    io_("kf", [[0, 128]], 127, -1)
                    io_("kb", [[0, 128]], 0, 1)
                    io_("c128", [[0, 128]], 128, 0)
                    S_.add("dve", lambda e: e.tensor_single_scalar(out=io["relp"][:], in_=io["rel"][:], scalar=0.0, op=ALU.max), R=[tb_], W=[tb_])
                    S_.add("dve", lambda e: e.tensor_scalar(out=io["reln"][:], in0=io["rel"][:], scalar1=-1.0, scalar2=0.0, op0=ALU.mult, op1=ALU.max),
                           R=[tb_], W=[tb_])
                    S_.add("dve", lambda e: e.tensor_single_scalar(out=io["mp"][:], in_=io["rel"][:], scalar=0.0, op=ALU.is_ge), R=[tb_], W=[tb_])
                    S_.add("dve", lambda e: e.tensor_single_scalar(out=io["mn"][:], in_=io["rel"][:], scalar=0.0, op=ALU.is_lt), R=[tb_], W=[tb_])
                    for h in range(4):
                        hs = slice(h * 128, (h + 1) * 128)
                        ex = lambda dst, src, col: S_.add("act", lambda e: e.activation(out=dst, in_=src, func=AF.Exp, scale=lg[:, col:col + 1]),
                                                          R=[tb_], W=[tb_])
                        ex(io["e1"][:], io["relp"][:], h)
                        ex(io["e2"][:], io["reln"][:], 4 + h)
                        S_.add("dve", lambda e: e.tensor_tensor(out=io["e1"][:], in0=io["e1"][:], in1=io["mp"][:], op=ALU.mult), R=[tb_], W=[tb_])
                        S_.add("dve", lambda e: e.tensor_tensor(out=io["e2"][:], in0=io["e2"][:], in1=io["mn"][:], op=ALU.mult), R=[tb_], W=[tb_])
                        S_.add("dve", (lambda hs: lambda e: e.tensor_tensor(out=tabs["DT"][:, hs], in0=io["e1"][:], in1=io["e2"][:], op=ALU.add))(hs),
                               R=[tb_], W=[tb_])
                        ex(tabs["QF"][:, hs], io["i1"][:], h)
                        ex(tabs["QB"][:, hs], io["ib"][:], 4 + h)
                        ex(tabs["KF"][:, hs], io["kf"][:], h)
                        ex(tabs["KB"][:, hs], io["kb"][:], 4 + h)
                        ex(tabs["CF"][:, hs], io["c128"][:], h)
                        ex(tabs["CB"][:, hs], io["c128"][:], 4 + h)
                    S_.flush(); chk(1)
                ws.close()
                nt = NormT(ps, psB)
                xin = Pool(ps, sb, "xin", [128, 1024], F32, 3)
                xnT = Pool(ps, sb, "xnT", [128, 8, 128], BF16, 2)
                ropt = Pool(ps, sb, "ropt", [128, 256], F32, 3)
                rA = Pool(ps, sb, "rA", [128, 512], F32, 2)
                rT = Pool(ps, sb, "rT", [128, 512], F32, 2)
                kr = Pool(ps, sb, "kr", [128, 512], BF16, 3)
                vv = Pool(ps, sb, "vv", [128, 512], BF16, 3)
                sbl = Pool(ps, sb, "sbl", [128, 512], BF16, 2)
                kd = Pool(ps, sb, "kd", [128, 512], BF16, 2)
                acS = Pool(ps, sb, "acS", [128, 512], F32, 2)
                uP = Pool(ps, sb, "uP", [128, 512], F32, 2)
                u3p = Pool(ps, sb, "u3p", [128, 512], F32, 6)
                yall = Pool(ps, sb, "yall", [128, 1024], BF16, 2)
                S32 = ps.enter_context(sb("S32", [128, 512], F32))
                S16 = Pool(ps, sb, "S16", [128, 512], BF16, 2)
                qr = Pool(ps, sb, "qr", [128, 512], BF16, 2)
                qT = Pool(ps, sb, "qT", [128, 512], BF16, 2)
                qfT = Pool(ps, sb, "qfT", [128, 512], BF16, 2)
                qbT = Pool(ps, sb, "qbT", [128, 512], BF16, 2)
                kT = Pool(ps, sb, "kT", [128, 512], BF16, 2)
                AT = Pool(ps, sb, "AT", [128, 512], BF16, 2)
                gst = Pool(ps, sb, "gst", [128, 4, 6], F32, 2)
                gmv = Pool(ps, sb, "gmv", [128, 4, 2], F32, 2)
                grs = Pool(ps, sb, "grs", [128, 8], F32, 2)
                on = Pool(ps, sb, "on", [128, 512], F32, 2)
                sg = Pool(ps, sb, "sg", [128, 512], F32, 2)
                abS = Pool(ps, sb, "abS", [128, 512], F32, 2)
                yT = Pool(ps, sb, "yT", [128, 8, 128], BF16, 2)
                x1t = Pool(ps, sb, "x1t", [128, 1024], F32, 2)
                S32b = Buf()

                def run_job(job):
                    jn, S = job["n"], job["S"]
                    NCH = S // 128
                    xd = xin_d[jn]
                    sc = scr[jn]
                    zt, ztb = uP.get()
                    S_.add("pool", lambda e: e.memset(zt[0:1, :], 0.0), W=[ztb])
                    S_.add("sp", lambda e: e.dma_start(out=sc["u"][0:1, :], in_=zt[0:1, :]), R=[ztb], W=[S_.dbuf("uT", jn, -1)], chan="st0")
                    S_.add("sp", lambda e: e.dma_start(out=sc["u"][S + 1:S + 2, :], in_=zt[0:1, :]), R=[ztb], W=[S_.dbuf("uT", jn, -2)], chan="st0")
                    S_.add("dve", lambda e: e.memset(S32[:], 0.0), W=[S32b])
                    stv = list(S16.get())
                    tick = [0]
                    S_.add("pool", (lambda s16: lambda e: e.memset(s16[:], 0.0))(stv[0]), W=[stv[1]])

                    def pre_chunk(k, c):
                        x, xb = xin.get()
                        S_.add("sp", (lambda x, c: lambda e: e.dma_start(out=x[:], in_=xd[c * 128:(c + 1) * 128, :]))(x, c), W=[xb], chan="xl%d" % (c % 3))
                        rtb_t, rtb = ropt.get()
                        S_.add("sp", (lambda t, c: lambda e: e.dma_start(out=t[:], in_=rt_d[c * 128:(c + 1) * 128, :]))(rtb_t, c), W=[rtb], chan="rl%d" % (c % 3))
                        xt_, xtb = xnT.get()
                        nt.run(x[:], xb, xt_[:], xtb)
                        yield
                        pk, pkb = psF.get()
                        mm_group(pk, pkb, [(pk[:], xt_[:, kc, :], w_in[:, kc, 2048:2560], kc == 0, kc == 7) for kc in range(8)], [xtb, wb_in])
                        pv, pvb = psF.get()
                        mm_group(pv, pvb, [(pv[:], xt_[:, kc, :], w_in[:, kc, 2560:3072], kc == 0, kc == 7) for kc in range(8)], [xtb, wb_in])
                        pc, pcb = psF.get()
                        mm_group(pc, pcb, [(pc[:], xt_[:, kc, :], w_in[:, kc, 512:1024], kc == 0, kc == 7) for kc in range(8)], [xtb, wb_in])
                        ph, phb = psF.get()
                        mm_group(ph, phb, [(ph[:], xt_[:, kc, :], w_in[:, kc, 1024:1536], kc == 0, kc == 7) for kc in range(8)], [xtb, wb_in])
                        ac, acb = acS.get()
                        S_.add("act", (lambda ac, pc: lambda e: e.copy(out=ac[:], in_=pc[:]))(ac, pc), R=[pcb], W=[acb])
                        u, ub = uP.get()
                        S_.add("dve", (lambda u, ac, ph: lambda e: e.tensor_tensor(out=u[:], in0=ac[:], in1=ph[:], op=ALU.mult))(u, ac, ph),
                               R=[acb, phb], W=[ub])
                        S_.add("sp", (lambda u, c: lambda e: e.dma_start(out=sc["u"][1 + c * 128:1 + (c + 1) * 128, :], in_=u[:]))(u, c),
                               R=[ub], W=[S_.dbuf("uT", jn, c)], chan="st%d" % (c % 2))
                        A, Ab = rA.get()
                        T, Tb = rT.get()
                        k_, kb_ = kr.get()
                        rope("dve", "pool" if False else "dve", pk[:], pkb, 4, 64, rtb_t[:, 128:192], rtb_t[:, 192:256], rtb, A[:], Ab, T[:], Tb, k_[:], kb_)
                        v_, vb_ = vv.get()
                        S_.add("act", (lambda v_, pv: lambda e: e.copy(out=v_[:], in_=pv[:]))(v_, pv), R=[pvb], W=[vb_])
                        S_.add("sp", (lambda k_, c: lambda e: e.dma_start(out=sc["kr"][c * 128:(c + 1) * 128, :], in_=k_[:]))(k_, c),
                               R=[kb_], W=[S_.dbuf("kr", jn, c)], chan="st%d" % (c % 2))
                        S_.add("sp", (lambda v_, c: lambda e: e.dma_start(out=sc["vr"][c * 128:(c + 1) * 128, :], in_=v_[:]))(v_, c),
                               R=[vb_], W=[S_.dbuf("vr", jn, c)], chan="st%d" % (c % 2))
                        yield
                        while tick[0] != k:
                            yield
                        S_.add("sp", (lambda s16, c: lambda e: e.dma_start(out=sc["sb"][c, :, :], in_=s16[:]))(stv[0], c),
                               R=[stv[1]], W=[S_.dbuf("sb", jn, c)], chan="st%d" % (c % 2))
                        kd_, kdb = kd.get()
                        S_.add("pool", (lambda kd_, k_: lambda e: e.tensor_tensor(out=kd_[:], in0=k_[:], in1=tabs["KB"][:], op=ALU.mult))(kd_, k_),
                               R=[kb_, tb_], W=[kdb])
                        pS, pSb = psF.get()
                        mm_group(pS, pSb, [(pS[:, h * 128:(h + 1) * 128], kd_[:, h * 128:(h + 1) * 128], v_[:, h * 128:(h + 1) * 128], True, True) for h in range(4)],
                                 [kdb, vb_])
                        S_.add("pool", lambda e: e.tensor_tensor(out=S32[:], in0=S32[:], in1=tabs["CB"][:], op=ALU.mult), R=[S32b, tb_], W=[S32b])
                        S_.add("dve", (lambda pS: lambda e: e.tensor_tensor(out=S32[:], in0=S32[:], in1=pS[:], op=ALU.add))(pS), R=[S32b, pSb], W=[S32b])
                        stv[0], stv[1] = S16.get()
                        S_.add("act", (lambda s16: lambda e: e.copy(out=s16[:], in_=S32[:]))(stv[0]), R=[S32b], W=[stv[1]])
                        tick[0] = k + 1
                        yield
                    staggered(pre_chunk, list(range(NCH - 1, -1, -1)), 4)
                    S_.add("dve", lambda e: e.memset(S32[:], 0.0), W=[S32b])
                    stv[0], stv[1] = S16.get()
                    tick[0] = 0
                    S_.add("pool", (lambda s16: lambda e: e.memset(s16[:], 0.0))(stv[0]), W=[stv[1]])

                    def main_chunk(k, c):
                        x, xb = xin.get()
                        S_.add("sp", (lambda x, c: lambda e: e.dma_start(out=x[:], in_=xd[c * 128:(c + 1) * 128, :]))(x, c), W=[xb], chan="xl%d" % (c % 3))
                        rtb_t, rtb = ropt.get()
                        S_.add("sp", (lambda t, c: lambda e: e.dma_start(out=t[:], in_=rt_d[c * 128:(c + 1) * 128, :]))(rtb_t, c), W=[rtb], chan="rl%d" % (c % 3))
                        k_, kb_ = kr.get()
                        S_.add("sp", (lambda k_, c: lambda e: e.dma_start(out=k_[:], in_=sc["kr"][c * 128:(c + 1) * 128, :]))(k_, c),
                               R=[S_.dbuf("kr", jn, c)], W=[kb_], chan="kl%d" % (c % 3))
                        v_, vb_ = vv.get()
                        S_.add("sp", (lambda v_, c: lambda e: e.dma_start(out=v_[:], in_=sc["vr"][c * 128:(c + 1) * 128, :]))(v_, c),
                               R=[S_.dbuf("vr", jn, c)], W=[vb_], chan="vl%d" % (c % 3))
                        sl, slb = sbl.get()
                        S_.add("sp", (lambda sl, c: lambda e: e.dma_start(out=sl[:], in_=sc["sb"][c, :, :]))(sl, c),
                               R=[S_.dbuf("sb", jn, c)], W=[slb], chan="sl%d" % (c % 2))
                        u3 = [u3p.get() for _ in range(3)]
                        urd = [S_.dbuf("uT", jn, cc) for cc in (c - 1, c, c + 1) if 0 <= cc < NCH] + [S_.dbuf("uT", jn, -1), S_.dbuf("uT", jn, -2)]
                        for s_ in range(3):
                            S_.add("sp", (lambda t, c, s_: lambda e: e.dma_start(out=t[:], in_=sc["u"][c * 128 + s_:c * 128 + s_ + 128, :]))(u3[s_][0], c, s_),
                                   R=urd, W=[u3[s_][1]], chan="ul%d" % ((c * 3 + s_) % 3))
                        yield
                        xt_, xtb = xnT.get()
                        nt.run(x[:], xb, xt_[:], xtb)
                        yield
                        pq, pqb = psF.get()
                        mm_group(pq, pqb, [(pq[:], xt_[:, kc, :], w_in[:, kc, 1536:2048], kc == 0, kc == 7) for kc in range(8)], [xtb, wb_in])
                        pg, pgb = psF.get()
                        mm_group(pg, pgb, [(pg[:], xt_[:, kc, :], w_in[:, kc, 3072:3584], kc == 0, kc == 7) for kc in range(8)], [xtb, wb_in])
                        pab, pabb = psF.get()
                        mm_group(pab, pabb, [(pab[:], xt_[:, kc, :], w_in[:, kc, 0:512], kc == 0, kc == 7) for kc in range(8)], [xtb, wb_in])
                        sg_, sgb = sg.get()
                        S_.add("act", (lambda sg_, pg: lambda e: e.activation(out=sg_[:], in_=pg[:], func=AF.Silu))(sg_, pg), R=[pgb], W=[sgb])
                        ab_, abb = abS.get()
                        S_.add("act", (lambda ab_, pab: lambda e: e.copy(out=ab_[:], in_=pab[:]))(ab_, pab), R=[pabb], W=[abb])
                        ya_, yab = yall.get()
                        (u0, u0b), (u1, u1b), (u2, u2b) = u3
                        for (ut, utb, tap) in ((u1, u1b, 1), (u0, u0b, 0), (u2, u2b, 2)):
                            S_.add("pool", (lambda ut, tap: lambda e: e.tensor_tensor(out=ut[:], in0=ut[:], in1=cwt[:, tap, :], op=ALU.mult))(ut, tap),
                                   R=[utb, consts], W=[utb])
                        S_.add("dve", (lambda u1, u0: lambda e: e.tensor_tensor(out=u1[:], in0=u1[:], in1=u0[:], op=ALU.add))(u1, u0), R=[u1b, u0b], W=[u1b])
                        S_.add("dve", (lambda u1, u2: lambda e: e.tensor_tensor(out=u1[:], in0=u1[:], in1=u2[:], op=ALU.add))(u1, u2), R=[u1b, u2b], W=[u1b])
                        S_.add("pool", (lambda ya_, u1, ab_: lambda e: e.tensor_tensor(out=ya_[:, 0:512], in0=u1[:], in1=ab_[:], op=ALU.mult))(ya_, u1, ab_),
                               R=[u1b, abb], W=[yab])
                        yield
                        A, Ab = rA.get()
                        T, Tb = rT.get()
                        q_, qb_ = qr.get()
                        rope("dve", "dve", pq[:], pqb, 4, 64, rtb_t[:, 0:64], rtb_t[:, 64:128], rtb, A[:], Ab, T[:], Tb, q_[:], qb_)
                        yield
                        pt, ptb = psB.get()

                        def trqk(e, pt=pt, q_=q_, k_=k_):
                            for h in range(4):
                                e.transpose(out=pt[:, h * 128:(h + 1) * 128], in_=q_[:, h * 128:(h + 1) * 128], identity=ident[:])
                            for h in range(4):
                                ins = e.transpose(out=pt[:, 512 + h * 128:512 + (h + 1) * 128], in_=k_[:, h * 128:(h + 1) * 128], identity=ident[:])
                            return ins
                        S_.add("pe", trqk, R=[qb_, kb_, consts], W=[ptb])
                        qT_, qTb = qT.get()
                        qf_, qfb = qfT.get()
                        qb2, qbb = qbT.get()
                        kT_, kTb = kT.get()
                        S_.add("act", (lambda o, pt: lambda e: e.copy(out=o[:], in_=pt[:, 0:512]))(qT_, pt), R=[ptb], W=[qTb])
                        S_.add("act", (lambda o, pt: lambda e: e.copy(out=o[:], in_=pt[:, 512:1024]))(kT_, pt), R=[ptb], W=[kTb])
                        S_.add("dve", (lambda o, pt: lambda e: e.tensor_tensor(out=o[:], in0=pt[:, 0:512], in1=tabs["QF"][:], op=ALU.mult))(qf_, pt),
                               R=[ptb, tb_], W=[qfb])
                        S_.add("dve", (lambda o, pt: lambda e: e.tensor_tensor(out=o[:], in0=pt[:, 0:512], in1=tabs["QB"][:], op=ALU.mult))(qb2, pt),
                               R=[ptb, tb_], W=[qbb])
                        yield
                        psc, pscb = psF.get()
                        hs = lambda h: slice(h * 128, (h + 1) * 128)
                        mm_group(psc, pscb, [(psc[:, hs(h)], kT_[:, hs(h)], qT_[:, hs(h)], True, True) for h in range(4)], [kTb, qTb])
                        at, atb = AT.get()
                        S_.add("dve", (lambda at, psc: lambda e: e.tensor_tensor(out=at[:], in0=psc[:], in1=tabs["DT"][:], op=ALU.mult))(at, psc),
                               R=[pscb, tb_], W=[atb])
                        yield
                        while tick[0] != k:
                            yield
                        s16, s16b = stv
                        po, pob = psF.get()
                        prs = []
                        for h in range(4):
                            prs += [(po[:, hs(h)], at[:, hs(h)], v_[:, hs(h)], True, False),
                                    (po[:, hs(h)], qf_[:, hs(h)], s16[:, hs(h)], False, False),
                                    (po[:, hs(h)], qb2[:, hs(h)], sl[:, hs(h)], False, True)]
                        mm_group(po, pob, prs, [atb, vb_, qfb, qbb, s16b, slb])
                        kd_, kdb = kd.get()
                        S_.add("pool", (lambda kd_, k_: lambda e: e.tensor_tensor(out=kd_[:], in0=k_[:], in1=tabs["KF"][:], op=ALU.mult))(kd_, k_),
                               R=[kb_, tb_], W=[kdb])
                        pS, pSb = psF.get()
                        mm_group(pS, pSb, [(pS[:, hs(h)], kd_[:, hs(h)], v_[:, hs(h)], True, True) for h in range(4)], [kdb, vb_])
                        S_.add("pool", lambda e: e.tensor_tensor(out=S32[:], in0=S32[:], in1=tabs["CF"][:], op=ALU.mult), R=[S32b, tb_], W=[S32b])
                        S_.add("dve", (lambda pS: lambda e: e.tensor_tensor(out=S32[:], in0=S32[:], in1=pS[:], op=ALU.add))(pS), R=[S32b, pSb], W=[S32b])
                        stv[0], stv[1] = S16.get()
                        S_.add("act", (lambda s16: lambda e: e.copy(out=s16[:], in_=S32[:]))(stv[0]), R=[S32b], W=[stv[1]])
                        tick[0] = k + 1
                        yield
                        st6, st6b = gst.get()
                        mv, mvb = gmv.get()
                        rs, rsb = grs.get()

                        def bns(e, st6=st6, po=po):
                            for h in range(4):
                                ins = e.bn_stats(out=st6[:, h, :], in_=po[:, h * 128:(h + 1) * 128])
                            return ins
                        S_.add("dve", bns, R=[pob], W=[st6b])

                        def bna(e, st6=st6, mv=mv):
                            for h in range(4):
                                ins = e.bn_aggr(out=mv[:, h, :], in_=st6[:, h, :])
                            return ins
                        S_.add("dve", bna, R=[st6b], W=[mvb])
                        S_.add("act", (lambda rs, mv: lambda e: e.activation(out=rs[:, 0:4], in_=mv[:, :, 1], func=AF.Sqrt, bias=epsN[:, 1:2], scale=1.0))(rs, mv),
                               R=[mvb, consts], W=[rsb])
                        S_.add("dve", (lambda rs: lambda e: e.reciprocal(out=rs[:, 4:8], in_=rs[:, 0:4]))(rs), R=[rsb], W=[rsb])
                        on_, onb = on.get()

                        def gnn(e, on_=on_, po=po, mv=mv, rs=rs):
                            for h in range(4):
                                ins = e.tensor_scalar(out=on_[:, h * 128:(h + 1) * 128], in0=po[:, h * 128:(h + 1) * 128], scalar1=mv[:, h, 0:1],
                                                      scalar2=rs[:, 4 + h:5 + h], op0=ALU.subtract, op1=ALU.mult)
                            return ins
                        S_.add("dve", gnn, R=[pob, mvb, rsb], W=[onb])
                        yield
                        S_.add("pool", (lambda ya_, on_, sg_: lambda e: e.tensor_tensor(out=ya_[:, 512:1024], in0=on_[:], in1=sg_[:], op=ALU.mult))(ya_, on_, sg_),
                               R=[onb, sgb], W=[yab])
                        pt2, pt2b = psB.get()

                        def try_(e, pt2=pt2, ya_=ya_):
                            for f in range(8):
                                ins = e.transpose(out=pt2[:, f * 128:(f + 1) * 128], in_=ya_[:, f * 128:(f + 1) * 128], identity=ident[:])
                            return ins
                        S_.add("pe", try_, R=[yab, consts], W=[pt2b])
                        yT_, yTb = yT.get()
                        S_.add("act", (lambda yT_, pt2: lambda e: e.copy(out=yT_[:], in_=pt2[:].rearrange("p (k t) -> p k t", k=8)))(yT_, pt2),
                               R=[pt2b], W=[yTb])
                        x1_, x1b = x1t.get()
                        for n in range(2):
                            pp, ppb = psF.get()
                            mm_group(pp, ppb, [(pp[:], yT_[:, f, :], w_out[:, f, n * 512:(n + 1) * 512], f == 0, f == 7) for f in range(8)], [yTb, wb_out])
                            S_.add("dve", (lambda x1_, pp, x, n: lambda e: e.tensor_tensor(out=x1_[:, n * 512:(n + 1) * 512], in0=pp[:],
                                                                                          in1=x[:, n * 512:(n + 1) * 512], op=ALU.add))(x1_, pp, x, n),
                                   R=[ppb, xb], W=[x1b])
                        S_.add("sp", (lambda x1_, c: lambda e: e.dma_start(out=sc["x1"][c * 128:(c + 1) * 128, :], in_=x1_[:]))(x1_, c),
                               R=[x1b], W=[S_.dbuf("x1", jn, c)], chan="st%d" % (c % 2))
                        if debug:
                            S_.add("sp", (lambda x1_, c: lambda e: e.dma_start(out=dbg[jn]["x1"][c * 128:(c + 1) * 128, :], in_=x1_[:]))(x1_, c),
                                   R=[x1b], W=[S_.dbuf("dx1", jn, c)], chan="dbg")
                        yield
                    staggered(main_chunk, list(range(NCH)), 6)
                for job in jobs:
                    run_job(job)
                S_.flush(); chk(2)

            def ffn_phase(layer, final):
                with ExitStack() as ps:
                    psF = Pool(ps, pst, "psF", [128, 512], F32, 6)
                    psB = Pool(ps, pst, "psB", [128, 1024], BF16, 2)
                    wg = ps.enter_context(sb("wg", [128, 8, DFF], BF16))
                    wu = ps.enter_context(sb("wu", [128, 8, DFF], BF16))
                    wd = ps.enter_context(sb("wd", [128, NFC, 1024], BF16))
                    with ExitStack() as ws:
                        stg = Pool(ws, sb, "wstg", [128, 1024], F32, 3)
                        sidx = [0]
                        nr = lambda kc: nffn[:, layer * 8 + kc:layer * 8 + kc + 1]
                        wbg = load_weight(wg, w_g_d[layer], 8, DFF, nr, stg, sidx)
                        wbu = load_weight(wu, w_u_d[layer], 8, DFF, nr, stg, sidx)
                        wbd = load_weight(wd, w_d_d[layer], NFC, 1024, None, stg, sidx)
                        S_.flush(); chk(3)
                    nt = NormT(ps, psB)
                    xin = Pool(ps, sb, "xin", [128, 1024], F32, 2)
                    xres = Pool(ps, sb, "xres", [128, 1024], F32, 2)
                    xnT = Pool(ps, sb, "xnT", [128, 8, 512], BF16, 2)
                    hT = Pool(ps, sb, "hT", [128, NFC, 512], BF16, 1)
                    sgp = Pool(ps, sb, "sgp", [128, 512], F32, 2)
                    nf = None
                    if final:
                        nf = ps.enter_context(sb("nf", [128, 1024], F32))
                        S_.add("sp", lambda e: e.dma_start(out=nf[:], in_=nfin_d.partition_broadcast(128)), W=[consts], chan="set")
                        fj = Pool(ps, sb, "fj", [128, 1024], BF16, 1)
                        fst = Pool(ps, sb, "fst", [128, 4], F32, 2)
                    def run_job(job):
                        jn = job["n"]
                        S = job["SQ"] if final else job["S"]
                        sc = scr[jn]
                        src, srcn = (sc["x3"], "x3") if final else (sc["x1"], "x1")
                        dst, dstn = (y_d[jn], "y") if final else (sc["x2"], "x2")
                        for t in range(S // 512):
                            xt_, xtb = xnT.get()
                            for ci in range(4):
                                c = t * 4 + ci
                                x, xb = xin.get()
                                S_.add("sp", (lambda x, c: lambda e: e.dma_start(out=x[:], in_=src[c * 128:(c + 1) * 128, :]))(x, c),
                                       R=[S_.dbuf(srcn, jn, c)], W=[xb], chan="xl%d" % (c % 2))
                                nt.run(x[:], xb, xt_[:, :, ci * 128:(ci + 1) * 128], xtb)
                            h_, hb = hT.get()
                            for f in range(NFC):
                                pg, pgb = psF.get()
                                mm_group(pg, pgb, [(pg[:], wg[:, kc, f * 128:(f + 1) * 128], xt_[:, kc, :], kc == 0, kc == 7) for kc in range(8)], [xtb, wbg])
                                pu, pub = psF.get()
                                mm_group(pu, pub, [(pu[:], wu[:, kc, f * 128:(f + 1) * 128], xt_[:, kc, :], kc == 0, kc == 7) for kc in range(8)], [xtb, wbu])
                                s_, sb_ = sgp.get()
                                S_.add("act", (lambda s_, pg: lambda e: e.activation(out=s_[:], in_=pg[:], func=AF.Silu))(s_, pg), R=[pgb], W=[sb_])
                                S_.add("dve", (lambda h_, s_, pu, f: lambda e: e.tensor_tensor(out=h_[:, f, :], in0=s_[:], in1=pu[:], op=ALU.mult))(h_, s_, pu, f),
                                       R=[sb_, pub], W=[hb])
                            for ci in range(4):
                                c = t * 4 + ci
                                xr, xrb = xres.get()
                                S_.add("sp", (lambda xr, c: lambda e: e.dma_start(out=xr[:], in_=src[c * 128:(c + 1) * 128, :]))(xr, c),
                                       R=[S_.dbuf(srcn, jn, c)], W=[xrb], chan="rl%d" % (c % 2))
                                for n in range(2):
                                    pp, ppb = psF.get()
                                    mm_group(pp, ppb, [(pp[:], h_[:, f, ci * 128:(ci + 1) * 128], wd[:, f, n * 512:(n + 1) * 512], f == 0, f == NFC - 1)
                                                       for f in range(NFC)], [hb, wbd])
                                    S_.add("pool" if False else "dve", (lambda xr, pp, n: lambda e: e.tensor_tensor(out=xr[:, n * 512:(n + 1) * 512], in0=pp[:],
                                                                                                                   in1=xr[:, n * 512:(n + 1) * 512], op=ALU.add))(xr, pp, n),
                                           R=[ppb, xrb], W=[xrb])
                                if final:
                                    jk, jkb = fj.get()
                                    st, stb = fst.get()
                                    S_.add("act", (lambda jk, xr, st: lambda e: e.activation(out=jk[:], in_=xr[:], func=AF.Square, accum_out=st[:, 0:1]))(jk, xr, st),
                                           R=[xrb], W=[jkb, stb])
                                    S_.add("act", (lambda st: lambda e: e.activation(out=st[:, 1:2], in_=st[:, 0:1], func=AF.Sqrt, bias=epsN[:, 0:1], scale=1.0 / D))(st),
                                           R=[stb, consts], W=[stb])
                                    S_.add("dve", (lambda st: lambda e: e.reciprocal(out=st[:, 2:3], in_=st[:, 1:2]))(st), R=[stb], W=[stb])
                                    S_.add("dve", (lambda xr, st: lambda e: e.scalar_tensor_tensor(out=xr[:], in0=xr[:], scalar=st[:, 2:3], in1=nf[:],
                                                                                                    op0=ALU.mult, op1=ALU.mult))(xr, st),
                                           R=[xrb, stb, consts], W=[xrb])
                                S_.add("sp", (lambda xr, c: lambda e: e.dma_start(out=dst[c * 128:(c + 1) * 128, :], in_=xr[:]))(xr, c),
                                       R=[xrb], W=[S_.dbuf(dstn, jn, c)], chan="st%d" % (c % 2))
                                if debug and not final:
                                    S_.add("sp", (lambda xr, c: lambda e: e.dma_start(out=dbg[jn]["x2"][c * 128:(c + 1) * 128, :], in_=xr[:]))(xr, c),
                                           R=[xrb], W=[S_.dbuf("dx2", jn, c)], chan="dbg")
                    for job in jobs:
                        run_job(job)
                    S_.flush(); chk(4)

            ffn_phase(0, False)

            with ExitStack() as ps:
                psF = Pool(ps, pst, "psF", [128, 512], F32, 6)
                psB = Pool(ps, pst, "psB", [128, 1024], BF16, 2)
                wqkv = ps.enter_context(sb("wqkv", [128, 8, 3072], BF16))
                with ExitStack() as ws:
                    stg = Pool(ws, sb, "wstg", [128, 1024], F32, 3)
                    sidx = [0]
                    wbq = load_weight(wqkv, w_qkv_d, 8, 3072, lambda kc: nmix[:, 8 + kc:9 + kc], stg, sidx)
                    S_.flush(); chk(5)
                nt = NormT(ps, psB)
                xin = Pool(ps, sb, "xin", [128, 1024], F32, 3)
                xnT = Pool(ps, sb, "xnT", [128, 8, 128], BF16, 2)
                ropt = Pool(ps, sb, "ropt", [128, 64], F32, 3)
                rA = Pool(ps, sb, "rA", [128, 1024], F32, 2)
                rT = Pool(ps, sb, "rT", [128, 1024], F32, 2)
                rr = Pool(ps, sb, "rr", [128, 1024], BF16, 2)
                vS = Pool(ps, sb, "vS", [128, 1024], BF16, 2)
                stT = Pool(ps, sb, "stT", [128, 8, 512], BF16, 2)

                def qk_path(x_src_fn, R_x, rope_src, S, c0, dstT, dstname, jn, col0, with_v, vdst):
                    st_, stb = stT.get()
                    for ci in range(4):
                        c = c0 + ci
                        x, xb = xin.get()
                        x_src_fn(x, xb, c)
                        rt_, rtb = ropt.get()
                        S_.add("sp", (lambda t, c: lambda e: e.dma_start(out=t[:], in_=rope_src[c * 128:(c + 1) * 128, :]))(rt_, c), W=[rtb], chan="rl%d" % (c % 3))
                        yield
                        xt_, xtb = xnT.get()
                        nt.run(x[:], xb, xt_[:], xtb)
                        yield
                        pq = [psF.get(), psF.get()]
                        for n in range(2):
                            mm_group(pq[n][0], pq[n][1], [(pq[n][0][:], xt_[:, kc, :], wqkv[:, kc, col0 + n * 512:col0 + (n + 1) * 512], kc == 0, kc == 7)
                                                          for kc in range(8)], [xtb, wbq])
                        yield
                        A, Ab = rA.get()
                        T, Tb = rT.get()
                        r_, rb_ = rr.get()
                        for n in range(2):
                            sl = slice(n * 512, (n + 1) * 512)
                            rope("dve", "pool" if False else "dve", pq[n][0][:], pq[n][1], 8, 32, rt_[:, 0:32], rt_[:, 32:64], rtb, A[:, sl], Ab, T[:, sl], Tb, r_[:, sl], rb_)
                        pt, ptb = psB.get()

                        def tr(e, pt=pt, r_=r_):
                            for hp in range(8):
                                ins = e.transpose(out=pt[:, hp * 128:(hp + 1) * 128], in_=r_[:, hp * 128:(hp + 1) * 128], identity=ident[:])
                            return ins
                        S_.add("pe", tr, R=[rb_, consts], W=[ptb])
                        S_.add("act", (lambda st_, pt, ci: lambda e: e.copy(out=st_[:, :, ci * 128:(ci + 1) * 128], in_=pt[:].rearrange("p (k t) -> p k t", k=8)))(st_, pt, ci),
                               R=[ptb], W=[stb])
                        yield
                        if with_v:
                            pv = [psF.get(), psF.get()]
                            v_, vb_ = vS.get()
                            for n in range(2):
                                mm_group(pv[n][0], pv[n][1], [(pv[n][0][:], xt_[:, kc, :], wqkv[:, kc, 2048 + n * 512:2048 + (n + 1) * 512], kc == 0, kc == 7)
                                                              for kc in range(8)], [xtb, wbq])
                                S_.add("act", (lambda v_, p, n: lambda e: e.copy(out=v_[:, n * 512:(n + 1) * 512], in_=p[:]))(v_, pv[n][0], n), R=[pv[n][1]], W=[vb_])
                            S_.add("sp", (lambda v_, c: lambda e: e.dma_start(out=vdst[:, c * 128:(c + 1) * 128, :].rearrange("h t e -> t h e"),
                                                                              in_=v_[:].rearrange("p (h e) -> p h e", h=8)))(v_, c),
                                   R=[vb_], W=[S_.dbuf("V", jn, c)], chan="st%d" % (c % 2))
                    S_.add("sp", (lambda st_, c0: lambda e: e.dma_start(out=dstT[:, :, c0 * 128:c0 * 128 + 512].rearrange("h p t -> p h t"), in_=st_[:]))(st_, c0),
                           R=[stb], W=[S_.dbuf(dstname, jn, c0 // 4)], chan="st%d" % ((c0 // 4) % 2))

                def run_job(job):
                    jn, S, SQ = job["n"], job["S"], job["SQ"]
                    sc = scr[jn]

                    def src_plain(x, xb, c, sc=sc, jn=jn):
                        S_.add("sp", (lambda x, c: lambda e: e.dma_start(out=x[:], in_=sc["x2"][c * 128:(c + 1) * 128, :]))(x, c),
                               R=[S_.dbuf("x2", jn, c)], W=[xb], chan="xl%d" % (c % 3))

                    def src_gather(x, xb, c, sc=sc, jn=jn, S=S):
                        S_.add("pool", (lambda x, c: lambda e: e.indirect_dma_start(out=x[:, :], out_offset=None, in_=sc["x2"][:, :],
                                                                                    in_offset=bass.IndirectOffsetOnAxis(ap=idxq[:, c:c + 1], axis=0)))(x, c),
                               R=[S_.dbuf("x2", jn, cc) for cc in range(S // 128)] + [consts], W=[xb], chan="gl%d" % (c % 3))
                    staggered(lambda k, t: qk_path(src_plain, None, dtk_d, S, t * 4, sc["KT"], "KT", jn, 1024, True, sc["V"]),
                              list(range(S // 512)), 8)
                    staggered(lambda k, t: qk_path(src_gather if job["own"] else src_plain, None, dtq_d[jn], SQ, t * 4, sc["QT"], "QT", jn, 0, False, None),
                              list(range(SQ // 512)), 6)
                for job in jobs:
                    run_job(job)
                S_.flush(); chk(6)

            with ExitStack() as ps:
                psS = Pool(ps, pst, "psS", [128, 1024], F32, 2)
                psO = Pool(ps, pst, "psO", [128, 2, 256], F32, 4)
                SMX = max(SA, SB)
                KTt = Pool(ps, sb, "KTt", [128, SMX], BF16, 2)
                Vt = Pool(ps, sb, "Vt", [128, SMX // 128, 130], BF16, 2)
                QTt = Pool(ps, sb, "QTt", [128, max(SA, SQB)], BF16, 2)
                PT = Pool(ps, sb, "PT", [128, 1024], BF16, 3)
                rc = Pool(ps, sb, "rc", [128, 8], F32, 4)
                o1 = Pool(ps, sb, "o1", [128, 128], F32, 3)
                oj = Pool(ps, sb, "oj", [128, 128], F32, 2)
                ob = Pool(ps, sb, "ob", [128, 4, 128], BF16, 2)
                for i in range(2):
                    S_.add("pool", (lambda t: lambda e: e.memset(t[:, :, 128:130], 1.0))(Vt.t[i]), W=[Vt.b[i]])
                def run_job(job):
                    jn, S, SQ = job["n"], job["S"], job["SQ"]
                    sc = scr[jn]
                    NKT = S // 128
                    for h in range(8):
                        kt_, ktb = KTt.get()
                        S_.add("sp", (lambda kt_, h: lambda e: e.dma_start(out=kt_[:, 0:S], in_=sc["KT"][h, :, :]))(kt_, h),
                               R=[S_.dbuf("KT", jn, t) for t in range(S // 512)], W=[ktb], chan="kl%d" % (h % 2))
                        v_, vb_ = Vt.get()
                        VP = min(32, NKT)
                        for part in range(0, NKT, VP):
                            S_.add("sp", (lambda v_, h, part: lambda e: e.dma_start(out=v_[:, part:part + VP, 0:128],
                                                                                   in_=sc["V"][h, part * 128:(part + VP) * 128, :].rearrange("(k p) e -> p k e", p=128)))(v_, h, part),
                                   R=[S_.dbuf("V", jn, c) for c in range(part, min(part + VP, NKT))], W=[vb_], chan="vl%d" % (h % 2))
                        q_, qb_ = QTt.get()
                        S_.add("sp", (lambda q_, h: lambda e: e.dma_start(out=q_[:, 0:SQ], in_=sc["QT"][h, :, :]))(q_, h),
                               R=[S_.dbuf("QT", jn, t) for t in range(SQ // 512)], W=[qb_], chan="ql%d" % (h % 2))
                        for qt in range(SQ // 512):
                            acc = [psO.get(), psO.get(), psO.get(), psO.get()]
                            qs = slice(qt * 512, (qt + 1) * 512)
                            def qk(kt, qs=qs):
                                ks = slice(kt * 128, (kt + 1) * 128)
                                pS_, pSb = psS.get()
                                mm_group(pS_, pSb, [(pS_[:, 0:512], kt_[0:64, ks], q_[0:64, qs], True, True),
                                                    (pS_[:, 512:1024], kt_[64:128, ks], q_[64:128, qs], True, True)], [ktb, qb_])
                                return pS_, pSb
                            pend = qk(0)
                            for kt in range(NKT):
                                pS_, pSb = pend
                                if kt + 1 < NKT:
                                    pend = qk(kt + 1)
                                p_, pb_ = PT.get()
                                S_.add("act", (lambda p_, pS_: lambda e: e.activation(out=p_[:], in_=pS_[:], func=AF.Exp))(p_, pS_), R=[pSb], W=[pb_])
                                for sub in range(2):
                                    for pair in range(2):
                                        a_, ab_ = acc[sub * 2 + pair]
                                        mm_group(a_, ab_, [(a_[:, j, 0:129], p_[:, sub * 512 + (pair * 2 + j) * 128: sub * 512 + (pair * 2 + j + 1) * 128],
                                                            v_[:, kt, 0:129], kt == 0 and j == 0, kt == NKT - 1) for j in range(2)], [pb_, vb_])
                            ob_, obb = ob.get()
                            for qb4 in range(4):
                                pair, j = qb4 // 2, qb4 % 2
                                a0, a0b = acc[0 + pair]
                                a1, a1b = acc[2 + pair]
                                r_, rb2 = rc.get()
                                S_.add("dve", (lambda r_, a0, j: lambda e: e.reciprocal(out=r_[:, 0:1], in_=a0[:, j, 128:129]))(r_, a0, j), R=[a0b], W=[rb2])
                                S_.add("dve", (lambda r_, a1, j: lambda e: e.reciprocal(out=r_[:, 1:2], in_=a1[:, j, 128:129]))(r_, a1, j), R=[a1b, rb2], W=[rb2])
                                S_.add("dve", (lambda r_: lambda e: e.tensor_tensor(out=r_[:, 2:3], in0=r_[:, 1:2], in1=lamt[:, 0:1], op=ALU.mult))(r_),
                                       R=[rb2, consts], W=[rb2])
                                o_, o_b = o1.get()
                                S_.add("dve", (lambda o_, a0, j, r_: lambda e: e.tensor_scalar(out=o_[:], in0=a0[:, j, 0:128], scalar1=r_[:, 0:1], scalar2=None,
                                                                                              op0=ALU.mult))(o_, a0, j, r_), R=[a0b, rb2], W=[o_b])
                                S_.add("dve", (lambda o_, a1, j, r_: lambda e: e.scalar_tensor_tensor(out=o_[:], in0=a1[:, j, 0:128], scalar=r_[:, 2:3], in1=o_[:],
                                                                                                     op0=ALU.mult, op1=ALU.add))(o_, a1, j, r_),
                                       R=[a1b, rb2, o_b], W=[o_b])
                                jk, jkb = oj.get()
                                S_.add("dve", (lambda jk, o_: lambda e: e.tensor_tensor(out=jk[:], in0=o_[:], in1=o_[:], op=ALU.mult))(jk, o_),
                                       R=[o_b], W=[jkb])
                                S_.add("dve", (lambda jk, r_: lambda e: e.reduce_sum(out=r_[:, 3:4], in_=jk[:], axis=AX.X))(jk, r_), R=[jkb, rb2], W=[rb2])
                                S_.add("act", (lambda r_: lambda e: e.activation(out=r_[:, 4:5], in_=r_[:, 3:4], func=AF.Ln, bias=epsN[:, 1:2], scale=1.0 / 128))(r_),
                                       R=[rb2, consts], W=[rb2])
                                S_.add("act", (lambda r_: lambda e: e.activation(out=r_[:, 5:6], in_=r_[:, 4:5], func=AF.Exp, scale=-0.5))(r_),
                                       R=[rb2], W=[rb2])
                                S_.add("pool", (lambda ob_, o_, r_, qb4: lambda e: e.tensor_scalar(out=ob_[:, qb4, :], in0=o_[:], scalar1=r_[:, 5:6], scalar2=None,
                                                                                                  op0=ALU.mult))(ob_, o_, r_, qb4), R=[o_b, rb2], W=[obb])
                            S_.add("sp", (lambda ob_, qt, h: lambda e: e.dma_start(
                                out=sc["O"][qt * 512:(qt + 1) * 512, h * 128:(h + 1) * 128].rearrange("(k p) e -> p k e", p=128), in_=ob_[:]))(ob_, qt, h),
                                   R=[obb], W=[S_.dbuf("O", jn, qt, h)], chan="st%d" % (qt % 2))
                for job in jobs:
                    run_job(job)
                S_.flush(); chk(7)

            with ExitStack() as ps:
                psF = Pool(ps, pst, "psF", [128, 512], F32, 6)
                psB = Pool(ps, pst, "psB", [128, 1024], BF16, 2)
                wdo = ps.enter_context(sb("wdo", [128, 8, 1024], BF16))
                with ExitStack() as ws:
                    stg = Pool(ws, sb, "wstg", [128, 1024], F32, 3)
                    sidx = [0]
                    wbo = load_weight(wdo, w_do_d, 8, 1024, lambda kc: subl[:, 0:1], stg, sidx)
                    S_.flush(); chk(8)
                oin = Pool(ps, sb, "oin", [128, 1024], BF16, 3)
                oT = Pool(ps, sb, "oT", [128, 8, 128], BF16, 2)
                xres = Pool(ps, sb, "xres", [128, 1024], F32, 3)
                def run_job(job):
                    jn, S, SQ = job["n"], job["S"], job["SQ"]
                    sc = scr[jn]
                    for c in range(SQ // 128):
                        o_, o_b = oin.get()
                        S_.add("sp", (lambda o_, c: lambda e: e.dma_start(out=o_[:], in_=sc["O"][c * 128:(c + 1) * 128, :]))(o_, c),
                               R=[S_.dbuf("O", jn, c // 4, h) for h in range(8)], W=[o_b], chan="xl%d" % (c % 3))
                        xr, xrb = xres.get()
                        if job["own"]:
                            S_.add("pool", (lambda xr, c: lambda e: e.indirect_dma_start(out=xr[:, :], out_offset=None, in_=sc["x2"][:, :],
                                                                                        in_offset=bass.IndirectOffsetOnAxis(ap=idxq[:, c:c + 1], axis=0)))(xr, c),
                                   R=[consts], W=[xrb], chan="gl%d" % (c % 3))
                        else:
                            S_.add("sp", (lambda xr, c: lambda e: e.dma_start(out=xr[:], in_=sc["x2"][c * 128:(c + 1) * 128, :]))(xr, c), W=[xrb], chan="rl%d" % (c % 3))
                        pt, ptb = psB.get()

                        def tr(e, pt=pt, o_=o_):
                            for f in range(8):
                                ins = e.transpose(out=pt[:, f * 128:(f + 1) * 128], in_=o_[:, f * 128:(f + 1) * 128], identity=ident[:])
                            return ins
                        S_.add("pe", tr, R=[o_b, consts], W=[ptb])
                        oT_, oTb = oT.get()
                        S_.add("act", (lambda oT_, pt: lambda e: e.copy(out=oT_[:], in_=pt[:].rearrange("p (k t) -> p k t", k=8)))(oT_, pt), R=[ptb], W=[oTb])
                        for n in range(2):
                            pp, ppb = psF.get()
                            mm_group(pp, ppb, [(pp[:], oT_[:, f, :], wdo[:, f, n * 512:(n + 1) * 512], f == 0, f == 7) for f in range(8)], [oTb, wbo])
                            S_.add("dve", (lambda xr, pp, n: lambda e: e.tensor_tensor(out=xr[:, n * 512:(n + 1) * 512], in0=pp[:], in1=xr[:, n * 512:(n + 1) * 512],
                                                                                      op=ALU.add))(xr, pp, n), R=[ppb, xrb], W=[xrb])
                        S_.add("sp", (lambda xr, c: lambda e: e.dma_start(out=sc["x3"][c * 128:(c + 1) * 128, :], in_=xr[:]))(xr, c),
                               R=[xrb], W=[S_.dbuf("x3", jn, c)], chan="st%d" % (c % 2))
                        if debug:
                            S_.add("sp", (lambda xr, c: lambda e: e.dma_start(out=dbg[jn]["x3"][c * 128:(c + 1) * 128, :], in_=xr[:]))(xr, c),
                                   R=[xrb], W=[S_.dbuf("dx3", jn, c)], chan="dbg")
                for job in jobs:
                    run_job(job)
                S_.flush(); chk(9)

            ffn_phase(1, True)

        except _Stop:
            pass
        sch.stopped = False
        sch.maxops = 10 ** 9
        S_.add("sp", lambda e: e.dma_start(out=scr["a"]["u"][0:1, 0:1], in_=scr["a"]["u"][0:1, 1:2]), chan="fin")
        S_.flush(); chk(10)
        fin = S_.ops[-1]

        with nc.Block() as block:
            @block.sync
            def _(e):
                e.wait_ge(S_.csem["fin"], fin.val)
    return nc


def _rope_tabs(S, dim, scale):
    inv = (10000.0 ** (-np.arange(0, dim, 2, dtype=np.float32) / np.float32(dim))).astype(np.float32)
    ang = np.arange(S, dtype=np.float32)[:, None] * inv[None, :]
    return (np.cos(ang) * scale).astype(np.float32), (np.sin(ang) * scale).astype(np.float32)


def make_inputs(core, SA, SB, NQB, inp, xa, xb, qoff):
    f = lambda a: np.ascontiguousarray(np.asarray(a, dtype=np.float32))
    SM = max(SA, SB)
    c1, s1 = _rope_tabs(SM, 128, 1.0)
    c2, s2 = _rope_tabs(SM, 128, 128 ** -0.5)
    rt = np.concatenate([c1, s1, c2, s2], axis=1)
    ck, sk = _rope_tabs(SM, 64, 1.0)
    cq, sq = _rope_tabs(SM, 64, 0.125)
    dtk = np.concatenate([ck, sk], 1)
    dtq = np.concatenate([cq, sq], 1)
    SQB = NQB * 128
    idx = (qoff + np.arange(NQB)[None, :] * 128 + np.arange(128)[:, None]).astype(np.int32)
    pk = lambda v: f(np.asarray(v).reshape(-1, 128).T)
    m = {
        "xa": f(xa), "xb": f(xb), "idxq": np.ascontiguousarray(idx),
        "rt": f(rt), "dtk": f(dtk), "dtqa": f(dtq[:SA]), "dtqb": f(dtq[qoff:qoff + SQB]),
        "w_in": f(inp["hyb_w_in"][0]), "w_out": f(inp["hyb_w_out"][0]), "w_qkv": f(inp["diff_w_qkv"][0]), "w_do": f(inp["diff_w_out"][0]),
        "nmix": pk(np.asarray(inp["norm_mix"]).reshape(-1)), "nffn": pk(np.asarray(inp["norm_ffn"]).reshape(-1)),
        "nfin": f(np.asarray(inp["norm_final"]).reshape(1, D)),
        "convw3": f(np.asarray(inp["hyb_conv_w"][0]).reshape(3, 512)),
        "convw": f(np.asarray(inp["hyb_conv_w"][0]).reshape(3, 4, 128).transpose(2, 1, 0).reshape(128, 12)),
        "decf": f(np.asarray(inp["hyb_decay_fwd"]).reshape(1, 4)), "decb": f(np.asarray(inp["hyb_decay_bwd"]).reshape(1, 4)),
        "gnw": pk(np.asarray(inp["hyb_gn"]).reshape(-1)),
        "lamv": f(np.concatenate([np.asarray(inp[k]).reshape(-1) for k in ("diff_lq1", "diff_lk1", "diff_lq2", "diff_lk2")]).reshape(1, 256)),
        "subln": f(np.asarray(inp["diff_subln"]).reshape(128, 1)),
    }
    for l in range(2):
        m["w_g%d" % l] = f(inp["ffn_w_gate"][l])
        m["w_u%d" % l] = f(inp["ffn_w_up"][l])
        m["w_d%d" % l] = f(inp["ffn_w_down"][l])
    return m


def kernel(**inp):
    xp = np.asarray(inp["x_prompt"])
    xs = np.asarray(inp["x_sample"])
    SA, SB = xs.shape[1], xp.shape[1]
    NQB = SB // 4 // 128
    nc = build(SA, SB, NQB)
    in_maps = [make_inputs(c, SA, SB, NQB, inp, xs[c], xp[c // 4], (c % 4) * (SB // 4)) for c in range(8)]
    res = run_bass_kernel_spmd(nc, in_maps, core_ids=list(range(8)))
    ys = np.stack([res.results[c]["ya"] for c in range(8)], 0).astype(np.float32)
    yp = np.stack([np.concatenate([res.results[g * 4 + r]["yb"] for r in range(4)], 0) for g in range(2)], 0).astype(np.float32)
    return (yp, ys)
```

```python
import math
from contextlib import ExitStack
import numpy as np
import concourse.bass as bass
import concourse.mybir as mybir
from concourse.bass_utils import run_bass_kernel_spmd

F32 = mybir.dt.float32
BF16 = mybir.dt.bfloat16
I32 = mybir.dt.int32
AF = mybir.ActivationFunctionType
ALU = mybir.AluOpType
AX = mybir.AxisListType

D = 1024
DFF = 2816
NFC = DFF // 128
LAMBDA_INIT = 0.8 - 0.6 * math.exp(-0.3 * 1)
ENG = ("pe", "act", "dve", "pool", "sp")


class Buf:
    __slots__ = ("w", "r", "excl")

    def __init__(self, excl=False):
        self.w = None
        self.r = {}
        self.excl = excl


class Op:
    __slots__ = ("eng", "fn", "deps", "chan", "signal", "val")

    def __init__(self, eng, fn, deps, chan):
        self.eng = eng
        self.fn = fn
        self.deps = deps
        self.chan = chan
        self.signal = chan is not None
        self.val = None


class Sched:
    def __init__(self, nc, es, nchan=40):
        self.nc = nc
        self.ops = []
        self.emitted = 0
        self.last = {}
        self.lastchan = {}
        self.bar = ()
        self.esem = {e: es.enter_context(nc.semaphore("s_" + e)) for e in ENG}
        self.freechan = [es.enter_context(nc.semaphore("c%d" % i)) for i in range(nchan)]
        self.csem = {}
        self.cnt = {e: 0 for e in ENG}
        self.ccnt = {}
        self.waited = {e: {} for e in ENG}
        self.dram = {}

    def dbuf(self, *key):
        b = self.dram.get(key)
        if b is None:
            b = self.dram[key] = Buf()
        return b

    stopped = False
    maxops = 10 ** 9
    names = {}

    def add(self, eng, fn, R=(), W=(), chan=None):
        if self.stopped:
            return -1
        if len(self.ops) >= self.maxops:
            self.stopped = True
            return -1
        deps = set(self.bar)
        key0 = chan if chan is not None else eng
        for b in R:
            if b.w is not None:
                deps.add(b.w)
            if b.excl:
                deps.update(v for k, v in b.r.items() if k != key0)
        for b in W:
            if b.w is not None:
                deps.add(b.w)
            deps.update(b.r.values())
        idx = len(self.ops)
        if chan is not None:
            if chan in self.lastchan:
                deps.add(self.lastchan[chan])
            self.lastchan[chan] = idx
        self.ops.append(Op(eng, fn, deps, chan))
        key = chan if chan is not None else eng
        for b in R:
            b.r[key] = idx
        for b in W:
            b.w = idx
            b.r = {}
        self.last[eng] = idx
        return idx

    def _event(self, op):
        if op.chan is not None:
            return self.csem[op.chan], op.val
        return self.esem[op.eng], op.val

    def flush(self):
        nc = self.nc
        lo = self.emitted
        if lo == len(self.ops):
            return
        ops = self.ops
        for i in range(lo, len(ops)):
            for d in ops[i].deps:
                if d >= lo:
                    od = ops[d]
                    if od.chan is None and od.eng == "pe" and ops[i].eng == "pe" and ops[i].chan is None:
                        continue
                    od.signal = True
        for e, i in self.last.items():
            ops[i].signal = True
        for i in range(lo, len(ops)):
            op = ops[i]
            if op.chan is not None:
                if op.chan not in self.csem:
                    self.csem[op.chan] = self.freechan.pop()
                    self.ccnt[op.chan] = 0
                self.ccnt[op.chan] += 16
                op.val = self.ccnt[op.chan]
            elif op.signal:
                self.cnt[op.eng] += 1
                op.val = self.cnt[op.eng]
        per = {e: [] for e in ENG}
        for i in range(lo, len(ops)):
            per[ops[i].eng].append(i)

        def emit(engname, e):
            waited = self.waited[engname]
            for i in per[engname]:
                op = ops[i]
                for d in sorted(op.deps):
                    od = ops[d]
                    if d < lo and d not in self.bar:
                        continue
                    if od.chan is None and od.eng == "pe" and engname == "pe" and op.chan is None:
                        continue
                    sem, val = self._event(od)
                    k = id(sem)
                    if waited.get(k, 0) < val:
                        e.wait_ge(sem, val)
                        waited[k] = val
                ins = op.fn(e)
                if op.signal:
                    sem, val = self._event(op)
                    ins.then_inc(sem, 16 if op.chan is not None else 1)

        with nc.Block() as block:
            @block.tensor
            def _(e):
                emit("pe", e)

            @block.scalar
            def _(e):
                emit("act", e)

            @block.vector
            def _(e):
                emit("dve", e)

            @block.gpsimd
            def _(e):
                emit("pool", e)

            @block.sync
            def _(e):
                emit("sp", e)
        self.emitted = len(ops)
        self.bar = tuple(set(list(self.last.values()) + list(self.lastchan.values())))
        for i in self.bar:
            assert ops[i].signal


def staggered(fn, chunks, delay, n=2):
    def thread(idx):
        for k in idx:
            yield from fn(k, chunks[k])
    g = [thread(range(t, len(chunks), n)) for t in range(n)]
    alive = [True] * n
    step = 0
    while any(alive):
        for i in range(n):
            if alive[i] and step >= i * delay:
                try:
                    next(g[i])
                except StopIteration:
                    alive[i] = False
        step += 1


class Pool:
    def __init__(self, es, alloc, name, shape, dt, n):
        self.t = [es.enter_context(alloc("%s%d" % (name, i), shape, dt)) for i in range(n)]
        self.b = [Buf(excl=(alloc.__name__ == 'pst')) for _ in range(n)]
        self.i = 0

    def get(self):
        k = self.i % len(self.t)
        self.i += 1
        return self.t[k], self.b[k]


class _Stop(Exception):
    pass


def build(SA, SB, NQB, debug=False, nph=99):
    nc = bass.Bass("TRN2", target_bir_lowering=False)
    dt_in = lambda n, s, d=F32: nc.dram_tensor(n, s, d, kind="ExternalInput").ap()
    dt_scr = lambda n, s, d: nc.dram_tensor(n, s, d, kind="Internal").ap()
    SQB = NQB * 128
    jobs = [dict(n="a", S=SA, SQ=SA, own=False), dict(n="b", S=SB, SQ=SQB, own=True)]
    xin_d = {"a": dt_in("xa", [SA, D]), "b": dt_in("xb", [SB, D])}
    idxq_d = dt_in("idxq", [128, NQB], I32)
    SM = max(SA, SB)
    rt_d = dt_in("rt", [SM, 256])
    dtk_d = dt_in("dtk", [SM, 64])
    dtq_d = {"a": dt_in("dtqa", [SA, 64]), "b": dt_in("dtqb", [SQB, 64])}
    w_in_d = dt_in("w_in", [D, 3584])
    w_out_d = dt_in("w_out", [D, D])
    w_qkv_d = dt_in("w_qkv", [D, 3072])
    w_do_d = dt_in("w_do", [D, D])
    w_g_d = [dt_in("w_g%d" % l, [D, DFF]) for l in range(2)]
    w_u_d = [dt_in("w_u%d" % l, [D, DFF]) for l in range(2)]
    w_d_d = [dt_in("w_d%d" % l, [DFF, D]) for l in range(2)]
    nmix_d = dt_in("nmix", [128, 16])
    nffn_d = dt_in("nffn", [128, 16])
    nfin_d = dt_in("nfin", [1, D])
    convw_d = dt_in("convw", [128, 12])
    convw3_d = dt_in("convw3", [3, 512])
    decf_d = dt_in("decf", [1, 4])
    decb_d = dt_in("decb", [1, 4])
    gn_d = dt_in("gnw", [128, 4])
    lam_d = dt_in("lamv", [1, 256])
    subln_d = dt_in("subln", [128, 1])
    y_d = {j["n"]: nc.dram_tensor("y" + j["n"], [j["SQ"], D], F32, kind="ExternalOutput").ap() for j in jobs}
    scr = {}
    for j in jobs:
        n, S, SQ = j["n"], j["S"], j["SQ"]
        scr[n] = dict(
            u=dt_scr("u" + n, [S + 2, 512], F32),
            kr=dt_scr("kr" + n, [S, 512], BF16),
            vr=dt_scr("vr" + n, [S, 512], BF16),
            sb=dt_scr("sb" + n, [S // 128, 128, 512], BF16),
            x1=dt_scr("x1" + n, [S, D], F32),
            x2=dt_scr("x2" + n, [S, D], F32),
            QT=dt_scr("QT" + n, [8, 128, SQ], BF16),
            KT=dt_scr("KT" + n, [8, 128, S], BF16),
            V=dt_scr("V" + n, [8, S, 128], BF16),
            O=dt_scr("O" + n, [SQ, D], BF16),
            x3=dt_scr("x3" + n, [SQ, D], F32),
        )
    dbg = {}
    if debug:
        for j in jobs:
            n, S, SQ = j["n"], j["S"], j["SQ"]
            dbg[n] = dict(
                x1=nc.dram_tensor("dbg_x1" + n, [S, D], F32, kind="ExternalOutput").ap(),
                x2=nc.dram_tensor("dbg_x2" + n, [S, D], F32, kind="ExternalOutput").ap(),
                x3=nc.dram_tensor("dbg_x3" + n, [SQ, D], F32, kind="ExternalOutput").ap(),
            )

    with ExitStack() as es:
        sch = Sched(nc, es)
        import os
        sch.maxops = int(os.environ.get('KSTOP', 10 ** 9))
        S_ = sch
        _uid = [0]

        def sb(name, shape, dt):
            _uid[0] += 1
            return nc.sbuf_tensor("s%d_%s" % (_uid[0], name), shape, dt)

        def pst(name, shape, dt):
            _uid[0] += 1
            return nc.psum_tensor("p%d_%s" % (_uid[0], name), shape, dt)
        ident = es.enter_context(sb("ident", [128, 128], BF16))
        epsN = es.enter_context(sb("epsN", [128, 4], F32))
        nmix = es.enter_context(sb("nmix", [128, 16], F32))
        nffn = es.enter_context(sb("nffn", [128, 16], F32))
        gnw = es.enter_context(sb("gnw", [128, 4], F32))
        subl = es.enter_context(sb("subl", [128, 1], F32))
        convw = es.enter_context(sb("convw", [128, 12], F32))
        idxq = es.enter_context(sb("idxq", [128, NQB], I32))
        lamt = es.enter_context(sb("lamt", [128, 8], F32))
        consts = Buf()

        def chk(k):
            if nph == k:
                sch.stopped = True

        try:
            with ExitStack() as ps:
                tmpa = ps.enter_context(sb("tmpa", [128, 128], F32))
                lamv = ps.enter_context(sb("lamv", [128, 256], F32))
                lamp = ps.enter_context(sb("lamp", [128, 128], F32))
                S_.add("pool", lambda e: e.iota(tmpa[:], pattern=[[1, 128]], base=0, channel_multiplier=-1,
                                                allow_small_or_imprecise_dtypes=True), W=[consts])
                S_.add("dve", lambda e: e.tensor_single_scalar(out=ident[:], in_=tmpa[:], scalar=0.0, op=ALU.is_equal),
                       R=[consts], W=[consts])
                S_.add("dve", lambda e: e.memset(epsN[:, 0:1], 1e-6), W=[consts])
                S_.add("dve", lambda e: e.memset(epsN[:, 1:3], 1e-5), W=[consts])
                S_.add("dve", lambda e: e.memset(epsN[:, 3:4], 0.0), W=[consts])
                for t, d in ((nmix, nmix_d), (nffn, nffn_d), (gnw, gn_d), (subl, subln_d), (convw, convw_d), (idxq, idxq_d)):
                    S_.add("sp", (lambda t, d: lambda e: e.dma_start(out=t[:], in_=d[:, :]))(t, d), W=[consts], chan="set")
                S_.add("sp", lambda e: e.dma_start(out=lamv[:], in_=lam_d.partition_broadcast(128)), W=[consts], chan="set")
                S_.add("dve", lambda e: e.tensor_single_scalar(out=subl[:], in_=subl[:], scalar=1.0 - LAMBDA_INIT, op=ALU.mult),
                       R=[consts], W=[consts])
                S_.add("dve", lambda e: e.tensor_tensor(out=lamp[:, 0:64], in0=lamv[:, 0:64], in1=lamv[:, 64:128], op=ALU.mult),
                       R=[consts], W=[consts])
                S_.add("dve", lambda e: e.tensor_tensor(out=lamp[:, 64:128], in0=lamv[:, 128:192], in1=lamv[:, 192:256], op=ALU.mult),
                       R=[consts], W=[consts])
                S_.add("dve", lambda e: e.reduce_sum(out=lamt[:, 1:2], in_=lamp[:, 0:64], axis=AX.X), R=[consts], W=[consts])
                S_.add("dve", lambda e: e.reduce_sum(out=lamt[:, 2:3], in_=lamp[:, 64:128], axis=AX.X), R=[consts], W=[consts])
                S_.add("act", lambda e: e.activation(out=lamt[:, 3:5], in_=lamt[:, 1:3], func=AF.Exp), R=[consts], W=[consts])
                S_.add("dve", lambda e: e.tensor_tensor(out=lamt[:, 5:6], in0=lamt[:, 4:5], in1=lamt[:, 3:4], op=ALU.subtract),
                       R=[consts], W=[consts])
                S_.add("dve", lambda e: e.tensor_single_scalar(out=lamt[:, 0:1], in_=lamt[:, 5:6], scalar=-LAMBDA_INIT, op=ALU.add),
                       R=[consts], W=[consts])
                S_.flush(); chk(0)

            def load_weight(dst, src, KC, C, rows, stg, sidx):
                srcv = src.rearrange("(kc p) c -> p kc c", p=128)
                wb = Buf()
                for kc in range(KC):
                    for c0 in range(0, C, 1024):
                        cw = min(1024, C - c0)
                        st, stb = stg.get()
                        S_.add("sp", (lambda st, kc, c0, cw: lambda e: e.dma_start(out=st[:, 0:cw], in_=srcv[:, kc, c0:c0 + cw]))(st, kc, c0, cw),
                               W=[stb], chan="wl%d" % (sidx[0] % 3))
                        eng = ("dve", "pool")[sidx[0] % 2]
                        sidx[0] += 1
                        r = rows(kc) if rows is not None else None
                        if r is None:
                            S_.add(eng, (lambda st, kc, c0, cw: lambda e: e.tensor_copy(out=dst[:, kc, c0:c0 + cw], in_=st[:, 0:cw]))(st, kc, c0, cw),
                                   R=[stb, consts], W=[wb])
                        else:
                            S_.add(eng, (lambda st, kc, c0, cw, r: lambda e: e.tensor_scalar(
                                out=dst[:, kc, c0:c0 + cw], in0=st[:, 0:cw], scalar1=r, scalar2=None, op0=ALU.mult))(st, kc, c0, cw, r),
                                   R=[stb, consts], W=[wb])
                return wb

            class NormT:
                def __init__(self, es_, psB, n=2):
                    self.junk = Pool(es_, sb, "njunk", [128, 1024], BF16, n)
                    self.xs = Pool(es_, sb, "nxs", [128, 1024], BF16, n)
                    self.st = Pool(es_, sb, "nst", [128, 4], F32, 4)
                    self.psB = psB

                def run(self, x, xb, dst, dstb, dst_is_write=True):
                    jk, jkb = self.junk.get()
                    xs, xsb = self.xs.get()
                    st, stb = self.st.get()
                    S_.add("act", lambda e: e.activation(out=jk[:], in_=x, func=AF.Square, accum_out=st[:, 0:1]), R=[xb], W=[jkb, stb])
                    S_.add("act", lambda e: e.activation(out=st[:, 1:2], in_=st[:, 0:1], func=AF.Sqrt, bias=epsN[:, 0:1], scale=1.0 / D),
                           R=[stb, consts], W=[stb])
                    S_.add("dve", lambda e: e.reciprocal(out=st[:, 2:3], in_=st[:, 1:2]), R=[stb], W=[stb])
                    S_.add("dve", lambda e: e.tensor_scalar(out=xs[:], in0=x, scalar1=st[:, 2:3], scalar2=None, op0=ALU.mult),
                           R=[xb, stb], W=[xsb])
                    pt, ptb = self.psB.get()

                    def tr(e):
                        for kc in range(8):
                            ins = e.transpose(out=pt[:, kc * 128:(kc + 1) * 128], in_=xs[:, kc * 128:(kc + 1) * 128], identity=ident[:])
                        return ins
                    S_.add("pe", tr, R=[xsb, consts], W=[ptb])
                    S_.add("act", lambda e: e.copy(out=dst, in_=pt[:].rearrange("p (k t) -> p k t", k=8)), R=[ptb], W=[dstb])
                    return st, stb

            def rope(eng1, eng2, src, srcb, H, half, cos, sin, tb, A, Ab, T, Tb, out, outb):
                v4 = lambda ap: ap.rearrange("p (h two d) -> p h two d", h=H, two=2)
                cb4 = cos.unsqueeze(1).unsqueeze(1).broadcast_to([128, H, 2, half])
                sb3 = sin.unsqueeze(1).broadcast_to([128, H, half])
                S_.add(eng1, lambda e: e.tensor_tensor(out=v4(A), in0=v4(src), in1=cb4, op=ALU.mult), R=[srcb, tb], W=[Ab])
                S_.add(eng2, lambda e: e.tensor_tensor(out=v4(T)[:, :, 0, :], in0=v4(src)[:, :, 1, :], in1=sb3, op=ALU.mult), R=[srcb, tb], W=[Tb])
                S_.add(eng2, lambda e: e.tensor_tensor(out=v4(T)[:, :, 1, :], in0=v4(src)[:, :, 0, :], in1=sb3, op=ALU.mult), R=[srcb, tb], W=[Tb])
                S_.add(eng1, lambda e: e.tensor_tensor(out=v4(out)[:, :, 0, :], in0=v4(A)[:, :, 0, :], in1=v4(T)[:, :, 0, :], op=ALU.subtract),
                       R=[Ab, Tb], W=[outb])
                S_.add(eng1, lambda e: e.tensor_tensor(out=v4(out)[:, :, 1, :], in0=v4(A)[:, :, 1, :], in1=v4(T)[:, :, 1, :], op=ALU.add),
                       R=[Ab, Tb], W=[outb])

            def mm_group(ps, psb, pairs, R):
                def f(e):
                    for (o, l, r, s0, s1) in pairs:
                        ins = e.matmul(o, lhsT=l, rhs=r, start=s0, stop=s1)
                    return ins
                S_.add("pe", f, R=R, W=[psb])

            with ExitStack() as ps:
                psF = Pool(ps, pst, "psF", [128, 512], F32, 6)
                psB = Pool(ps, pst, "psB", [128, 1024], BF16, 2)
                w_in = ps.enter_context(sb("w_in", [128, 8, 3584], BF16))
                w_out = ps.enter_context(sb("w_out", [128, 8, 1024], BF16))
                lg = ps.enter_context(sb("lg", [128, 8], F32))
                tabs = {k: ps.enter_context(sb("tab" + k, [128, 512], F32)) for k in ("DT", "QF", "QB", "KF", "KB", "CF", "CB")}
                tb_ = Buf()
                cwt = ps.enter_context(sb("cwt", [128, 3, 512], F32))
                for t in range(3):
                    S_.add("sp", (lambda t: lambda e: e.dma_start(out=cwt[:, t, :], in_=convw3_d[t:t + 1, :].partition_broadcast(128)))(t), W=[consts], chan="set")
                ws = ExitStack()
                stg = Pool(ws, sb, "wstg", [128, 1024], F32, 3)
                sidx = [0]
                wb_in = load_weight(w_in, w_in_d, 8, 3584, lambda kc: nmix[:, kc:kc + 1], stg, sidx)
                wb_out = load_weight(w_out, w_out_d, 8, 1024, lambda kc: (gnw[:, kc - 4:kc - 3] if kc >= 4 else None), stg, sidx)
                with ExitStack() as ts:
                    io = {k: ts.enter_context(sb("io" + k, [128, 128], F32)) for k in ("rel", "relp", "reln", "mp", "mn", "i1", "ib", "kf", "kb", "c128", "e1", "e2")}
                    S_.add("sp", lambda e: e.dma_start(out=lg[:, 0:4], in_=decf_d.partition_broadcast(128)), W=[tb_], chan="set")
                    S_.add("sp", lambda e: e.dma_start(out=lg[:, 4:8], in_=decb_d.partition_broadcast(128)), W=[tb_], chan="set")
                    S_.add("act", lambda e: e.activation(out=lg[:], in_=lg[:], func=AF.Exp), R=[tb_], W=[tb_])
                    S_.add("dve", lambda e: e.tensor_single_scalar(out=lg[:], in_=lg[:], scalar=-1.0, op=ALU.mult), R=[tb_], W=[tb_])
                    io_ = lambda k, pat, base, cm: S_.add("pool", lambda e: e.iota(io[k][:], pattern=pat, base=base, channel_multiplier=cm,
                                                                                    allow_small_or_imprecise_dtypes=True), W=[tb_])
                    io_("rel", [[1, 128]], 0, -1)
                    io_("i1", [[1, 128]], 1, 0)
                    io_("ib", [[-1, 128]], 128, 0)
                    io_("kf", [[0, 128]], 127, -1)
                    io_("kb", [[0, 128]], 0, 1)
                    io_("c128", [[0, 128]], 128, 0)
                    S_.add("dve", lambda e: e.tensor_single_scalar(out=io["relp"][:], in_=io["rel"][:], scalar=0.0, op=ALU.max), R=[tb_], W=[tb_])
                    S_.add("dve", lambda e: e.tensor_scalar(out=io["reln"][:], in0=io["rel"][:], scalar1=-1.0, scalar2=0.0, op0=ALU.mult, op1=ALU.max),
                           R=[tb_], W=[tb_])
                    S_.add("dve", lambda e: e.tensor_single_scalar(out=io["mp"][:], in_=io["rel"][:], scalar=0.0, op=ALU.is_ge), R=[tb_], W=[tb_])
                    S_.add("dve", lambda e: e.tensor_single_scalar(out=io["mn"][:], in_=io["rel"][:], scalar=0.0, op=ALU.is_lt), R=[tb_], W=[tb_])
                    for h in range(4):
                        hs = slice(h * 128, (h + 1) * 128)
                        ex = lambda dst, src, col: S_.add("act", lambda e: e.activation(out=dst, in_=src, func=AF.Exp, scale=lg[:, col:col + 1]),
                                                          R=[tb_], W=[tb_])
                        ex(io["e1"][:], io["relp"][:], h)
                        ex(io["e2"][:], io["reln"][:], 4 + h)
                        S_.add("dve", lambda e: e.tensor_tensor(out=io["e1"][:], in0=io["e1"][:], in1=io["mp"][:], op=ALU.mult), R=[tb_], W=[tb_])
                        S_.add("dve", lambda e: e.tensor_tensor(out=io["e2"][:], in0=io["e2"][:], in1=io["mn"][:], op=ALU.mult), R=[tb_], W=[tb_])
                        S_.add("dve", (lambda hs: lambda e: e.tensor_tensor(out=tabs["DT"][:, hs], in0=io["e1"][:], in1=io["e2"][:], op=ALU.add))(hs),
                               R=[tb_], W=[tb_])
                        ex(tabs["QF"][:, hs], io["i1"][:], h)
                        ex(tabs["QB"][:, hs], io["ib"][:], 4 + h)
                        ex(tabs["KF"][:, hs], io["kf"][:], h)
                        ex(tabs["KB"][:, hs], io["kb"][:], 4 + h)
                        ex(tabs["CF"][:, hs], io["c128"][:], h)
                        ex(tabs["CB"][:, hs], io["c128"][:], 4 + h)
                    S_.flush(); chk(1)
                ws.close()
                nt = NormT(ps, psB)
                xin = Pool(ps, sb, "xin", [128, 1024], F32, 3)
                xnT = Pool(ps, sb, "xnT", [128, 8, 128], BF16, 2)
                ropt = Pool(ps, sb, "ropt", [128, 256], F32, 3)
                rA = Pool(ps, sb, "rA", [128, 512], F32, 2)
                rT = Pool(ps, sb, "rT", [128, 512], F32, 2)
                kr = Pool(ps, sb, "kr", [128, 512], BF16, 3)
                vv = Pool(ps, sb, "vv", [128, 512], BF16, 3)
                sbl = Pool(ps, sb, "sbl", [128, 512], BF16, 2)
                kd = Pool(ps, sb, "kd", [128, 512], BF16, 2)
                acS = Pool(ps, sb, "acS", [128, 512], F32, 2)
                uP = Pool(ps, sb, "uP", [128, 512], F32, 2)
                u3p = Pool(ps, sb, "u3p", [128, 512], F32, 6)
                yall = Pool(ps, sb, "yall", [128, 1024], BF16, 2)
                S32 = ps.enter_context(sb("S32", [128, 512], F32))
                S16 = Pool(ps, sb, "S16", [128, 512], BF16, 2)
                qr = Pool(ps, sb, "qr", [128, 512], BF16, 2)
                qT = Pool(ps, sb, "qT", [128, 512], BF16, 2)
                qfT = Pool(ps, sb, "qfT", [128, 512], BF16, 2)
                qbT = Pool(ps, sb, "qbT", [128, 512], BF16, 2)
                kT = Pool(ps, sb, "kT", [128, 512], BF16, 2)
                AT = Pool(ps, sb, "AT", [128, 512], BF16, 2)
                gst = Pool(ps, sb, "gst", [128, 4, 6], F32, 2)
                gmv = Pool(ps, sb, "gmv", [128, 4, 2], F32, 2)
                grs = Pool(ps, sb, "grs", [128, 8], F32, 2)
                on = Pool(ps, sb, "on", [128, 512], F32, 2)
                sg = Pool(ps, sb, "sg", [128, 512], F32, 2)
                abS = Pool(ps, sb, "abS", [128, 512], F32, 2)
                yT = Pool(ps, sb, "yT", [128, 8, 128], BF16, 2)
                x1t = Pool(ps, sb, "x1t", [128, 1024], F32, 2)
                S32b = Buf()

                def run_job(job):
                    jn, S = job["n"], job["S"]
                    NCH = S // 128
                    xd = xin_d[jn]
                    sc = scr[jn]
                    zt, ztb = uP.get()
                    S_.add("pool", lambda e: e.memset(zt[0:1, :], 0.0), W=[ztb])
                    S_.add("sp", lambda e: e.dma_start(out=sc["u"][0:1, :], in_=zt[0:1, :]), R=[ztb], W=[S_.dbuf("uT", jn, -1)], chan="st0")
                    S_.add("sp", lambda e: e.dma_start(out=sc["u"][S + 1:S + 2, :], in_=zt[0:1, :]), R=[ztb], W=[S_.dbuf("uT", jn, -2)], chan="st0")
                    S_.add("dve", lambda e: e.memset(S32[:], 0.0), W=[S32b])
                    stv = list(S16.get())
                    tick = [0]
                    S_.add("pool", (lambda s16: lambda e: e.memset(s16[:], 0.0))(stv[0]), W=[stv[1]])

                    def pre_chunk(k, c):
                        x, xb = xin.get()
                        S_.add("sp", (lambda x, c: lambda e: e.dma_start(out=x[:], in_=xd[c * 128:(c + 1) * 128, :]))(x, c), W=[xb], chan="xl%d" % (c % 3))
                        rtb_t, rtb = ropt.get()
                        S_.add("sp", (lambda t, c: lambda e: e.dma_start(out=t[:], in_=rt_d[c * 128:(c + 1) * 128, :]))(rtb_t, c), W=[rtb], chan="rl%d" % (c % 3))
                        xt_, xtb = xnT.get()
                        nt.run(x[:], xb, xt_[:], xtb)
                        yield
                        pk, pkb = psF.get()
                        mm_group(pk, pkb, [(pk[:], xt_[:, kc, :], w_in[:, kc, 2048:2560], kc == 0, kc == 7) for kc in range(8)], [xtb, wb_in])
                        pv, pvb = psF.get()
                        mm_group(pv, pvb, [(pv[:], xt_[:, kc, :], w_in[:, kc, 2560:3072], kc == 0, kc == 7) for kc in range(8)], [xtb, wb_in])
                        pc, pcb = psF.get()
                        mm_group(pc, pcb, [(pc[:], xt_[:, kc, :], w_in[:, kc, 512:1024], kc == 0, kc == 7) for kc in range(8)], [xtb, wb_in])
                        ph, phb = psF.get()
                        mm_group(ph, phb, [(ph[:], xt_[:, kc, :], w_in[:, kc, 1024:1536], kc == 0, kc == 7) for kc in range(8)], [xtb, wb_in])
                        ac, acb = acS.get()
                        S_.add("act", (lambda ac, pc: lambda e: e.copy(out=ac[:], in_=pc[:]))(ac, pc), R=[pcb], W=[acb])
                        u, ub = uP.get()
                        S_.add("dve", (lambda u, ac, ph: lambda e: e.tensor_tensor(out=u[:], in0=ac[:], in1=ph[:], op=ALU.mult))(u, ac, ph),
                               R=[acb, phb], W=[ub])
                        S_.add("sp", (lambda u, c: lambda e: e.dma_start(out=sc["u"][1 + c * 128:1 + (c + 1) * 128, :], in_=u[:]))(u, c),
                               R=[ub], W=[S_.dbuf("uT", jn, c)], chan="st%d" % (c % 2))
                        A, Ab = rA.get()
                        T, Tb = rT.get()
                        k_, kb_ = kr.get()
                        rope("dve", "pool" if False else "dve", pk[:], pkb, 4, 64, rtb_t[:, 128:192], rtb_t[:, 192:256], rtb, A[:], Ab, T[:], Tb, k_[:], kb_)
                        v_, vb_ = vv.get()
                        S_.add("act", (lambda v_, pv: lambda e: e.copy(out=v_[:], in_=pv[:]))(v_, pv), R=[pvb], W=[vb_])
                        S_.add("sp", (lambda k_, c: lambda e: e.dma_start(out=sc["kr"][c * 128:(c + 1) * 128, :], in_=k_[:]))(k_, c),
                               R=[kb_], W=[S_.dbuf("kr", jn, c)], chan="st%d" % (c % 2))
                        S_.add("sp", (lambda v_, c: lambda e: e.dma_start(out=sc["vr"][c * 128:(c + 1) * 128, :], in_=v_[:]))(v_, c),
                               R=[vb_], W=[S_.dbuf("vr", jn, c)], chan="st%d" % (c % 2))
                        yield
                        while tick[0] != k:
                            yield
                        S_.add("sp", (lambda s16, c: lambda e: e.dma_start(out=sc["sb"][c, :, :], in_=s16[:]))(stv[0], c),
                               R=[stv[1]], W=[S_.dbuf("sb", jn, c)], chan="st%d" % (c % 2))
                        kd_, kdb = kd.get()
                        S_.add("pool", (lambda kd_, k_: lambda e: e.tensor_tensor(out=kd_[:], in0=k_[:], in1=tabs["KB"][:], op=ALU.mult))(kd_, k_),
                               R=[kb_, tb_], W=[kdb])
                        pS, pSb = psF.get()
                        mm_group(pS, pSb, [(pS[:, h * 128:(h + 1) * 128], kd_[:, h * 128:(h + 1) * 128], v_[:, h * 128:(h + 1) * 128], True, True) for h in range(4)],
                                 [kdb, vb_])
                        S_.add("pool", lambda e: e.tensor_tensor(out=S32[:], in0=S32[:], in1=tabs["CB"][:], op=ALU.mult), R=[S32b, tb_], W=[S32b])
                        S_.add("dve", (lambda pS: lambda e: e.tensor_tensor(out=S32[:], in0=S32[:], in1=pS[:], op=ALU.add))(pS), R=[S32b, pSb], W=[S32b])
                        stv[0], stv[1] = S16.get()
                        S_.add("act", (lambda s16: lambda e: e.copy(out=s16[:], in_=S32[:]))(stv[0]), R=[S32b], W=[stv[1]])
                        tick[0] = k + 1
                        yield
                    staggered(pre_chunk, list(range(NCH - 1, -1, -1)), 4)
                    S_.add("dve", lambda e: e.memset(S32[:], 0.0), W=[S32b])
                    stv[0], stv[1] = S16.get()
                    tick[0] = 0
                    S_.add("pool", (lambda s16: lambda e: e.memset(s16[:], 0.0))(stv[0]), W=[stv[1]])

                    def main_chunk(k, c):
                        x, xb = xin.get()
                        S_.add("sp", (lambda x, c: lambda e: e.dma_start(out=x[:], in_=xd[c * 128:(c + 1) * 128, :]))(x, c), W=[xb], chan="xl%d" % (c % 3))
                        rtb_t, rtb = ropt.get()
                        S_.add("sp", (lambda t, c: lambda e: e.dma_start(out=t[:], in_=rt_d[c * 128:(c + 1) * 128, :]))(rtb_t, c), W=[rtb], chan="rl%d" % (c % 3))
                        k_, kb_ = kr.get()
                        S_.add("sp", (lambda k_, c: lambda e: e.dma_start(out=k_[:], in_=sc["kr"][c * 128:(c + 1) * 128, :]))(k_, c),
                               R=[S_.dbuf("kr", jn, c)], W=[kb_], chan="kl%d" % (c % 3))
                        v_, vb_ = vv.get()
                        S_.add("sp", (lambda v_, c: lambda e: e.dma_start(out=v_[:], in_=sc["vr"][c * 128:(c + 1) * 128, :]))(v_, c),
                               R=[S_.dbuf("vr", jn, c)], W=[vb_], chan="vl%d" % (c % 3))
                        sl, slb = sbl.get()
                        S_.add("sp", (lambda sl, c: lambda e: e.dma_start(out=sl[:], in_=sc["sb"][c, :, :]))(sl, c),
                               R=[S_.dbuf("sb", jn, c)], W=[slb], chan="sl%d" % (c % 2))
                        u3 = [u3p.get() for _ in range(3)]
                        urd = [S_.dbuf("uT", jn, cc) for cc in (c - 1, c, c + 1) if 0 <= cc < NCH] + [S_.dbuf("uT", jn, -1), S_.dbuf("uT", jn, -2)]
                        for s_ in range(3):
                            S_.add("sp", (lambda t, c, s_: lambda e: e.dma_start(out=t[:], in_=sc["u"][c * 128 + s_:c * 128 + s_ + 128, :]))(u3[s_][0], c, s_),
                                   R=urd, W=[u3[s_][1]], chan="ul%d" % ((c * 3 + s_) % 3))
                        yield
                        xt_, xtb = xnT.get()
                        nt.run(x[:], xb, xt_[:], xtb)
                        yield
                        pq, pqb = psF.get()
                        mm_group(pq, pqb, [(pq[:], xt_[:, kc, :], w_in[:, kc, 1536:2048], kc == 0, kc == 7) for kc in range(8)], [xtb, wb_in])
                        pg, pgb = psF.get()
                        mm_group(pg, pgb, [(pg[:], xt_[:, kc, :], w_in[:, kc, 3072:3584], kc == 0, kc == 7) for kc in range(8)], [xtb, wb_in])
                        pab, pabb = psF.get()
                        mm_group(pab, pabb, [(pab[:], xt_[:, kc, :], w_in[:, kc, 0:512], kc == 0, kc == 7) for kc in range(8)], [xtb, wb_in])
                        sg_, sgb = sg.get()
                        S_.add("act", (lambda sg_, pg: lambda e: e.activation(out=sg_[:], in_=pg[:], func=AF.Silu))(sg_, pg), R=[pgb], W=[sgb])
                        ab_, abb = abS.get()
                        S_.add("act", (lambda ab_, pab: lambda e: e.copy(out=ab_[:], in_=pab[:]))(ab_, pab), R=[pabb], W=[abb])
                        ya_, yab = yall.get()
                        (u0, u0b), (u1, u1b), (u2, u2b) = u3
                        for (ut, utb, tap) in ((u1, u1b, 1), (u0, u0b, 0), (u2, u2b, 2)):
                            S_.add("pool", (lambda ut, tap: lambda e: e.tensor_tensor(out=ut[:], in0=ut[:], in1=cwt[:, tap, :], op=ALU.mult))(ut, tap),
                                   R=[utb, consts], W=[utb])
                        S_.add("dve", (lambda u1, u0: lambda e: e.tensor_tensor(out=u1[:], in0=u1[:], in1=u0[:], op=ALU.add))(u1, u0), R=[u1b, u0b], W=[u1b])
                        S_.add("dve", (lambda u1, u2: lambda e: e.tensor_tensor(out=u1[:], in0=u1[:], in1=u2[:], op=ALU.add))(u1, u2), R=[u1b, u2b], W=[u1b])
                        S_.add("pool", (lambda ya_, u1, ab_: lambda e: e.tensor_tensor(out=ya_[:, 0:512], in0=u1[:], in1=ab_[:], op=ALU.mult))(ya_, u1, ab_),
                               R=[u1b, abb], W=[yab])
                        yield
                        A, Ab = rA.get()
                        T, Tb = rT.get()
                        q_, qb_ = qr.get()
                        rope("dve", "dve", pq[:], pqb, 4, 64, rtb_t[:, 0:64], rtb_t[:, 64:128], rtb, A[:], Ab, T[:], Tb, q_[:], qb_)
                        yield
                        pt, ptb = psB.get()

                        def trqk(e, pt=pt, q_=q_, k_=k_):
                            for h in range(4):
                                e.transpose(out=pt[:, h * 128:(h + 1) * 128], in_=q_[:, h * 128:(h + 1) * 128], identity=ident[:])
                            for h in range(4):
                                ins = e.transpose(out=pt[:, 512 + h * 128:512 + (h + 1) * 128], in_=k_[:, h * 128:(h + 1) * 128], identity=ident[:])
                            return ins
                        S_.add("pe", trqk, R=[qb_, kb_, consts], W=[ptb])
                        qT_, qTb = qT.get()
                        qf_, qfb = qfT.get()
                        qb2, qbb = qbT.get()
                        kT_, kTb = kT.get()
                        S_.add("act", (lambda o, pt: lambda e: e.copy(out=o[:], in_=pt[:, 0:512]))(qT_, pt), R=[ptb], W=[qTb])
                        S_.add("act", (lambda o, pt: lambda e: e.copy(out=o[:], in_=pt[:, 512:1024]))(kT_, pt), R=[ptb], W=[kTb])
                        S_.add("dve", (lambda o, pt: lambda e: e.tensor_tensor(out=o[:], in0=pt[:, 0:512], in1=tabs["QF"][:], op=ALU.mult))(qf_, pt),
                               R=[ptb, tb_], W=[qfb])
                        S_.add("dve", (lambda o, pt: lambda e: e.tensor_tensor(out=o[:], in0=pt[:, 0:512], in1=tabs["QB"][:], op=ALU.mult))(qb2, pt),
                               R=[ptb, tb_], W=[qbb])
                        yield
                        psc, pscb = psF.get()
                        hs = lambda h: slice(h * 128, (h + 1) * 128)
                        mm_group(psc, pscb, [(psc[:, hs(h)], kT_[:, hs(h)], qT_[:, hs(h)], True, True) for h in range(4)], [kTb, qTb])
                        at, atb = AT.get()
                        S_.add("dve", (lambda at, psc: lambda e: e.tensor_tensor(out=at[:], in0=psc[:], in1=tabs["DT"][:], op=ALU.mult))(at, psc),
                               R=[pscb, tb_], W=[atb])
                        yield
                        while tick[0] != k:
                            yield
                        s16, s16b = stv
                        po, pob = psF.get()
                        prs = []
                        for h in range(4):
                            prs += [(po[:, hs(h)], at[:, hs(h)], v_[:, hs(h)], True, False),
                                    (po[:, hs(h)], qf_[:, hs(h)], s16[:, hs(h)], False, False),
                                    (po[:, hs(h)], qb2[:, hs(h)], sl[:, hs(h)], False, True)]
                        mm_group(po, pob, prs, [atb, vb_, qfb, qbb, s16b, slb])
                        kd_, kdb = kd.get()
                        S_.add("pool", (lambda kd_, k_: lambda e: e.tensor_tensor(out=kd_[:], in0=k_[:], in1=tabs["KF"][:], op=ALU.mult))(kd_, k_),
                               R=[kb_, tb_], W=[kdb])
                        pS, pSb = psF.get()
                        mm_group(pS, pSb, [(pS[:, hs(h)], kd_[:, hs(h)], v_[:, hs(h)], True, True) for h in range(4)], [kdb, vb_])
                        S_.add("pool", lambda e: e.tensor_tensor(out=S32[:], in0=S32[:], in1=tabs["CF"][:], op=ALU.mult), R=[S32b, tb_], W=[S32b])
                        S_.add("dve", (lambda pS: lambda e: e.tensor_tensor(out=S32[:], in0=S32[:], in1=pS[:], op=ALU.add))(pS), R=[S32b, pSb], W=[S32b])
                        stv[0], stv[1] = S16.get()
                        S_.add("act", (lambda s16: lambda e: e.copy(out=s16[:], in_=S32[:]))(stv[0]), R=[S32b], W=[stv[1]])
                        tick[0] = k + 1
                        yield
                        st6, st6b = gst.get()
                        mv, mvb = gmv.get()
                        rs, rsb = grs.get()

                        def bns(e, st6=st6, po=po):
                            for h in range(4):
                                ins = e.bn_stats(out=st6[:, h, :], in_=po[:, h * 128:(h + 1) * 128])
                            return ins
                        S_.add("dve", bns, R=[pob], W=[st6b])

                        def bna(e, st6=st6, mv=mv):
                            for h in range(4):
                                ins = e.bn_aggr(out=mv[:, h, :], in_=st6[:, h, :])
                            return ins
                        S_.add("dve", bna, R=[st6b], W=[mvb])
                        S_.add("act", (lambda rs, mv: lambda e: e.activation(out=rs[:, 0:4], in_=mv[:, :, 1], func=AF.Sqrt, bias=epsN[:, 1:2], scale=1.0))(rs, mv),
                               R=[mvb, consts], W=[rsb])
                        S_.add("dve", (lambda rs: lambda e: e.reciprocal(out=rs[:, 4:8], in_=rs[:, 0:4]))(rs), R=[rsb], W=[rsb])
                        on_, onb = on.get()

                        def gnn(e, on_=on_, po=po, mv=mv, rs=rs):
                            for h in range(4):
                                ins = e.tensor_scalar(out=on_[:, h * 128:(h + 1) * 128], in0=po[:, h * 128:(h + 1) * 128], scalar1=mv[:, h, 0:1],
                                                      scalar2=rs[:, 4 + h:5 + h], op0=ALU.subtract, op1=ALU.mult)
                            return ins
                        S_.add("dve", gnn, R=[pob, mvb, rsb], W=[onb])
                        yield
                        S_.add("pool", (lambda ya_, on_, sg_: lambda e: e.tensor_tensor(out=ya_[:, 512:1024], in0=on_[:], in1=sg_[:], op=ALU.mult))(ya_, on_, sg_),
                               R=[onb, sgb], W=[yab])
                        pt2, pt2b = psB.get()

                        def try_(e, pt2=pt2, ya_=ya_):
                            for f in range(8):
                                ins = e.transpose(out=pt2[:, f * 128:(f + 1) * 128], in_=ya_[:, f * 128:(f + 1) * 128], identity=ident[:])
                            return ins
                        S_.add("pe", try_, R=[yab, consts], W=[pt2b])
                        yT_, yTb = yT.get()
                        S_.add("act", (lambda yT_, pt2: lambda e: e.copy(out=yT_[:], in_=pt2[:].rearrange("p (k t) -> p k t", k=8)))(yT_, pt2),
                               R=[pt2b], W=[yTb])
                        x1_, x1b = x1t.get()
                        for n in range(2):
                            pp, ppb = psF.get()
                            mm_group(pp, ppb, [(pp[:], yT_[:, f, :], w_out[:, f, n * 512:(n + 1) * 512], f == 0, f == 7) for f in range(8)], [yTb, wb_out])
                            S_.add("dve", (lambda x1_, pp, x, n: lambda e: e.tensor_tensor(out=x1_[:, n * 512:(n + 1) * 512], in0=pp[:],
                                                                                          in1=x[:, n * 512:(n + 1) * 512], op=ALU.add))(x1_, pp, x, n),
                                   R=[ppb, xb], W=[x1b])
                        S_.add("sp", (lambda x1_, c: lambda e: e.dma_start(out=sc["x1"][c * 128:(c + 1) * 128, :], in_=x1_[:]))(x1_, c),
                               R=[x1b], W=[S_.dbuf("x1", jn, c)], chan="st%d" % (c % 2))
                        if debug:
                            S_.add("sp", (lambda x1_, c: lambda e: e.dma_start(out=dbg[jn]["x1"][c * 128:(c + 1) * 128, :], in_=x1_[:]))(x1_, c),
                                   R=[x1b], W=[S_.dbuf("dx1", jn, c)], chan="dbg")
                        yield
                    staggered(main_chunk, list(range(NCH)), 6)
                for job in jobs:
                    run_job(job)
                S_.flush(); chk(2)

            def ffn_phase(layer, final):
                with ExitStack() as ps:
                    psF = Pool(ps, pst, "psF", [128, 512], F32, 6)
                    psB = Pool(ps, pst, "psB", [128, 1024], BF16, 2)
                    wg = ps.enter_context(sb("wg", [128, 8, DFF], BF16))
                    wu = ps.enter_context(sb("wu", [128, 8, DFF], BF16))
                    wd = ps.enter_context(sb("wd", [128, NFC, 1024], BF16))
                    with ExitStack() as ws:
                        stg = Pool(ws, sb, "wstg", [128, 1024], F32, 3)
                        sidx = [0]
                        nr = lambda kc: nffn[:, layer * 8 + kc:layer * 8 + kc + 1]
                        wbg = load_weight(wg, w_g_d[layer], 8, DFF, nr, stg, sidx)
                        wbu = load_weight(wu, w_u_d[layer], 8, DFF, nr, stg, sidx)
                        wbd = load_weight(wd, w_d_d[layer], NFC, 1024, None, stg, sidx)
                        S_.flush(); chk(3)
                    nt = NormT(ps, psB)
                    xin = Pool(ps, sb, "xin", [128, 1024], F32, 2)
                    xres = Pool(ps, sb, "xres", [128, 1024], F32, 2)
                    xnT = Pool(ps, sb, "xnT", [128, 8, 512], BF16, 2)
                    hT = Pool(ps, sb, "hT", [128, NFC, 512], BF16, 1)
                    sgp = Pool(ps, sb, "sgp", [128, 512], F32, 2)
                    nf = None
                    if final:
                        nf = ps.enter_context(sb("nf", [128, 1024], F32))
                        S_.add("sp", lambda e: e.dma_start(out=nf[:], in_=nfin_d.partition_broadcast(128)), W=[consts], chan="set")
                        fj = Pool(ps, sb, "fj", [128, 1024], BF16, 1)
                        fst = Pool(ps, sb, "fst", [128, 4], F32, 2)
                    def run_job(job):
                        jn = job["n"]
                        S = job["SQ"] if final else job["S"]
                        sc = scr[jn]
                        src, srcn = (sc["x3"], "x3") if final else (sc["x1"], "x1")
                        dst, dstn = (y_d[jn], "y") if final else (sc["x2"], "x2")
                        for t in range(S // 512):
                            xt_, xtb = xnT.get()
                            for ci in range(4):
                                c = t * 4 + ci
                                x, xb = xin.get()
                                S_.add("sp", (lambda x, c: lambda e: e.dma_start(out=x[:], in_=src[c * 128:(c + 1) * 128, :]))(x, c),
                                       R=[S_.dbuf(srcn, jn, c)], W=[xb], chan="xl%d" % (c % 2))
                                nt.run(x[:], xb, xt_[:, :, ci * 128:(ci + 1) * 128], xtb)
                            h_, hb = hT.get()
                            for f in range(NFC):
                                pg, pgb = psF.get()
                                mm_group(pg, pgb, [(pg[:], wg[:, kc, f * 128:(f + 1) * 128], xt_[:, kc, :], kc == 0, kc == 7) for kc in range(8)], [xtb, wbg])
                                pu, pub = psF.get()
                                mm_group(pu, pub, [(pu[:], wu[:, kc, f * 128:(f + 1) * 128], xt_[:, kc, :], kc == 0, kc == 7) for kc in range(8)], [xtb, wbu])
                                s_, sb_ = sgp.get()
                                S_.add("act", (lambda s_, pg: lambda e: e.activation(out=s_[:], in_=pg[:], func=AF.Silu))(s_, pg), R=[pgb], W=[sb_])
                                S_.add("dve", (lambda h_, s_, pu, f: lambda e: e.tensor_tensor(out=h_[:, f, :], in0=s_[:], in1=pu[:], op=ALU.mult))(h_, s_, pu, f),
                                       R=[sb_, pub], W=[hb])
                            for ci in range(4):
                                c = t * 4 + ci
                                xr, xrb = xres.get()
                                S_.add("sp", (lambda xr, c: lambda e: e.dma_start(out=xr[:], in_=src[c * 128:(c + 1) * 128, :]))(xr, c),
                                       R=[S_.dbuf(srcn, jn, c)], W=[xrb], chan="rl%d" % (c % 2))
                                for n in range(2):
                                    pp, ppb = psF.get()
                                    mm_group(pp, ppb, [(pp[:], h_[:, f, ci * 128:(ci + 1) * 128], wd[:, f, n * 512:(n + 1) * 512], f == 0, f == NFC - 1)
                                                       for f in range(NFC)], [hb, wbd])
                                    S_.add("pool" if False else "dve", (lambda xr, pp, n: lambda e: e.tensor_tensor(out=xr[:, n * 512:(n + 1) * 512], in0=pp[:],
                                                                                                                   in1=xr[:, n * 512:(n + 1) * 512], op=ALU.add))(xr, pp, n),
                                           R=[ppb, xrb], W=[xrb])
                                if final:
                                    jk, jkb = fj.get()
                                    st, stb = fst.get()
                                    S_.add("act", (lambda jk, xr, st: lambda e: e.activation(out=jk[:], in_=xr[:], func=AF.Square, accum_out=st[:, 0:1]))(jk, xr, st),
                                           R=[xrb], W=[jkb, stb])
                                    S_.add("act", (lambda st: lambda e: e.activation(out=st[:, 1:2], in_=st[:, 0:1], func=AF.Sqrt, bias=epsN[:, 0:1], scale=1.0 / D))(st),
                                           R=[stb, consts], W=[stb])
                                    S_.add("dve", (lambda st: lambda e: e.reciprocal(out=st[:, 2:3], in_=st[:, 1:2]))(st), R=[stb], W=[stb])
                                    S_.add("dve", (lambda xr, st: lambda e: e.scalar_tensor_tensor(out=xr[:], in0=xr[:], scalar=st[:, 2:3], in1=nf[:],
                                                                                                    op0=ALU.mult, op1=ALU.mult))(xr, st),
                                           R=[xrb, stb, consts], W=[xrb])
                                S_.add("sp", (lambda xr, c: lambda e: e.dma_start(out=dst[c * 128:(c + 1) * 128, :], in_=xr[:]))(xr, c),
                                       R=[xrb], W=[S_.dbuf(dstn, jn, c)], chan="st%d" % (c % 2))
                                if debug and not final:
                                    S_.add("sp", (lambda xr, c: lambda e: e.dma_start(out=dbg[jn]["x2"][c * 128:(c + 1) * 128, :], in_=xr[:]))(xr, c),
                                           R=[xrb], W=[S_.dbuf("dx2", jn, c)], chan="dbg")
                    for job in jobs:
                        run_job(job)
                    S_.flush(); chk(4)

            ffn_phase(0, False)

            with ExitStack() as ps:
                psF = Pool(ps, pst, "psF", [128, 512], F32, 6)
                psB = Pool(ps, pst, "psB", [128, 1024], BF16, 2)
                wqkv = ps.enter_context(sb("wqkv", [128, 8, 3072], BF16))
                with ExitStack() as ws:
                    stg = Pool(ws, sb, "wstg", [128, 1024], F32, 3)
                    sidx = [0]
                    wbq = load_weight(wqkv, w_qkv_d, 8, 3072, lambda kc: nmix[:, 8 + kc:9 + kc], stg, sidx)
                    S_.flush(); chk(5)
                nt = NormT(ps, psB, 3)
                xin = Pool(ps, sb, "xin", [128, 1024], F32, 4)
                xnT = Pool(ps, sb, "xnT", [128, 8, 128], BF16, 3)
                ropt = Pool(ps, sb, "ropt", [128, 64], F32, 4)
                rA = Pool(ps, sb, "rA", [128, 1024], F32, 3)
                rT = Pool(ps, sb, "rT", [128, 1024], F32, 3)
                rr = Pool(ps, sb, "rr", [128, 1024], BF16, 3)
                vS = Pool(ps, sb, "vS", [128, 1024], BF16, 3)
                stT = Pool(ps, sb, "stT", [128, 8, 512], BF16, 3)

                def qk_path(x_src_fn, R_x, rope_src, S, c0, dstT, dstname, jn, col0, with_v, vdst):
                    st_, stb = stT.get()
                    for ci in range(4):
                        c = c0 + ci
                        x, xb = xin.get()
                        x_src_fn(x, xb, c)
                        rt_, rtb = ropt.get()
                        S_.add("sp", (lambda t, c: lambda e: e.dma_start(out=t[:], in_=rope_src[c * 128:(c + 1) * 128, :]))(rt_, c), W=[rtb], chan="rl%d" % (c % 3))
                        yield
                        xt_, xtb = xnT.get()
                        nt.run(x[:], xb, xt_[:], xtb)
                        yield
                        pq = [psF.get(), psF.get()]
                        for n in range(2):
                            mm_group(pq[n][0], pq[n][1], [(pq[n][0][:], xt_[:, kc, :], wqkv[:, kc, col0 + n * 512:col0 + (n + 1) * 512], kc == 0, kc == 7)
                                                          for kc in range(8)], [xtb, wbq])
                        yield
                        A, Ab = rA.get()
                        T, Tb = rT.get()
                        r_, rb_ = rr.get()
                        for n in range(2):
                            sl = slice(n * 512, (n + 1) * 512)
                            rope("dve", "pool" if False else "dve", pq[n][0][:], pq[n][1], 8, 32, rt_[:, 0:32], rt_[:, 32:64], rtb, A[:, sl], Ab, T[:, sl], Tb, r_[:, sl], rb_)
                        pt, ptb = psB.get()

                        def tr(e, pt=pt, r_=r_):
                            for hp in range(8):
                                ins = e.transpose(out=pt[:, hp * 128:(hp + 1) * 128], in_=r_[:, hp * 128:(hp + 1) * 128], identity=ident[:])
                            return ins
                        S_.add("pe", tr, R=[rb_, consts], W=[ptb])
                        S_.add("act", (lambda st_, pt, ci: lambda e: e.copy(out=st_[:, :, ci * 128:(ci + 1) * 128], in_=pt[:].rearrange("p (k t) -> p k t", k=8)))(st_, pt, ci),
                               R=[ptb], W=[stb])
                        yield
                        if with_v:
                            pv = [psF.get(), psF.get()]
                            v_, vb_ = vS.get()
                            for n in range(2):
                                mm_group(pv[n][0], pv[n][1], [(pv[n][0][:], xt_[:, kc, :], wqkv[:, kc, 2048 + n * 512:2048 + (n + 1) * 512], kc == 0, kc == 7)
                                                              for kc in range(8)], [xtb, wbq])
                                S_.add("act", (lambda v_, p, n: lambda e: e.copy(out=v_[:, n * 512:(n + 1) * 512], in_=p[:]))(v_, pv[n][0], n), R=[pv[n][1]], W=[vb_])
                            S_.add("sp", (lambda v_, c: lambda e: e.dma_start(out=vdst[:, c * 128:(c + 1) * 128, :].rearrange("h t e -> t h e"),
                                                                              in_=v_[:].rearrange("p (h e) -> p h e", h=8)))(v_, c),
                                   R=[vb_], W=[S_.dbuf("V", jn, c)], chan="st%d" % (c % 2))
                    S_.add("sp", (lambda st_, c0: lambda e: e.dma_start(out=dstT[:, :, c0 * 128:c0 * 128 + 512].rearrange("h p t -> p h t"), in_=st_[:]))(st_, c0),
                           R=[stb], W=[S_.dbuf(dstname, jn, c0 // 4)], chan="st%d" % ((c0 // 4) % 2))

                def run_job(job):
                    jn, S, SQ = job["n"], job["S"], job["SQ"]
                    sc = scr[jn]

                    def src_plain(x, xb, c, sc=sc, jn=jn):
                        S_.add("sp", (lambda x, c: lambda e: e.dma_start(out=x[:], in_=sc["x2"][c * 128:(c + 1) * 128, :]))(x, c),
                               R=[S_.dbuf("x2", jn, c)], W=[xb], chan="xl%d" % (c % 3))

                    def src_gather(x, xb, c, sc=sc, jn=jn, S=S):
                        S_.add("pool", (lambda x, c: lambda e: e.indirect_dma_start(out=x[:, :], out_offset=None, in_=sc["x2"][:, :],
                                                                                    in_offset=bass.IndirectOffsetOnAxis(ap=idxq[:, c:c + 1], axis=0)))(x, c),
                               R=[S_.dbuf("x2", jn, cc) for cc in range(S // 128)] + [consts], W=[xb], chan="gl%d" % (c % 3))
                    staggered(lambda k, t: qk_path(src_plain, None, dtk_d, S, t * 4, sc["KT"], "KT", jn, 1024, True, sc["V"]),
                              list(range(S // 512)), 5, 3)
                    staggered(lambda k, t: qk_path(src_gather if job["own"] else src_plain, None, dtq_d[jn], SQ, t * 4, sc["QT"], "QT", jn, 0, False, None),
                              list(range(SQ // 512)), 4, 3)
                for job in jobs:
                    run_job(job)
                S_.flush(); chk(6)

            with ExitStack() as ps:
                psS = Pool(ps, pst, "psS", [128, 1024], F32, 2)
                psO = Pool(ps, pst, "psO", [128, 2, 256], F32, 4)
                SMX = max(SA, SB)
                KTt = Pool(ps, sb, "KTt", [128, SMX], BF16, 2)
                Vt = Pool(ps, sb, "Vt", [128, SMX // 128, 130], BF16, 2)
                QTt = Pool(ps, sb, "QTt", [128, max(SA, SQB)], BF16, 2)
                PT = Pool(ps, sb, "PT", [128, 1024], BF16, 3)
                rc = Pool(ps, sb, "rc", [128, 8], F32, 4)
                o1 = Pool(ps, sb, "o1", [128, 128], F32, 3)
                oj = Pool(ps, sb, "oj", [128, 128], F32, 2)
                ob = Pool(ps, sb, "ob", [128, 4, 128], BF16, 2)
                for i in range(2):
                    S_.add("pool", (lambda t: lambda e: e.memset(t[:, :, 128:130], 1.0))(Vt.t[i]), W=[Vt.b[i]])
                def run_job(job):
                    jn, S, SQ = job["n"], job["S"], job["SQ"]
                    sc = scr[jn]
                    NKT = S // 128
                    for h in range(8):
                        kt_, ktb = KTt.get()
                        S_.add("sp", (lambda kt_, h: lambda e: e.dma_start(out=kt_[:, 0:S], in_=sc["KT"][h, :, :]))(kt_, h),
                               R=[S_.dbuf("KT", jn, t) for t in range(S // 512)], W=[ktb], chan="kl%d" % (h % 2))
                        v_, vb_ = Vt.get()
                        VP = min(32, NKT)
                        for part in range(0, NKT, VP):
                            S_.add("sp", (lambda v_, h, part: lambda e: e.dma_start(out=v_[:, part:part + VP, 0:128],
                                                                                   in_=sc["V"][h, part * 128:(part + VP) * 128, :].rearrange("(k p) e -> p k e", p=128)))(v_, h, part),
                                   R=[S_.dbuf("V", jn, c) for c in range(part, min(part + VP, NKT))], W=[vb_], chan="vl%d" % (h % 2))
                        q_, qb_ = QTt.get()
                        S_.add("sp", (lambda q_, h: lambda e: e.dma_start(out=q_[:, 0:SQ], in_=sc["QT"][h, :, :]))(q_, h),
                               R=[S_.dbuf("QT", jn, t) for t in range(SQ // 512)], W=[qb_], chan="ql%d" % (h % 2))
                        for qt in range(SQ // 512):
                            acc = [psO.get(), psO.get(), psO.get(), psO.get()]
                            qs = slice(qt * 512, (qt + 1) * 512)
                            def qk(kt, qs=qs):
                                ks = slice(kt * 128, (kt + 1) * 128)
                                pS_, pSb = psS.get()
                                mm_group(pS_, pSb, [(pS_[:, 0:512], kt_[0:64, ks], q_[0:64, qs], True, True),
                                                    (pS_[:, 512:1024], kt_[64:128, ks], q_[64:128, qs], True, True)], [ktb, qb_])
                                return pS_, pSb
                            pend = qk(0)
                            for kt in range(NKT):
                                pS_, pSb = pend
                                if kt + 1 < NKT:
                                    pend = qk(kt + 1)
                                p_, pb_ = PT.get()
                                S_.add("act", (lambda p_, pS_: lambda e: e.activation(out=p_[:], in_=pS_[:], func=AF.Exp))(p_, pS_), R=[pSb], W=[pb_])
                                for sub in range(2):
                                    for pair in range(2):
                                        a_, ab_ = acc[sub * 2 + pair]
                                        mm_group(a_, ab_, [(a_[:, j, 0:129], p_[:, sub * 512 + (pair * 2 + j) * 128: sub * 512 + (pair * 2 + j + 1) * 128],
                                                            v_[:, kt, 0:129], kt == 0 and j == 0, kt == NKT - 1) for j in range(2)], [pb_, vb_])
                            ob_, obb = ob.get()
                            for qb4 in range(4):
                                pair, j = qb4 // 2, qb4 % 2
                                a0, a0b = acc[0 + pair]
                                a1, a1b = acc[2 + pair]
                                r_, rb2 = rc.get()
                                S_.add("dve", (lambda r_, a0, j: lambda e: e.reciprocal(out=r_[:, 0:1], in_=a0[:, j, 128:129]))(r_, a0, j), R=[a0b], W=[rb2])
                                S_.add("dve", (lambda r_, a1, j: lambda e: e.reciprocal(out=r_[:, 1:2], in_=a1[:, j, 128:129]))(r_, a1, j), R=[a1b, rb2], W=[rb2])
                                S_.add("dve", (lambda r_: lambda e: e.tensor_tensor(out=r_[:, 2:3], in0=r_[:, 1:2], in1=lamt[:, 0:1], op=ALU.mult))(r_),
                                       R=[rb2, consts], W=[rb2])
                                o_, o_b = o1.get()
                                S_.add("dve", (lambda o_, a0, j, r_: lambda e: e.tensor_scalar(out=o_[:], in0=a0[:, j, 0:128], scalar1=r_[:, 0:1], scalar2=None,
                                                                                              op0=ALU.mult))(o_, a0, j, r_), R=[a0b, rb2], W=[o_b])
                                S_.add("dve", (lambda o_, a1, j, r_: lambda e: e.scalar_tensor_tensor(out=o_[:], in0=a1[:, j, 0:128], scalar=r_[:, 2:3], in1=o_[:],
                                                                                                     op0=ALU.mult, op1=ALU.add))(o_, a1, j, r_),
                                       R=[a1b, rb2, o_b], W=[o_b])
                                jk, jkb = oj.get()
                                S_.add("dve", (lambda jk, o_: lambda e: e.tensor_tensor(out=jk[:], in0=o_[:], in1=o_[:], op=ALU.mult))(jk, o_),
                                       R=[o_b], W=[jkb])
                                S_.add("dve", (lambda jk, r_: lambda e: e.reduce_sum(out=r_[:, 3:4], in_=jk[:], axis=AX.X))(jk, r_), R=[jkb, rb2], W=[rb2])
                                S_.add("act", (lambda r_: lambda e: e.activation(out=r_[:, 4:5], in_=r_[:, 3:4], func=AF.Ln, bias=epsN[:, 1:2], scale=1.0 / 128))(r_),
                                       R=[rb2, consts], W=[rb2])
                                S_.add("act", (lambda r_: lambda e: e.activation(out=r_[:, 5:6], in_=r_[:, 4:5], func=AF.Exp, scale=-0.5))(r_),
                                       R=[rb2], W=[rb2])
                                S_.add("pool", (lambda ob_, o_, r_, qb4: lambda e: e.tensor_scalar(out=ob_[:, qb4, :], in0=o_[:], scalar1=r_[:, 5:6], scalar2=None,
                                                                                                  op0=ALU.mult))(ob_, o_, r_, qb4), R=[o_b, rb2], W=[obb])
                            S_.add("sp", (lambda ob_, qt, h: lambda e: e.dma_start(
                                out=sc["O"][qt * 512:(qt + 1) * 512, h * 128:(h + 1) * 128].rearrange("(k p) e -> p k e", p=128), in_=ob_[:]))(ob_, qt, h),
                                   R=[obb], W=[S_.dbuf("O", jn, qt, h)], chan="st%d" % (qt % 2))
                for job in jobs:
                    run_job(job)
                S_.flush(); chk(7)

            with ExitStack() as ps:
                psF = Pool(ps, pst, "psF", [128, 512], F32, 6)
                psB = Pool(ps, pst, "psB", [128, 1024], BF16, 2)
                wdo = ps.enter_context(sb("wdo", [128, 8, 1024], BF16))
                with ExitStack() as ws:
                    stg = Pool(ws, sb, "wstg", [128, 1024], F32, 3)
                    sidx = [0]
                    wbo = load_weight(wdo, w_do_d, 8, 1024, lambda kc: subl[:, 0:1], stg, sidx)
                    S_.flush(); chk(8)
                oin = Pool(ps, sb, "oin", [128, 1024], BF16, 3)
                oT = Pool(ps, sb, "oT", [128, 8, 128], BF16, 2)
                xres = Pool(ps, sb, "xres", [128, 1024], F32, 3)
                def run_job(job):
                    jn, S, SQ = job["n"], job["S"], job["SQ"]
                    sc = scr[jn]
                    for c in range(SQ // 128):
                        o_, o_b = oin.get()
                        S_.add("sp", (lambda o_, c: lambda e: e.dma_start(out=o_[:], in_=sc["O"][c * 128:(c + 1) * 128, :]))(o_, c),
                               R=[S_.dbuf("O", jn, c // 4, h) for h in range(8)], W=[o_b], chan="xl%d" % (c % 3))
                        xr, xrb = xres.get()
                        if job["own"]:
                            S_.add("pool", (lambda xr, c: lambda e: e.indirect_dma_start(out=xr[:, :], out_offset=None, in_=sc["x2"][:, :],
                                                                                        in_offset=bass.IndirectOffsetOnAxis(ap=idxq[:, c:c + 1], axis=0)))(xr, c),
                                   R=[consts], W=[xrb], chan="gl%d" % (c % 3))
                        else:
                            S_.add("sp", (lambda xr, c: lambda e: e.dma_start(out=xr[:], in_=sc["x2"][c * 128:(c + 1) * 128, :]))(xr, c), W=[xrb], chan="rl%d" % (c % 3))
                        pt, ptb = psB.get()

                        def tr(e, pt=pt, o_=o_):
                            for f in range(8):
                                ins = e.transpose(out=pt[:, f * 128:(f + 1) * 128], in_=o_[:, f * 128:(f + 1) * 128], identity=ident[:])
                            return ins
                        S_.add("pe", tr, R=[o_b, consts], W=[ptb])
                        oT_, oTb = oT.get()
                        S_.add("act", (lambda oT_, pt: lambda e: e.copy(out=oT_[:], in_=pt[:].rearrange("p (k t) -> p k t", k=8)))(oT_, pt), R=[ptb], W=[oTb])
                        for n in range(2):
                            pp, ppb = psF.get()
                            mm_group(pp, ppb, [(pp[:], oT_[:, f, :], wdo[:, f, n * 512:(n + 1) * 512], f == 0, f == 7) for f in range(8)], [oTb, wbo])
                            S_.add("dve", (lambda xr, pp, n: lambda e: e.tensor_tensor(out=xr[:, n * 512:(n + 1) * 512], in0=pp[:], in1=xr[:, n * 512:(n + 1) * 512],
                                                                                      op=ALU.add))(xr, pp, n), R=[ppb, xrb], W=[xrb])
                        S_.add("sp", (lambda xr, c: lambda e: e.dma_start(out=sc["x3"][c * 128:(c + 1) * 128, :], in_=xr[:]))(xr, c),
                               R=[xrb], W=[S_.dbuf("x3", jn, c)], chan="st%d" % (c % 2))
                        if debug:
                            S_.add("sp", (lambda xr, c: lambda e: e.dma_start(out=dbg[jn]["x3"][c * 128:(c + 1) * 128, :], in_=xr[:]))(xr, c),
                                   R=[xrb], W=[S_.dbuf("dx3", jn, c)], chan="dbg")
                for job in jobs:
                    run_job(job)
                S_.flush(); chk(9)

            ffn_phase(1, True)

        except _Stop:
            pass
        sch.stopped = False
        sch.maxops = 10 ** 9
        S_.add("sp", lambda e: e.dma_start(out=scr["a"]["u"][0:1, 0:1], in_=scr["a"]["u"][0:1, 1:2]), chan="fin")
        S_.flush(); chk(10)
        fin = S_.ops[-1]

        with nc.Block() as block:
            @block.sync
            def _(e):
                e.wait_ge(S_.csem["fin"], fin.val)
    return nc


def _rope_tabs(S, dim, scale):
    inv = (10000.0 ** (-np.arange(0, dim, 2, dtype=np.float32) / np.float32(dim))).astype(np.float32)
    ang = np.arange(S, dtype=np.float32)[:, None] * inv[None, :]
    return (np.cos(ang) * scale).astype(np.float32), (np.sin(ang) * scale).astype(np.float32)


def make_inputs(core, SA, SB, NQB, inp, xa, xb, qoff):
    f = lambda a: np.ascontiguousarray(np.asarray(a, dtype=np.float32))
    SM = max(SA, SB)
    c1, s1 = _rope_tabs(SM, 128, 1.0)
    c2, s2 = _rope_tabs(SM, 128, 128 ** -0.5)
    rt = np.concatenate([c1, s1, c2, s2], axis=1)
    ck, sk = _rope_tabs(SM, 64, 1.0)
    cq, sq = _rope_tabs(SM, 64, 0.125)
    dtk = np.concatenate([ck, sk], 1)
    dtq = np.concatenate([cq, sq], 1)
    SQB = NQB * 128
    idx = (qoff + np.arange(NQB)[None, :] * 128 + np.arange(128)[:, None]).astype(np.int32)
    pk = lambda v: f(np.asarray(v).reshape(-1, 128).T)
    m = {
        "xa": f(xa), "xb": f(xb), "idxq": np.ascontiguousarray(idx),
        "rt": f(rt), "dtk": f(dtk), "dtqa": f(dtq[:SA]), "dtqb": f(dtq[qoff:qoff + SQB]),
        "w_in": f(inp["hyb_w_in"][0]), "w_out": f(inp["hyb_w_out"][0]), "w_qkv": f(inp["diff_w_qkv"][0]), "w_do": f(inp["diff_w_out"][0]),
        "nmix": pk(np.asarray(inp["norm_mix"]).reshape(-1)), "nffn": pk(np.asarray(inp["norm_ffn"]).reshape(-1)),
        "nfin": f(np.asarray(inp["norm_final"]).reshape(1, D)),
        "convw3": f(np.asarray(inp["hyb_conv_w"][0]).reshape(3, 512)),
        "convw": f(np.asarray(inp["hyb_conv_w"][0]).reshape(3, 4, 128).transpose(2, 1, 0).reshape(128, 12)),
        "decf": f(np.asarray(inp["hyb_decay_fwd"]).reshape(1, 4)), "decb": f(np.asarray(inp["hyb_decay_bwd"]).reshape(1, 4)),
        "gnw": pk(np.asarray(inp["hyb_gn"]).reshape(-1)),
        "lamv": f(np.concatenate([np.asarray(inp[k]).reshape(-1) for k in ("diff_lq1", "diff_lk1", "diff_lq2", "diff_lk2")]).reshape(1, 256)),
        "subln": f(np.asarray(inp["diff_subln"]).reshape(128, 1)),
    }
    for l in range(2):
        m["w_g%d" % l] = f(inp["ffn_w_gate"][l])
        m["w_u%d" % l] = f(inp["ffn_w_up"][l])
        m["w_d%d" % l] = f(inp["ffn_w_down"][l])
    return m


def kernel(**inp):
    xp = np.asarray(inp["x_prompt"])
    xs = np.asarray(inp["x_sample"])
    SA, SB = xs.shape[1], xp.shape[1]
    NQB = SB // 4 // 128
    nc = build(SA, SB, NQB)
    in_maps = [make_inputs(c, SA, SB, NQB, inp, xs[c], xp[c // 4], (c % 4) * (SB // 4)) for c in range(8)]
    res = run_bass_kernel_spmd(nc, in_maps, core_ids=list(range(8)))
    ys = np.stack([res.results[c]["ya"] for c in range(8)], 0).astype(np.float32)
    yp = np.stack([np.concatenate([res.results[g * 4 + r]["yb"] for r in range(4)], 0) for g in range(2)], 0).astype(np.float32)
    return (yp, ys)
```

```python
import math
from contextlib import ExitStack
import numpy as np
import concourse.bass as bass
import concourse.mybir as mybir
from concourse.bass_utils import run_bass_kernel_spmd

F32 = mybir.dt.float32
BF16 = mybir.dt.bfloat16
I32 = mybir.dt.int32
AF = mybir.ActivationFunctionType
ALU = mybir.AluOpType
AX = mybir.AxisListType

D = 1024
DFF = 2816
NFC = DFF // 128
LAMBDA_INIT = 0.8 - 0.6 * math.exp(-0.3 * 1)
ENG = ("pe", "act", "dve", "pool", "sp")


class Buf:
    __slots__ = ("w", "r", "excl")

    def __init__(self, excl=False):
        self.w = None
        self.r = {}
        self.excl = excl


class Op:
    __slots__ = ("eng", "fn", "deps", "chan", "signal", "val")

    def __init__(self, eng, fn, deps, chan):
        self.eng = eng
        self.fn = fn
        self.deps = deps
        self.chan = chan
        self.signal = chan is not None
        self.val = None


class Sched:
    def __init__(self, nc, es, nchan=40):
        self.nc = nc
        self.ops = []
        self.emitted = 0
        self.last = {}
        self.lastchan = {}
        self.bar = ()
        self.esem = {e: es.enter_context(nc.semaphore("s_" + e)) for e in ENG}
        self.freechan = [es.enter_context(nc.semaphore("c%d" % i)) for i in range(nchan)]
        self.csem = {}
        self.cnt = {e: 0 for e in ENG}
        self.ccnt = {}
        self.waited = {e: {} for e in ENG}
        self.dram = {}

    def dbuf(self, *key):
        b = self.dram.get(key)
        if b is None:
            b = self.dram[key] = Buf()
        return b

    stopped = False
    maxops = 10 ** 9
    names = {}

    def add(self, eng, fn, R=(), W=(), chan=None):
        if self.stopped:
            return -1
        if len(self.ops) >= self.maxops:
            self.stopped = True
            return -1
        deps = set(self.bar)
        key0 = chan if chan is not None else eng
        for b in R:
            if b.w is not None:
                deps.add(b.w)
            if b.excl:
                deps.update(v for k, v in b.r.items() if k != key0)
        for b in W:
            if b.w is not None:
                deps.add(b.w)
            deps.update(b.r.values())
        idx = len(self.ops)
        if chan is not None:
            if chan in self.lastchan:
                deps.add(self.lastchan[chan])
            self.lastchan[chan] = idx
        self.ops.append(Op(eng, fn, deps, chan))
        key = chan if chan is not None else eng
        for b in R:
            b.r[key] = idx
        for b in W:
            b.w = idx
            b.r = {}
        self.last[eng] = idx
        return idx

    def _event(self, op):
        if op.chan is not None:
            return self.csem[op.chan], op.val
        return self.esem[op.eng], op.val

    def flush(self):
        nc = self.nc
        lo = self.emitted
        if lo == len(self.ops):
            return
        ops = self.ops
        for i in range(lo, len(ops)):
            for d in ops[i].deps:
                if d >= lo:
                    od = ops[d]
                    if od.chan is None and od.eng == "pe" and ops[i].eng == "pe" and ops[i].chan is None:
                        continue
                    od.signal = True
        for e, i in self.last.items():
            ops[i].signal = True
        for i in range(lo, len(ops)):
            op = ops[i]
            if op.chan is not None:
                if op.chan not in self.csem:
                    self.csem[op.chan] = self.freechan.pop()
                    self.ccnt[op.chan] = 0
                self.ccnt[op.chan] += 16
                op.val = self.ccnt[op.chan]
            elif op.signal:
                self.cnt[op.eng] += 1
                op.val = self.cnt[op.eng]
        per = {e: [] for e in ENG}
        for i in range(lo, len(ops)):
            per[ops[i].eng].append(i)

        def emit(engname, e):
            waited = self.waited[engname]
            for i in per[engname]:
                op = ops[i]
                for d in sorted(op.deps):
                    od = ops[d]
                    if d < lo and d not in self.bar:
                        continue
                    if od.chan is None and od.eng == "pe" and engname == "pe" and op.chan is None:
                        continue
                    sem, val = self._event(od)
                    k = id(sem)
                    if waited.get(k, 0) < val:
                        e.wait_ge(sem, val)
                        waited[k] = val
                ins = op.fn(e)
                if op.signal:
                    sem, val = self._event(op)
                    ins.then_inc(sem, 16 if op.chan is not None else 1)

        with nc.Block() as block:
            @block.tensor
            def _(e):
                emit("pe", e)

            @block.scalar
            def _(e):
                emit("act", e)

            @block.vector
            def _(e):
                emit("dve", e)

            @block.gpsimd
            def _(e):
                emit("pool", e)

            @block.sync
            def _(e):
                emit("sp", e)
        self.emitted = len(ops)
        self.bar = tuple(set(list(self.last.values()) + list(self.lastchan.values())))
        for i in self.bar:
            assert ops[i].signal


def staggered(fn, chunks, delay, n=2):
    def thread(idx):
        for k in idx:
            yield from fn(k, chunks[k])
    g = [thread(range(t, len(chunks), n)) for t in range(n)]
    alive = [True] * n
    step = 0
    while any(alive):
        for i in range(n):
            if alive[i] and step >= i * delay:
                try:
                    next(g[i])
                except StopIteration:
                    alive[i] = False
        step += 1


class Pool:
    def __init__(self, es, alloc, name, shape, dt, n):
        self.t = [es.enter_context(alloc("%s%d" % (name, i), shape, dt)) for i in range(n)]
        self.b = [Buf(excl=(alloc.__name__ == 'pst')) for _ in range(n)]
        self.i = 0

    def get(self):
        k = self.i % len(self.t)
        self.i += 1
        return self.t[k], self.b[k]


class _Stop(Exception):
    pass


def build(SA, SB, NQB, debug=False, nph=99):
    nc = bass.Bass("TRN2", target_bir_lowering=False)
    dt_in = lambda n, s, d=F32: nc.dram_tensor(n, s, d, kind="ExternalInput").ap()
    dt_scr = lambda n, s, d: nc.dram_tensor(n, s, d, kind="Internal").ap()
    SQB = NQB * 128
    jobs = [dict(n="a", S=SA, SQ=SA, own=False), dict(n="b", S=SB, SQ=SQB, own=True)]
    xin_d = {"a": dt_in("xa", [SA, D]), "b": dt_in("xb", [SB, D])}
    idxq_d = dt_in("idxq", [128, NQB], I32)
    SM = max(SA, SB)
    rt_d = dt_in("rt", [SM, 256])
    dtk_d = dt_in("dtk", [SM, 64])
    dtq_d = {"a": dt_in("dtqa", [SA, 64]), "b": dt_in("dtqb", [SQB, 64])}
    w_in_d = dt_in("w_in", [D, 3584])
    w_out_d = dt_in("w_out", [D, D])
    w_qkv_d = dt_in("w_qkv", [D, 3072])
    w_do_d = dt_in("w_do", [D, D])
    w_g_d = [dt_in("w_g%d" % l, [D, DFF]) for l in range(2)]
    w_u_d = [dt_in("w_u%d" % l, [D, DFF]) for l in range(2)]
    w_d_d = [dt_in("w_d%d" % l, [DFF, D]) for l in range(2)]
    nmix_d = dt_in("nmix", [128, 16])
    nffn_d = dt_in("nffn", [128, 16])
    nfin_d = dt_in("nfin", [1, D])
    convw_d = dt_in("convw", [128, 12])
    convw3_d = dt_in("convw3", [3, 512])
    decf_d = dt_in("decf", [1, 4])
    decb_d = dt_in("decb", [1, 4])
    gn_d = dt_in("gnw", [128, 4])
    lam_d = dt_in("lamv", [1, 256])
    subln_d = dt_in("subln", [128, 1])
    y_d = {j["n"]: nc.dram_tensor("y" + j["n"], [j["SQ"], D], F32, kind="ExternalOutput").ap() for j in jobs}
    scr = {}
    for j in jobs:
        n, S, SQ = j["n"], j["S"], j["SQ"]
        scr[n] = dict(
            u=dt_scr("u" + n, [S + 2, 512], F32),
            kr=dt_scr("kr" + n, [S, 512], BF16),
            vr=dt_scr("vr" + n, [S, 512], BF16),
            sb=dt_scr("sb" + n, [S // 128, 128, 512], BF16),
            x1=dt_scr("x1" + n, [S, D], F32),
            x2=dt_scr("x2" + n, [S, D], F32),
            QT=dt_scr("QT" + n, [8, 128, SQ], BF16),
            KT=dt_scr("KT" + n, [8, 128, S], BF16),
            V=dt_scr("V" + n, [8, S, 128], BF16),
            O=dt_scr("O" + n, [SQ, D], BF16),
            x3=dt_scr("x3" + n, [SQ, D], F32),
        )
    dbg = {}
    if debug:
        for j in jobs:
            n, S, SQ = j["n"], j["S"], j["SQ"]
            dbg[n] = dict(
                x1=nc.dram_tensor("dbg_x1" + n, [S, D], F32, kind="ExternalOutput").ap(),
                x2=nc.dram_tensor("dbg_x2" + n, [S, D], F32, kind="ExternalOutput").ap(),
                x3=nc.dram_tensor("dbg_x3" + n, [SQ, D], F32, kind="ExternalOutput").ap(),
            )

    with ExitStack() as es:
        sch = Sched(nc, es)
        import os
        sch.maxops = int(os.environ.get('KSTOP', 10 ** 9))
        S_ = sch
        _uid = [0]

        def sb(name, shape, dt):
            _uid[0] += 1
            return nc.sbuf_tensor("s%d_%s" % (_uid[0], name), shape, dt)

        def pst(name, shape, dt):
            _uid[0] += 1
            return nc.psum_tensor("p%d_%s" % (_uid[0], name), shape, dt)
        ident = es.enter_context(sb("ident", [128, 128], BF16))
        epsN = es.enter_context(sb("epsN", [128, 4], F32))
        nmix = es.enter_context(sb("nmix", [128, 16], F32))
        nffn = es.enter_context(sb("nffn", [128, 16], F32))
        gnw = es.enter_context(sb("gnw", [128, 4], F32))
        subl = es.enter_context(sb("subl", [128, 1], F32))
        convw = es.enter_context(sb("convw", [128, 12], F32))
        idxq = es.enter_context(sb("idxq", [128, NQB], I32))
        lamt = es.enter_context(sb("lamt", [128, 8], F32))
        consts = Buf()

        def chk(k):
            if nph == k:
                sch.stopped = True

        try:
            with ExitStack() as ps:
                tmpa = ps.enter_context(sb("tmpa", [128, 128], F32))
                lamv = ps.enter_context(sb("lamv", [128, 256], F32))
                lamp = ps.enter_context(sb("lamp", [128, 128], F32))
                S_.add("pool", lambda e: e.iota(tmpa[:], pattern=[[1, 128]], base=0, channel_multiplier=-1,
                                                allow_small_or_imprecise_dtypes=True), W=[consts])
                S_.add("dve", lambda e: e.tensor_single_scalar(out=ident[:], in_=tmpa[:], scalar=0.0, op=ALU.is_equal),
                       R=[consts], W=[consts])
                S_.add("dve", lambda e: e.memset(epsN[:, 0:1], 1e-6), W=[consts])
                S_.add("dve", lambda e: e.memset(epsN[:, 1:3], 1e-5), W=[consts])
                S_.add("dve", lambda e: e.memset(epsN[:, 3:4], 0.0), W=[consts])
                for t, d in ((nmix, nmix_d), (nffn, nffn_d), (gnw, gn_d), (subl, subln_d), (convw, convw_d), (idxq, idxq_d)):
                    S_.add("sp", (lambda t, d: lambda e: e.dma_start(out=t[:], in_=d[:, :]))(t, d), W=[consts], chan="set")
                S_.add("sp", lambda e: e.dma_start(out=lamv[:], in_=lam_d.partition_broadcast(128)), W=[consts], chan="set")
                S_.add("dve", lambda e: e.tensor_single_scalar(out=subl[:], in_=subl[:], scalar=1.0 - LAMBDA_INIT, op=ALU.mult),
                       R=[consts], W=[consts])
                S_.add("dve", lambda e: e.tensor_tensor(out=lamp[:, 0:64], in0=lamv[:, 0:64], in1=lamv[:, 64:128], op=ALU.mult),
                       R=[consts], W=[consts])
                S_.add("dve", lambda e: e.tensor_tensor(out=lamp[:, 64:128], in0=lamv[:, 128:192], in1=lamv[:, 192:256], op=ALU.mult),
                       R=[consts], W=[consts])
                S_.add("dve", lambda e: e.reduce_sum(out=lamt[:, 1:2], in_=lamp[:, 0:64], axis=AX.X), R=[consts], W=[consts])
                S_.add("dve", lambda e: e.reduce_sum(out=lamt[:, 2:3], in_=lamp[:, 64:128], axis=AX.X), R=[consts], W=[consts])
                S_.add("act", lambda e: e.activation(out=lamt[:, 3:5], in_=lamt[:, 1:3], func=AF.Exp), R=[consts], W=[consts])
                S_.add("dve", lambda e: e.tensor_tensor(out=lamt[:, 5:6], in0=lamt[:, 4:5], in1=lamt[:, 3:4], op=ALU.subtract),
                       R=[consts], W=[consts])
                S_.add("dve", lambda e: e.tensor_single_scalar(out=lamt[:, 0:1], in_=lamt[:, 5:6], scalar=-LAMBDA_INIT, op=ALU.add),
                       R=[consts], W=[consts])
                S_.flush(); chk(0)

            def load_weight(dst, src, KC, C, rows, stg, sidx):
                srcv = src.rearrange("(kc p) c -> p kc c", p=128)
                wb = Buf()
                for kc in range(KC):
                    for c0 in range(0, C, 1024):
                        cw = min(1024, C - c0)
                        st, stb = stg.get()
                        S_.add("sp", (lambda st, kc, c0, cw: lambda e: e.dma_start(out=st[:, 0:cw], in_=srcv[:, kc, c0:c0 + cw]))(st, kc, c0, cw),
                               W=[stb], chan="wl%d" % (sidx[0] % 3))
                        eng = ("dve", "pool")[sidx[0] % 2]
                        sidx[0] += 1
                        r = rows(kc) if rows is not None else None
                        if r is None:
                            S_.add(eng, (lambda st, kc, c0, cw: lambda e: e.tensor_copy(out=dst[:, kc, c0:c0 + cw], in_=st[:, 0:cw]))(st, kc, c0, cw),
                                   R=[stb, consts], W=[wb])
                        else:
                            S_.add(eng, (lambda st, kc, c0, cw, r: lambda e: e.tensor_scalar(
                                out=dst[:, kc, c0:c0 + cw], in0=st[:, 0:cw], scalar1=r, scalar2=None, op0=ALU.mult))(st, kc, c0, cw, r),
                                   R=[stb, consts], W=[wb])
                return wb

            class NormT:
                def __init__(self, es_, psB, n=2):
                    self.junk = Pool(es_, sb, "njunk", [128, 1024], BF16, n)
                    self.xs = Pool(es_, sb, "nxs", [128, 1024], BF16, n)
                    self.st = Pool(es_, sb, "nst", [128, 4], F32, 4)
                    self.psB = psB

                def run(self, x, xb, dst, dstb, dst_is_write=True):
                    jk, jkb = self.junk.get()
                    xs, xsb = self.xs.get()
                    st, stb = self.st.get()
                    S_.add("act", lambda e: e.activation(out=jk[:], in_=x, func=AF.Square, accum_out=st[:, 0:1]), R=[xb], W=[jkb, stb])
                    S_.add("act", lambda e: e.activation(out=st[:, 1:2], in_=st[:, 0:1], func=AF.Sqrt, bias=epsN[:, 0:1], scale=1.0 / D),
                           R=[stb, consts], W=[stb])
                    S_.add("dve", lambda e: e.reciprocal(out=st[:, 2:3], in_=st[:, 1:2]), R=[stb], W=[stb])
                    S_.add("dve", lambda e: e.tensor_scalar(out=xs[:], in0=x, scalar1=st[:, 2:3], scalar2=None, op0=ALU.mult),
                           R=[xb, stb], W=[xsb])
                    pt, ptb = self.psB.get()

                    def tr(e):
                        for kc in range(8):
                            ins = e.transpose(out=pt[:, kc * 128:(kc + 1) * 128], in_=xs[:, kc * 128:(kc + 1) * 128], identity=ident[:])
                        return ins
                    S_.add("pe", tr, R=[xsb, consts], W=[ptb])
                    S_.add("act", lambda e: e.copy(out=dst, in_=pt[:].rearrange("p (k t) -> p k t", k=8)), R=[ptb], W=[dstb])
                    return st, stb

            def rope(eng1, eng2, src, srcb, H, half, cos, sin, tb, A, Ab, T, Tb, out, outb):
                v4 = lambda ap: ap.rearrange("p (h two d) -> p h two d", h=H, two=2)
                cb4 = cos.unsqueeze(1).unsqueeze(1).broadcast_to([128, H, 2, half])
                sb3 = sin.unsqueeze(1).broadcast_to([128, H, half])
                S_.add(eng1, lambda e: e.tensor_tensor(out=v4(A), in0=v4(src), in1=cb4, op=ALU.mult), R=[srcb, tb], W=[Ab])
                S_.add(eng2, lambda e: e.tensor_tensor(out=v4(T)[:, :, 0, :], in0=v4(src)[:, :, 1, :], in1=sb3, op=ALU.mult), R=[srcb, tb], W=[Tb])
                S_.add(eng2, lambda e: e.tensor_tensor(out=v4(T)[:, :, 1, :], in0=v4(src)[:, :, 0, :], in1=sb3, op=ALU.mult), R=[srcb, tb], W=[Tb])
                S_.add(eng1, lambda e: e.tensor_tensor(out=v4(out)[:, :, 0, :], in0=v4(A)[:, :, 0, :], in1=v4(T)[:, :, 0, :], op=ALU.subtract),
                       R=[Ab, Tb], W=[outb])
                S_.add(eng1, lambda e: e.tensor_tensor(out=v4(out)[:, :, 1, :], in0=v4(A)[:, :, 1, :], in1=v4(T)[:, :, 1, :], op=ALU.add),
                       R=[Ab, Tb], W=[outb])

            def mm_group(ps, psb, pairs, R):
                def f(e):
                    for (o, l, r, s0, s1) in pairs:
                        ins = e.matmul(o, lhsT=l, rhs=r, start=s0, stop=s1)
                    return ins
                S_.add("pe", f, R=R, W=[psb])

            with ExitStack() as ps:
                psF = Pool(ps, pst, "psF", [128, 512], F32, 6)
                psB = Pool(ps, pst, "psB", [128, 1024], BF16, 2)
                w_in = ps.enter_context(sb("w_in", [128, 8, 3584], BF16))
                w_out = ps.enter_context(sb("w_out", [128, 8, 1024], BF16))
                lg = ps.enter_context(sb("lg", [128, 8], F32))
                tabs = {k: ps.enter_context(sb("tab" + k, [128, 512], F32)) for k in ("DT", "QF", "QB", "KF", "KB", "CF", "CB")}
                tb_ = Buf()
                cwt = ps.enter_context(sb("cwt", [128, 3, 512], F32))
                for t in range(3):
                    S_.add("sp", (lambda t: lambda e: e.dma_start(out=cwt[:, t, :], in_=convw3_d[t:t + 1, :].partition_broadcast(128)))(t), W=[consts], chan="set")
                ws = ExitStack()
                stg = Pool(ws, sb, "wstg", [128, 1024], F32, 3)
                sidx = [0]
                wb_in = load_weight(w_in, w_in_d, 8, 3584, lambda kc: nmix[:, kc:kc + 1], stg, sidx)
                wb_out = load_weight(w_out, w_out_d, 8, 1024, lambda kc: (gnw[:, kc - 4:kc - 3] if kc >= 4 else None), stg, sidx)
                with ExitStack() as ts:
                    io = {k: ts.enter_context(sb("io" + k, [128, 128], F32)) for k in ("rel", "relp", "reln", "mp", "mn", "i1", "ib", "kf", "kb", "c128", "e1", "e2")}
                    S_.add("sp", lambda e: e.dma_start(out=lg[:, 0:4], in_=decf_d.partition_broadcast(128)), W=[tb_], chan="set")
                    S_.add("sp", lambda e: e.dma_start(out=lg[:, 4:8], in_=decb_d.partition_broadcast(128)), W=[tb_], chan="set")
                    S_.add("act", lambda e: e.activation(out=lg[:], in_=lg[:], func=AF.Exp), R=[tb_], W=[tb_])
                    S_.add("dve", lambda e: e.tensor_single_scalar(out=lg[:], in_=lg[:], scalar=-1.0, op=ALU.mult), R=[tb_], W=[tb_])
                    io_ = lambda k, pat, base, cm: S_.add("pool", lambda e: e.iota(io[k][:], pattern=pat, base=base, channel_multiplier=cm,
                                                                                    allow_small_or_imprecise_dtypes=True), W=[tb_])
                    io_("rel", [[1, 128]], 0, -1)
                    io_("i1", [[1, 128]], 1, 0)
                    io_("ib", [[-1, 128]], 128, 0)
                    io_("kf", [[0, 128]], 127, -1)
                    io_("kb", [[0, 128]], 0, 1)
                    io_("c128", [[0, 128]], 128, 0)
                    S_.add("dve", lambda e: e.tensor_single_scalar(out=io["relp"][:], in_=io["rel"][:], scalar=0.0, op=ALU.max), R=[tb_], W=[tb_])
                    S_.add("dve", lambda e: e.tensor_scalar(out=io["reln"][:], in0=io["rel"][:], scalar1=-1.0, scalar2=0.0, op0=ALU.mult, op1=ALU.max),
                           R=[tb_], W=[tb_])
                    S_.add("dve", lambda e: e.tensor_single_scalar(out=io["mp"][:], in_=io["rel"][:], scalar=0.0, op=ALU.is_ge), R=[tb_], W=[tb_])
                    S_.add("dve", lambda e: e.tensor_single_scalar(out=io["mn"][:], in_=io["rel"][:], scalar=0.0, op=ALU.is_lt), R=[tb_], W=[tb_])
                    for h in range(4):
                        hs = slice(h * 128, (h + 1) * 128)
                        ex = lambda dst, src, col: S_.add("act", lambda e: e.activation(out=dst, in_=src, func=AF.Exp, scale=lg[:, col:col + 1]),
                                                          R=[tb_], W=[tb_])
                        ex(io["e1"][:], io["relp"][:], h)
                        ex(io["e2"][:], io["reln"][:], 4 + h)
                        S_.add("dve", lambda e: e.tensor_tensor(out=io["e1"][:], in0=io["e1"][:], in1=io["mp"][:], op=ALU.mult), R=[tb_], W=[tb_])
                        S_.add("dve", lambda e: e.tensor_tensor(out=io["e2"][:], in0=io["e2"][:], in1=io["mn"][:], op=ALU.mult), R=[tb_], W=[tb_])
                        S_.add("dve", (lambda hs: lambda e: e.tensor_tensor(out=tabs["DT"][:, hs], in0=io["e1"][:], in1=io["e2"][:], op=ALU.add))(hs),
                               R=[tb_], W=[tb_])
                        ex(tabs["QF"][:, hs], io["i1"][:], h)
                        ex(tabs["QB"][:, hs], io["ib"][:], 4 + h)
                        ex(tabs["KF"][:, hs], io["kf"][:], h)
                        ex(tabs["KB"][:, hs], io["kb"][:], 4 + h)
                        ex(tabs["CF"][:, hs], io["c128"][:], h)
                        ex(tabs["CB"][:, hs], io["c128"][:], 4 + h)
                    S_.flush(); chk(1)
                ws.close()
                nt = NormT(ps, psB)
                xin = Pool(ps, sb, "xin", [128, 1024], F32, 3)
                xnT = Pool(ps, sb, "xnT", [128, 8, 128], BF16, 2)
                ropt = Pool(ps, sb, "ropt", [128, 256], F32, 3)
                rA = Pool(ps, sb, "rA", [128, 512], F32, 2)
                rT = Pool(ps, sb, "rT", [128, 512], F32, 2)
                kr = Pool(ps, sb, "kr", [128, 512], BF16, 3)
                vv = Pool(ps, sb, "vv", [128, 512], BF16, 3)
                sbl = Pool(ps, sb, "sbl", [128, 512], BF16, 2)
                kd = Pool(ps, sb, "kd", [128, 512], BF16, 2)
                acS = Pool(ps, sb, "acS", [128, 512], F32, 2)
                uP = Pool(ps, sb, "uP", [128, 512], F32, 2)
                u3p = Pool(ps, sb, "u3p", [128, 512], F32, 6)
                yall = Pool(ps, sb, "yall", [128, 1024], BF16, 2)
                S32 = ps.enter_context(sb("S32", [128, 512], F32))
                S16 = Pool(ps, sb, "S16", [128, 512], BF16, 2)
                qr = Pool(ps, sb, "qr", [128, 512], BF16, 2)
                qT = Pool(ps, sb, "qT", [128, 512], BF16, 2)
                qfT = Pool(ps, sb, "qfT", [128, 512], BF16, 2)
                qbT = Pool(ps, sb, "qbT", [128, 512], BF16, 2)
                kT = Pool(ps, sb, "kT", [128, 512], BF16, 2)
                AT = Pool(ps, sb, "AT", [128, 512], BF16, 2)
                gst = Pool(ps, sb, "gst", [128, 4, 6], F32, 2)
                gmv = Pool(ps, sb, "gmv", [128, 4, 2], F32, 2)
                grs = Pool(ps, sb, "grs", [128, 8], F32, 2)
                on = Pool(ps, sb, "on", [128, 512], F32, 2)
                sg = Pool(ps, sb, "sg", [128, 512], F32, 2)
                abS = Pool(ps, sb, "abS", [128, 512], F32, 2)
                yT = Pool(ps, sb, "yT", [128, 8, 128], BF16, 2)
                x1t = Pool(ps, sb, "x1t", [128, 1024], F32, 2)
                S32b = Buf()

                def run_job(job):
                    jn, S = job["n"], job["S"]
                    NCH = S // 128
                    xd = xin_d[jn]
                    sc = scr[jn]
                    zt, ztb = uP.get()
                    S_.add("pool", lambda e: e.memset(zt[0:1, :], 0.0), W=[ztb])
                    S_.add("sp", lambda e: e.dma_start(out=sc["u"][0:1, :], in_=zt[0:1, :]), R=[ztb], W=[S_.dbuf("uT", jn, -1)], chan="st0")
                    S_.add("sp", lambda e: e.dma_start(out=sc["u"][S + 1:S + 2, :], in_=zt[0:1, :]), R=[ztb], W=[S_.dbuf("uT", jn, -2)], chan="st0")
                    S_.add("dve", lambda e: e.memset(S32[:], 0.0), W=[S32b])
                    stv = list(S16.get())
                    tick = [0]
                    S_.add("pool", (lambda s16: lambda e: e.memset(s16[:], 0.0))(stv[0]), W=[stv[1]])

                    def pre_chunk(k, c):
                        x, xb = xin.get()
                        S_.add("sp", (lambda x, c: lambda e: e.dma_start(out=x[:], in_=xd[c * 128:(c + 1) * 128, :]))(x, c), W=[xb], chan="xl%d" % (c % 3))
                        rtb_t, rtb = ropt.get()
                        S_.add("sp", (lambda t, c: lambda e: e.dma_start(out=t[:], in_=rt_d[c * 128:(c + 1) * 128, :]))(rtb_t, c), W=[rtb], chan="rl%d" % (c % 3))
                        xt_, xtb = xnT.get()
                        nt.run(x[:], xb, xt_[:], xtb)
                        yield
                        pk, pkb = psF.get()
                        mm_group(pk, pkb, [(pk[:], xt_[:, kc, :], w_in[:, kc, 2048:2560], kc == 0, kc == 7) for kc in range(8)], [xtb, wb_in])
                        pv, pvb = psF.get()
                        mm_group(pv, pvb, [(pv[:], xt_[:, kc, :], w_in[:, kc, 2560:3072], kc == 0, kc == 7) for kc in range(8)], [xtb, wb_in])
                        pc, pcb = psF.get()
                        mm_group(pc, pcb, [(pc[:], xt_[:, kc, :], w_in[:, kc, 512:1024], kc == 0, kc == 7) for kc in range(8)], [xtb, wb_in])
                        ph, phb = psF.get()
                        mm_group(ph, phb, [(ph[:], xt_[:, kc, :], w_in[:, kc, 1024:1536], kc == 0, kc == 7) for kc in range(8)], [xtb, wb_in])
                        ac, acb = acS.get()
                        S_.add("act", (lambda ac, pc: lambda e: e.copy(out=ac[:], in_=pc[:]))(ac, pc), R=[pcb], W=[acb])
                        u, ub = uP.get()
                        S_.add("dve", (lambda u, ac, ph: lambda e: e.tensor_tensor(out=u[:], in0=ac[:], in1=ph[:], op=ALU.mult))(u, ac, ph),
                               R=[acb, phb], W=[ub])
                        S_.add("sp", (lambda u, c: lambda e: e.dma_start(out=sc["u"][1 + c * 128:1 + (c + 1) * 128, :], in_=u[:]))(u, c),
                               R=[ub], W=[S_.dbuf("uT", jn, c)], chan="st%d" % (c % 2))
                        A, Ab = rA.get()
                        T, Tb = rT.get()
                        k_, kb_ = kr.get()
                        rope("dve", "pool" if False else "dve", pk[:], pkb, 4, 64, rtb_t[:, 128:192], rtb_t[:, 192:256], rtb, A[:], Ab, T[:], Tb, k_[:], kb_)
                        v_, vb_ = vv.get()
                        S_.add("act", (lambda v_, pv: lambda e: e.copy(out=v_[:], in_=pv[:]))(v_, pv), R=[pvb], W=[vb_])
                        S_.add("sp", (lambda k_, c: lambda e: e.dma_start(out=sc["kr"][c * 128:(c + 1) * 128, :], in_=k_[:]))(k_, c),
                               R=[kb_], W=[S_.dbuf("kr", jn, c)], chan="st%d" % (c % 2))
                        S_.add("sp", (lambda v_, c: lambda e: e.dma_start(out=sc["vr"][c * 128:(c + 1) * 128, :], in_=v_[:]))(v_, c),
                               R=[vb_], W=[S_.dbuf("vr", jn, c)], chan="st%d" % (c % 2))
                        yield
                        while tick[0] != k:
                            yield
                        S_.add("sp", (lambda s16, c: lambda e: e.dma_start(out=sc["sb"][c, :, :], in_=s16[:]))(stv[0], c),
                               R=[stv[1]], W=[S_.dbuf("sb", jn, c)], chan="st%d" % (c % 2))
                        kd_, kdb = kd.get()
                        S_.add("pool", (lambda kd_, k_: lambda e: e.tensor_tensor(out=kd_[:], in0=k_[:], in1=tabs["KB"][:], op=ALU.mult))(kd_, k_),
                               R=[kb_, tb_], W=[kdb])
                        pS, pSb = psF.get()
                        mm_group(pS, pSb, [(pS[:, h * 128:(h + 1) * 128], kd_[:, h * 128:(h + 1) * 128], v_[:, h * 128:(h + 1) * 128], True, True) for h in range(4)],
                                 [kdb, vb_])
                        S_.add("pool", lambda e: e.tensor_tensor(out=S32[:], in0=S32[:], in1=tabs["CB"][:], op=ALU.mult), R=[S32b, tb_], W=[S32b])
                        S_.add("dve", (lambda pS: lambda e: e.tensor_tensor(out=S32[:], in0=S32[:], in1=pS[:], op=ALU.add))(pS), R=[S32b, pSb], W=[S32b])
                        stv[0], stv[1] = S16.get()
                        S_.add("act", (lambda s16: lambda e: e.copy(out=s16[:], in_=S32[:]))(stv[0]), R=[S32b], W=[stv[1]])
                        tick[0] = k + 1
                        yield
                    staggered(pre_chunk, list(range(NCH - 1, -1, -1)), 4)
                    S_.add("dve", lambda e: e.memset(S32[:], 0.0), W=[S32b])
                    stv[0], stv[1] = S16.get()
                    tick[0] = 0
                    S_.add("pool", (lambda s16: lambda e: e.memset(s16[:], 0.0))(stv[0]), W=[stv[1]])

                    def main_chunk(k, c):
                        x, xb = xin.get()
                        S_.add("sp", (lambda x, c: lambda e: e.dma_start(out=x[:], in_=xd[c * 128:(c + 1) * 128, :]))(x, c), W=[xb], chan="xl%d" % (c % 3))
                        rtb_t, rtb = ropt.get()
                        S_.add("sp", (lambda t, c: lambda e: e.dma_start(out=t[:], in_=rt_d[c * 128:(c + 1) * 128, :]))(rtb_t, c), W=[rtb], chan="rl%d" % (c % 3))
                        k_, kb_ = kr.get()
                        S_.add("sp", (lambda k_, c: lambda e: e.dma_start(out=k_[:], in_=sc["kr"][c * 128:(c + 1) * 128, :]))(k_, c),
                               R=[S_.dbuf("kr", jn, c)], W=[kb_], chan="kl%d" % (c % 3))
                        v_, vb_ = vv.get()
                        S_.add("sp", (lambda v_, c: lambda e: e.dma_start(out=v_[:], in_=sc["vr"][c * 128:(c + 1) * 128, :]))(v_, c),
                               R=[S_.dbuf("vr", jn, c)], W=[vb_], chan="vl%d" % (c % 3))
                        sl, slb = sbl.get()
                        S_.add("sp", (lambda sl, c: lambda e: e.dma_start(out=sl[:], in_=sc["sb"][c, :, :]))(sl, c),
                               R=[S_.dbuf("sb", jn, c)], W=[slb], chan="sl%d" % (c % 2))
                        u3 = [u3p.get() for _ in range(3)]
                        urd = [S_.dbuf("uT", jn, cc) for cc in (c - 1, c, c + 1) if 0 <= cc < NCH] + [S_.dbuf("uT", jn, -1), S_.dbuf("uT", jn, -2)]
                        for s_ in range(3):
                            S_.add("sp", (lambda t, c, s_: lambda e: e.dma_start(out=t[:], in_=sc["u"][c * 128 + s_:c * 128 + s_ + 128, :]))(u3[s_][0], c, s_),
                                   R=urd, W=[u3[s_][1]], chan="ul%d" % ((c * 3 + s_) % 3))
                        yield
                        xt_, xtb = xnT.get()
                        nt.run(x[:], xb, xt_[:], xtb)
                        yield
                        pq, pqb = psF.get()
                        mm_group(pq, pqb, [(pq[:], xt_[:, kc, :], w_in[:, kc, 1536:2048], kc == 0, kc == 7) for kc in range(8)], [xtb, wb_in])
                        pg, pgb = psF.get()
                        mm_group(pg, pgb, [(pg[:], xt_[:, kc, :], w_in[:, kc, 3072:3584], kc == 0, kc == 7) for kc in range(8)], [xtb, wb_in])
                        pab, pabb = psF.get()
                        mm_group(pab, pabb, [(pab[:], xt_[:, kc, :], w_in[:, kc, 0:512], kc == 0, kc == 7) for kc in range(8)], [xtb, wb_in])
                        sg_, sgb = sg.get()
                        S_.add("act", (lambda sg_, pg: lambda e: e.activation(out=sg_[:], in_=pg[:], func=AF.Silu))(sg_, pg), R=[pgb], W=[sgb])
                        ab_, abb = abS.get()
                        S_.add("act", (lambda ab_, pab: lambda e: e.copy(out=ab_[:], in_=pab[:]))(ab_, pab), R=[pabb], W=[abb])
                        ya_, yab = yall.get()
                        (u0, u0b), (u1, u1b), (u2, u2b) = u3
                        for (ut, utb, tap) in ((u1, u1b, 1), (u0, u0b, 0), (u2, u2b, 2)):
                            S_.add("pool", (lambda ut, tap: lambda e: e.tensor_tensor(out=ut[:], in0=ut[:], in1=cwt[:, tap, :], op=ALU.mult))(ut, tap),
                                   R=[utb, consts], W=[utb])
                        S_.add("dve", (lambda u1, u0: lambda e: e.tensor_tensor(out=u1[:], in0=u1[:], in1=u0[:], op=ALU.add))(u1, u0), R=[u1b, u0b], W=[u1b])
                        S_.add("dve", (lambda u1, u2: lambda e: e.tensor_tensor(out=u1[:], in0=u1[:], in1=u2[:], op=ALU.add))(u1, u2), R=[u1b, u2b], W=[u1b])
                        S_.add("pool", (lambda ya_, u1, ab_: lambda e: e.tensor_tensor(out=ya_[:, 0:512], in0=u1[:], in1=ab_[:], op=ALU.mult))(ya_, u1, ab_),
                               R=[u1b, abb], W=[yab])
                        yield
                        A, Ab = rA.get()
                        T, Tb = rT.get()
                        q_, qb_ = qr.get()
                        rope("dve", "dve", pq[:], pqb, 4, 64, rtb_t[:, 0:64], rtb_t[:, 64:128], rtb, A[:], Ab, T[:], Tb, q_[:], qb_)
                        yield
                        pt, ptb = psB.get()

                        def trqk(e, pt=pt, q_=q_, k_=k_):
                            for h in range(4):
                                e.transpose(out=pt[:, h * 128:(h + 1) * 128], in_=q_[:, h * 128:(h + 1) * 128], identity=ident[:])
                            for h in range(4):
                                ins = e.transpose(out=pt[:, 512 + h * 128:512 + (h + 1) * 128], in_=k_[:, h * 128:(h + 1) * 128], identity=ident[:])
                            return ins
                        S_.add("pe", trqk, R=[qb_, kb_, consts], W=[ptb])
                        qT_, qTb = qT.get()
                        qf_, qfb = qfT.get()
                        qb2, qbb = qbT.get()
                        kT_, kTb = kT.get()
                        S_.add("act", (lambda o, pt: lambda e: e.copy(out=o[:], in_=pt[:, 0:512]))(qT_, pt), R=[ptb], W=[qTb])
                        S_.add("act", (lambda o, pt: lambda e: e.copy(out=o[:], in_=pt[:, 512:1024]))(kT_, pt), R=[ptb], W=[kTb])
                        S_.add("dve", (lambda o, pt: lambda e: e.tensor_tensor(out=o[:], in0=pt[:, 0:512], in1=tabs["QF"][:], op=ALU.mult))(qf_, pt),
                               R=[ptb, tb_], W=[qfb])
                        S_.add("dve", (lambda o, pt: lambda e: e.tensor_tensor(out=o[:], in0=pt[:, 0:512], in1=tabs["QB"][:], op=ALU.mult))(qb2, pt),
                               R=[ptb, tb_], W=[qbb])
                        yield
                        psc, pscb = psF.get()
                        hs = lambda h: slice(h * 128, (h + 1) * 128)
                        mm_group(psc, pscb, [(psc[:, hs(h)], kT_[:, hs(h)], qT_[:, hs(h)], True, True) for h in range(4)], [kTb, qTb])
                        at, atb = AT.get()
                        S_.add("dve", (lambda at, psc: lambda e: e.tensor_tensor(out=at[:], in0=psc[:], in1=tabs["DT"][:], op=ALU.mult))(at, psc),
                               R=[pscb, tb_], W=[atb])
                        yield
                        while tick[0] != k:
                            yield
                        s16, s16b = stv
                        po, pob = psF.get()
                        prs = []
                        for h in range(4):
                            prs += [(po[:, hs(h)], at[:, hs(h)], v_[:, hs(h)], True, False),
                                    (po[:, hs(h)], qf_[:, hs(h)], s16[:, hs(h)], False, False),
                                    (po[:, hs(h)], qb2[:, hs(h)], sl[:, hs(h)], False, True)]
                        mm_group(po, pob, prs, [atb, vb_, qfb, qbb, s16b, slb])
                        kd_, kdb = kd.get()
                        S_.add("pool", (lambda kd_, k_: lambda e: e.tensor_tensor(out=kd_[:], in0=k_[:], in1=tabs["KF"][:], op=ALU.mult))(kd_, k_),
                               R=[kb_, tb_], W=[kdb])
                        pS, pSb = psF.get()
                        mm_group(pS, pSb, [(pS[:, hs(h)], kd_[:, hs(h)], v_[:, hs(h)], True, True) for h in range(4)], [kdb, vb_])
                        S_.add("pool", lambda e: e.tensor_tensor(out=S32[:], in0=S32[:], in1=tabs["CF"][:], op=ALU.mult), R=[S32b, tb_], W=[S32b])
                        S_.add("dve", (lambda pS: lambda e: e.tensor_tensor(out=S32[:], in0=S32[:], in1=pS[:], op=ALU.add))(pS), R=[S32b, pSb], W=[S32b])
                        stv[0], stv[1] = S16.get()
                        S_.add("act", (lambda s16: lambda e: e.copy(out=s16[:], in_=S32[:]))(stv[0]), R=[S32b], W=[stv[1]])
                        tick[0] = k + 1
                        yield
                        st6, st6b = gst.get()
                        mv, mvb = gmv.get()
                        rs, rsb = grs.get()

                        def bns(e, st6=st6, po=po):
                            for h in range(4):
                                ins = e.bn_stats(out=st6[:, h, :], in_=po[:, h * 128:(h + 1) * 128])
                            return ins
                        S_.add("dve", bns, R=[pob], W=[st6b])

                        def bna(e, st6=st6, mv=mv):
                            for h in range(4):
                                ins = e.bn_aggr(out=mv[:, h, :], in_=st6[:, h, :])
                            return ins
                        S_.add("dve", bna, R=[st6b], W=[mvb])
                        S_.add("act", (lambda rs, mv: lambda e: e.activation(out=rs[:, 0:4], in_=mv[:, :, 1], func=AF.Sqrt, bias=epsN[:, 1:2], scale=1.0))(rs, mv),
                               R=[mvb, consts], W=[rsb])
                        S_.add("dve", (lambda rs: lambda e: e.reciprocal(out=rs[:, 4:8], in_=rs[:, 0:4]))(rs), R=[rsb], W=[rsb])
                        on_, onb = on.get()

                        def gnn(e, on_=on_, po=po, mv=mv, rs=rs):
                            for h in range(4):
                                ins = e.tensor_scalar(out=on_[:, h * 128:(h + 1) * 128], in0=po[:, h * 128:(h + 1) * 128], scalar1=mv[:, h, 0:1],
                                                      scalar2=rs[:, 4 + h:5 + h], op0=ALU.subtract, op1=ALU.mult)
                            return ins
                        S_.add("dve", gnn, R=[pob, mvb, rsb], W=[onb])
                        yield
                        S_.add("pool", (lambda ya_, on_, sg_: lambda e: e.tensor_tensor(out=ya_[:, 512:1024], in0=on_[:], in1=sg_[:], op=ALU.mult))(ya_, on_, sg_),
                               R=[onb, sgb], W=[yab])
                        pt2, pt2b = psB.get()

                        def try_(e, pt2=pt2, ya_=ya_):
                            for f in range(8):
                                ins = e.transpose(out=pt2[:, f * 128:(f + 1) * 128], in_=ya_[:, f * 128:(f + 1) * 128], identity=ident[:])
                            return ins
                        S_.add("pe", try_, R=[yab, consts], W=[pt2b])
                        yT_, yTb = yT.get()
                        S_.add("act", (lambda yT_, pt2: lambda e: e.copy(out=yT_[:], in_=pt2[:].rearrange("p (k t) -> p k t", k=8)))(yT_, pt2),
                               R=[pt2b], W=[yTb])
                        x1_, x1b = x1t.get()
                        for n in range(2):
                            pp, ppb = psF.get()
                            mm_group(pp, ppb, [(pp[:], yT_[:, f, :], w_out[:, f, n * 512:(n + 1) * 512], f == 0, f == 7) for f in range(8)], [yTb, wb_out])
                            S_.add("dve", (lambda x1_, pp, x, n: lambda e: e.tensor_tensor(out=x1_[:, n * 512:(n + 1) * 512], in0=pp[:],
                                                                                          in1=x[:, n * 512:(n + 1) * 512], op=ALU.add))(x1_, pp, x, n),
                                   R=[ppb, xb], W=[x1b])
                        S_.add("sp", (lambda x1_, c: lambda e: e.dma_start(out=sc["x1"][c * 128:(c + 1) * 128, :], in_=x1_[:]))(x1_, c),
                               R=[x1b], W=[S_.dbuf("x1", jn, c)], chan="st%d" % (c % 2))
                        if debug:
                            S_.add("sp", (lambda x1_, c: lambda e: e.dma_start(out=dbg[jn]["x1"][c * 128:(c + 1) * 128, :], in_=x1_[:]))(x1_, c),
                                   R=[x1b], W=[S_.dbuf("dx1", jn, c)], chan="dbg")
                        yield
                    staggered(main_chunk, list(range(NCH)), 6)
                for job in jobs:
                    run_job(job)
                S_.flush(); chk(2)

            def ffn_phase(layer, final):
                with ExitStack() as ps:
                    psF = Pool(ps, pst, "psF", [128, 512], F32, 6)
                    psB = Pool(ps, pst, "psB", [128, 1024], BF16, 2)
                    wg = ps.enter_context(sb("wg", [128, 8, DFF], BF16))
                    wu = ps.enter_context(sb("wu", [128, 8, DFF], BF16))
                    wd = ps.enter_context(sb("wd", [128, NFC, 1024], BF16))
                    with ExitStack() as ws:
                        stg = Pool(ws, sb, "wstg", [128, 1024], F32, 3)
                        sidx = [0]
                        nr = lambda kc: nffn[:, layer * 8 + kc:layer * 8 + kc + 1]
                        wbg = load_weight(wg, w_g_d[layer], 8, DFF, nr, stg, sidx)
                        wbu = load_weight(wu, w_u_d[layer], 8, DFF, nr, stg, sidx)
                        wbd = load_weight(wd, w_d_d[layer], NFC, 1024, None, stg, sidx)
                        S_.flush(); chk(3)
                    nt = NormT(ps, psB)
                    xin = Pool(ps, sb, "xin", [128, 1024], F32, 2)
                    xres = Pool(ps, sb, "xres", [128, 1024], F32, 2)
                    xnT = Pool(ps, sb, "xnT", [128, 8, 512], BF16, 2)
                    hT = Pool(ps, sb, "hT", [128, NFC, 512], BF16, 1)
                    sgp = Pool(ps, sb, "sgp", [128, 512], F32, 2)
                    nf = None
                    if final:
                        nf = ps.enter_context(sb("nf", [128, 1024], F32))
                        S_.add("sp", lambda e: e.dma_start(out=nf[:], in_=nfin_d.partition_broadcast(128)), W=[consts], chan="set")
                        fj = Pool(ps, sb, "fj", [128, 1024], BF16, 1)
                        fst = Pool(ps, sb, "fst", [128, 4], F32, 2)
                    def run_job(job):
                        jn = job["n"]
                        S = job["SQ"] if final else job["S"]
                        sc = scr[jn]
                        src, srcn = (sc["x3"], "x3") if final else (sc["x1"], "x1")
                        dst, dstn = (y_d[jn], "y") if final else (sc["x2"], "x2")
                        for t in range(S // 512):
                            xt_, xtb = xnT.get()
                            for ci in range(4):
                                c = t * 4 + ci
                                x, xb = xin.get()
                                S_.add("sp", (lambda x, c: lambda e: e.dma_start(out=x[:], in_=src[c * 128:(c + 1) * 128, :]))(x, c),
                                       R=[S_.dbuf(srcn, jn, c)], W=[xb], chan="xl%d" % (c % 2))
                                nt.run(x[:], xb, xt_[:, :, ci * 128:(ci + 1) * 128], xtb)
                            h_, hb = hT.get()
                            for f in range(NFC):
                                pg, pgb = psF.get()
                                mm_group(pg, pgb, [(pg[:], wg[:, kc, f * 128:(f + 1) * 128], xt_[:, kc, :], kc == 0, kc == 7) for kc in range(8)], [xtb, wbg])
                                pu, pub = psF.get()
                                mm_group(pu, pub, [(pu[:], wu[:, kc, f * 128:(f + 1) * 128], xt_[:, kc, :], kc == 0, kc == 7) for kc in range(8)], [xtb, wbu])
                                s_, sb_ = sgp.get()
                                S_.add("act", (lambda s_, pg: lambda e: e.activation(out=s_[:], in_=pg[:], func=AF.Silu))(s_, pg), R=[pgb], W=[sb_])
                                S_.add("dve", (lambda h_, s_, pu, f: lambda e: e.tensor_tensor(out=h_[:, f, :], in0=s_[:], in1=pu[:], op=ALU.mult))(h_, s_, pu, f),
                                       R=[sb_, pub], W=[hb])
                            for ci in range(4):
                                c = t * 4 + ci
                                xr, xrb = xres.get()
                                S_.add("sp", (lambda xr, c: lambda e: e.dma_start(out=xr[:], in_=src[c * 128:(c + 1) * 128, :]))(xr, c),
                                       R=[S_.dbuf(srcn, jn, c)], W=[xrb], chan="rl%d" % (c % 2))
                                for n in range(2):
                                    pp, ppb = psF.get()
                                    mm_group(pp, ppb, [(pp[:], h_[:, f, ci * 128:(ci + 1) * 128], wd[:, f, n * 512:(n + 1) * 512], f == 0, f == NFC - 1)
                                                       for f in range(NFC)], [hb, wbd])
                                    S_.add("pool" if False else "dve", (lambda xr, pp, n: lambda e: e.tensor_tensor(out=xr[:, n * 512:(n + 1) * 512], in0=pp[:],
                                                                                                                   in1=xr[:, n * 512:(n + 1) * 512], op=ALU.add))(xr, pp, n),
                                           R=[ppb, xrb], W=[xrb])
                                if final:
                                    jk, jkb = fj.get()
                                    st, stb = fst.get()
                                    S_.add("act", (lambda jk, xr, st: lambda e: e.activation(out=jk[:], in_=xr[:], func=AF.Square, accum_out=st[:, 0:1]))(jk, xr, st),
                                           R=[xrb], W=[jkb, stb])
                                    S_.add("act", (lambda st: lambda e: e.activation(out=st[:, 1:2], in_=st[:, 0:1], func=AF.Sqrt, bias=epsN[:, 0:1], scale=1.0 / D))(st),
                                           R=[stb, consts], W=[stb])
                                    S_.add("dve", (lambda st: lambda e: e.reciprocal(out=st[:, 2:3], in_=st[:, 1:2]))(st), R=[stb], W=[stb])
                                    S_.add("dve", (lambda xr, st: lambda e: e.scalar_tensor_tensor(out=xr[:], in0=xr[:], scalar=st[:, 2:3], in1=nf[:],
                                                                                                    op0=ALU.mult, op1=ALU.mult))(xr, st),
                                           R=[xrb, stb, consts], W=[xrb])
                                S_.add("sp", (lambda xr, c: lambda e: e.dma_start(out=dst[c * 128:(c + 1) * 128, :], in_=xr[:]))(xr, c),
                                       R=[xrb], W=[S_.dbuf(dstn, jn, c)], chan="st%d" % (c % 2))
                                if debug and not final:
                                    S_.add("sp", (lambda xr, c: lambda e: e.dma_start(out=dbg[jn]["x2"][c * 128:(c + 1) * 128, :], in_=xr[:]))(xr, c),
                                           R=[xrb], W=[S_.dbuf("dx2", jn, c)], chan="dbg")
                    for job in jobs:
                        run_job(job)
                    S_.flush(); chk(4)

            ffn_phase(0, False)

            with ExitStack() as ps:
                psF = Pool(ps, pst, "psF", [128, 512], F32, 6)
                psB = Pool(ps, pst, "psB", [128, 1024], BF16, 2)
                wqkv = ps.enter_context(sb("wqkv", [128, 8, 3072], BF16))
                with ExitStack() as ws:
                    stg = Pool(ws, sb, "wstg", [128, 1024], F32, 3)
                    sidx = [0]
                    wbq = load_weight(wqkv, w_qkv_d, 8, 3072, lambda kc: nmix[:, 8 + kc:9 + kc], stg, sidx)
                    S_.flush(); chk(5)
                nt = NormT(ps, psB, 3)
                xin = Pool(ps, sb, "xin", [128, 1024], F32, 4)
                xnT = Pool(ps, sb, "xnT", [128, 8, 128], BF16, 3)
                ropt = Pool(ps, sb, "ropt", [128, 64], F32, 4)
                rA = Pool(ps, sb, "rA", [128, 1024], F32, 3)
                rT = Pool(ps, sb, "rT", [128, 1024], F32, 3)
                rr = Pool(ps, sb, "rr", [128, 1024], BF16, 3)
                vS = Pool(ps, sb, "vS", [128, 1024], BF16, 3)
                stT = Pool(ps, sb, "stT", [128, 8, 512], BF16, 3)

                def qk_path(x_src_fn, R_x, rope_src, S, c0, dstT, dstname, jn, col0, with_v, vdst):
                    st_, stb = stT.get()
                    for ci in range(4):
                        c = c0 + ci
                        x, xb = xin.get()
                        x_src_fn(x, xb, c)
                        rt_, rtb = ropt.get()
                        S_.add("sp", (lambda t, c: lambda e: e.dma_start(out=t[:], in_=rope_src[c * 128:(c + 1) * 128, :]))(rt_, c), W=[rtb], chan="rl%d" % (c % 3))
                        yield
                        xt_, xtb = xnT.get()
                        nt.run(x[:], xb, xt_[:], xtb)
                        yield
                        pq = [psF.get(), psF.get()]
                        for n in range(2):
                            mm_group(pq[n][0], pq[n][1], [(pq[n][0][:], xt_[:, kc, :], wqkv[:, kc, col0 + n * 512:col0 + (n + 1) * 512], kc == 0, kc == 7)
                                                          for kc in range(8)], [xtb, wbq])
                        yield
                        A, Ab = rA.get()
                        T, Tb = rT.get()
                        r_, rb_ = rr.get()
                        for n in range(2):
                            sl = slice(n * 512, (n + 1) * 512)
                            rope("dve", "pool" if False else "dve", pq[n][0][:], pq[n][1], 8, 32, rt_[:, 0:32], rt_[:, 32:64], rtb, A[:, sl], Ab, T[:, sl], Tb, r_[:, sl], rb_)
                        pt, ptb = psB.get()

                        def tr(e, pt=pt, r_=r_):
                            for hp in range(8):
                                ins = e.transpose(out=pt[:, hp * 128:(hp + 1) * 128], in_=r_[:, hp * 128:(hp + 1) * 128], identity=ident[:])
                            return ins
                        S_.add("pe", tr, R=[rb_, consts], W=[ptb])
                        S_.add("act", (lambda st_, pt, ci: lambda e: e.copy(out=st_[:, :, ci * 128:(ci + 1) * 128], in_=pt[:].rearrange("p (k t) -> p k t", k=8)))(st_, pt, ci),
                               R=[ptb], W=[stb])
                        yield
                        if with_v:
                            pv = [psF.get(), psF.get()]
                            v_, vb_ = vS.get()
                            for n in range(2):
                                mm_group(pv[n][0], pv[n][1], [(pv[n][0][:], xt_[:, kc, :], wqkv[:, kc, 2048 + n * 512:2048 + (n + 1) * 512], kc == 0, kc == 7)
                                                              for kc in range(8)], [xtb, wbq])
                                S_.add("act", (lambda v_, p, n: lambda e: e.copy(out=v_[:, n * 512:(n + 1) * 512], in_=p[:]))(v_, pv[n][0], n), R=[pv[n][1]], W=[vb_])
                            S_.add("sp", (lambda v_, c: lambda e: e.dma_start(out=vdst[:, c * 128:(c + 1) * 128, :].rearrange("h t e -> t h e"),
                                                                              in_=v_[:].rearrange("p (h e) -> p h e", h=8)))(v_, c),
                                   R=[vb_], W=[S_.dbuf("V", jn, c)], chan="st%d" % (c % 2))
                    S_.add("sp", (lambda st_, c0: lambda e: e.dma_start(out=dstT[:, :, c0 * 128:c0 * 128 + 512].rearrange("h p t -> p h t"), in_=st_[:]))(st_, c0),
                           R=[stb], W=[S_.dbuf(dstname, jn, c0 // 4)], chan="st%d" % ((c0 // 4) % 2))

                def run_job(job):
                    jn, S, SQ = job["n"], job["S"], job["SQ"]
                    sc = scr[jn]

                    def src_plain(x, xb, c, sc=sc, jn=jn):
                        S_.add("sp", (lambda x, c: lambda e: e.dma_start(out=x[:], in_=sc["x2"][c * 128:(c + 1) * 128, :]))(x, c),
                               R=[S_.dbuf("x2", jn, c)], W=[xb], chan="xl%d" % (c % 3))

                    def src_gather(x, xb, c, sc=sc, jn=jn, S=S):
                        S_.add("pool", (lambda x, c: lambda e: e.indirect_dma_start(out=x[:, :], out_offset=None, in_=sc["x2"][:, :],
                                                                                    in_offset=bass.IndirectOffsetOnAxis(ap=idxq[:, c:c + 1], axis=0)))(x, c),
                               R=[S_.dbuf("x2", jn, cc) for cc in range(S // 128)] + [consts], W=[xb], chan="gl%d" % (c % 3))
                    staggered(lambda k, t: qk_path(src_plain, None, dtk_d, S, t * 4, sc["KT"], "KT", jn, 1024, True, sc["V"]),
                              list(range(S // 512)), 5, 3)
                    staggered(lambda k, t: qk_path(src_gather if job["own"] else src_plain, None, dtq_d[jn], SQ, t * 4, sc["QT"], "QT", jn, 0, False, None),
                              list(range(SQ // 512)), 4, 3)
                for job in jobs:
                    run_job(job)
                S_.flush(); chk(6)

            with ExitStack() as ps:
                psS = Pool(ps, pst, "psS", [128, 1024], F32, 2)
                psO = Pool(ps, pst, "psO", [128, 2, 256], F32, 4)
                SMX = max(SA, SB)
                KTt = Pool(ps, sb, "KTt", [128, SMX], BF16, 2)
                Vt = Pool(ps, sb, "Vt", [128, SMX // 128, 130], BF16, 2)
                QTt = Pool(ps, sb, "QTt", [128, max(SA, SQB)], BF16, 2)
                PT = Pool(ps, sb, "PT", [128, 1024], BF16, 3)
                rc = Pool(ps, sb, "rc", [128, 8], F32, 4)
                o1 = Pool(ps, sb, "o1", [128, 128], F32, 3)
                oj = Pool(ps, sb, "oj", [128, 128], F32, 2)
                ob = Pool(ps, sb, "ob", [128, 4, 128], BF16, 2)
                for i in range(2):
                    S_.add("pool", (lambda t: lambda e: e.memset(t[:, :, 128:130], 1.0))(Vt.t[i]), W=[Vt.b[i]])
                def run_job(job):
                    jn, S, SQ = job["n"], job["S"], job["SQ"]
                    sc = scr[jn]
                    NKT = S // 128
                    for h in range(8):
                        kt_, ktb = KTt.get()
                        S_.add("sp", (lambda kt_, h: lambda e: e.dma_start(out=kt_[:, 0:S], in_=sc["KT"][h, :, :]))(kt_, h),
                               R=[S_.dbuf("KT", jn, t) for t in range(S // 512)], W=[ktb], chan="kl%d" % (h % 2))
                        v_, vb_ = Vt.get()
                        VP = min(32, NKT)
                        for part in range(0, NKT, VP):
                            S_.add("sp", (lambda v_, h, part: lambda e: e.dma_start(out=v_[:, part:part + VP, 0:128],
                                                                                   in_=sc["V"][h, part * 128:(part + VP) * 128, :].rearrange("(k p) e -> p k e", p=128)))(v_, h, part),
                                   R=[S_.dbuf("V", jn, c) for c in range(part, min(part + VP, NKT))], W=[vb_], chan="vl%d" % (h % 2))
                        q_, qb_ = QTt.get()
                        S_.add("sp", (lambda q_, h: lambda e: e.dma_start(out=q_[:, 0:SQ], in_=sc["QT"][h, :, :]))(q_, h),
                               R=[S_.dbuf("QT", jn, t) for t in range(SQ // 512)], W=[qb_], chan="ql%d" % (h % 2))
                        for qt in range(SQ // 512):
                            acc = [psO.get(), psO.get(), psO.get(), psO.get()]
                            qs = slice(qt * 512, (qt + 1) * 512)
                            def qk(kt, qs=qs):
                                ks = slice(kt * 128, (kt + 1) * 128)
                                pS_, pSb = psS.get()
                                mm_group(pS_, pSb, [(pS_[:, 0:512], kt_[0:64, ks], q_[0:64, qs], True, True),
                                                    (pS_[:, 512:1024], kt_[64:128, ks], q_[64:128, qs], True, True)], [ktb, qb_])
                                return pS_, pSb
                            pend = qk(0)
                            for kt in range(NKT):
                                pS_, pSb = pend
                                if kt + 1 < NKT:
                                    pend = qk(kt + 1)
                                p_, pb_ = PT.get()
                                S_.add("act", (lambda p_, pS_: lambda e: e.activation(out=p_[:], in_=pS_[:], func=AF.Exp))(p_, pS_), R=[pSb], W=[pb_])
                                for sub in range(2):
                                    for pair in range(2):
                                        a_, ab_ = acc[sub * 2 + pair]
                                        mm_group(a_, ab_, [(a_[:, j, 0:129], p_[:, sub * 512 + (pair * 2 + j) * 128: sub * 512 + (pair * 2 + j + 1) * 128],
                                                            v_[:, kt, 0:129], kt == 0 and j == 0, kt == NKT - 1) for j in range(2)], [pb_, vb_])
                            ob_, obb = ob.get()
                            for qb4 in range(4):
                                pair, j = qb4 // 2, qb4 % 2
                                a0, a0b = acc[0 + pair]
                                a1, a1b = acc[2 + pair]
                                r_, rb2 = rc.get()
                                S_.add("dve", (lambda r_, a0, j: lambda e: e.reciprocal(out=r_[:, 0:1], in_=a0[:, j, 128:129]))(r_, a0, j), R=[a0b], W=[rb2])
                                S_.add("dve", (lambda r_, a1, j: lambda e: e.reciprocal(out=r_[:, 1:2], in_=a1[:, j, 128:129]))(r_, a1, j), R=[a1b, rb2], W=[rb2])
                                S_.add("dve", (lambda r_: lambda e: e.tensor_tensor(out=r_[:, 2:3], in0=r_[:, 1:2], in1=lamt[:, 0:1], op=ALU.mult))(r_),
                                       R=[rb2, consts], W=[rb2])
                                o_, o_b = o1.get()
                                S_.add("dve", (lambda o_, a0, j, r_: lambda e: e.tensor_scalar(out=o_[:], in0=a0[:, j, 0:128], scalar1=r_[:, 0:1], scalar2=None,
                                                                                              op0=ALU.mult))(o_, a0, j, r_), R=[a0b, rb2], W=[o_b])
                                S_.add("dve", (lambda o_, a1, j, r_: lambda e: e.scalar_tensor_tensor(out=o_[:], in0=a1[:, j, 0:128], scalar=r_[:, 2:3], in1=o_[:],
                                                                                                     op0=ALU.mult, op1=ALU.add))(o_, a1, j, r_),
                                       R=[a1b, rb2, o_b], W=[o_b])
                                jk, jkb = oj.get()
                                S_.add("dve", (lambda jk, o_: lambda e: e.tensor_tensor(out=jk[:], in0=o_[:], in1=o_[:], op=ALU.mult))(jk, o_),
                                       R=[o_b], W=[jkb])
                                S_.add("dve", (lambda jk, r_: lambda e: e.reduce_sum(out=r_[:, 3:4], in_=jk[:], axis=AX.X))(jk, r_), R=[jkb, rb2], W=[rb2])
                                S_.add("act", (lambda r_: lambda e: e.activation(out=r_[:, 4:5], in_=r_[:, 3:4], func=AF.Ln, bias=epsN[:, 1:2], scale=1.0 / 128))(r_),
                                       R=[rb2, consts], W=[rb2])
                                S_.add("act", (lambda r_: lambda e: e.activation(out=r_[:, 5:6], in_=r_[:, 4:5], func=AF.Exp, scale=-0.5))(r_),
                                       R=[rb2], W=[rb2])
                                S_.add("pool", (lambda ob_, o_, r_, qb4: lambda e: e.tensor_scalar(out=ob_[:, qb4, :], in0=o_[:], scalar1=r_[:, 5:6], scalar2=None,
                                                                                                  op0=ALU.mult))(ob_, o_, r_, qb4), R=[o_b, rb2], W=[obb])
                            S_.add("sp", (lambda ob_, qt, h: lambda e: e.dma_start(
                                out=sc["O"][qt * 512:(qt + 1) * 512, h * 128:(h + 1) * 128].rearrange("(k p) e -> p k e", p=128), in_=ob_[:]))(ob_, qt, h),
                                   R=[obb], W=[S_.dbuf("O", jn, qt, h)], chan="st%d" % (qt % 2))
                for job in jobs:
                    run_job(job)
                S_.flush(); chk(7)

            with ExitStack() as ps:
                psF = Pool(ps, pst, "psF", [128, 512], F32, 6)
                psB = Pool(ps, pst, "psB", [128, 1024], BF16, 2)
                wdo = ps.enter_context(sb("wdo", [128, 8, 1024], BF16))
                with ExitStack() as ws:
                    stg = Pool(ws, sb, "wstg", [128, 1024], F32, 3)
                    sidx = [0]
                    wbo = load_weight(wdo, w_do_d, 8, 1024, lambda kc: subl[:, 0:1], stg, sidx)
                    S_.flush(); chk(8)
                oin = Pool(ps, sb, "oin", [128, 1024], BF16, 4)
                oT = Pool(ps, sb, "oT", [128, 8, 128], BF16, 3)
                xres = Pool(ps, sb, "xres", [128, 1024], F32, 4)
                def run_job(job):
                    jn, S, SQ = job["n"], job["S"], job["SQ"]
                    sc = scr[jn]
                    def op_chunk(k, c):
                        o_, o_b = oin.get()
                        S_.add("sp", (lambda o_, c: lambda e: e.dma_start(out=o_[:], in_=sc["O"][c * 128:(c + 1) * 128, :]))(o_, c),
                               R=[S_.dbuf("O", jn, c // 4, h) for h in range(8)], W=[o_b], chan="xl%d" % (c % 3))
                        xr, xrb = xres.get()
                        if job["own"]:
                            S_.add("pool", (lambda xr, c: lambda e: e.indirect_dma_start(out=xr[:, :], out_offset=None, in_=sc["x2"][:, :],
                                                                                        in_offset=bass.IndirectOffsetOnAxis(ap=idxq[:, c:c + 1], axis=0)))(xr, c),
                                   R=[consts], W=[xrb], chan="gl%d" % (c % 3))
                        else:
                            S_.add("sp", (lambda xr, c: lambda e: e.dma_start(out=xr[:], in_=sc["x2"][c * 128:(c + 1) * 128, :]))(xr, c), W=[xrb], chan="rl%d" % (c % 3))
                        yield
                        pt, ptb = psB.get()

                        def tr(e, pt=pt, o_=o_):
                            for f in range(8):
                                ins = e.transpose(out=pt[:, f * 128:(f + 1) * 128], in_=o_[:, f * 128:(f + 1) * 128], identity=ident[:])
                            return ins
                        S_.add("pe", tr, R=[o_b, consts], W=[ptb])
                        oT_, oTb = oT.get()
                        S_.add("act", (lambda oT_, pt: lambda e: e.copy(out=oT_[:], in_=pt[:].rearrange("p (k t) -> p k t", k=8)))(oT_, pt), R=[ptb], W=[oTb])
                        yield
                        for n in range(2):
                            pp, ppb = psF.get()
                            mm_group(pp, ppb, [(pp[:], oT_[:, f, :], wdo[:, f, n * 512:(n + 1) * 512], f == 0, f == 7) for f in range(8)], [oTb, wbo])
                            S_.add("dve", (lambda xr, pp, n: lambda e: e.tensor_tensor(out=xr[:, n * 512:(n + 1) * 512], in0=pp[:], in1=xr[:, n * 512:(n + 1) * 512],
                                                                                      op=ALU.add))(xr, pp, n), R=[ppb, xrb], W=[xrb])
                        S_.add("sp", (lambda xr, c: lambda e: e.dma_start(out=sc["x3"][c * 128:(c + 1) * 128, :], in_=xr[:]))(xr, c),
                               R=[xrb], W=[S_.dbuf("x3", jn, c)], chan="st%d" % (c % 2))
                        if debug:
                            S_.add("sp", (lambda xr, c: lambda e: e.dma_start(out=dbg[jn]["x3"][c * 128:(c + 1) * 128, :], in_=xr[:]))(xr, c),
                                   R=[xrb], W=[S_.dbuf("dx3", jn, c)], chan="dbg")
                        yield
                    staggered(op_chunk, list(range(SQ // 128)), 2, 2)
                for job in jobs:
                    run_job(job)
                S_.flush(); chk(9)

            ffn_phase(1, True)

        except _Stop:
            pass
        sch.stopped = False
        sch.maxops = 10 ** 9
        S_.add("sp", lambda e: e.dma_start(out=scr["a"]["u"][0:1, 0:1], in_=scr["a"]["u"][0:1, 1:2]), chan="fin")
        S_.flush(); chk(10)
        fin = S_.ops[-1]

        with nc.Block() as block:
            @block.sync
            def _(e):
                e.wait_ge(S_.csem["fin"], fin.val)
    return nc


def _rope_tabs(S, dim, scale):
    inv = (10000.0 ** (-np.arange(0, dim, 2, dtype=np.float32) / np.float32(dim))).astype(np.float32)
    ang = np.arange(S, dtype=np.float32)[:, None] * inv[None, :]
    return (np.cos(ang) * scale).astype(np.float32), (np.sin(ang) * scale).astype(np.float32)


def make_inputs(core, SA, SB, NQB, inp, xa, xb, qoff):
    f = lambda a: np.ascontiguousarray(np.asarray(a, dtype=np.float32))
    SM = max(SA, SB)
    c1, s1 = _rope_tabs(SM, 128, 1.0)
    c2, s2 = _rope_tabs(SM, 128, 128 ** -0.5)
    rt = np.concatenate([c1, s1, c2, s2], axis=1)
    ck, sk = _rope_tabs(SM, 64, 1.0)
    cq, sq = _rope_tabs(SM, 64, 0.125)
    dtk = np.concatenate([ck, sk], 1)
    dtq = np.concatenate([cq, sq], 1)
    SQB = NQB * 128
    idx = (qoff + np.arange(NQB)[None, :] * 128 + np.arange(128)[:, None]).astype(np.int32)
    pk = lambda v: f(np.asarray(v).reshape(-1, 128).T)
    m = {
        "xa": f(xa), "xb": f(xb), "idxq": np.ascontiguousarray(idx),
        "rt": f(rt), "dtk": f(dtk), "dtqa": f(dtq[:SA]), "dtqb": f(dtq[qoff:qoff + SQB]),
        "w_in": f(inp["hyb_w_in"][0]), "w_out": f(inp["hyb_w_out"][0]), "w_qkv": f(inp["diff_w_qkv"][0]), "w_do": f(inp["diff_w_out"][0]),
        "nmix": pk(np.asarray(inp["norm_mix"]).reshape(-1)), "nffn": pk(np.asarray(inp["norm_ffn"]).reshape(-1)),
        "nfin": f(np.asarray(inp["norm_final"]).reshape(1, D)),
        "convw3": f(np.asarray(inp["hyb_conv_w"][0]).reshape(3, 512)),
        "convw": f(np.asarray(inp["hyb_conv_w"][0]).reshape(3, 4, 128).transpose(2, 1, 0).reshape(128, 12)),
        "decf": f(np.asarray(inp["hyb_decay_fwd"]).reshape(1, 4)), "decb": f(np.asarray(inp["hyb_decay_bwd"]).reshape(1, 4)),
        "gnw": pk(np.asarray(inp["hyb_gn"]).reshape(-1)),
        "lamv": f(np.concatenate([np.asarray(inp[k]).reshape(-1) for k in ("diff_lq1", "diff_lk1", "diff_lq2", "diff_lk2")]).reshape(1, 256)),
        "subln": f(np.asarray(inp["diff_subln"]).reshape(128, 1)),
    }
    for l in range(2):
        m["w_g%d" % l] = f(inp["ffn_w_gate"][l])
        m["w_u%d" % l] = f(inp["ffn_w_up"][l])
        m["w_d%d" % l] = f(inp["ffn_w_down"][l])
    return m


def kernel(**inp):
    xp = np.asarray(inp["x_prompt"])
    xs = np.asarray(inp["x_sample"])
    SA, SB = xs.shape[1], xp.shape[1]
    NQB = SB // 4 // 128
    nc = build(SA, SB, NQB)
    in_maps = [make_inputs(c, SA, SB, NQB, inp, xs[c], xp[c // 4], (c % 4) * (SB // 4)) for c in range(8)]
    res = run_bass_kernel_spmd(nc, in_maps, core_ids=list(range(8)))
    ys = np.stack([res.results[c]["ya"] for c in range(8)], 0).astype(np.float32)
    yp = np.stack([np.concatenate([res.results[g * 4 + r]["yb"] for r in range(4)], 0) for g in range(2)], 0).astype(np.float32)
    return (yp, ys)
```
